# Optimizing a Trainium2 kernel written in Bass

```python
import jax
import jax.numpy as jnp
from jax import lax
import numpy as np

D_MODEL = 2048
BATCH = 1
SEQ = 8192
DEPTH = 4
DEC_BATCH = 4
DEC_SEQ = 4096
PAST_LEN = 128

N_MIXERS = 3
HEAD_DIM = 128
NORM_EPS = 1e-6
A_HEADS = 16
A_KV_HEADS = 4
A_WINDOW = 128
B_HEADS = 16
B_GROUPS = ((128, 1), (512, 4), (2048, 16))
C_HEADS = 16
C_Q_RANK = 512
C_KV_RANK = 512
C_NOPE = 128
C_ROPE = 64
C_V = 128
C_QBLOCK = 128
ROPE_THETA = 10000.0
D_FF = 5632
CONV_WIDTH = 3

kernel_name = 'hybrid_bidir_encoder_swa_dilated_mla_convglu'


def _rmsnorm(x, g):
    xf = x.astype(jnp.float32)
    y = xf * lax.rsqrt(jnp.mean(xf * xf, axis=-1, keepdims=True) + NORM_EPS)
    return (y * g.astype(jnp.float32)).astype(x.dtype)


def _alibi_slopes(n):
    return jnp.power(2.0, -8.0 * jnp.arange(1, n + 1, dtype=jnp.float32) / n)


def _band_blocks(t, blk):
    n, lp = t.shape[0], t.shape[1]
    nb = lp // blk
    pad = [(0, 0), (blk, blk)] + [(0, 0)] * (t.ndim - 2)
    tp = jnp.pad(t, pad).reshape((n, nb + 2, blk) + t.shape[2:])
    return jnp.concatenate([tp[:, :-2], tp[:, 1:-1], tp[:, 2:]], axis=2)


def _banded_attention(q, k, v, half, slopes, step, sink=None):
    n, length, kvh, g, dh = q.shape
    blk = half
    lp = -(-length // blk) * blk
    if lp != length:
        pw = lp - length
        q = jnp.pad(q, [(0, 0), (0, pw), (0, 0), (0, 0), (0, 0)])
        k = jnp.pad(k, [(0, 0), (0, pw), (0, 0), (0, 0)])
        v = jnp.pad(v, [(0, 0), (0, pw), (0, 0), (0, 0)])
    nb = lp // blk
    qb = q.reshape(n, nb, blk, kvh, g, dh)
    kb = _band_blocks(k, blk)
    vb = _band_blocks(v, blk)
    s = jnp.einsum('nbqkgd,nbjkd->nbkgqj', qb, kb).astype(jnp.float32) * (dh ** -0.5)
    qi = jnp.arange(blk)[:, None]
    kj = jnp.arange(3 * blk)[None, :]
    rel = qi + blk - kj
    kpos = jnp.arange(nb)[:, None, None] * blk + kj[None] - blk
    valid = (jnp.abs(rel) <= half)[None] & (kpos >= 0) & (kpos < length)
    bias = -(slopes.astype(jnp.float32) * step)[:, :, None, None] * jnp.abs(rel).astype(jnp.float32)
    s = jnp.where(valid[None, :, None, None], s + bias, -jnp.inf)
    if sink is not None:
        sink_col = jnp.broadcast_to(sink.astype(jnp.float32)[:, :, None, None], s.shape[:-1] + (1,))
        s = jnp.concatenate([s, sink_col], axis=-1)
    lse = jax.nn.logsumexp(s, axis=-1)
    p = jnp.exp(s - lse[..., None])
    if sink is not None:
        p = p[..., :-1]
    out = jnp.einsum('nbkgqj,nbjkd->nbqkgd', p.astype(v.dtype), vb).reshape(n, lp, kvh, g, dh)[:, :length]
    lse = jnp.transpose(lse, (0, 1, 4, 2, 3)).reshape(n, lp, kvh, g)[:, :length]
    return out, lse


def _mixer_a(h, w_qkv, sink, w_o):
    b, s, _ = h.shape
    g = A_HEADS // A_KV_HEADS
    nq, nk = A_HEADS * HEAD_DIM, A_KV_HEADS * HEAD_DIM
    qkv = h @ w_qkv
    q = qkv[..., :nq].reshape(b, s, A_KV_HEADS, g, HEAD_DIM)
    k = qkv[..., nq:nq + nk].reshape(b, s, A_KV_HEADS, HEAD_DIM)
    v = qkv[..., nq + nk:].reshape(b, s, A_KV_HEADS, HEAD_DIM)
    slopes = _alibi_slopes(A_HEADS).reshape(A_KV_HEADS, g)
    out, _ = _banded_attention(q, k, v, A_WINDOW, slopes, 1, sink.reshape(A_KV_HEADS, g))
    return out.reshape(b, s, nq) @ w_o


def _to_strided(t, dil):
    b, s = t.shape[0], t.shape[1]
    t = t.reshape((b, s // dil, dil) + t.shape[2:])
    return jnp.moveaxis(t, 2, 1).reshape((b * dil, s // dil) + t.shape[3:])


def _from_strided(t, b, dil):
    l = t.shape[1]
    t = t.reshape((b, dil, l) + t.shape[2:])
    return jnp.moveaxis(t, 1, 2).reshape((b, l * dil) + t.shape[3:])


def _mixer_b(h, w_qkv, w_o):
    b, s, _ = h.shape
    qkv = (h @ w_qkv).reshape(b, s, len(B_GROUPS), 3, B_HEADS, HEAD_DIM)
    slopes = _alibi_slopes(B_HEADS).reshape(B_HEADS, 1)
    outs, lses = [], []
    for gi, (window, dil) in enumerate(B_GROUPS):
        half = (window // 2) // dil
        q = _to_strided(qkv[:, :, gi, 0], dil)
        k = _to_strided(qkv[:, :, gi, 1], dil)
        v = _to_strided(qkv[:, :, gi, 2], dil)
        o, lse = _banded_attention(q[:, :, :, None], k, v, half, slopes, dil)
        outs.append(_from_strided(o[:, :, :, 0], b, dil))
        lses.append(_from_strided(lse[..., 0], b, dil))
    alpha = jax.nn.softmax(jnp.stack(lses), axis=0)
    out = jnp.einsum('gbsh,gbshd->bshd', alpha, jnp.stack(outs).astype(jnp.float32))
    return out.astype(h.dtype).reshape(b, s, B_HEADS * HEAD_DIM) @ w_o


def _rope_tables(s):
    pos = jnp.arange(s, dtype=jnp.float32)
    inv = jnp.power(ROPE_THETA, -jnp.arange(0, C_ROPE, 2, dtype=jnp.float32) / C_ROPE)
    ang = pos[:, None] * inv[None, :]
    return jnp.cos(ang), jnp.sin(ang)


def _apply_rope(x, cos, sin):
    xf = x.astype(jnp.float32)
    x1, x2 = xf[..., :C_ROPE // 2], xf[..., C_ROPE // 2:]
    c, sn = cos[None, :, None, :], sin[None, :, None, :]
    return jnp.concatenate([x1 * c - x2 * sn, x2 * c + x1 * sn], axis=-1).astype(x.dtype)


def _mixer_c(h, w_down, q_norm, kv_norm, w_uq, w_ukv, w_o):
    b, s, _ = h.shape
    c = h @ w_down
    cq = _rmsnorm(c[..., :C_Q_RANK], q_norm)
    ckv = _rmsnorm(c[..., C_Q_RANK:C_Q_RANK + C_KV_RANK], kv_norm)
    cos, sin = _rope_tables(s)
    k_rope = _apply_rope(c[..., C_Q_RANK + C_KV_RANK:][:, :, None, :], cos, sin)[:, :, 0]
    q = (cq @ w_uq).reshape(b, s, C_HEADS, C_NOPE + C_ROPE)
    kv = (ckv @ w_ukv).reshape(b, s, C_HEADS, C_NOPE + C_V)
    q_nope = q[..., :C_NOPE]
    q_rope = _apply_rope(q[..., C_NOPE:], cos, sin)
    k_nope, v = kv[..., :C_NOPE], kv[..., C_NOPE:]
    scale = (C_NOPE + C_ROPE) ** -0.5
    nb = s // C_QBLOCK

    def to_blocks(t):
        return jnp.moveaxis(t.reshape((b, nb, C_QBLOCK) + t.shape[2:]), 1, 0)

    def attend(blk):
        qn, qr = blk
        sc = (jnp.einsum('bqhd,bkhd->bhqk', qn, k_nope).astype(jnp.float32)
              + jnp.einsum('bqhr,bkr->bhqk', qr, k_rope).astype(jnp.float32)) * scale
        p = jax.nn.softmax(sc, axis=-1).astype(v.dtype)
        return jnp.einsum('bhqk,bkhd->bqhd', p, v)

    o = lax.map(attend, (to_blocks(q_nope), to_blocks(q_rope)))
    o = jnp.moveaxis(o, 0, 1).reshape(b, s, C_HEADS * C_V)
    return o @ w_o


def _conv_glu(x, norm, w_in, conv_w, conv_b, w_out):
    h = _rmsnorm(x, norm)
    u = h @ w_in
    a, val = u[..., :D_FF], u[..., D_FF:]
    a = lax.conv_general_dilated(
        a, conv_w[:, None, :].astype(a.dtype), window_strides=(1,),
        padding=((CONV_WIDTH // 2, CONV_WIDTH // 2),),
        dimension_numbers=('NWC', 'WIO', 'NWC'), feature_group_count=D_FF) + conv_b.astype(a.dtype)
    return (jax.nn.gelu(a, approximate=False) * val) @ w_out


def _trunk(x, layers, final_norm):
    mixers = (_mixer_a, _mixer_b, _mixer_c)
    for i in range(DEPTH):
        mix_norm, mix_params, ffn_params = layers[i]
        x = x + mixers[i % N_MIXERS](_rmsnorm(x, mix_norm), *mix_params)
        x = x + _conv_glu(x, *ffn_params)
    return _rmsnorm(x, final_norm)


def _dense(key, fan_in, fan_out):
    return jax.random.normal(key, (fan_in, fan_out), jnp.float32) * fan_in ** -0.5


def _gain(key, n):
    return 1.0 + 0.01 * jax.random.normal(key, (n,), jnp.float32)


def setup_inputs(seed: int = 0) -> dict:
    key = jax.random.key(seed)
    ks = jax.random.split(key, 64)
    cnt = [0]

    def nk():
        cnt[0] += 1
        return ks[cnt[0] - 1]

    p = {}
    p['x_prompt'] = jax.random.normal(nk(), (BATCH, SEQ, D_MODEL), jnp.float32)
    p['x_sample'] = jax.random.normal(nk(), (DEC_BATCH, DEC_SEQ, D_MODEL), jnp.float32)
    for i in range(DEPTH):
        pre = 'l%d_' % i
        kind = i % N_MIXERS
        p[pre + 'mix_norm'] = _gain(nk(), D_MODEL)
        if kind == 0:
            p[pre + 'a_w_qkv'] = _dense(nk(), D_MODEL, (A_HEADS + 2 * A_KV_HEADS) * HEAD_DIM)
            p[pre + 'a_sink'] = 0.5 * jax.random.normal(nk(), (A_HEADS,), jnp.float32)
            p[pre + 'a_w_o'] = _dense(nk(), A_HEADS * HEAD_DIM, D_MODEL)
        elif kind == 1:
            p[pre + 'b_w_qkv'] = _dense(nk(), D_MODEL, len(B_GROUPS) * 3 * B_HEADS * HEAD_DIM)
            p[pre + 'b_w_o'] = _dense(nk(), B_HEADS * HEAD_DIM, D_MODEL)
        else:
            p[pre + 'c_w_down'] = _dense(nk(), D_MODEL, C_Q_RANK + C_KV_RANK + C_ROPE)
            p[pre + 'c_q_norm'] = _gain(nk(), C_Q_RANK)
            p[pre + 'c_kv_norm'] = _gain(nk(), C_KV_RANK)
            p[pre + 'c_w_uq'] = _dense(nk(), C_Q_RANK, C_HEADS * (C_NOPE + C_ROPE))
            p[pre + 'c_w_ukv'] = _dense(nk(), C_KV_RANK, C_HEADS * (C_NOPE + C_V))
            p[pre + 'c_w_o'] = _dense(nk(), C_HEADS * C_V, D_MODEL)
        p[pre + 'ffn_norm'] = _gain(nk(), D_MODEL)
        p[pre + 'ffn_w_in'] = _dense(nk(), D_MODEL, 2 * D_FF)
        p[pre + 'ffn_conv_w'] = jax.random.normal(nk(), (CONV_WIDTH, D_FF), jnp.float32) * CONV_WIDTH ** -0.5
        p[pre + 'ffn_conv_b'] = 0.01 * jax.random.normal(nk(), (D_FF,), jnp.float32)
        p[pre + 'ffn_w_out'] = _dense(nk(), D_FF, D_MODEL)
    p['final_norm'] = _gain(nk(), D_MODEL)
    return p


def reference(x_prompt, x_sample,
              l0_mix_norm, l0_a_w_qkv, l0_a_sink, l0_a_w_o,
              l0_ffn_norm, l0_ffn_w_in, l0_ffn_conv_w, l0_ffn_conv_b, l0_ffn_w_out,
              l1_mix_norm, l1_b_w_qkv, l1_b_w_o,
              l1_ffn_norm, l1_ffn_w_in, l1_ffn_conv_w, l1_ffn_conv_b, l1_ffn_w_out,
              l2_mix_norm, l2_c_w_down, l2_c_q_norm, l2_c_kv_norm, l2_c_w_uq, l2_c_w_ukv, l2_c_w_o,
              l2_ffn_norm, l2_ffn_w_in, l2_ffn_conv_w, l2_ffn_conv_b, l2_ffn_w_out,
              l3_mix_norm, l3_a_w_qkv, l3_a_sink, l3_a_w_o,
              l3_ffn_norm, l3_ffn_w_in, l3_ffn_conv_w, l3_ffn_conv_b, l3_ffn_w_out,
              final_norm):
    layers = [
        (l0_mix_norm, (l0_a_w_qkv, l0_a_sink, l0_a_w_o),
         (l0_ffn_norm, l0_ffn_w_in, l0_ffn_conv_w, l0_ffn_conv_b, l0_ffn_w_out)),
        (l1_mix_norm, (l1_b_w_qkv, l1_b_w_o),
         (l1_ffn_norm, l1_ffn_w_in, l1_ffn_conv_w, l1_ffn_conv_b, l1_ffn_w_out)),
        (l2_mix_norm, (l2_c_w_down, l2_c_q_norm, l2_c_kv_norm, l2_c_w_uq, l2_c_w_ukv, l2_c_w_o),
         (l2_ffn_norm, l2_ffn_w_in, l2_ffn_conv_w, l2_ffn_conv_b, l2_ffn_w_out)),
        (l3_mix_norm, (l3_a_w_qkv, l3_a_sink, l3_a_w_o),
         (l3_ffn_norm, l3_ffn_w_in, l3_ffn_conv_w, l3_ffn_conv_b, l3_ffn_w_out)),
    ]
    y_prompt = _trunk(x_prompt, layers, final_norm)
    y_sample = _trunk(x_sample, layers, final_norm)
    return (y_prompt, y_sample)
```

```python
import contextlib
import numpy as np
import ml_dtypes
import concourse.bass as bass
import concourse.mybir as mybir
from concourse.bass_utils import run_bass_kernel_spmd

F32 = mybir.dt.float32
BF16 = mybir.dt.bfloat16
AF = mybir.ActivationFunctionType
ALU = mybir.AluOpType

NCORES = 8


class Op:
    __slots__ = ("eng", "fn", "deps", "is_dma", "signal", "idx", "sem", "val", "prewait", "inc", "force", "raw")

    def __init__(self, eng, fn, is_dma, inc):
        self.eng = eng
        self.fn = fn
        self.deps = set()
        self.is_dma = is_dma
        self.signal = False
        self.sem = None
        self.val = None
        self.prewait = None
        self.inc = inc
        self.force = False
        self.raw = set()


ENGS = ("pe", "act", "dve", "pool", "sp")
SEM_ROT = 12000
N_DMA_SEMS = 12


class Prog:
    def __init__(self, nc, stack):
        self.nc = nc
        self.stack = stack
        self.ops = []
        self.eng_ops = {e: [] for e in ENGS}
        self.last_w = {}
        self.readers = {}
        self.dma_rr = {e: 0 for e in ENGS}
        self.dma_sems = {}
        self.dma_sem_last = {}
        self.eng_sems = {e: [] for e in ENGS}
        self.bar_from = 0
        self.prev_bar = []

    def op(self, eng, fn, reads=(), writes=(), dma=False, inc=16, force=False):
        o = Op(eng, fn, dma, inc)
        o.force = force
        o.idx = len(self.ops)
        for b in reads:
            w = self.last_w.get(b)
            if w is not None:
                o.deps.add(w)
                o.raw.add(w)
        for b in writes:
            w = self.last_w.get(b)
            if w is not None:
                o.deps.add(w)
            for r in self.readers.get(b, ()):
                o.deps.add(r)
        for b in reads:
            self.readers.setdefault(b, []).append(o.idx)
        for b in writes:
            self.last_w[b] = o.idx
            self.readers[b] = []
        o.deps.discard(o.idx)
        self.ops.append(o)
        self.eng_ops[eng].append(o)
        return o

    def barrier(self):
        lasts = []
        for e in ("pe", "act", "dve"):
            for o in reversed(self.eng_ops[e]):
                if o.fn is not None and not o.is_dma:
                    lasts.append(o.idx)
                    break
        dmas = [o.idx for o in self.ops[self.bar_from:] if o.is_dma]
        self.bar_from = len(self.ops)
        prev = list(self.prev_bar)
        self.prev_bar = []
        for e in ENGS:
            o = Op(e, None, False, 0)
            o.idx = len(self.ops)
            o.deps = set(lasts) | set(dmas) | set(prev)
            self.ops.append(o)
            self.eng_ops[e].append(o)
            self.prev_bar.append(o.idx)
        self.last_w = {}
        self.readers = {}

    def finalize(self):
        nc = self.nc
        ops = self.ops
        for o in ops:
            for d in list(o.deps):
                do = ops[d]
                if (not do.is_dma) and do.eng == o.eng and not o.is_dma and not o.force and not (d in o.raw and o.eng != "pe"):
                    o.deps.discard(d)
                    continue
                do.signal = True
        cnt = {e: 0 for e in ENGS}
        for e in ENGS:
            for o in self.eng_ops[e]:
                if o.is_dma:
                    cc = "cc" if o.inc == 1 else "d"
                    rrk = (e, cc)
                    k = self.dma_rr.get(rrk, 0) % (N_DMA_SEMS if cc == "d" else 4)
                    self.dma_rr[rrk] = self.dma_rr.get(rrk, 0) + 1
                    key = (e, cc, k)
                    if key not in self.dma_sems:
                        self.dma_sems[key] = [self.stack.enter_context(nc.semaphore("%s_%s_%d" % (cc, e, k))), 0]
                    ent = self.dma_sems[key]
                    o.prewait = (ent[0], ent[1]) if ent[1] > 0 else None
                    ent[1] += o.inc
                    o.sem, o.val = ent[0], ent[1]
                elif o.signal and o.fn is not None:
                    ph = cnt[e] // SEM_ROT
                    while len(self.eng_sems[e]) <= ph:
                        self.eng_sems[e].append(
                            self.stack.enter_context(nc.semaphore("c_%s_%d" % (e, len(self.eng_sems[e])))))
                    cnt[e] += 1
                    o.sem = self.eng_sems[e][ph]
                    o.val = cnt[e] - ph * SEM_ROT
                elif o.signal and o.fn is None:
                    pass

        def resolve(d, acc, seen):
            do = ops[d]
            if do.fn is None:
                if d in seen:
                    return
                seen.add(d)
                for dd in do.deps:
                    resolve(dd, acc, seen)
                return
            key = id(do.sem)
            if key not in acc or acc[key][1] < do.val:
                acc[key] = (do.sem, do.val)

        self._resolve = resolve

        with nc.Block() as block:
            def run(e, handle_name):
                deco = getattr(block, handle_name)

                @deco
                def _(h):
                    known = {}
                    for o in self.eng_ops[e]:
                        acc = {}
                        seen = set()
                        for d in o.deps:
                            resolve(d, acc, seen)
                        if o.prewait is not None:
                            s, v = o.prewait
                            if id(s) not in acc or acc[id(s)][1] < v:
                                acc[id(s)] = (s, v)
                        for key, (s, v) in acc.items():
                            if known.get(key, 0) >= v:
                                continue
                            known[key] = v
                            h.wait_ge(s, v)
                        if o.fn is None:
                            continue
                        ins = o.fn(h)
                        if o.sem is not None:
                            if o.is_dma:
                                ins.then_inc(o.sem, o.inc)
                            else:
                                ins.then_inc(o.sem, 1)

            run("sp", "sync")
            run("pool", "gpsimd")
            run("act", "scalar")
            run("dve", "vector")
            run("pe", "tensor")


D = 2048
KC = 16
DFF = 5632
FC = 44
TT = 512
NT = 6
LT = 3072
HD = 128
NH = 16
EPS = 1e-6
PSEG = 1024
SSEG = 512
NEG = -30000.0
BIGR = 1.0e6
B_GROUPS = ((128, 1), (512, 4), (2048, 16))
I32 = mybir.dt.int32


def slopes16():
    return [2.0 ** (-8.0 * (h + 1) / 16.0) for h in range(16)]


def tile_cols(t):
    return t * TT


def tile_type(t):
    return 0 if t == 0 else (1 if t == 1 else 2)


def window_pieces(t, halo):
    base = 0 if t < 2 else PSEG + SSEG * (t - 2)
    seg = PSEG if t < 2 else SSEG
    off = TT * t if t < 2 else 0
    lo = off - halo
    hi = off + TT + halo
    pieces = []
    rel_lo = lo // seg
    rel_hi = (hi - 1) // seg
    for rel in range(rel_lo, rel_hi + 1):
        a = max(lo, rel * seg)
        b = min(hi, (rel + 1) * seg)
        pieces.append((rel, base + a - rel * seg, b - a))
    return pieces


class Arena:
    def __init__(self, ap, nwords):
        self.ap = ap
        self.n = nwords
        self.off = 0
        self.cnt = 0

    def alloc(self, shape, dtype, key=None):
        n = int(np.prod(shape))
        sz = 4 if dtype in (F32, I32) else 2
        words = (n * sz + 3) // 4
        words = (words + 15) // 16 * 16
        assert self.off + words <= self.n, ("arena overflow", self.off, words, self.n)
        a = self.ap[:, self.off:self.off + words]
        if dtype != F32:
            a = a.bitcast(dtype)
        a = a[:, 0:n]
        if len(shape) == 2:
            a = a.rearrange("p (a b) -> p a b", b=shape[1])
        elif len(shape) == 3:
            a = a.rearrange("p (a b c) -> p a b c", b=shape[1], c=shape[2])
        self.off += words
        self.cnt += 1
        return a, (key or ("ar%d" % self.cnt)) + "@%d" % self.off


class Rot:
    def __init__(self, items):
        self.items = items
        self.i = 0

    def next(self):
        it = self.items[self.i % len(self.items)]
        self.i += 1
        return it


def weight_specs():
    specs = []
    for i in range(4):
        pre = "l%d_" % i
        kind = i % 3
        if kind == 0:
            specs += [(pre + "a_w_qkv", 2048, 3072), (pre + "a_w_o", 2048, 2048)]
        elif kind == 1:
            specs += [(pre + "b_w_qkv", 2048, 18432), (pre + "b_w_o", 2048, 2048)]
        else:
            specs += [(pre + "c_w_down", 2048, 1088), (pre + "c_w_uq", 512, 3072),
                      (pre + "c_w_ukv", 512, 4096), (pre + "c_w_o", 2048, 2048)]
        specs += [(pre + "ffn_w_in", 2048, 11264), (pre + "ffn_w_out", 5632, 2048)]
    return specs


CF = {}
_o = 0
for _n, _w in (("gvec", 144), ("convw", 528), ("convb", 176), ("sink", 32), ("cnorm", 8), ("RA", 384),
               ("RB", 256), ("EA", 18), ("EB", 45), ("fl", 2)):
    CF[_n] = _o
    _o += _w
NCF = _o


class Builder:
    def __init__(self, n_layers=4, stop_mid=False, arena_words=46000):
        self.n_layers = n_layers
        self.stop_mid = stop_mid
        self.nc = bass.Bass("TRN2", target_bir_lowering=False)
        nc = self.nc
        self.st = contextlib.ExitStack()
        self.P = Prog(nc, self.st)
        self.x0 = nc.dram_tensor("x0T", [D, LT], F32, kind="ExternalInput").ap()
        self.cf_d = nc.dram_tensor("cf32", [128, NCF], F32, kind="ExternalInput").ap()
        self.rope_d = nc.dram_tensor("rope", [2, 32, LT], F32, kind="ExternalInput").ap()
        self.nb_d = nc.dram_tensor("nb", [1, 8], I32, kind="ExternalInput").ap()
        self.yT = nc.dram_tensor("yT", [D, LT], F32, kind="ExternalOutput").ap()
        self.w32 = {}
        self.wb = {}
        self.wkeys = {}
        for name, k, n in weight_specs():
            if int(name[1]) >= n_layers:
                continue
            self.w32[name] = nc.dram_tensor(name, [k, n], F32, kind="ExternalInput").ap()
            self.wb[name] = nc.dram_tensor("wb_" + name, [k, n], BF16).ap()
        dt = nc.dram_tensor
        self.XS = dt("XS", [KC, 128, LT], F32).ap()
        self.XM = dt("XM", [KC, 128, LT], F32).ap()
        self.H2 = dt("H2", [KC, 128, LT], BF16).ap()
        self.ATT = dt("ATT", [NH, 128, LT], BF16).ap()
        self.QS = dt("QS", [48, 128, LT], BF16).ap()
        self.QR = dt("QR", [NH, 64, LT], BF16).ap()
        self.KSa = dt("KSa", [512, LT], BF16).ap()
        self.NKa = {rel: dt("NKa%d" % (rel + 2), [512, LT], BF16).ap() for rel in (-1, 1)}
        self.NVa = {rel: dt("NVa%d" % (rel + 2), [LT, 512], BF16).ap() for rel in (-1, 1)}
        self.NHB = {rel: dt("NHB%d" % (rel + 2), [128, 160], BF16).ap() for rel in (-1, 1)}
        self.VSa = dt("VSa", [LT, 512], BF16).ap()
        self.HBs = dt("HBs", [128, 160], BF16).ap()
        self.arena_t = self.st.enter_context(nc.sbuf_tensor("arena", [128, arena_words], F32))
        self.ar = Arena(self.arena_t[:, :], arena_words)
        self.ps = []
        for i in range(8):
            t = self.st.enter_context(nc.psum_tensor("ps%d" % i, [128, 512], F32))
            self.ps.append((t[:, :], "ps%d" % i))
        self.psrot = Rot(self.ps)
        self.evac_i = 0
        self.regv = {}
        self.ag_bufs = {}

    def dma(self, out, in_, r, w, eng="sp"):
        return self.P.op(eng, lambda e: e.dma_start(out=out, in_=in_), reads=r, writes=w, dma=True)

    def localize(self, dst, g8view, rel, r, w):
        eng = "sp" if self.dyn_cnt["sp"] <= self.dyn_cnt["pool"] else "pool"
        self.dyn_cnt[eng] += 1
        assert self.dyn_cnt[eng] <= 21, "dynamic DMA register budget exceeded"

        def fn(e):
            v = self.regv[(eng, rel)]
            return e.dma_start(out=dst, in_=g8view[bass.ds(v, 1)])
        return self.P.op(eng, fn, reads=r, writes=w, dma=True)

    def pe(self, fn, r, w):
        return self.P.op("pe", fn, reads=r, writes=w)

    def act(self, fn, r, w):
        return self.P.op("act", fn, reads=r, writes=w)

    def dve(self, fn, r, w, force=False):
        return self.P.op("dve", fn, reads=r, writes=w, force=force)

    def evac(self, out, in_, r, w):
        self.evac_i += 1
        if self.evac_i % 2 == 0:
            return self.act(lambda e: e.activation(out=out, in_=in_, func=AF.Copy), r, w)
        return self.dve(lambda e: e.tensor_copy(out, in_), r, w)

    def allgather(self, send, R, C, name, key_send, key_out, rpc_force=None):
        nc = self.nc
        rpc = 1
        for cand in range(1, R + 1):
            if R % cand == 0 and cand * C * 2 <= 512 * 1024:
                rpc = cand
        if rpc_force:
            rpc = rpc_force
        nch = R // rpc
        if name not in self.ag_bufs:
            self.ag_bufs[name] = (nc.dram_tensor("g4_" + name, [nch * 4 * rpc, C], BF16).ap(),
                                  nc.dram_tensor("g8_" + name, [nch * 8 * rpc, C], BF16).ap())
        g4, g8 = self.ag_bufs[name]
        for c in range(nch):
            s_ap = send[c * rpc:(c + 1) * rpc, :]
            g4c = g4[c * 4 * rpc:(c + 1) * 4 * rpc, :]
            g8c = g8[c * 8 * rpc:(c + 1) * 8 * rpc, :]

            def c1(e, s_ap=s_ap, g4c=g4c):
                return e.collective_compute("AllGather", ALU.bypass, replica_groups=[[0, 1, 2, 3], [4, 5, 6, 7]],
                                            ins=[s_ap.opt()], outs=[g4c.opt()])

            def c2(e, g4c=g4c, g8c=g8c):
                return e.collective_compute("AllGather", ALU.bypass, replica_groups=[[0, 4], [1, 5], [2, 6], [3, 7]],
                                            ins=[g4c.opt()], outs=[g8c.opt()])
            k4 = (key_out, "g4", c)
            self.P.op("pool", c1, reads=key_send, writes=[k4], dma=True, inc=1)
            self.P.op("pool", c2, reads=[k4], writes=[(key_out, c)], dma=True, inc=1)
        keys = [(key_out, c) for c in range(nch)]
        if "flush" not in self.ag_bufs:
            self.ag_bufs["flush"] = (nc.dram_tensor("fl_s", [16, 64], BF16).ap(), nc.dram_tensor("fl_4", [64, 64], BF16).ap(),
                                     nc.dram_tensor("fl_8", [128, 64], BF16).ap())
        fs, f4, f8 = self.ag_bufs["flush"]
        for rep in range(2):
            def d1(e):
                return e.collective_compute("AllGather", ALU.bypass, replica_groups=[[0, 1, 2, 3], [4, 5, 6, 7]],
                                            ins=[fs.opt()], outs=[f4.opt()])

            def d2(e):
                return e.collective_compute("AllGather", ALU.bypass, replica_groups=[[0, 4], [1, 5], [2, 6], [3, 7]],
                                            ins=[f4.opt()], outs=[f8.opt()])
            self.P.op("pool", d1, reads=keys + ["fl8"], writes=["fl4"], dma=True, inc=1)
            self.P.op("pool", d2, reads=["fl4"], writes=["fl8"], dma=True, inc=1)
        keys = keys + ["fl8"]
        return g8.rearrange("(n r i) c -> r n i c", r=8, i=rpc), keys, (nch, rpc)

    def convert_weights(self):
        for name, k, n in weight_specs():
            if name not in self.w32:
                continue
            rows = 64 if n > 4096 else 256
            keys = []
            for r0 in range(0, k, rows):
                r1 = min(k, r0 + rows)
                key = ("wb", name, r0)
                keys.append(key)
                self.dma(self.wb[name][r0:r1, :], self.w32[name][r0:r1, :], [], [key], eng="pool")
            self.wkeys[name] = keys

    def setup(self):
        ar = self.ar
        P = self.P
        self.cf, self.cf_k = ar.alloc([NCF], F32, "cf")
        self.cf = self.cf
        self.dma(self.cf, self.cf_d, [], [self.cf_k])
        self.nbs, self.nbs_k = ar.alloc([8], I32, "nbs")
        self.dma(self.nbs[0:1, :], self.nb_d, [], [self.nbs_k])

        self.dyn_cnt = {"sp": 0, "pool": 0}
        for eng in ("sp", "pool"):
            def ldregs(e, eng=eng):
                for rel in (-2, -1, 1, 2):
                    reg = e.alloc_register("nbr%d" % (rel + 2))
                    e.reg_load(reg, self.nbs[0:1, rel + 2:rel + 3])
                    self.regv[(eng, rel)] = e.snap(reg)
                return None
            P.op(eng, ldregs, reads=[self.nbs_k], writes=[])
        self.ones, self.ones_k = ar.alloc([128], BF16, "ones")
        self.dve(lambda e: e.memset(self.ones, 1.0), [], [self.ones_k])
        self.esink, self.esink_k = ar.alloc([32], F32, "esink")
        o = CF["sink"]
        self.act(lambda e: e.activation(out=self.esink, in_=self.cf[:, o:o + 32], func=AF.Exp),
                 [self.cf_k], [self.esink_k])
        self.haloL, self.haloL_k = ar.alloc([16, 5], BF16, "haloL")
        self.haloR, self.haloR_k = ar.alloc([16, 5], BF16, "haloR")
        self.hb, self.hb_k = ar.alloc([2, 16, 5], BF16, "hb")
        self.mark = ar.off
        self.x0v = self.x0.rearrange("(k p) t -> k p t", p=128)

    def phase_begin(self):
        self.P.barrier()
        self.ar.off = self.mark

    def cfcol(self, name, idx):
        o = CF[name] + idx
        return self.cf[:, o:o + 1]

    def rmsnorm(self, xt, xt_k, nchunks, width, gname, gidx0, out, out_k, tmp, out_fn=None, post=None):
        psum, psk = self.psrot.next()
        n_feat = nchunks * 128
        for c in range(nchunks):
            sq, sqk = tmp["sq"].next()
            self.act(lambda e, c=c, sq=sq: e.activation(out=sq[:, 0:width], in_=xt[:, c, 0:width], func=AF.Square),
                     [xt_k], [sqk])
            self.pe(lambda e, c=c, sq=sq: e.matmul(psum[:, 0:width], self.ones[:, :], sq[:, 0:width],
                                                  start=(c == 0), stop=(c == nchunks - 1)),
                    [sqk, self.ones_k], [psk])
        rs, rsk = tmp["rstd"]
        self.act(lambda e: e.activation(out=rs[:, 0:width], in_=psum[:, 0:width], func=AF.Sqrt,
                                        scale=1.0 / n_feat, bias=EPS), [psk], [rsk])
        self.dve(lambda e: e.reciprocal(rs[:, 0:width], rs[:, 0:width]), [rsk], [rsk])
        for c in range(nchunks):
            g = self.cfcol(gname, gidx0 + c)
            if out_fn is not None:
                o_ap, o_k = out_fn(c)
            else:
                o_ap, o_k = out[:, c, 0:width], out_k
            self.dve(lambda e, c=c, g=g, o_ap=o_ap: e.scalar_tensor_tensor(out=o_ap, in0=xt[:, c, 0:width],
                                                                          scalar=g, in1=rs[:, 0:width],
                                                                          op0=ALU.mult, op1=ALU.mult),
                     [xt_k, rsk, self.cf_k], [o_k])
            if post is not None:
                post(c, o_ap, o_k)

    def lin_fm(self, wv, wkey, kc_n, nchunk, rhs_fn, rhs_keys, width, consumer, m0=0, mw=128):
        for m in range(nchunk):
            psum, psk = self.psrot.next()
            for kc in range(kc_n):
                self.pe(lambda e, m=m, kc=kc, psum=psum: e.matmul(psum[0:mw, 0:width], wv[:, kc, m * mw:(m + 1) * mw],
                                                                rhs_fn(kc), start=(kc == 0), stop=(kc == kc_n - 1)),
                        [wkey] + rhs_keys, [psk])
            consumer(m0 + m, psum, psk)

    def lin_tm(self, wv_cols_fn, wkey, kc_n, ncols, lhs_fn, lhs_keys, nsub, consumer):
        for s in range(nsub):
            psum, psk = self.psrot.next()
            for kc in range(kc_n):
                self.pe(lambda e, s=s, kc=kc, psum=psum: e.matmul(psum[:, 0:ncols], lhs_fn(kc, s), wv_cols_fn(kc),
                                                                start=(kc == 0), stop=(kc == kc_n - 1)),
                        [wkey] + lhs_keys, [psk])
            consumer(s, psum, psk)

    def alloc_common(self, wsize=8192, nw=3):
        ar = self.ar
        self.wbufs = Rot([ar.alloc([wsize], BF16, "wbuf%d" % i) for i in range(nw)])
        self.stg16 = Rot([ar.alloc([512], BF16, "stg16_%d" % i) for i in range(4)])
        self.sqr = Rot([ar.alloc([512], BF16, "sq%d" % i) for i in range(2)])
        self.rstd = ar.alloc([512], F32, "rstd")
        self.ntmp = dict(sq=self.sqr, rstd=self.rstd)

    def wload(self, name, kc_n, c0, ncols):
        ap, key = self.wbufs.next()
        dst = ap[:, 0:kc_n * ncols].rearrange("p (k n) -> p k n", n=ncols)
        src = self.wb[name][:, c0:c0 + ncols].rearrange("(k p) n -> p k n", p=128)
        self.dma(dst, src, self.wkeys[name], [key])
        return dst, key

    def run_jobs(self, jobs, depth=2):
        loaded = {}
        for i in range(min(depth, len(jobs))):
            loaded[i] = self.wload(*jobs[i][0:4])
        for i, job in enumerate(jobs):
            if i + depth < len(jobs):
                loaded[i + depth] = self.wload(*jobs[i + depth][0:4])
            wv, wkey = loaded.pop(i)
            job[4](wv, wkey)

    def load_x_tile(self, src, srcname, t, xt, xt_k):
        c0 = t * TT
        self.dma(xt[:, :, 0:TT], src[:, :, c0:c0 + TT].rearrange("k p t -> p k t"), [(srcname, t)], [xt_k])

    def store_stage(self, psum, psk, dst, dst_key, width=TT, npart=128):
        st, stk = self.stg16.next()
        self.evac(st[0:npart, 0:width], psum[0:npart, 0:width], [psk], [stk])
        self.dma(dst, st[0:npart, 0:width], [stk], [dst_key])

    def a_phase1(self, li):
        wn = "l%d_a_w_qkv" % li
        xsrc, xname = (self.x0v, "x0") if li == 0 else (self.XS, "XS")
        self.phase_begin()
        ar = self.ar
        self.alloc_common()
        xts = [ar.alloc([KC, TT], F32, "xt%d" % i) for i in range(2)]
        hts = [ar.alloc([KC, TT], BF16, "ht%d" % i) for i in range(2)]
        jobs = []
        for t in range(NT):
            c0 = t * TT
            xt, xt_k = xts[t % 2]
            ht, ht_k = hts[t % 2]
            for blk in range(6):
                def fn(wv, wkey, t=t, c0=c0, blk=blk, xt=xt, xt_k=xt_k, ht=ht, ht_k=ht_k):
                    if blk == 0:
                        if t == 0:
                            self.load_x_tile(xsrc, xname, 0, xt, xt_k)
                        if t + 1 < NT:
                            self.load_x_tile(xsrc, xname, t + 1, *xts[(t + 1) % 2])
                        self.rmsnorm(xt, xt_k, KC, TT, "gvec", (2 * li) * 16, ht, ht_k, self.ntmp)
                    rhs = lambda kc: ht[:, kc, :]
                    if blk < 4:
                        def cons(m, psum, psk):
                            self.store_stage(psum, psk, self.QS[m, :, c0:c0 + TT], ("QS", m, t))
                        self.lin_fm(wv, wkey, KC, 4, rhs, [ht_k], TT, cons, m0=blk * 4)
                    elif blk == 4:
                        def cons(m, psum, psk):
                            self.store_stage(psum, psk, self.KSa[m * 128:(m + 1) * 128, c0:c0 + TT], ("KS", m, t))
                        self.lin_fm(wv, wkey, KC, 4, rhs, [ht_k], TT, cons)
                    else:
                        def cons(s, psum, psk):
                            self.store_stage(psum, psk, self.VSa[c0 + s * 128:c0 + (s + 1) * 128, :], ("VS", s, t))
                        self.lin_tm(lambda kc: wv[:, kc, 0:512], wkey, KC, 512,
                                    lambda kc, s: ht[:, kc, s * 128:(s + 1) * 128], [ht_k], 4, cons)
                jobs.append((wn, KC, blk * 512, 512, fn))
        self.run_jobs(jobs)
        ksend = [("KS", m, t) for m in range(4) for t in range(NT)]
        vsend = [("VS", s, t) for s in range(4) for t in range(NT)]
        K8v, kk, (kn, kr) = self.allgather(self.KSa, 512, LT, "Ka", ksend, "K8")
        V8v, vk, (vn, vr) = self.allgather(self.VSa, LT, 512, "Va", vsend, "V8")
        for rel in (-1, 1):
            self.localize(self.NKa[rel].rearrange("(n i) c -> n i c", i=kr), K8v, rel, kk, [("NKa", rel)])
            self.localize(self.NVa[rel].rearrange("(n i) c -> n i c", i=vr), V8v, rel, vk, [("NVa", rel)])

    def a_phase3(self, li):
        self.phase_begin()
        ar = self.ar
        sl = slopes16()
        scale = HD ** -0.5
        sink_base = (0 if li == 0 else 1) * 16
        KTs = Rot([ar.alloc([768], BF16, "KTw%d" % i) for i in range(2)])
        Vws = Rot([ar.alloc([6, 128], BF16, "Vw%d" % i) for i in range(2)])
        QTs = Rot([ar.alloc([512], BF16, "QT%d" % i) for i in range(3)])
        tmps = Rot([ar.alloc([384], F32, "tmp%d" % i) for i in range(3)])
        PTs = Rot([ar.alloc([384], BF16, "PT%d" % i) for i in range(3)])
        recs = Rot([ar.alloc([512], F32, "rec%d" % i) for i in range(2)])
        oats = Rot([ar.alloc([512], BF16, "oat%d" % i) for i in range(3)])
        RA0 = CF["RA"]
        hh = 0
        for t in range(NT):
            c0 = t * TT
            tt_ = tile_type(t)
            pieces = window_pieces(t, 128)
            for kvh in range(4):
                KT, KT_k = KTs.next()
                Vw, Vw_k = Vws.next()
                w0 = 0
                for (rel, lc, ln) in pieces:
                    ksrc = self.KSa if rel == 0 else self.NKa[rel]
                    vsrc = self.VSa if rel == 0 else self.NVa[rel]
                    self.dma(KT[:, w0:w0 + ln], ksrc[kvh * 128:(kvh + 1) * 128, lc:lc + ln], [], [KT_k])
                    b0 = w0 // 128
                    nb = ln // 128
                    self.dma(Vw[:, b0:b0 + nb, :],
                             vsrc[lc:lc + ln, kvh * 128:(kvh + 1) * 128].rearrange("(b p) d -> p b d", p=128), [], [Vw_k])
                    w0 += ln
                assert w0 == 768
                for g4 in range(4):
                    h = kvh * 4 + g4
                    QT, QT_k = QTs.next()
                    self.dma(QT, self.QS[h, :, c0:c0 + TT], [("QS", h, t)], [QT_k])
                    num, num_k = self.ps[3 + hh % 2]
                    den, den_k = self.ps[5 + hh % 2]
                    hh += 1
                    for j in range(6):
                        q_lo = max(0, 128 * j - 256)
                        q_hi = min(512, 128 * j + 128)
                        n = q_hi - q_lo
                        cb = q_lo - (128 * j - 256)
                        S, S_k = self.ps[j % 3]
                        tmp, tmp_k = tmps.next()
                        PT, PT_k = PTs.next()
                        self.pe(lambda e, S=S, KT=KT, QT=QT, j=j, q_lo=q_lo, q_hi=q_hi, n=n:
                                e.matmul(S[:, 0:n], KT[:, 128 * j:128 * j + 128], QT[:, q_lo:q_hi], start=True, stop=True),
                                [KT_k, QT_k], [S_k])
                        coef = -sl[h] / scale
                        self.dve(lambda e, S=S, tmp=tmp, cb=cb, n=n, coef=coef:
                                 e.scalar_tensor_tensor(out=tmp[:, 0:n], in0=self.cf[:, RA0 + cb:RA0 + cb + n], scalar=coef,
                                                        in1=S[:, 0:n], op0=ALU.mult, op1=ALU.add),
                                 [S_k, self.cf_k], [tmp_k])
                        ecol = self.cfcol("EA", tt_ * 6 + j)
                        self.act(lambda e, tmp=tmp, PT=PT, n=n, ecol=ecol:
                                 e.activation(out=PT[:, 0:n], in_=tmp[:, 0:n], func=AF.Exp, bias=ecol, scale=scale),
                                 [tmp_k, self.cf_k], [PT_k])
                        self.pe(lambda e, num=num, Vw=Vw, PT=PT, j=j, q_lo=q_lo, q_hi=q_hi, n=n:
                                e.matmul(num[:, q_lo:q_hi], Vw[:, j, :], PT[:, 0:n], start=(j == 0), stop=(j == 5),
                                         skip_group_check=True),
                                [Vw_k, PT_k], [num_k])
                        self.pe(lambda e, den=den, PT=PT, j=j, q_lo=q_lo, q_hi=q_hi, n=n:
                                e.matmul(den[:, q_lo:q_hi], self.ones[:, :], PT[:, 0:n], start=(j == 0), stop=(j == 5),
                                         skip_group_check=True),
                                [self.ones_k, PT_k], [den_k])
                    rec, rec_k = recs.next()
                    oat, oat_k = oats.next()
                    sk = sink_base + h
                    self.dve(lambda e, rec=rec, den=den, sk=sk:
                             e.tensor_scalar(out=rec, in0=den, scalar1=self.esink[:, sk:sk + 1], scalar2=None, op0=ALU.add),
                             [den_k, self.esink_k], [rec_k])
                    self.dve(lambda e, rec=rec: e.reciprocal(rec, rec), [rec_k], [rec_k])
                    self.dve(lambda e, oat=oat, num=num, rec=rec: e.tensor_tensor(out=oat, in0=num, in1=rec, op=ALU.mult),
                             [num_k, rec_k], [oat_k])
                    self.dma(self.ATT[h, :, c0:c0 + TT], oat, [oat_k], [("ATT", h, t)])

    def oproj_phase(self, li, wn):
        xsrc, xname = (self.x0v, "x0") if li == 0 else (self.XS, "XS")
        self.phase_begin()
        ar = self.ar
        self.alloc_common()
        xts = [ar.alloc([KC, TT], F32, "xt%d" % i) for i in range(2)]
        ats = [ar.alloc([KC, TT], BF16, "at%d" % i) for i in range(2)]
        h2s = [ar.alloc([KC, TT], BF16, "h2_0")] * 2
        jobs = []

        def load_tile(t):
            c0 = t * TT
            self.load_x_tile(xsrc, xname, t, *xts[t % 2])
            at, at_k = ats[t % 2]
            self.dma(at, self.ATT[:, :, c0:c0 + TT].rearrange("h p t -> p h t"), [("ATT", h, t) for h in range(NH)], [at_k])

        for t in range(NT):
            c0 = t * TT
            xt, xt_k = xts[t % 2]
            at, at_k = ats[t % 2]
            h2, h2_k = h2s[t % 2]
            for blk in range(4):
                def fn(wv, wkey, t=t, c0=c0, blk=blk, xt=xt, xt_k=xt_k, at=at, at_k=at_k, h2=h2, h2_k=h2_k):
                    if blk == 0:
                        if t == 0:
                            load_tile(0)
                        if t + 1 < NT:
                            load_tile(t + 1)

                    def cons(m, psum, psk):
                        self.dve(lambda e: e.tensor_tensor(out=xt[:, m, :], in0=xt[:, m, :], in1=psum[:, 0:TT], op=ALU.add),
                                 [psk, xt_k], [xt_k])
                    self.lin_fm(wv, wkey, KC, 4, lambda kc: at[:, kc, :], [at_k], TT, cons, m0=blk * 4)
                    if blk == 3:
                        self.dma(self.XM[:, :, c0:c0 + TT].rearrange("k p t -> p k t"), xt, [xt_k], [("XM", t)])
                        self.rmsnorm(xt, xt_k, KC, TT, "gvec", (2 * li + 1) * 16, h2, h2_k, self.ntmp)
                        self.dma(self.H2[:, :, c0:c0 + TT].rearrange("k p t -> p k t"), h2, [h2_k], [("H2", t)])
                        bl = []
                        if t == 0:
                            bl = [(0, 0, 0)]
                        elif t == 1:
                            bl = [(1, 0, TT - 1)]
                        else:
                            bl = [(0, t - 1, 0), (1, t - 1, TT - 1)]
                        for (side, seg, col) in bl:
                            self.dve(lambda e, side=side, seg=seg, col=col: e.tensor_copy(self.hb[:, side, :, seg], h2[:, :, col]),
                                     [h2_k], [self.hb_k], force=True)
                jobs.append((wn, KC, blk * 512, 512, fn))
        self.run_jobs(jobs)
        self.dma(self.HBs, self.hb.rearrange("p a b c -> p (a b c)"), [self.hb_k], ["HBs"])
        HBv, hk, (hn, hr) = self.allgather(self.HBs, 128, 160, "HB", ["HBs"], "HB8")
        tl, tl_k = ar.alloc([80], BF16, "tl")
        tr, tr_k = ar.alloc([80], BF16, "tr")
        self.localize(self.NHB[-1].rearrange("(n i) c -> n i c", i=hr), HBv, -1, hk, [("NHB", -1)])
        self.localize(self.NHB[1].rearrange("(n i) c -> n i c", i=hr), HBv, 1, hk, [("NHB", 1)])
        self.dma(tl, self.NHB[-1][:, 80:160], [("NHB", -1)], [tl_k])
        self.dma(tr, self.NHB[1][:, 0:80], [("NHB", 1)], [tr_k])
        fl = CF["fl"]
        self.dve(lambda e: e.tensor_scalar(out=self.haloL.rearrange("p a b -> p (a b)"), in0=tl,
                                           scalar1=self.cf[:, fl:fl + 1], scalar2=None, op0=ALU.mult),
                 [tl_k, self.cf_k], [self.haloL_k])
        self.dve(lambda e: e.tensor_scalar(out=self.haloR.rearrange("p a b -> p (a b)"), in0=tr,
                                           scalar1=self.cf[:, fl + 1:fl + 2], scalar2=None, op0=ALU.mult),
                 [tr_k, self.cf_k], [self.haloR_k])

    def ffn_phase(self, li, last):
        self.phase_begin()
        ar = self.ar
        self.alloc_common(wsize=5632, nw=3)
        win = "l%d_ffn_w_in" % li
        wout = "l%d_ffn_w_out" % li
        g, _ = ar.alloc([FC, TT], BF16, "g")
        h2es = [ar.alloc([KC, TT + 2], BF16, "h2e%d" % i) for i in range(2)]
        xt, xt_k = ar.alloc([KC, TT], F32, "xt")
        xmcs = Rot([ar.alloc([512], F32, "xmc%d" % i) for i in range(2)])
        aexts = Rot([ar.alloc([TT + 2], F32, "aext%d" % i) for i in range(2)])
        cbs = Rot([ar.alloc([512], F32, "cb%d" % i) for i in range(2)])
        gls = Rot([ar.alloc([512], F32, "gl%d" % i) for i in range(4)])
        cw0 = CF["convw"] + li * FC * 3
        cb0 = CF["convb"] + li * FC

        def load_h2e(t):
            c0 = t * TT
            h2e, k = h2es[t % 2]
            if t == 0:
                self.dma(h2e[:, :, 1:TT + 2], self.H2[:, :, c0:c0 + TT + 1].rearrange("k p t -> p k t"),
                         [("H2", 0), ("H2", 1)], [k])
                self.dve(lambda e: e.tensor_copy(h2e[:, :, 0], self.haloL[:, :, 0]), [self.haloL_k], [k])
            elif t == 1:
                self.dma(h2e[:, :, 0:TT + 1], self.H2[:, :, c0 - 1:c0 + TT].rearrange("k p t -> p k t"),
                         [("H2", 0), ("H2", 1)], [k])
                self.dve(lambda e: e.tensor_copy(h2e[:, :, TT + 1], self.haloR[:, :, 0]), [self.haloR_k], [k])
            else:
                self.dma(h2e[:, :, 1:TT + 1], self.H2[:, :, c0:c0 + TT].rearrange("k p t -> p k t"), [("H2", t)], [k])
                self.dve(lambda e: e.tensor_copy(h2e[:, :, 0], self.haloL[:, :, t - 1]), [self.haloL_k], [k])
                self.dve(lambda e: e.tensor_copy(h2e[:, :, TT + 1], self.haloR[:, :, t - 1]), [self.haloR_k], [k])

        jobs = []
        for t in range(NT):
            c0 = t * TT
            h2e, h2e_k = h2es[t % 2]
            glbuf = {}
            for jb in range(FC // 2):
                def gate(wv, wkey, t=t, jb=jb, h2e=h2e, h2e_k=h2e_k, glbuf=glbuf):
                    if jb == 0:
                        if t == 0:
                            load_h2e(0)
                        if t + 1 < NT:
                            load_h2e(t + 1)
                    for jj in range(2):
                        j = jb * 2 + jj
                        a_ps, a_k = self.psrot.next()
                        ah_ps, ah_k = self.psrot.next()
                        for kc in range(KC):
                            self.pe(lambda e, kc=kc, jj=jj, a_ps=a_ps: e.matmul(a_ps[:, 0:TT], wv[:, kc, jj * 128:(jj + 1) * 128],
                                                                              h2e[:, kc, 1:TT + 1], start=(kc == 0), stop=(kc == KC - 1)),
                                    [wkey, h2e_k], [a_k])
                        for kc in range(KC):
                            self.pe(lambda e, kc=kc, jj=jj, ah_ps=ah_ps: e.matmul(ah_ps[:, 0:2], wv[:, kc, jj * 128:(jj + 1) * 128],
                                                                                h2e[:, kc, 0:TT + 2:TT + 1], start=(kc == 0), stop=(kc == KC - 1)),
                                    [wkey, h2e_k], [ah_k])
                        aext, ax_k = aexts.next()
                        cb, cb_k = cbs.next()
                        gl, gl_k = gls.next()
                        glbuf[j] = (gl, gl_k)
                        self.act(lambda e, aext=aext, a_ps=a_ps: e.activation(out=aext[:, 1:TT + 1], in_=a_ps[:, 0:TT], func=AF.Copy),
                                 [a_k], [ax_k])
                        self.act(lambda e, aext=aext, ah_ps=ah_ps: e.activation(out=aext[:, 0:TT + 2:TT + 1], in_=ah_ps[:, 0:2], func=AF.Copy),
                                 [ah_k], [ax_k])
                        w0c = self.cf[:, cw0 + j * 3 + 0:cw0 + j * 3 + 1]
                        w1c = self.cf[:, cw0 + j * 3 + 1:cw0 + j * 3 + 2]
                        w2c = self.cf[:, cw0 + j * 3 + 2:cw0 + j * 3 + 3]
                        bc = self.cf[:, cb0 + j:cb0 + j + 1]
                        self.act(lambda e, cb=cb, a_ps=a_ps, w1c=w1c, bc=bc: e.activation(out=cb, in_=a_ps[:, 0:TT], func=AF.Identity,
                                                                                         bias=bc, scale=w1c),
                                 [a_k, self.cf_k], [cb_k])
                        self.dve(lambda e, cb=cb, aext=aext, w0c=w0c: e.scalar_tensor_tensor(out=cb, in0=aext[:, 0:TT], scalar=w0c, in1=cb,
                                                                                            op0=ALU.mult, op1=ALU.add),
                                 [ax_k, cb_k, self.cf_k], [cb_k])
                        self.dve(lambda e, cb=cb, aext=aext, w2c=w2c: e.scalar_tensor_tensor(out=cb, in0=aext[:, 2:TT + 2], scalar=w2c, in1=cb,
                                                                                            op0=ALU.mult, op1=ALU.add),
                                 [ax_k, cb_k, self.cf_k], [cb_k])
                        self.act(lambda e, gl=gl, cb=cb: e.activation(out=gl, in_=cb, func=AF.Gelu), [cb_k], [gl_k])

                def val(wv, wkey, t=t, jb=jb, h2e=h2e, h2e_k=h2e_k, glbuf=glbuf):
                    for jj in range(2):
                        j = jb * 2 + jj
                        gl, gl_k = glbuf[j]

                        def cons(m, psum, psk, j=j, gl=gl, gl_k=gl_k):
                            self.dve(lambda e: e.tensor_tensor(out=g[:, j, :], in0=gl, in1=psum[:, 0:TT], op=ALU.mult),
                                     [gl_k, psk], [("g", j)])
                        self.lin_fm(wv[:, :, jj * 128:(jj + 1) * 128], wkey, KC, 1, lambda kc: h2e[:, kc, 1:TT + 1], [h2e_k], TT, cons)
                jobs.append((win, KC, jb * 256, 256, gate))
                jobs.append((win, KC, DFF + jb * 256, 256, val))
            for m in range(KC):
                def outp(wv, wkey, t=t, c0=c0, m=m):
                    xmc, xmc_k = xmcs.next()
                    self.dma(xmc, self.XM[m, :, c0:c0 + TT], [("XM", t)], [xmc_k])

                    def cons(mm, psum, psk):
                        self.dve(lambda e: e.tensor_tensor(out=xt[:, m, :], in0=xmc, in1=psum[:, 0:TT], op=ALU.add),
                                 [psk, xmc_k], [xt_k])
                    self.lin_fm(wv, wkey, FC, 1, lambda kc: g[:, kc, :], [("g", j) for j in range(FC)], TT, cons)
                    if m == KC - 1:
                        if not last:
                            self.dma(self.XS[:, :, c0:c0 + TT].rearrange("k p t -> p k t"), xt, [xt_k], [("XS", t)])
                        else:
                            def out_fn(c):
                                return xmcs.next()

                            def post(c, o_ap, o_k):
                                self.dma(self.yT[c * 128:(c + 1) * 128, c0:c0 + TT], o_ap, [o_k], [("yT", c, t)])
                            self.rmsnorm(xt, xt_k, KC, TT, "gvec", 8 * 16, None, None, self.ntmp, out_fn=out_fn, post=post)
                jobs.append((wout, FC, m * 128, 128, outp))
        self.run_jobs(jobs)


def build_program(n_layers=4, stop_mid=False):
    B = Builder(n_layers, stop_mid)
    import os
    stage = int(os.environ.get("DBG_STAGE", "99"))
    if stage >= 0:
        B.convert_weights()
    B.setup()
    done = False
    for li in range(n_layers):
        kind = li % 3
        if kind == 0:
            if stage >= 1:
                B.a_phase1(li)
            if stage >= 2:
                B.a_phase3(li)
            if stage >= 3:
                B.oproj_phase(li, "l%d_a_w_o" % li)
        elif kind == 1:
            B.b_phase1(li)
            B.b_phase3(li)
            B.oproj_phase(li, "l%d_b_w_o" % li)
        else:
            B.c_phase1(li)
            B.c_phase3(li)
            B.oproj_phase(li, "l%d_c_w_o" % li)
        if stop_mid and li == n_layers - 1:
            B.phase_begin()
            for t in range(NT):
                c0 = t * TT
                B.dma(B.yT[:, c0:c0 + TT].rearrange("(k p) t -> k p t", p=128), B.XM[:, :, c0:c0 + TT], [("XM", t)], [("yT", t)])
            done = True
            break
        B.ffn_phase(li, last=(li == 3))
    if not done and n_layers < 4:
        B.phase_begin()
        for t in range(NT):
            c0 = t * TT
            B.dma(B.yT[:, c0:c0 + TT].rearrange("(k p) t -> k p t", p=128), B.XS[:, :, c0:c0 + TT], [("XS", t)], [("yT", t)])
    B.P.barrier()
    B.P.finalize()
    B.st.close()
    return B.nc


def _vec_cols(v, nch):
    return np.ascontiguousarray(np.asarray(v, np.float32).reshape(nch, 128).T)


def host_inputs(inputs, n_layers=4):
    f32 = np.float32
    xp = np.asarray(inputs["x_prompt"], f32)
    xs = np.asarray(inputs["x_sample"], f32)
    cf = np.zeros((128, NCF), f32)
    for i in range(4):
        cf[:, CF["gvec"] + (2 * i) * 16:CF["gvec"] + (2 * i + 1) * 16] = _vec_cols(inputs["l%d_mix_norm" % i], 16)
        cf[:, CF["gvec"] + (2 * i + 1) * 16:CF["gvec"] + (2 * i + 2) * 16] = _vec_cols(inputs["l%d_ffn_norm" % i], 16)
        cw = np.asarray(inputs["l%d_ffn_conv_w" % i], f32)
        cwl = cw.T.reshape(FC, 128, 3).transpose(1, 0, 2).reshape(128, FC * 3)
        cf[:, CF["convw"] + i * FC * 3:CF["convw"] + (i + 1) * FC * 3] = cwl
        cf[:, CF["convb"] + i * FC:CF["convb"] + (i + 1) * FC] = _vec_cols(inputs["l%d_ffn_conv_b" % i], FC)
    cf[:, CF["gvec"] + 128:CF["gvec"] + 144] = _vec_cols(inputs["final_norm"], 16)
    cf[:, CF["sink"]:CF["sink"] + 16] = np.asarray(inputs["l0_a_sink"], f32)[None, :]
    cf[:, CF["sink"] + 16:CF["sink"] + 32] = np.asarray(inputs["l3_a_sink"], f32)[None, :]
    cf[:, CF["cnorm"]:CF["cnorm"] + 4] = _vec_cols(inputs["l2_c_q_norm"], 4)
    cf[:, CF["cnorm"] + 4:CF["cnorm"] + 8] = _vec_cols(inputs["l2_c_kv_norm"], 4)
    p = np.arange(128)[:, None]
    c = np.arange(384)[None, :]
    ra = np.abs(c - 128 - p).astype(f32)
    ra[ra > 128] = BIGR
    cf[:, CF["RA"]:CF["RA"] + 384] = ra
    c = np.arange(256)[None, :]
    rb = np.abs(c - 64 - p).astype(f32)
    rb[rb > 64] = BIGR
    cf[:, CF["RB"]:CF["RB"] + 256] = rb
    inv = ROPE_THETA_ ** (-np.arange(0, 64, 2, dtype=np.float32) / 64.0)
    maps = []
    wfull = {}
    for core in range(NCORES):
        cfc = cf.copy()
        for tt_, t in ((0, 0), (1, 1), (2, 2)):
            pcs = window_pieces(t, 128)
            w0 = 0
            for (rel, lc, ln) in pcs:
                valid = 0 <= core + rel <= 7
                for b in range(w0 // 128, (w0 + ln) // 128):
                    cfc[:, CF["EA"] + tt_ * 6 + b] = 0.0 if valid else NEG
                w0 += ln
            for gi, (window, d) in enumerate(B_GROUPS):
                pcs = window_pieces(t, 64 * d)
                nj = (TT + 128 * d) // d
                colv = np.zeros(nj, f32)
                w0 = 0
                for (rel, lc, ln) in pcs:
                    valid = 0 <= core + rel <= 7
                    colv[w0 // d:(w0 + ln) // d] = 0.0 if valid else NEG
                    w0 += ln
                for b in range(5):
                    seg = colv[128 * b:128 * (b + 1)]
                    col = np.zeros(128, f32)
                    col[:len(seg)] = seg
                    cfc[:, CF["EB"] + (tt_ * 3 + gi) * 5 + b] = col
        cfc[:, CF["fl"]] = 1.0 if core > 0 else 0.0
        cfc[:, CF["fl"] + 1] = 1.0 if core < 7 else 0.0
        xl = np.concatenate([xp[0, PSEG * core:PSEG * (core + 1)]] + [xs[b, SSEG * core:SSEG * (core + 1)] for b in range(4)], axis=0)
        x0T = np.ascontiguousarray(xl.T)
        pos = np.concatenate([np.arange(PSEG * core, PSEG * (core + 1))] + [np.arange(SSEG * core, SSEG * (core + 1))] * 4).astype(np.float32)
        ang = pos[None, :] * inv[:, None]
        rope = np.stack([np.cos(ang), np.sin(ang)]).astype(f32)
        nb = np.zeros((1, 8), np.int32)
        for rel in range(-2, 3):
            nb[0, rel + 2] = min(7, max(0, core + rel))
        m = {"x0T": x0T, "cf32": cfc, "rope": rope, "nb": nb}
        for name, k, n in weight_specs():
            if int(name[1]) >= n_layers:
                continue
            w = inputs[name]
            m[name] = wfull.setdefault(name, np.ascontiguousarray(np.asarray(w, f32)))
        maps.append(m)
    return maps


ROPE_THETA_ = 10000.0
_NC_CACHE = {}


def run(inputs, n_layers=4, stop_mid=False):
    key = (n_layers, stop_mid)
    if key not in _NC_CACHE:
        _NC_CACHE[key] = build_program(n_layers, stop_mid)
    nc = _NC_CACHE[key]
    maps = host_inputs(inputs, n_layers)
    res = run_bass_kernel_spmd(nc, maps, core_ids=list(range(NCORES)))
    yp = np.zeros((1, 8192, D), np.float32)
    ys = np.zeros((4, 4096, D), np.float32)
    for core in range(NCORES):
        yT = np.asarray(res.results[core]["yT"])
        yp[0, PSEG * core:PSEG * (core + 1), :] = yT[:, 0:PSEG].T
        for b in range(4):
            ys[b, SSEG * core:SSEG * (core + 1), :] = yT[:, PSEG + SSEG * b:PSEG + SSEG * (b + 1)].T
    return yp, ys


def kernel(**inputs):
    return run(inputs, 4, False)


def _b_init(self):
    dt = self.nc.dram_tensor
    if hasattr(self, "KSb"):
        return
    self.KSb = [dt("KSb%d" % g, [2048, LT], BF16).ap() for g in range(3)]
    self.VSb = [dt("VSb%d" % g, [LT, 2048], BF16).ap() for g in range(3)]
    self.NKb = [{rel: dt("NKb%d_%d" % (g, rel + 2), [2048, LT], BF16).ap() for rel in ((-1, 1) if g < 2 else (-2, -1, 1, 2))}
                for g in range(3)]
    self.NVb = [{rel: dt("NVb%d_%d" % (g, rel + 2), [LT, 2048], BF16).ap() for rel in ((-1, 1) if g < 2 else (-2, -1, 1, 2))}
                for g in range(3)]


def _b_phase1(self, li):
    _b_init(self)
    wn = "l%d_b_w_qkv" % li
    xsrc, xname = (self.XS, "XS")
    self.phase_begin()
    ar = self.ar
    self.alloc_common()
    xts = [ar.alloc([KC, TT], F32, "xt%d" % i) for i in range(2)]
    hts = [ar.alloc([KC, TT], BF16, "ht%d" % i) for i in range(2)]
    jobs = []
    for t in range(NT):
        c0 = t * TT
        xt, xt_k = xts[t % 2]
        ht, ht_k = hts[t % 2]
        for blk in range(36):
            g = blk // 12
            kind = (blk % 12) // 4
            hb4 = blk % 4

            def fn(wv, wkey, t=t, c0=c0, blk=blk, g=g, kind=kind, hb4=hb4, xt=xt, xt_k=xt_k, ht=ht, ht_k=ht_k):
                if blk == 0:
                    if t == 0:
                        self.load_x_tile(xsrc, xname, 0, xt, xt_k)
                    if t + 1 < NT:
                        self.load_x_tile(xsrc, xname, t + 1, *xts[(t + 1) % 2])
                    self.rmsnorm(xt, xt_k, KC, TT, "gvec", (2 * li) * 16, ht, ht_k, self.ntmp)
                rhs = lambda kc: ht[:, kc, :]
                if kind == 0:
                    def cons(m, psum, psk):
                        self.store_stage(psum, psk, self.QS[g * 16 + m, :, c0:c0 + TT], ("QS", g * 16 + m, t))
                    self.lin_fm(wv, wkey, KC, 4, rhs, [ht_k], TT, cons, m0=hb4 * 4)
                elif kind == 1:
                    def cons(m, psum, psk):
                        self.store_stage(psum, psk, self.KSb[g][m * 128:(m + 1) * 128, c0:c0 + TT], ("KS", g, m, t))
                    self.lin_fm(wv, wkey, KC, 4, rhs, [ht_k], TT, cons, m0=hb4 * 4)
                else:
                    def cons(s, psum, psk):
                        self.store_stage(psum, psk, self.VSb[g][c0 + s * 128:c0 + (s + 1) * 128, hb4 * 512:(hb4 + 1) * 512],
                                         ("VS", g, hb4, s, t))
                    self.lin_tm(lambda kc: wv[:, kc, 0:512], wkey, KC, 512,
                                lambda kc, s: ht[:, kc, s * 128:(s + 1) * 128], [ht_k], 4, cons)
            jobs.append((wn, KC, blk * 512, 512, fn))
    self.run_jobs(jobs)
    for g in range(3):
        ksend = [("KS", g, m, t) for m in range(16) for t in range(NT)]
        vsend = [("VS", g, hb4, s, t) for hb4 in range(4) for s in range(4) for t in range(NT)]
        K8v, kk, (kn, kr) = self.allgather(self.KSb[g], 2048, LT, "Kb%d" % g, ksend, "K8b%d" % g)
        V8v, vk, (vn, vr) = self.allgather(self.VSb[g], LT, 2048, "Vb%d" % g, vsend, "V8b%d" % g)
        for rel in self.NKb[g].keys():
            self.localize(self.NKb[g][rel].rearrange("(n i) c -> n i c", i=kr), K8v, rel, kk, [("NKb", g, rel)])
            self.localize(self.NVb[g][rel].rearrange("(n i) c -> n i c", i=vr), V8v, rel, vk, [("NVb", g, rel)])


def _b_phase3(self, li):
    self.phase_begin()
    ar = self.ar
    sl = slopes16()
    scale = HD ** -0.5
    KTs = Rot([ar.alloc([2560], BF16, "KTw%d" % i) for i in range(2)])
    Vws = Rot([ar.alloc([4096], BF16, "Vw%d" % i) for i in range(2)])
    QTs = Rot([ar.alloc([512], BF16, "QT%d" % i) for i in range(3)])
    tmps = Rot([ar.alloc([256], F32, "tmp%d" % i) for i in range(3)])
    PTs = Rot([ar.alloc([256], BF16, "PT%d" % i) for i in range(3)])
    NUMs = Rot([ar.alloc([512], F32, "NUM%d" % i) for i in range(2)])
    DENs = Rot([ar.alloc([512], F32, "DEN%d" % i) for i in range(2)])
    oats = Rot([ar.alloc([512], BF16, "oat%d" % i) for i in range(3)])
    RB0 = CF["RB"]
    cnt = 0
    sidx = 0
    for t in range(NT):
        c0 = t * TT
        tt_ = tile_type(t)
        for h in range(NH):
            NUM, NUM_k = NUMs.next()
            DEN, DEN_k = DENs.next()
            for g, (window, d) in enumerate(B_GROUPS):
                nq = TT // d
                nj = nq + 128
                nblk = (nj + 127) // 128
                W = TT + 128 * d
                KTf, KT_k = KTs.next()
                Vwf, Vw_k = Vws.next()
                KT = KTf[:, 0:W]
                Vw = Vwf[:, 0:d * nblk * 128].rearrange("p (r b x) -> p r b x", r=d, b=nblk)
                QT, QT_k = QTs.next()
                self.dma(QT, self.QS[g * 16 + h, :, c0:c0 + TT], [("QS", g * 16 + h, t)], [QT_k])
                w0 = 0
                for (rel, lc, ln) in window_pieces(t, 64 * d):
                    ksrc = self.KSb[g] if rel == 0 else self.NKb[g][rel]
                    vsrc = self.VSb[g] if rel == 0 else self.NVb[g][rel]
                    self.dma(KT[:, w0:w0 + ln], ksrc[h * 128:(h + 1) * 128, lc:lc + ln], [], [KT_k])
                    ja, je = w0 // d, (w0 + ln) // d
                    j = ja
                    while j < je:
                        b = j // 128
                        jn = min(je, (b + 1) * 128)
                        p0 = j % 128
                        n = jn - j
                        r0 = lc + (j - ja) * d
                        self.dma(Vw[p0:p0 + n, :, b, :],
                                 vsrc[r0:r0 + n * d, h * 128:(h + 1) * 128].rearrange("(jj r) x -> jj r x", r=d), [], [Vw_k])
                        j = jn
                    w0 += ln
                assert w0 == W
                num, num_k = self.ps[3 + cnt % 2]
                den, den_k = self.ps[5 + cnt % 2]
                cnt += 1
                coef = -sl[h] * d / scale
                for r in range(d):
                    for b in range(nblk):
                        nk = min(128, nj - 128 * b)
                        q_lo = max(0, 128 * b - 128)
                        q_hi = min(nq, 128 * b + nk)
                        n = q_hi - q_lo
                        cb = q_lo - (128 * b - 128)
                        S, S_k = self.ps[sidx % 3]
                        sidx += 1
                        tmp, tmp_k = tmps.next()
                        PT, PT_k = PTs.next()
                        k0 = 128 * b * d + r
                        q0 = q_lo * d + r
                        self.pe(lambda e, S=S, KT=KT, QT=QT, k0=k0, q0=q0, nk=nk, n=n, d=d:
                                e.matmul(S[0:nk, 0:n], KT[:, k0:k0 + (nk - 1) * d + 1:d], QT[:, q0:q0 + (n - 1) * d + 1:d], start=True, stop=True),
                                [KT_k, QT_k], [S_k])
                        self.dve(lambda e, S=S, tmp=tmp, cb=cb, n=n, nk=nk, coef=coef:
                                 e.scalar_tensor_tensor(out=tmp[0:nk, 0:n], in0=self.cf[0:nk, RB0 + cb:RB0 + cb + n], scalar=coef,
                                                        in1=S[0:nk, 0:n], op0=ALU.mult, op1=ALU.add),
                                 [S_k, self.cf_k], [tmp_k])
                        eo = CF["EB"] + (tt_ * 3 + g) * 5 + b
                        self.act(lambda e, tmp=tmp, PT=PT, n=n, nk=nk, eo=eo:
                                 e.activation(out=PT[0:nk, 0:n], in_=tmp[0:nk, 0:n], func=AF.Exp, bias=self.cf[0:nk, eo:eo + 1], scale=scale),
                                 [tmp_k, self.cf_k], [PT_k])
                        first = (r == 0 and b == 0)
                        last = (r == d - 1 and b == nblk - 1)
                        o0 = r * nq + q_lo
                        self.pe(lambda e, num=num, Vw=Vw, PT=PT, r=r, b=b, nk=nk, n=n, o0=o0, first=first, last=last:
                                e.matmul(num[:, o0:o0 + n], Vw[0:nk, r, b, :], PT[0:nk, 0:n], start=first, stop=last,
                                         skip_group_check=True),
                                [Vw_k, PT_k], [num_k])
                        self.pe(lambda e, den=den, PT=PT, nk=nk, n=n, o0=o0, first=first, last=last:
                                e.matmul(den[:, o0:o0 + n], self.ones[0:nk, :], PT[0:nk, 0:n], start=first, stop=last,
                                         skip_group_check=True),
                                [self.ones_k, PT_k], [den_k])
                for (ACC, ACC_k, src, src_k) in ((NUM, NUM_k, num, num_k), (DEN, DEN_k, den, den_k)):
                    if g == 0:
                        self.dve(lambda e, ACC=ACC, src=src: e.tensor_copy(ACC, src[:, 0:TT]), [src_k], [ACC_k])
                    else:
                        accv = ACC.rearrange("p (q r) -> p r q", r=d)
                        srcv = src[:, 0:TT].rearrange("p (r q) -> p r q", r=d)
                        self.dve(lambda e, accv=accv, srcv=srcv: e.tensor_tensor(out=accv, in0=accv, in1=srcv, op=ALU.add),
                                 [src_k, ACC_k], [ACC_k])
            oat, oat_k = oats.next()
            self.dve(lambda e, DEN=DEN: e.reciprocal(DEN, DEN), [DEN_k], [DEN_k])
            self.dve(lambda e, oat=oat, NUM=NUM, DEN=DEN: e.tensor_tensor(out=oat, in0=NUM, in1=DEN, op=ALU.mult),
                     [NUM_k, DEN_k], [oat_k])
            self.dma(self.ATT[h, :, c0:c0 + TT], oat, [oat_k], [("ATT", h, t)])


Builder.b_phase1 = _b_phase1
Builder.b_phase3 = _b_phase3


C_SCALE = (128 + 64) ** -0.5


def _c_init(self):
    dt = self.nc.dram_tensor
    if hasattr(self, "KSc"):
        return
    self.KSc = dt("KSc", [2112, LT], BF16).ap()
    self.VSc = dt("VSc", [LT, 2048], BF16).ap()


def _rope(self, x1, x1_k, x2, x2_k, cs, sn, csn_k, o1, o2, o_k, tmps):
    (ta, ta_k), (tb, tb_k) = tmps.next(), tmps.next()
    P32 = slice(0, 32)
    self.dve(lambda e: e.tensor_tensor(out=ta[P32, :], in0=x1[P32, 0:TT], in1=cs[P32, :], op=ALU.mult), [x1_k, csn_k], [ta_k])
    self.dve(lambda e: e.tensor_tensor(out=tb[P32, :], in0=x2[P32, 0:TT], in1=sn[P32, :], op=ALU.mult), [x2_k, csn_k], [tb_k])
    self.dve(lambda e: e.tensor_tensor(out=o1[P32, :], in0=ta[P32, :], in1=tb[P32, :], op=ALU.subtract), [ta_k, tb_k], [o_k])
    (tc, tc_k), (td, td_k) = tmps.next(), tmps.next()
    self.dve(lambda e: e.tensor_tensor(out=tc[P32, :], in0=x2[P32, 0:TT], in1=cs[P32, :], op=ALU.mult), [x2_k, csn_k], [tc_k])
    self.dve(lambda e: e.tensor_tensor(out=td[P32, :], in0=x1[P32, 0:TT], in1=sn[P32, :], op=ALU.mult), [x1_k, csn_k], [td_k])
    self.dve(lambda e: e.tensor_tensor(out=o2[P32, :], in0=tc[P32, :], in1=td[P32, :], op=ALU.add), [tc_k, td_k], [o_k])


def _c_phase1(self, li):
    _c_init(self)
    wd, wuq, wukv = "l%d_c_w_down" % li, "l%d_c_w_uq" % li, "l%d_c_w_ukv" % li
    self.phase_begin()
    ar = self.ar
    self.alloc_common()
    xt, xt_k = ar.alloc([KC, TT], F32, "xt")
    hts = [ar.alloc([KC, TT], BF16, "ht%d" % i) for i in range(2)]
    c32, c32_k = ar.alloc([4, TT], F32, "c32")
    cqn, cqn_k = ar.alloc([4, TT], BF16, "cqn")
    ckvn, ckvn_k = ar.alloc([4, TT], BF16, "ckvn")
    cs, csn_k = ar.alloc([TT], F32, "cos")
    sn, _ = ar.alloc([TT], F32, "sin")
    rtmps = Rot([ar.alloc([TT], F32, "rt%d" % i) for i in range(4)])
    ropo = Rot([(ar.alloc([TT], BF16, "ro1_%d" % i), ar.alloc([TT], BF16, "ro2_%d" % i)) for i in range(2)])
    jobs = []
    for t in range(NT):
        c0 = t * TT
        ht, ht_k = hts[t % 2]

        def j_down(wv, wkey, which, t=t, c0=c0, ht=ht, ht_k=ht_k):
            if which == 0:
                self.load_x_tile(self.XS, "XS", t, xt, xt_k)
                self.dma(cs[0:32, :], self.rope_d[0, :, c0:c0 + TT], [], [csn_k])
                self.dma(sn[0:32, :], self.rope_d[1, :, c0:c0 + TT], [], [csn_k])
                self.rmsnorm(xt, xt_k, KC, TT, "gvec", (2 * li) * 16, ht, ht_k, self.ntmp)
            rhs = lambda kc: ht[:, kc, :]
            if which < 2:
                def cons(m, psum, psk):
                    self.evac(c32[:, m, :], psum[:, 0:TT], [psk], [c32_k])
                self.lin_fm(wv, wkey, KC, 4, rhs, [ht_k], TT, cons)
                dst, dst_k = (cqn, cqn_k) if which == 0 else (ckvn, ckvn_k)
                self.rmsnorm(c32, c32_k, 4, TT, "cnorm", which * 4, dst, dst_k, self.ntmp)
            else:
                got = {}

                def cons(m, psum, psk):
                    got[m] = (psum, psk)
                self.lin_fm(wv, wkey, KC, 2, rhs, [ht_k], TT, cons, mw=32)
                (o1, o1_k), (o2, o2_k) = ropo.next()
                _rope(self, got[0][0], got[0][1], got[1][0], got[1][1], cs, sn, csn_k, o1, o2, o1_k, rtmps)
                self.dma(self.KSc[2048:2080, c0:c0 + TT], o1[0:32, :], [o1_k], [("KSr", 0, t)])
                self.dma(self.KSc[2080:2112, c0:c0 + TT], o2[0:32, :], [o1_k], [("KSr", 1, t)])
        jobs.append((wd, KC, 0, 512, lambda wv, wkey, f=j_down: f(wv, wkey, 0)))
        jobs.append((wd, KC, 512, 512, lambda wv, wkey, f=j_down: f(wv, wkey, 1)))
        jobs.append((wd, KC, 1024, 64, lambda wv, wkey, f=j_down: f(wv, wkey, 2)))
        for hf in range(2):
            def j_uq(wv, wkey, hf=hf, t=t, c0=c0):
                for hl in range(8):
                    h = hf * 8 + hl
                    base = hl * 192
                    psum, psk = self.psrot.next()
                    p1, p1k = self.psrot.next()
                    p2, p2k = self.psrot.next()
                    for (pp, ppk, off, mw) in ((psum, psk, base, 128), (p1, p1k, base + 128, 32), (p2, p2k, base + 160, 32)):
                        for kc in range(4):
                            self.pe(lambda e, pp=pp, off=off, mw=mw, kc=kc: e.matmul(pp[0:mw, 0:TT], wv[:, kc, off:off + mw], cqn[:, kc, :],
                                                                                 start=(kc == 0), stop=(kc == 3)),
                                    [wkey, cqn_k], [ppk])
                    self.store_stage(psum, psk, self.QS[h, :, c0:c0 + TT], ("QS", h, t))
                    (o1, o1_k), (o2, o2_k) = ropo.next()
                    _rope(self, p1, p1k, p2, p2k, cs, sn, csn_k, o1, o2, o1_k, rtmps)
                    self.dma(self.QR[h, 0:32, c0:c0 + TT], o1[0:32, :], [o1_k], [("QR", h, 0, t)])
                    self.dma(self.QR[h, 32:64, c0:c0 + TT], o2[0:32, :], [o1_k], [("QR", h, 1, t)])
            jobs.append((wuq, 4, hf * 1536, 1536, j_uq))
        for hf in range(2):
            def j_ukv(wv, wkey, hf=hf, t=t, c0=c0):
                for hl in range(8):
                    h = hf * 8 + hl
                    psum, psk = self.psrot.next()
                    for kc in range(4):
                        self.pe(lambda e, psum=psum, hl=hl, kc=kc: e.matmul(psum[:, 0:TT], wv[:, kc, hl * 256:hl * 256 + 128], ckvn[:, kc, :],
                                                                          start=(kc == 0), stop=(kc == 3)),
                                [wkey, ckvn_k], [psk])
                    self.store_stage(psum, psk, self.KSc[h * 128:(h + 1) * 128, c0:c0 + TT], ("KS", h, t))
                wvv = wv.rearrange("p k (h x) -> p k h x", x=256)
                for q4 in range(2):
                    h0 = hf * 8 + q4 * 4
                    for s in range(4):
                        psum, psk = self.psrot.next()
                        for kc in range(4):
                            self.pe(lambda e, psum=psum, q4=q4, s=s, kc=kc: e.matmul(psum[:, 0:512], ckvn[:, kc, s * 128:(s + 1) * 128],
                                                                                  wvv[:, kc, q4 * 4:q4 * 4 + 4, 128:256],
                                                                                  start=(kc == 0), stop=(kc == 3)),
                                    [wkey, ckvn_k], [psk])
                        self.store_stage(psum, psk, self.VSc[c0 + s * 128:c0 + (s + 1) * 128, h0 * 128:(h0 + 4) * 128], ("VS", h0, s, t))
            jobs.append((wukv, 4, hf * 2048, 2048, j_ukv))
    self.run_jobs(jobs)
    ksend = [("KS", h, t) for h in range(NH) for t in range(NT)] + [("KSr", i, t) for i in range(2) for t in range(NT)]
    vsend = [("VS", h0, s, t) for h0 in range(0, 16, 4) for s in range(4) for t in range(NT)]
    self.cK8v, self.cKk, (kn, kr) = self.allgather(self.KSc, 2112, LT, "Kc", ksend, "K8c", rpc_force=64)
    self.cV8v, self.cVk, (vn, vr) = self.allgather(self.VSc, LT, 2048, "Vc", vsend, "V8c")
    assert kr == 64 and vr == 128


def _c_phase3(self, li):
    self.phase_begin()
    ar = self.ar
    K8v, V8v = self.cK8v, self.cV8v
    KTs = Rot([ar.alloc([8192], BF16, "cKT%d" % i) for i in range(2)])
    Vps = Rot([ar.alloc([64, 128], BF16, "cVp%d" % i) for i in range(2)])
    KRs = Rot([ar.alloc([8192], BF16, "cKR%d" % i) for i in range(2)])
    QNs = Rot([ar.alloc([512], BF16, "cQN%d" % i) for i in range(2)])
    QRs = Rot([ar.alloc([512], BF16, "cQR%d" % i) for i in range(2)])
    PTs = Rot([ar.alloc([512], BF16, "cPT%d" % i) for i in range(4)])
    recs = Rot([ar.alloc([512], F32, "crec%d" % i) for i in range(2)])
    oats = Rot([ar.alloc([512], BF16, "coat%d" % i) for i in range(3)])
    seqs = [(PSEG, 0, [0, 1])] + [(SSEG, PSEG + SSEG * b, [2 + b]) for b in range(4)]
    cnt = 0
    sidx = 0
    for (seg, lc0, tiles) in seqs:
        L = seg * 8
        nkb = L // 128
        KR, KR_k = KRs.next()
        for r in range(8):
            self.dma(KR[0:64, r * seg:(r + 1) * seg], K8v[r, 32, :, lc0:lc0 + seg], [], [KR_k])
        for h in range(NH):
            KT, KT_k = KTs.next()
            Vp, Vp_k = Vps.next()
            for r in range(8):
                for n2 in range(2):
                    self.dma(KT[64 * n2:64 * n2 + 64, r * seg:(r + 1) * seg], K8v[r, 2 * h + n2, :, lc0:lc0 + seg], [], [KT_k])
                nb = seg // 128
                self.dma(Vp[:, r * nb:(r + 1) * nb, :],
                         V8v[r, lc0 // 128:lc0 // 128 + nb, :, h * 128:(h + 1) * 128].rearrange("n p d -> p n d"), [], [Vp_k])
            for t in tiles:
                c0 = t * TT
                QN, QN_k = QNs.next()
                QRt, QR_k = QRs.next()
                self.dma(QN, self.QS[h, :, c0:c0 + TT], [], [QN_k])
                self.dma(QRt[0:64, :], self.QR[h, :, c0:c0 + TT], [], [QR_k])
                num, num_k = self.ps[4 + cnt % 2]
                den, den_k = self.ps[6 + cnt % 2]
                cnt += 1
                for kb in range(nkb):
                    S, S_k = self.ps[sidx % 4]
                    sidx += 1
                    PT, PT_k = PTs.next()
                    self.pe(lambda e, S=S, KT=KT, QN=QN, kb=kb: e.matmul(S[:, 0:TT], KT[:, kb * 128:(kb + 1) * 128], QN, start=True, stop=False),
                            [KT_k, QN_k], [S_k])
                    self.pe(lambda e, S=S, KR=KR, QRt=QRt, kb=kb: e.matmul(S[:, 0:TT], KR[0:64, kb * 128:(kb + 1) * 128], QRt[0:64, :],
                                                                         start=False, stop=True),
                            [KR_k, QR_k], [S_k])
                    self.act(lambda e, S=S, PT=PT: e.activation(out=PT, in_=S[:, 0:TT], func=AF.Exp, scale=C_SCALE), [S_k], [PT_k])
                    self.pe(lambda e, num=num, Vp=Vp, PT=PT, kb=kb: e.matmul(num[:, 0:TT], Vp[:, kb, :], PT, start=(kb == 0), stop=(kb == nkb - 1)),
                            [Vp_k, PT_k], [num_k])
                    self.pe(lambda e, den=den, PT=PT, kb=kb: e.matmul(den[:, 0:TT], self.ones[:, :], PT, start=(kb == 0), stop=(kb == nkb - 1)),
                            [self.ones_k, PT_k], [den_k])
                rec, rec_k = recs.next()
                oat, oat_k = oats.next()
                self.dve(lambda e, rec=rec, den=den: e.reciprocal(rec, den[:, 0:TT]), [den_k], [rec_k])
                self.dve(lambda e, oat=oat, num=num, rec=rec: e.tensor_tensor(out=oat, in0=num[:, 0:TT], in1=rec, op=ALU.mult),
                         [num_k, rec_k], [oat_k])
                self.dma(self.ATT[h, :, c0:c0 + TT], oat, [oat_k], [("ATT", h, t)])


Builder.c_phase1 = _c_phase1
Builder.c_phase3 = _c_phase3
```

```python
import contextlib
import numpy as np
import ml_dtypes
import concourse.bass as bass
import concourse.mybir as mybir
from concourse.bass_utils import run_bass_kernel_spmd

F32 = mybir.dt.float32
BF16 = mybir.dt.bfloat16
AF = mybir.ActivationFunctionType
ALU = mybir.AluOpType

NCORES = 8


class Op:
    __slots__ = ("eng", "fn", "deps", "is_dma", "signal", "idx", "sem", "val", "prewait", "inc", "force", "raw", "bg")

    def __init__(self, eng, fn, is_dma, inc):
        self.eng = eng
        self.fn = fn
        self.deps = set()
        self.is_dma = is_dma
        self.signal = False
        self.sem = None
        self.val = None
        self.prewait = None
        self.inc = inc
        self.force = False
        self.raw = set()
        self.bg = False


ENGS = ("pe", "act", "dve", "pool", "sp")
SEM_ROT = 12000
N_DMA_SEMS = 12


class Prog:
    def __init__(self, nc, stack):
        self.nc = nc
        self.stack = stack
        self.ops = []
        self.eng_ops = {e: [] for e in ENGS}
        self.last_w = {}
        self.readers = {}
        self.dma_rr = {e: 0 for e in ENGS}
        self.dma_sems = {}
        self.dma_sem_last = {}
        self.eng_sems = {e: [] for e in ENGS}
        self.bar_from = 0
        self.prev_bar = []
        self.bg_last_w = {}

    def op(self, eng, fn, reads=(), writes=(), dma=False, inc=16, force=False, bg=False):
        o = Op(eng, fn, dma, inc)
        o.force = force
        o.bg = bg
        o.idx = len(self.ops)
        for b in reads:
            w = self.last_w.get(b)
            if w is not None:
                o.deps.add(w)
                o.raw.add(w)
        for b in writes:
            w = self.last_w.get(b)
            if w is not None:
                o.deps.add(w)
            for r in self.readers.get(b, ()):
                o.deps.add(r)
        for b in reads:
            self.readers.setdefault(b, []).append(o.idx)
        for b in writes:
            self.last_w[b] = o.idx
            self.readers[b] = []
            if bg:
                self.bg_last_w[b] = o.idx
        o.deps.discard(o.idx)
        self.ops.append(o)
        self.eng_ops[eng].append(o)
        return o

    def barrier(self):
        lasts = []
        for e in ("pe", "act", "dve"):
            for o in reversed(self.eng_ops[e]):
                if o.fn is not None and not o.is_dma:
                    lasts.append(o.idx)
                    break
        dmas = [o.idx for o in self.ops[self.bar_from:] if o.is_dma and not o.bg]
        self.bar_from = len(self.ops)
        prev = list(self.prev_bar)
        self.prev_bar = []
        for e in ENGS:
            o = Op(e, None, False, 0)
            o.idx = len(self.ops)
            o.deps = set(lasts) | set(dmas) | set(prev)
            self.ops.append(o)
            self.eng_ops[e].append(o)
            self.prev_bar.append(o.idx)
        self.last_w = dict(self.bg_last_w)
        self.readers = {}

    def finalize(self):
        nc = self.nc
        ops = self.ops
        for o in ops:
            for d in list(o.deps):
                do = ops[d]
                if (not do.is_dma) and do.eng == o.eng and not o.is_dma and not o.force and not (d in o.raw and o.eng != "pe"):
                    o.deps.discard(d)
                    continue
                do.signal = True
        cnt = {e: 0 for e in ENGS}
        for e in ENGS:
            for o in self.eng_ops[e]:
                if o.is_dma:
                    cc = "cc" if o.inc == 1 else "d"
                    rrk = (e, cc)
                    k = self.dma_rr.get(rrk, 0) % (N_DMA_SEMS if cc == "d" else 8)
                    self.dma_rr[rrk] = self.dma_rr.get(rrk, 0) + 1
                    key = (e, cc, k)
                    if key not in self.dma_sems:
                        self.dma_sems[key] = [self.stack.enter_context(nc.semaphore("%s_%s_%d" % (cc, e, k))), 0]
                    ent = self.dma_sems[key]
                    o.prewait = (ent[0], ent[1]) if ent[1] > 0 else None
                    ent[1] += o.inc
                    o.sem, o.val = ent[0], ent[1]
                elif o.signal and o.fn is not None:
                    ph = cnt[e] // SEM_ROT
                    while len(self.eng_sems[e]) <= ph:
                        self.eng_sems[e].append(
                            self.stack.enter_context(nc.semaphore("c_%s_%d" % (e, len(self.eng_sems[e])))))
                    cnt[e] += 1
                    o.sem = self.eng_sems[e][ph]
                    o.val = cnt[e] - ph * SEM_ROT
                elif o.signal and o.fn is None:
                    pass

        def resolve(d, acc, seen):
            do = ops[d]
            if do.fn is None:
                if d in seen:
                    return
                seen.add(d)
                for dd in do.deps:
                    resolve(dd, acc, seen)
                return
            key = id(do.sem)
            if key not in acc or acc[key][1] < do.val:
                acc[key] = (do.sem, do.val)

        self._resolve = resolve

        with nc.Block() as block:
            def run(e, handle_name):
                deco = getattr(block, handle_name)

                @deco
                def _(h):
                    known = {}
                    for o in self.eng_ops[e]:
                        acc = {}
                        seen = set()
                        for d in o.deps:
                            resolve(d, acc, seen)
                        if o.prewait is not None:
                            s, v = o.prewait
                            if id(s) not in acc or acc[id(s)][1] < v:
                                acc[id(s)] = (s, v)
                        for key, (s, v) in acc.items():
                            if known.get(key, 0) >= v:
                                continue
                            known[key] = v
                            h.wait_ge(s, v)
                        if o.fn is None:
                            continue
                        ins = o.fn(h)
                        if o.sem is not None:
                            if o.is_dma:
                                ins.then_inc(o.sem, o.inc)
                            else:
                                ins.then_inc(o.sem, 1)

            run("sp", "sync")
            run("pool", "gpsimd")
            run("act", "scalar")
            run("dve", "vector")
            run("pe", "tensor")


D = 2048
KC = 16
DFF = 5632
FC = 44
TT = 512
NT = 6
LT = 3072
HD = 128
NH = 16
EPS = 1e-6
PSEG = 1024
SSEG = 512
NEG = -30000.0
BIGR = 1.0e6
B_GROUPS = ((128, 1), (512, 4), (2048, 16))
I32 = mybir.dt.int32


def slopes16():
    return [2.0 ** (-8.0 * (h + 1) / 16.0) for h in range(16)]


def tile_cols(t):
    return t * TT


def tile_type(t):
    return 0 if t == 0 else (1 if t == 1 else 2)


def window_pieces(t, halo):
    base = 0 if t < 2 else PSEG + SSEG * (t - 2)
    seg = PSEG if t < 2 else SSEG
    off = TT * t if t < 2 else 0
    lo = off - halo
    hi = off + TT + halo
    pieces = []
    rel_lo = lo // seg
    rel_hi = (hi - 1) // seg
    for rel in range(rel_lo, rel_hi + 1):
        a = max(lo, rel * seg)
        b = min(hi, (rel + 1) * seg)
        pieces.append((rel, base + a - rel * seg, b - a))
    return pieces


class Arena:
    def __init__(self, ap, nwords):
        self.ap = ap
        self.n = nwords
        self.off = 0
        self.cnt = 0

    def alloc(self, shape, dtype, key=None):
        n = int(np.prod(shape))
        sz = 4 if dtype in (F32, I32) else 2
        words = (n * sz + 3) // 4
        words = (words + 15) // 16 * 16
        assert self.off + words <= self.n, ("arena overflow", self.off, words, self.n)
        a = self.ap[:, self.off:self.off + words]
        if dtype != F32:
            a = a.bitcast(dtype)
        a = a[:, 0:n]
        if len(shape) == 2:
            a = a.rearrange("p (a b) -> p a b", b=shape[1])
        elif len(shape) == 3:
            a = a.rearrange("p (a b c) -> p a b c", b=shape[1], c=shape[2])
        self.off += words
        self.cnt += 1
        return a, (key or ("ar%d" % self.cnt)) + "@%d" % self.off


class Rot:
    def __init__(self, items):
        self.items = items
        self.i = 0

    def next(self):
        it = self.items[self.i % len(self.items)]
        self.i += 1
        return it


def weight_specs():
    specs = []
    for i in range(4):
        pre = "l%d_" % i
        kind = i % 3
        if kind == 0:
            specs += [(pre + "a_w_qkv", 2048, 3072), (pre + "a_w_o", 2048, 2048)]
        elif kind == 1:
            specs += [(pre + "b_w_qkv", 2048, 18432), (pre + "b_w_o", 2048, 2048)]
        else:
            specs += [(pre + "c_w_down", 2048, 1088), (pre + "c_w_uq", 512, 3072),
                      (pre + "c_w_ukv", 512, 4096), (pre + "c_w_o", 2048, 2048)]
        specs += [(pre + "ffn_w_in", 2048, 11264), (pre + "ffn_w_out", 5632, 2048)]
    return specs


CF = {}
_o = 0
for _n, _w in (("gvec", 144), ("convw", 528), ("convb", 176), ("sink", 32), ("cnorm", 8), ("RA", 384),
               ("RB", 256), ("EA", 18), ("EB", 45), ("fl", 2)):
    CF[_n] = _o
    _o += _w
NCF = _o


class Builder:
    def __init__(self, n_layers=4, stop_mid=False, arena_words=46000):
        self.n_layers = n_layers
        self.stop_mid = stop_mid
        self.nc = bass.Bass("TRN2", target_bir_lowering=False)
        nc = self.nc
        self.st = contextlib.ExitStack()
        self.P = Prog(nc, self.st)
        self.x0 = nc.dram_tensor("x0T", [D, LT], F32, kind="ExternalInput").ap()
        self.cf_d = nc.dram_tensor("cf32", [128, NCF], F32, kind="ExternalInput").ap()
        self.rope_d = nc.dram_tensor("rope", [2, 32, LT], F32, kind="ExternalInput").ap()
        self.nb_d = nc.dram_tensor("nb", [1, 8], I32, kind="ExternalInput").ap()
        self.yT = nc.dram_tensor("yT", [D, LT], F32, kind="ExternalOutput").ap()
        self.w32 = {}
        self.wb = {}
        self.wkeys = {}
        for name, k, n in weight_specs():
            if int(name[1]) >= n_layers:
                continue
            self.w32[name] = nc.dram_tensor(name, [k, n], F32, kind="ExternalInput").ap()
            self.wb[name] = nc.dram_tensor("wb_" + name, [k, n], BF16).ap()
        dt = nc.dram_tensor
        self.XS = dt("XS", [KC, 128, LT], F32).ap()
        self.XM = dt("XM", [KC, 128, LT], F32).ap()
        self.H2 = dt("H2", [KC, 128, LT], BF16).ap()
        self.ATT = dt("ATT", [NH, 128, LT], BF16).ap()
        self.QS = dt("QS", [48, 128, LT], BF16).ap()
        self.QR = dt("QR", [NH, 64, LT], BF16).ap()
        self.KSa = dt("KSa", [512, LT], BF16).ap()
        self.NKa = {rel: dt("NKa%d" % (rel + 2), [512, LT], BF16).ap() for rel in (-1, 1)}
        self.NVa = {rel: dt("NVa%d" % (rel + 2), [LT, 512], BF16).ap() for rel in (-1, 1)}
        self.NHB = {rel: dt("NHB%d" % (rel + 2), [128, 160], BF16).ap() for rel in (-1, 1)}
        self.VSa = dt("VSa", [LT, 512], BF16).ap()
        self.HBs = dt("HBs", [128, 160], BF16).ap()
        self.arena_t = self.st.enter_context(nc.sbuf_tensor("arena", [128, arena_words], F32))
        self.ar = Arena(self.arena_t[:, :], arena_words)
        self.ps = []
        for i in range(8):
            t = self.st.enter_context(nc.psum_tensor("ps%d" % i, [128, 512], F32))
            self.ps.append((t[:, :], "ps%d" % i))
        self.psrot = Rot(self.ps)
        self.evac_i = 0
        self.regv = {}
        self.ag_bufs = {}

    def dma(self, out, in_, r, w, eng="sp"):
        return self.P.op(eng, lambda e: e.dma_start(out=out, in_=in_), reads=r, writes=w, dma=True)

    def localize(self, dst, g8view, rel, r, w):
        eng = "sp" if self.dyn_cnt["sp"] <= self.dyn_cnt["pool"] else "pool"
        self.dyn_cnt[eng] += 1
        assert self.dyn_cnt[eng] <= 21, "dynamic DMA register budget exceeded"

        def fn(e):
            v = self.regv[(eng, rel)]
            return e.dma_start(out=dst, in_=g8view[bass.ds(v, 1)])
        return self.P.op(eng, fn, reads=r, writes=w, dma=True)

    def pe(self, fn, r, w):
        return self.P.op("pe", fn, reads=r, writes=w)

    def act(self, fn, r, w):
        return self.P.op("act", fn, reads=r, writes=w)

    def dve(self, fn, r, w, force=False):
        return self.P.op("dve", fn, reads=r, writes=w, force=force)

    def evac(self, out, in_, r, w):
        self.evac_i += 1
        if self.evac_i % 2 == 0:
            return self.act(lambda e: e.activation(out=out, in_=in_, func=AF.Copy), r, w)
        return self.dve(lambda e: e.tensor_copy(out, in_), r, w)

    def allgather(self, send, R, C, name, key_send, key_out, rpc_force=None):
        nc = self.nc
        rpc = 1
        for cand in range(1, R + 1):
            if R % cand == 0 and cand * C * 2 <= 512 * 1024:
                rpc = cand
        if rpc_force:
            rpc = rpc_force
        nch = R // rpc
        if name not in self.ag_bufs:
            self.ag_bufs[name] = (nc.dram_tensor("g4_" + name, [nch * 4 * rpc, C], BF16).ap(),
                                  nc.dram_tensor("g8_" + name, [nch * 8 * rpc, C], BF16).ap())
        g4, g8 = self.ag_bufs[name]
        for stage in (1, 2):
            for c in range(nch):
                s_ap = send[c * rpc:(c + 1) * rpc, :]
                g4c = g4[c * 4 * rpc:(c + 1) * 4 * rpc, :]
                g8c = g8[c * 8 * rpc:(c + 1) * 8 * rpc, :]
                k4 = (key_out, "g4", c)
                if stage == 1:
                    def c1(e, s_ap=s_ap, g4c=g4c):
                        return e.collective_compute("AllGather", ALU.bypass, replica_groups=[[0, 1, 2, 3], [4, 5, 6, 7]],
                                                    ins=[s_ap.opt()], outs=[g4c.opt()])
                    self.P.op("pool", c1, reads=key_send, writes=[k4], dma=True, inc=1)
                else:
                    def c2(e, g4c=g4c, g8c=g8c):
                        return e.collective_compute("AllGather", ALU.bypass, replica_groups=[[0, 4], [1, 5], [2, 6], [3, 7]],
                                                    ins=[g4c.opt()], outs=[g8c.opt()])
                    self.P.op("pool", c2, reads=[k4], writes=[(key_out, c)], dma=True, inc=1)
        keys = [(key_out, c) for c in range(nch)]
        return g8.rearrange("(n r i) c -> r n i c", r=8, i=rpc), keys, (nch, rpc)

    def convert_layer(self, li):
        for name, k, n in weight_specs():
            if name not in self.w32 or int(name[1]) != li:
                continue
            rows = 64 if n > 4096 else 256
            keys = []
            for r0 in range(0, k, rows):
                r1 = min(k, r0 + rows)
                key = ("wb", name, r0)
                keys.append(key)
                self.P.op("pool", lambda e, r0=r0, r1=r1, name=name: e.dma_start(out=self.wb[name][r0:r1, :], in_=self.w32[name][r0:r1, :]),
                          reads=[], writes=[key], dma=True, bg=True)
            self.wkeys[name] = keys

    def setup(self):
        ar = self.ar
        P = self.P
        self.cf, self.cf_k = ar.alloc([NCF], F32, "cf")
        self.cf = self.cf
        self.dma(self.cf, self.cf_d, [], [self.cf_k])
        self.nbs, self.nbs_k = ar.alloc([8], I32, "nbs")
        self.dma(self.nbs[0:1, :], self.nb_d, [], [self.nbs_k])

        self.dyn_cnt = {"sp": 0, "pool": 0}
        for eng in ("sp", "pool"):
            def ldregs(e, eng=eng):
                for rel in (-2, -1, 1, 2):
                    reg = e.alloc_register("nbr%d" % (rel + 2))
                    e.reg_load(reg, self.nbs[0:1, rel + 2:rel + 3])
                    self.regv[(eng, rel)] = e.snap(reg)
                return None
            P.op(eng, ldregs, reads=[self.nbs_k], writes=[])
        self.ones, self.ones_k = ar.alloc([128], BF16, "ones")
        self.dve(lambda e: e.memset(self.ones, 1.0), [], [self.ones_k])
        self.esink, self.esink_k = ar.alloc([32], F32, "esink")
        o = CF["sink"]
        self.act(lambda e: e.activation(out=self.esink, in_=self.cf[:, o:o + 32], func=AF.Exp),
                 [self.cf_k], [self.esink_k])
        self.haloL, self.haloL_k = ar.alloc([16, 5], BF16, "haloL")
        self.haloR, self.haloR_k = ar.alloc([16, 5], BF16, "haloR")
        self.hb, self.hb_k = ar.alloc([2, 16, 5], BF16, "hb")
        self.mark = ar.off
        self.x0v = self.x0.rearrange("(k p) t -> k p t", p=128)

    def phase_begin(self):
        self.P.barrier()
        self.ar.off = self.mark

    def cfcol(self, name, idx):
        o = CF[name] + idx
        return self.cf[:, o:o + 1]

    def rmsnorm(self, xt, xt_k, nchunks, width, gname, gidx0, out, out_k, tmp, out_fn=None, post=None):
        psum, psk = self.psrot.next()
        n_feat = nchunks * 128
        for c in range(nchunks):
            sq, sqk = tmp["sq"].next()
            self.act(lambda e, c=c, sq=sq: e.activation(out=sq[:, 0:width], in_=xt[:, c, 0:width], func=AF.Square),
                     [xt_k], [sqk])
            self.pe(lambda e, c=c, sq=sq: e.matmul(psum[:, 0:width], self.ones[:, :], sq[:, 0:width],
                                                  start=(c == 0), stop=(c == nchunks - 1)),
                    [sqk, self.ones_k], [psk])
        rs, rsk = tmp["rstd"]
        self.act(lambda e: e.activation(out=rs[:, 0:width], in_=psum[:, 0:width], func=AF.Sqrt,
                                        scale=1.0 / n_feat, bias=EPS), [psk], [rsk])
        self.dve(lambda e: e.reciprocal(rs[:, 0:width], rs[:, 0:width]), [rsk], [rsk])
        for c in range(nchunks):
            g = self.cfcol(gname, gidx0 + c)
            if out_fn is not None:
                o_ap, o_k = out_fn(c)
            else:
                o_ap, o_k = out[:, c, 0:width], out_k
            self.dve(lambda e, c=c, g=g, o_ap=o_ap: e.scalar_tensor_tensor(out=o_ap, in0=xt[:, c, 0:width],
                                                                          scalar=g, in1=rs[:, 0:width],
                                                                          op0=ALU.mult, op1=ALU.mult),
                     [xt_k, rsk, self.cf_k], [o_k])
            if post is not None:
                post(c, o_ap, o_k)

    def lin_fm(self, wv, wkey, kc_n, nchunk, rhs_fn, rhs_keys, width, consumer, m0=0, mw=128):
        for m in range(nchunk):
            psum, psk = self.psrot.next()
            for kc in range(kc_n):
                self.pe(lambda e, m=m, kc=kc, psum=psum: e.matmul(psum[0:mw, 0:width], wv[:, kc, m * mw:(m + 1) * mw],
                                                                rhs_fn(kc), start=(kc == 0), stop=(kc == kc_n - 1)),
                        [wkey] + rhs_keys, [psk])
            consumer(m0 + m, psum, psk)

    def lin_tm(self, wv_cols_fn, wkey, kc_n, ncols, lhs_fn, lhs_keys, nsub, consumer):
        for s in range(nsub):
            psum, psk = self.psrot.next()
            for kc in range(kc_n):
                self.pe(lambda e, s=s, kc=kc, psum=psum: e.matmul(psum[:, 0:ncols], lhs_fn(kc, s), wv_cols_fn(kc),
                                                                start=(kc == 0), stop=(kc == kc_n - 1)),
                        [wkey] + lhs_keys, [psk])
            consumer(s, psum, psk)

    def alloc_common(self, wsize=8192, nw=3):
        ar = self.ar
        self.wbufs = Rot([ar.alloc([wsize], BF16, "wbuf%d" % i) for i in range(nw)])
        self.stg16 = Rot([ar.alloc([512], BF16, "stg16_%d" % i) for i in range(4)])
        self.sqr = Rot([ar.alloc([512], BF16, "sq%d" % i) for i in range(2)])
        self.rstd = ar.alloc([512], F32, "rstd")
        self.ntmp = dict(sq=self.sqr, rstd=self.rstd)

    def wload(self, name, kc_n, c0, ncols):
        ap, key = self.wbufs.next()
        dst = ap[:, 0:kc_n * ncols].rearrange("p (k n) -> p k n", n=ncols)
        src = self.wb[name][:, c0:c0 + ncols].rearrange("(k p) n -> p k n", p=128)
        self.dma(dst, src, self.wkeys[name], [key])
        return dst, key

    def run_jobs(self, jobs, depth=2):
        loaded = {}
        for i in range(min(depth, len(jobs))):
            loaded[i] = self.wload(*jobs[i][0:4])
        for i, job in enumerate(jobs):
            if i + depth < len(jobs):
                loaded[i + depth] = self.wload(*jobs[i + depth][0:4])
            wv, wkey = loaded.pop(i)
            job[4](wv, wkey)

    def load_x_tile(self, src, srcname, t, xt, xt_k):
        c0 = t * TT
        self.dma(xt[:, :, 0:TT], src[:, :, c0:c0 + TT].rearrange("k p t -> p k t"), [(srcname, t)], [xt_k])

    def store_stage(self, psum, psk, dst, dst_key, width=TT, npart=128):
        st, stk = self.stg16.next()
        self.evac(st[0:npart, 0:width], psum[0:npart, 0:width], [psk], [stk])
        self.dma(dst, st[0:npart, 0:width], [stk], [dst_key])

    def a_phase1(self, li):
        wn = "l%d_a_w_qkv" % li
        xsrc, xname = (self.x0v, "x0") if li == 0 else (self.XS, "XS")
        self.phase_begin()
        ar = self.ar
        self.alloc_common()
        xts = [ar.alloc([KC, TT], F32, "xt%d" % i) for i in range(2)]
        hts = [ar.alloc([KC, TT], BF16, "ht%d" % i) for i in range(2)]
        jobs = []
        for t in range(NT):
            c0 = t * TT
            xt, xt_k = xts[t % 2]
            ht, ht_k = hts[t % 2]
            for blk in range(6):
                def fn(wv, wkey, t=t, c0=c0, blk=blk, xt=xt, xt_k=xt_k, ht=ht, ht_k=ht_k):
                    if blk == 0:
                        if t == 0:
                            self.load_x_tile(xsrc, xname, 0, xt, xt_k)
                        if t + 1 < NT:
                            self.load_x_tile(xsrc, xname, t + 1, *xts[(t + 1) % 2])
                        self.rmsnorm(xt, xt_k, KC, TT, "gvec", (2 * li) * 16, ht, ht_k, self.ntmp)
                    rhs = lambda kc: ht[:, kc, :]
                    if blk < 4:
                        def cons(m, psum, psk):
                            self.store_stage(psum, psk, self.QS[m, :, c0:c0 + TT], ("QS", m, t))
                        self.lin_fm(wv, wkey, KC, 4, rhs, [ht_k], TT, cons, m0=blk * 4)
                    elif blk == 4:
                        def cons(m, psum, psk):
                            self.store_stage(psum, psk, self.KSa[m * 128:(m + 1) * 128, c0:c0 + TT], ("KS", m, t))
                        self.lin_fm(wv, wkey, KC, 4, rhs, [ht_k], TT, cons)
                    else:
                        def cons(s, psum, psk):
                            self.store_stage(psum, psk, self.VSa[c0 + s * 128:c0 + (s + 1) * 128, :], ("VS", s, t))
                        self.lin_tm(lambda kc: wv[:, kc, 0:512], wkey, KC, 512,
                                    lambda kc, s: ht[:, kc, s * 128:(s + 1) * 128], [ht_k], 4, cons)
                jobs.append((wn, KC, blk * 512, 512, fn))
        self.run_jobs(jobs)
        ksend = [("KS", m, t) for m in range(4) for t in range(NT)]
        vsend = [("VS", s, t) for s in range(4) for t in range(NT)]
        K8v, kk, (kn, kr) = self.allgather(self.KSa, 512, LT, "Ka", ksend, "K8")
        V8v, vk, (vn, vr) = self.allgather(self.VSa, LT, 512, "Va", vsend, "V8")
        for rel in (-1, 1):
            self.localize(self.NKa[rel].rearrange("(n i) c -> n i c", i=kr), K8v, rel, kk, [("NKa", rel)])
            self.localize(self.NVa[rel].rearrange("(n i) c -> n i c", i=vr), V8v, rel, vk, [("NVa", rel)])


    def run_attn(self, groups):
        flat = [(gi, bi, b) for gi, (pre, blocks) in enumerate(groups) for bi, b in enumerate(blocks)]
        if not flat:
            return
        groups[0][0]()
        called = {0}
        flat[0][2][0]()
        for i, (gi, bi, b) in enumerate(flat):
            if bi == 0 and gi + 1 < len(groups) and (gi + 1) not in called:
                groups[gi + 1][0]()
                called.add(gi + 1)
            b[1]()
            if i + 1 < len(flat):
                flat[i + 1][2][0]()
            b[2]()
            if b[3] is not None:
                b[3]()

    def a_phase3(self, li):
        self.phase_begin()
        if li + 1 < self.n_layers:
            self.convert_layer(li + 1)
        ar = self.ar
        sl = slopes16()
        scale = HD ** -0.5
        sink_base = (0 if li == 0 else 1) * 16
        KTs = Rot([ar.alloc([768], BF16, "KTw%d" % i) for i in range(2)])
        Vws = Rot([ar.alloc([6, 128], BF16, "Vw%d" % i) for i in range(2)])
        QTs = Rot([ar.alloc([512], BF16, "QT%d" % i) for i in range(8)])
        tmps = Rot([ar.alloc([384], F32, "tmp%d" % i) for i in range(3)])
        PTs = Rot([ar.alloc([384], BF16, "PT%d" % i) for i in range(3)])
        recs = Rot([ar.alloc([512], F32, "rec%d" % i) for i in range(2)])
        oats = Rot([ar.alloc([512], BF16, "oat%d" % i) for i in range(3)])
        RA0 = CF["RA"]
        hh = 0
        sidx = 0
        groups = []
        for t in range(NT):
            c0 = t * TT
            tt_ = tile_type(t)
            pieces = window_pieces(t, 128)
            for kvh in range(4):
                KT, KT_k = KTs.next()
                Vw, Vw_k = Vws.next()
                qts = [QTs.next() for _ in range(4)]

                def pre(t=t, c0=c0, kvh=kvh, KT=KT, KT_k=KT_k, Vw=Vw, Vw_k=Vw_k, qts=qts, pieces=pieces):
                    w0 = 0
                    for (rel, lc, ln) in pieces:
                        ksrc = self.KSa if rel == 0 else self.NKa[rel]
                        vsrc = self.VSa if rel == 0 else self.NVa[rel]
                        self.dma(KT[:, w0:w0 + ln], ksrc[kvh * 128:(kvh + 1) * 128, lc:lc + ln], [], [KT_k])
                        b0 = w0 // 128
                        nb = ln // 128
                        self.dma(Vw[:, b0:b0 + nb, :],
                                 vsrc[lc:lc + ln, kvh * 128:(kvh + 1) * 128].rearrange("(b p) d -> p b d", p=128), [], [Vw_k])
                        w0 += ln
                    assert w0 == 768
                    for g4 in range(4):
                        h = kvh * 4 + g4
                        self.dma(qts[g4][0], self.QS[h, :, c0:c0 + TT], [], [qts[g4][1]])
                blocks = []
                for g4 in range(4):
                    h = kvh * 4 + g4
                    QT, QT_k = qts[g4]
                    num, num_k = self.ps[3 + hh % 2]
                    den, den_k = self.ps[5 + hh % 2]
                    hh += 1
                    for j in range(6):
                        q_lo = max(0, 128 * j - 256)
                        q_hi = min(512, 128 * j + 128)
                        n = q_hi - q_lo
                        cb = q_lo - (128 * j - 256)
                        S, S_k = self.ps[sidx % 3]
                        sidx += 1
                        tmp, tmp_k = tmps.next()
                        PT, PT_k = PTs.next()
                        coef = -sl[h] / scale
                        ecol = self.cfcol("EA", tt_ * 6 + j)

                        def fS(S=S, S_k=S_k, KT=KT, KT_k=KT_k, QT=QT, QT_k=QT_k, j=j, q_lo=q_lo, q_hi=q_hi, n=n):
                            self.pe(lambda e: e.matmul(S[:, 0:n], KT[:, 128 * j:128 * j + 128], QT[:, q_lo:q_hi], start=True, stop=True),
                                    [KT_k, QT_k], [S_k])

                        def fsoft(S=S, S_k=S_k, tmp=tmp, tmp_k=tmp_k, PT=PT, PT_k=PT_k, cb=cb, n=n, coef=coef, ecol=ecol):
                            self.dve(lambda e: e.scalar_tensor_tensor(out=tmp[:, 0:n], in0=self.cf[:, RA0 + cb:RA0 + cb + n], scalar=coef,
                                                                      in1=S[:, 0:n], op0=ALU.mult, op1=ALU.add),
                                     [S_k, self.cf_k], [tmp_k])
                            self.act(lambda e: e.activation(out=PT[:, 0:n], in_=tmp[:, 0:n], func=AF.Exp, bias=ecol, scale=scale),
                                     [tmp_k, self.cf_k], [PT_k])

                        def fPV(num=num, num_k=num_k, den=den, den_k=den_k, Vw=Vw, Vw_k=Vw_k, PT=PT, PT_k=PT_k, j=j, q_lo=q_lo, q_hi=q_hi, n=n):
                            self.pe(lambda e: e.matmul(num[:, q_lo:q_hi], Vw[:, j, :], PT[:, 0:n], start=(j == 0), stop=(j == 5),
                                                       skip_group_check=True), [Vw_k, PT_k], [num_k])
                            self.pe(lambda e: e.matmul(den[:, q_lo:q_hi], self.ones[:, :], PT[:, 0:n], start=(j == 0), stop=(j == 5),
                                                       skip_group_check=True), [self.ones_k, PT_k], [den_k])
                        post = None
                        if j == 5:
                            def post(num=num, num_k=num_k, den=den, den_k=den_k, h=h, t=t, c0=c0):
                                rec, rec_k = recs.next()
                                oat, oat_k = oats.next()
                                sk = sink_base + h
                                self.dve(lambda e: e.tensor_scalar(out=rec, in0=den, scalar1=self.esink[:, sk:sk + 1], scalar2=None, op0=ALU.add),
                                         [den_k, self.esink_k], [rec_k])
                                self.dve(lambda e: e.reciprocal(rec, rec), [rec_k], [rec_k])
                                self.dve(lambda e: e.tensor_tensor(out=oat, in0=num, in1=rec, op=ALU.mult), [num_k, rec_k], [oat_k])
                                self.dma(self.ATT[h, :, c0:c0 + TT], oat, [oat_k], [("ATT", h, t)])
                        blocks.append((fS, fsoft, fPV, post))
                groups.append((pre, blocks))
        self.run_attn(groups)

    def oproj_phase(self, li, wn):
        xsrc, xname = (self.x0v, "x0") if li == 0 else (self.XS, "XS")
        self.phase_begin()
        ar = self.ar
        self.alloc_common()
        xts = [ar.alloc([KC, TT], F32, "xt%d" % i) for i in range(2)]
        ats = [ar.alloc([KC, TT], BF16, "at%d" % i) for i in range(2)]
        h2s = [ar.alloc([KC, TT], BF16, "h2_0")] * 2
        jobs = []

        def load_tile(t):
            c0 = t * TT
            self.load_x_tile(xsrc, xname, t, *xts[t % 2])
            at, at_k = ats[t % 2]
            self.dma(at, self.ATT[:, :, c0:c0 + TT].rearrange("h p t -> p h t"), [("ATT", h, t) for h in range(NH)], [at_k])

        for t in range(NT):
            c0 = t * TT
            xt, xt_k = xts[t % 2]
            at, at_k = ats[t % 2]
            h2, h2_k = h2s[t % 2]
            for blk in range(4):
                def fn(wv, wkey, t=t, c0=c0, blk=blk, xt=xt, xt_k=xt_k, at=at, at_k=at_k, h2=h2, h2_k=h2_k):
                    if blk == 0:
                        if t == 0:
                            load_tile(0)
                        if t + 1 < NT:
                            load_tile(t + 1)

                    def cons(m, psum, psk):
                        self.dve(lambda e: e.tensor_tensor(out=xt[:, m, :], in0=xt[:, m, :], in1=psum[:, 0:TT], op=ALU.add),
                                 [psk, xt_k], [xt_k])
                    self.lin_fm(wv, wkey, KC, 4, lambda kc: at[:, kc, :], [at_k], TT, cons, m0=blk * 4)
                    if blk == 3:
                        self.dma(self.XM[:, :, c0:c0 + TT].rearrange("k p t -> p k t"), xt, [xt_k], [("XM", t)])
                        self.rmsnorm(xt, xt_k, KC, TT, "gvec", (2 * li + 1) * 16, h2, h2_k, self.ntmp)
                        self.dma(self.H2[:, :, c0:c0 + TT].rearrange("k p t -> p k t"), h2, [h2_k], [("H2", t)])
                        bl = []
                        if t == 0:
                            bl = [(0, 0, 0)]
                        elif t == 1:
                            bl = [(1, 0, TT - 1)]
                        else:
                            bl = [(0, t - 1, 0), (1, t - 1, TT - 1)]
                        for (side, seg, col) in bl:
                            self.dve(lambda e, side=side, seg=seg, col=col: e.tensor_copy(self.hb[:, side, :, seg], h2[:, :, col]),
                                     [h2_k], [self.hb_k], force=True)
                jobs.append((wn, KC, blk * 512, 512, fn))
        self.run_jobs(jobs)
        self.dma(self.HBs, self.hb.rearrange("p a b c -> p (a b c)"), [self.hb_k], ["HBs"])
        HBv, hk, (hn, hr) = self.allgather(self.HBs, 128, 160, "HB", ["HBs"], "HB8")
        tl, tl_k = ar.alloc([80], BF16, "tl")
        tr, tr_k = ar.alloc([80], BF16, "tr")
        self.localize(self.NHB[-1].rearrange("(n i) c -> n i c", i=hr), HBv, -1, hk, [("NHB", -1)])
        self.localize(self.NHB[1].rearrange("(n i) c -> n i c", i=hr), HBv, 1, hk, [("NHB", 1)])
        self.dma(tl, self.NHB[-1][:, 80:160], [("NHB", -1)], [tl_k])
        self.dma(tr, self.NHB[1][:, 0:80], [("NHB", 1)], [tr_k])
        fl = CF["fl"]
        self.dve(lambda e: e.tensor_scalar(out=self.haloL.rearrange("p a b -> p (a b)"), in0=tl,
                                           scalar1=self.cf[:, fl:fl + 1], scalar2=None, op0=ALU.mult),
                 [tl_k, self.cf_k], [self.haloL_k])
        self.dve(lambda e: e.tensor_scalar(out=self.haloR.rearrange("p a b -> p (a b)"), in0=tr,
                                           scalar1=self.cf[:, fl + 1:fl + 2], scalar2=None, op0=ALU.mult),
                 [tr_k, self.cf_k], [self.haloR_k])

    def ffn_phase(self, li, last):
        self.phase_begin()
        ar = self.ar
        self.alloc_common(wsize=5632, nw=3)
        win = "l%d_ffn_w_in" % li
        wout = "l%d_ffn_w_out" % li
        g, _ = ar.alloc([FC, TT], BF16, "g")
        h2es = [ar.alloc([KC, TT + 2], BF16, "h2e%d" % i) for i in range(2)]
        xt, xt_k = ar.alloc([KC, TT], F32, "xt")
        xmcs = Rot([ar.alloc([512], F32, "xmc%d" % i) for i in range(2)])
        aexts = Rot([ar.alloc([TT + 2], F32, "aext%d" % i) for i in range(2)])
        cbs = Rot([ar.alloc([512], F32, "cb%d" % i) for i in range(2)])
        gls = Rot([ar.alloc([512], F32, "gl%d" % i) for i in range(4)])
        cw0 = CF["convw"] + li * FC * 3
        cb0 = CF["convb"] + li * FC

        def load_h2e(t):
            c0 = t * TT
            h2e, k = h2es[t % 2]
            if t == 0:
                self.dma(h2e[:, :, 1:TT + 2], self.H2[:, :, c0:c0 + TT + 1].rearrange("k p t -> p k t"),
                         [("H2", 0), ("H2", 1)], [k])
                self.dve(lambda e: e.tensor_copy(h2e[:, :, 0], self.haloL[:, :, 0]), [self.haloL_k], [k])
            elif t == 1:
                self.dma(h2e[:, :, 0:TT + 1], self.H2[:, :, c0 - 1:c0 + TT].rearrange("k p t -> p k t"),
                         [("H2", 0), ("H2", 1)], [k])
                self.dve(lambda e: e.tensor_copy(h2e[:, :, TT + 1], self.haloR[:, :, 0]), [self.haloR_k], [k])
            else:
                self.dma(h2e[:, :, 1:TT + 1], self.H2[:, :, c0:c0 + TT].rearrange("k p t -> p k t"), [("H2", t)], [k])
                self.dve(lambda e: e.tensor_copy(h2e[:, :, 0], self.haloL[:, :, t - 1]), [self.haloL_k], [k])
                self.dve(lambda e: e.tensor_copy(h2e[:, :, TT + 1], self.haloR[:, :, t - 1]), [self.haloR_k], [k])

        jobs = []
        for t in range(NT):
            c0 = t * TT
            h2e, h2e_k = h2es[t % 2]
            glbuf = {}
            for jb in range(FC // 2):
                def gate(wv, wkey, t=t, jb=jb, h2e=h2e, h2e_k=h2e_k, glbuf=glbuf):
                    if jb == 0:
                        if t == 0:
                            load_h2e(0)
                        if t + 1 < NT:
                            load_h2e(t + 1)
                    for jj in range(2):
                        j = jb * 2 + jj
                        a_ps, a_k = self.psrot.next()
                        ah_ps, ah_k = self.psrot.next()
                        for kc in range(KC):
                            self.pe(lambda e, kc=kc, jj=jj, a_ps=a_ps: e.matmul(a_ps[:, 0:TT], wv[:, kc, jj * 128:(jj + 1) * 128],
                                                                              h2e[:, kc, 1:TT + 1], start=(kc == 0), stop=(kc == KC - 1)),
                                    [wkey, h2e_k], [a_k])
                        for kc in range(KC):
                            self.pe(lambda e, kc=kc, jj=jj, ah_ps=ah_ps: e.matmul(ah_ps[:, 0:2], wv[:, kc, jj * 128:(jj + 1) * 128],
                                                                                h2e[:, kc, 0:TT + 2:TT + 1], start=(kc == 0), stop=(kc == KC - 1)),
                                    [wkey, h2e_k], [ah_k])
                        aext, ax_k = aexts.next()
                        cb, cb_k = cbs.next()
                        gl, gl_k = gls.next()
                        glbuf[j] = (gl, gl_k)
                        self.act(lambda e, aext=aext, a_ps=a_ps: e.activation(out=aext[:, 1:TT + 1], in_=a_ps[:, 0:TT], func=AF.Copy),
                                 [a_k], [ax_k])
                        self.act(lambda e, aext=aext, ah_ps=ah_ps: e.activation(out=aext[:, 0:TT + 2:TT + 1], in_=ah_ps[:, 0:2], func=AF.Copy),
                                 [ah_k], [ax_k])
                        w0c = self.cf[:, cw0 + j * 3 + 0:cw0 + j * 3 + 1]
                        w1c = self.cf[:, cw0 + j * 3 + 1:cw0 + j * 3 + 2]
                        w2c = self.cf[:, cw0 + j * 3 + 2:cw0 + j * 3 + 3]
                        bc = self.cf[:, cb0 + j:cb0 + j + 1]
                        self.act(lambda e, cb=cb, a_ps=a_ps, w1c=w1c, bc=bc: e.activation(out=cb, in_=a_ps[:, 0:TT], func=AF.Identity,
                                                                                         bias=bc, scale=w1c),
                                 [a_k, self.cf_k], [cb_k])
                        self.dve(lambda e, cb=cb, aext=aext, w0c=w0c: e.scalar_tensor_tensor(out=cb, in0=aext[:, 0:TT], scalar=w0c, in1=cb,
                                                                                            op0=ALU.mult, op1=ALU.add),
                                 [ax_k, cb_k, self.cf_k], [cb_k])
                        self.dve(lambda e, cb=cb, aext=aext, w2c=w2c: e.scalar_tensor_tensor(out=cb, in0=aext[:, 2:TT + 2], scalar=w2c, in1=cb,
                                                                                            op0=ALU.mult, op1=ALU.add),
                                 [ax_k, cb_k, self.cf_k], [cb_k])
                        self.act(lambda e, gl=gl, cb=cb: e.activation(out=gl, in_=cb, func=AF.Gelu), [cb_k], [gl_k])

                def val(wv, wkey, t=t, jb=jb, h2e=h2e, h2e_k=h2e_k, glbuf=glbuf):
                    for jj in range(2):
                        j = jb * 2 + jj
                        gl, gl_k = glbuf[j]

                        def cons(m, psum, psk, j=j, gl=gl, gl_k=gl_k):
                            self.dve(lambda e: e.tensor_tensor(out=g[:, j, :], in0=gl, in1=psum[:, 0:TT], op=ALU.mult),
                                     [gl_k, psk], [("g", j)])
                        self.lin_fm(wv[:, :, jj * 128:(jj + 1) * 128], wkey, KC, 1, lambda kc: h2e[:, kc, 1:TT + 1], [h2e_k], TT, cons)
                jobs.append((win, KC, jb * 256, 256, gate))
                jobs.append((win, KC, DFF + jb * 256, 256, val))
            for m in range(KC):
                def outp(wv, wkey, t=t, c0=c0, m=m):
                    xmc, xmc_k = xmcs.next()
                    self.dma(xmc, self.XM[m, :, c0:c0 + TT], [("XM", t)], [xmc_k])

                    def cons(mm, psum, psk):
                        self.dve(lambda e: e.tensor_tensor(out=xt[:, m, :], in0=xmc, in1=psum[:, 0:TT], op=ALU.add),
                                 [psk, xmc_k], [xt_k])
                    self.lin_fm(wv, wkey, FC, 1, lambda kc: g[:, kc, :], [("g", j) for j in range(FC)], TT, cons)
                    if m == KC - 1:
                        if not last:
                            self.dma(self.XS[:, :, c0:c0 + TT].rearrange("k p t -> p k t"), xt, [xt_k], [("XS", t)])
                        else:
                            def out_fn(c):
                                return xmcs.next()

                            def post(c, o_ap, o_k):
                                self.dma(self.yT[c * 128:(c + 1) * 128, c0:c0 + TT], o_ap, [o_k], [("yT", c, t)])
                            self.rmsnorm(xt, xt_k, KC, TT, "gvec", 8 * 16, None, None, self.ntmp, out_fn=out_fn, post=post)
                jobs.append((wout, FC, m * 128, 128, outp))
        self.run_jobs(jobs)


def build_program(n_layers=4, stop_mid=False):
    B = Builder(n_layers, stop_mid)
    import os
    stage = int(os.environ.get("DBG_STAGE", "99"))
    B.convert_layer(0)
    B.setup()
    done = False
    for li in range(n_layers):
        kind = li % 3
        if kind == 0:
            if stage >= 1:
                B.a_phase1(li)
            if stage >= 2:
                B.a_phase3(li)
            if stage >= 3:
                B.oproj_phase(li, "l%d_a_w_o" % li)
        elif kind == 1:
            B.b_phase1(li)
            B.b_phase3(li)
            B.oproj_phase(li, "l%d_b_w_o" % li)
        else:
            B.c_phase1(li)
            B.c_phase3(li)
            B.oproj_phase(li, "l%d_c_w_o" % li)
        if stop_mid and li == n_layers - 1:
            B.phase_begin()
            for t in range(NT):
                c0 = t * TT
                B.dma(B.yT[:, c0:c0 + TT].rearrange("(k p) t -> k p t", p=128), B.XM[:, :, c0:c0 + TT], [("XM", t)], [("yT", t)])
            done = True
            break
        B.ffn_phase(li, last=(li == 3))
    if not done and n_layers < 4:
        B.phase_begin()
        for t in range(NT):
            c0 = t * TT
            B.dma(B.yT[:, c0:c0 + TT].rearrange("(k p) t -> k p t", p=128), B.XS[:, :, c0:c0 + TT], [("XS", t)], [("yT", t)])
    B.P.barrier()
    B.P.finalize()
    B.st.close()
    return B.nc


def _vec_cols(v, nch):
    return np.ascontiguousarray(np.asarray(v, np.float32).reshape(nch, 128).T)


def host_inputs(inputs, n_layers=4):
    f32 = np.float32
    xp = np.asarray(inputs["x_prompt"], f32)
    xs = np.asarray(inputs["x_sample"], f32)
    cf = np.zeros((128, NCF), f32)
    for i in range(4):
        cf[:, CF["gvec"] + (2 * i) * 16:CF["gvec"] + (2 * i + 1) * 16] = _vec_cols(inputs["l%d_mix_norm" % i], 16)
        cf[:, CF["gvec"] + (2 * i + 1) * 16:CF["gvec"] + (2 * i + 2) * 16] = _vec_cols(inputs["l%d_ffn_norm" % i], 16)
        cw = np.asarray(inputs["l%d_ffn_conv_w" % i], f32)
        cwl = cw.T.reshape(FC, 128, 3).transpose(1, 0, 2).reshape(128, FC * 3)
        cf[:, CF["convw"] + i * FC * 3:CF["convw"] + (i + 1) * FC * 3] = cwl
        cf[:, CF["convb"] + i * FC:CF["convb"] + (i + 1) * FC] = _vec_cols(inputs["l%d_ffn_conv_b" % i], FC)
    cf[:, CF["gvec"] + 128:CF["gvec"] + 144] = _vec_cols(inputs["final_norm"], 16)
    cf[:, CF["sink"]:CF["sink"] + 16] = np.asarray(inputs["l0_a_sink"], f32)[None, :]
    cf[:, CF["sink"] + 16:CF["sink"] + 32] = np.asarray(inputs["l3_a_sink"], f32)[None, :]
    cf[:, CF["cnorm"]:CF["cnorm"] + 4] = _vec_cols(inputs["l2_c_q_norm"], 4)
    cf[:, CF["cnorm"] + 4:CF["cnorm"] + 8] = _vec_cols(inputs["l2_c_kv_norm"], 4)
    p = np.arange(128)[:, None]
    c = np.arange(384)[None, :]
    ra = np.abs(c - 128 - p).astype(f32)
    ra[ra > 128] = BIGR
    cf[:, CF["RA"]:CF["RA"] + 384] = ra
    c = np.arange(256)[None, :]
    rb = np.abs(c - 64 - p).astype(f32)
    rb[rb > 64] = BIGR
    cf[:, CF["RB"]:CF["RB"] + 256] = rb
    inv = ROPE_THETA_ ** (-np.arange(0, 64, 2, dtype=np.float32) / 64.0)
    maps = []
    wfull = {}
    for core in range(NCORES):
        cfc = cf.copy()
        for tt_, t in ((0, 0), (1, 1), (2, 2)):
            pcs = window_pieces(t, 128)
            w0 = 0
            for (rel, lc, ln) in pcs:
                valid = 0 <= core + rel <= 7
                for b in range(w0 // 128, (w0 + ln) // 128):
                    cfc[:, CF["EA"] + tt_ * 6 + b] = 0.0 if valid else NEG
                w0 += ln
            for gi, (window, d) in enumerate(B_GROUPS):
                pcs = window_pieces(t, 64 * d)
                nj = (TT + 128 * d) // d
                colv = np.zeros(nj, f32)
                w0 = 0
                for (rel, lc, ln) in pcs:
                    valid = 0 <= core + rel <= 7
                    colv[w0 // d:(w0 + ln) // d] = 0.0 if valid else NEG
                    w0 += ln
                for b in range(5):
                    seg = colv[128 * b:128 * (b + 1)]
                    col = np.zeros(128, f32)
                    col[:len(seg)] = seg
                    cfc[:, CF["EB"] + (tt_ * 3 + gi) * 5 + b] = col
        cfc[:, CF["fl"]] = 1.0 if core > 0 else 0.0
        cfc[:, CF["fl"] + 1] = 1.0 if core < 7 else 0.0
        xl = np.concatenate([xp[0, PSEG * core:PSEG * (core + 1)]] + [xs[b, SSEG * core:SSEG * (core + 1)] for b in range(4)], axis=0)
        x0T = np.ascontiguousarray(xl.T)
        pos = np.concatenate([np.arange(PSEG * core, PSEG * (core + 1))] + [np.arange(SSEG * core, SSEG * (core + 1))] * 4).astype(np.float32)
        ang = pos[None, :] * inv[:, None]
        rope = np.stack([np.cos(ang), np.sin(ang)]).astype(f32)
        nb = np.zeros((1, 8), np.int32)
        for rel in range(-2, 3):
            nb[0, rel + 2] = min(7, max(0, core + rel))
        m = {"x0T": x0T, "cf32": cfc, "rope": rope, "nb": nb}
        for name, k, n in weight_specs():
            if int(name[1]) >= n_layers:
                continue
            w = inputs[name]
            m[name] = wfull.setdefault(name, np.ascontiguousarray(np.asarray(w, f32)))
        maps.append(m)
    return maps


ROPE_THETA_ = 10000.0
_NC_CACHE = {}


def run(inputs, n_layers=4, stop_mid=False):
    key = (n_layers, stop_mid)
    if key not in _NC_CACHE:
        _NC_CACHE[key] = build_program(n_layers, stop_mid)
    nc = _NC_CACHE[key]
    maps = host_inputs(inputs, n_layers)
    res = run_bass_kernel_spmd(nc, maps, core_ids=list(range(NCORES)))
    yp = np.zeros((1, 8192, D), np.float32)
    ys = np.zeros((4, 4096, D), np.float32)
    for core in range(NCORES):
        yT = np.asarray(res.results[core]["yT"])
        yp[0, PSEG * core:PSEG * (core + 1), :] = yT[:, 0:PSEG].T
        for b in range(4):
            ys[b, SSEG * core:SSEG * (core + 1), :] = yT[:, PSEG + SSEG * b:PSEG + SSEG * (b + 1)].T
    return yp, ys


def kernel(**inputs):
    return run(inputs, 4, False)


def _b_init(self):
    dt = self.nc.dram_tensor
    if hasattr(self, "KSb"):
        return
    self.KSb = [dt("KSb%d" % g, [2048, LT], BF16).ap() for g in range(3)]
    self.VSb = [dt("VSb%d" % g, [LT, 2048], BF16).ap() for g in range(3)]
    self.NKb = [{rel: dt("NKb%d_%d" % (g, rel + 2), [2048, LT], BF16).ap() for rel in ((-1, 1) if g < 2 else (-2, -1, 1, 2))}
                for g in range(3)]
    self.NVb = [{rel: dt("NVb%d_%d" % (g, rel + 2), [LT, 2048], BF16).ap() for rel in ((-1, 1) if g < 2 else (-2, -1, 1, 2))}
                for g in range(3)]


def _b_phase1(self, li):
    _b_init(self)
    wn = "l%d_b_w_qkv" % li
    xsrc, xname = (self.XS, "XS")
    self.phase_begin()
    ar = self.ar
    self.alloc_common()
    xts = [ar.alloc([KC, TT], F32, "xt%d" % i) for i in range(2)]
    hts = [ar.alloc([KC, TT], BF16, "ht%d" % i) for i in range(2)]
    jobs = []
    for t in range(NT):
        c0 = t * TT
        xt, xt_k = xts[t % 2]
        ht, ht_k = hts[t % 2]
        for blk in range(36):
            g = blk // 12
            kind = (blk % 12) // 4
            hb4 = blk % 4

            def fn(wv, wkey, t=t, c0=c0, blk=blk, g=g, kind=kind, hb4=hb4, xt=xt, xt_k=xt_k, ht=ht, ht_k=ht_k):
                if blk == 0:
                    if t == 0:
                        self.load_x_tile(xsrc, xname, 0, xt, xt_k)
                    if t + 1 < NT:
                        self.load_x_tile(xsrc, xname, t + 1, *xts[(t + 1) % 2])
                    self.rmsnorm(xt, xt_k, KC, TT, "gvec", (2 * li) * 16, ht, ht_k, self.ntmp)
                rhs = lambda kc: ht[:, kc, :]
                if kind == 0:
                    def cons(m, psum, psk):
                        self.store_stage(psum, psk, self.QS[g * 16 + m, :, c0:c0 + TT], ("QS", g * 16 + m, t))
                    self.lin_fm(wv, wkey, KC, 4, rhs, [ht_k], TT, cons, m0=hb4 * 4)
                elif kind == 1:
                    def cons(m, psum, psk):
                        self.store_stage(psum, psk, self.KSb[g][m * 128:(m + 1) * 128, c0:c0 + TT], ("KS", g, m, t))
                    self.lin_fm(wv, wkey, KC, 4, rhs, [ht_k], TT, cons, m0=hb4 * 4)
                else:
                    def cons(s, psum, psk):
                        self.store_stage(psum, psk, self.VSb[g][c0 + s * 128:c0 + (s + 1) * 128, hb4 * 512:(hb4 + 1) * 512],
                                         ("VS", g, hb4, s, t))
                    self.lin_tm(lambda kc: wv[:, kc, 0:512], wkey, KC, 512,
                                lambda kc, s: ht[:, kc, s * 128:(s + 1) * 128], [ht_k], 4, cons)
            jobs.append((wn, KC, blk * 512, 512, fn))
    self.run_jobs(jobs)
    for g in range(3):
        ksend = [("KS", g, m, t) for m in range(16) for t in range(NT)]
        vsend = [("VS", g, hb4, s, t) for hb4 in range(4) for s in range(4) for t in range(NT)]
        K8v, kk, (kn, kr) = self.allgather(self.KSb[g], 2048, LT, "Kb%d" % g, ksend, "K8b%d" % g)
        V8v, vk, (vn, vr) = self.allgather(self.VSb[g], LT, 2048, "Vb%d" % g, vsend, "V8b%d" % g)
        for rel in self.NKb[g].keys():
            self.localize(self.NKb[g][rel].rearrange("(n i) c -> n i c", i=kr), K8v, rel, kk, [("NKb", g, rel)])
            self.localize(self.NVb[g][rel].rearrange("(n i) c -> n i c", i=vr), V8v, rel, vk, [("NVb", g, rel)])


def _b_phase3(self, li):
    self.phase_begin()
    if li + 1 < self.n_layers:
        self.convert_layer(li + 1)
    ar = self.ar
    sl = slopes16()
    scale = HD ** -0.5
    KTs = Rot([ar.alloc([2560], BF16, "KTw%d" % i) for i in range(3)])
    Vws = Rot([ar.alloc([4096], BF16, "Vw%d" % i) for i in range(3)])
    QTs = Rot([ar.alloc([512], BF16, "QT%d" % i) for i in range(3)])
    tmps = Rot([ar.alloc([256], F32, "tmp%d" % i) for i in range(3)])
    PTs = Rot([ar.alloc([256], BF16, "PT%d" % i) for i in range(3)])
    NUMs = Rot([ar.alloc([512], F32, "NUM%d" % i) for i in range(2)])
    DENs = Rot([ar.alloc([512], F32, "DEN%d" % i) for i in range(2)])
    oats = Rot([ar.alloc([512], BF16, "oat%d" % i) for i in range(3)])
    RB0 = CF["RB"]
    cnt = 0
    sidx = 0
    groups = []
    for t in range(NT):
        c0 = t * TT
        tt_ = tile_type(t)
        for h in range(NH):
            NUM, NUM_k = NUMs.next()
            DEN, DEN_k = DENs.next()
            for g, (window, d) in enumerate(B_GROUPS):
                nq = TT // d
                nj = nq + 128
                nblk = (nj + 127) // 128
                W = TT + 128 * d
                KTf, KT_k = KTs.next()
                Vwf, Vw_k = Vws.next()
                KT = KTf[:, 0:W]
                Vw = Vwf[:, 0:d * nblk * 128].rearrange("p (r b x) -> p r b x", r=d, b=nblk)
                QT, QT_k = QTs.next()

                def pre(t=t, c0=c0, h=h, g=g, d=d, W=W, KT=KT, KT_k=KT_k, Vw=Vw, Vw_k=Vw_k, QT=QT, QT_k=QT_k):
                    self.dma(QT, self.QS[g * 16 + h, :, c0:c0 + TT], [], [QT_k])
                    w0 = 0
                    for (rel, lc, ln) in window_pieces(t, 64 * d):
                        ksrc = self.KSb[g] if rel == 0 else self.NKb[g][rel]
                        vsrc = self.VSb[g] if rel == 0 else self.NVb[g][rel]
                        self.dma(KT[:, w0:w0 + ln], ksrc[h * 128:(h + 1) * 128, lc:lc + ln], [], [KT_k])
                        ja, je = w0 // d, (w0 + ln) // d
                        j = ja
                        while j < je:
                            b = j // 128
                            jn = min(je, (b + 1) * 128)
                            p0 = j % 128
                            n = jn - j
                            r0 = lc + (j - ja) * d
                            self.dma(Vw[p0:p0 + n, :, b, :],
                                     vsrc[r0:r0 + n * d, h * 128:(h + 1) * 128].rearrange("(jj r) x -> jj r x", r=d), [], [Vw_k])
                            j = jn
                        w0 += ln
                    assert w0 == W
                num, num_k = self.ps[3 + cnt % 2]
                den, den_k = self.ps[5 + cnt % 2]
                cnt += 1
                coef = -sl[h] * d / scale
                blocks = []
                for r in range(d):
                    for b in range(nblk):
                        nk = min(128, nj - 128 * b)
                        q_lo = max(0, 128 * b - 128)
                        q_hi = min(nq, 128 * b + nk)
                        n = q_hi - q_lo
                        cb = q_lo - (128 * b - 128)
                        S, S_k = self.ps[sidx % 3]
                        sidx += 1
                        tmp, tmp_k = tmps.next()
                        PT, PT_k = PTs.next()
                        k0 = 128 * b * d + r
                        q0 = q_lo * d + r
                        eo = CF["EB"] + (tt_ * 3 + g) * 5 + b
                        first = (r == 0 and b == 0)
                        last = (r == d - 1 and b == nblk - 1)
                        o0 = r * nq + q_lo

                        def fS(S=S, S_k=S_k, KT=KT, KT_k=KT_k, QT=QT, QT_k=QT_k, k0=k0, q0=q0, nk=nk, n=n, d=d):
                            self.pe(lambda e: e.matmul(S[0:nk, 0:n], KT[:, k0:k0 + (nk - 1) * d + 1:d], QT[:, q0:q0 + (n - 1) * d + 1:d],
                                                       start=True, stop=True), [KT_k, QT_k], [S_k])

                        def fsoft(S=S, S_k=S_k, tmp=tmp, tmp_k=tmp_k, PT=PT, PT_k=PT_k, cb=cb, n=n, nk=nk, coef=coef, eo=eo):
                            self.dve(lambda e: e.scalar_tensor_tensor(out=tmp[0:nk, 0:n], in0=self.cf[0:nk, RB0 + cb:RB0 + cb + n], scalar=coef,
                                                                      in1=S[0:nk, 0:n], op0=ALU.mult, op1=ALU.add),
                                     [S_k, self.cf_k], [tmp_k])
                            self.act(lambda e: e.activation(out=PT[0:nk, 0:n], in_=tmp[0:nk, 0:n], func=AF.Exp,
                                                            bias=self.cf[0:nk, eo:eo + 1], scale=scale),
                                     [tmp_k, self.cf_k], [PT_k])

                        def fPV(num=num, num_k=num_k, den=den, den_k=den_k, Vw=Vw, Vw_k=Vw_k, PT=PT, PT_k=PT_k, r=r, b=b, nk=nk, n=n,
                                o0=o0, first=first, last=last):
                            self.pe(lambda e: e.matmul(num[:, o0:o0 + n], Vw[0:nk, r, b, :], PT[0:nk, 0:n], start=first, stop=last,
                                                       skip_group_check=True), [Vw_k, PT_k], [num_k])
                            self.pe(lambda e: e.matmul(den[:, o0:o0 + n], self.ones[0:nk, :], PT[0:nk, 0:n], start=first, stop=last,
                                                       skip_group_check=True), [self.ones_k, PT_k], [den_k])
                        post = None
                        if last:
                            def post(g=g, d=d, h=h, t=t, c0=c0, num=num, num_k=num_k, den=den, den_k=den_k,
                                     NUM=NUM, NUM_k=NUM_k, DEN=DEN, DEN_k=DEN_k):
                                for (ACC, ACC_k, src, src_k) in ((NUM, NUM_k, num, num_k), (DEN, DEN_k, den, den_k)):
                                    if g == 0:
                                        self.dve(lambda e, ACC=ACC, src=src: e.tensor_copy(ACC, src[:, 0:TT]), [src_k], [ACC_k])
                                    else:
                                        accv = ACC.rearrange("p (q r) -> p r q", r=d)
                                        srcv = src[:, 0:TT].rearrange("p (r q) -> p r q", r=d)
                                        self.dve(lambda e, accv=accv, srcv=srcv: e.tensor_tensor(out=accv, in0=accv, in1=srcv, op=ALU.add),
                                                 [src_k, ACC_k], [ACC_k])
                                if g == 2:
                                    oat, oat_k = oats.next()
                                    self.dve(lambda e: e.reciprocal(DEN, DEN), [DEN_k], [DEN_k])
                                    self.dve(lambda e: e.tensor_tensor(out=oat, in0=NUM, in1=DEN, op=ALU.mult), [NUM_k, DEN_k], [oat_k])
                                    self.dma(self.ATT[h, :, c0:c0 + TT], oat, [oat_k], [("ATT", h, t)])
                        blocks.append((fS, fsoft, fPV, post))
                groups.append((pre, blocks))
    self.run_attn(groups)


Builder.b_phase1 = _b_phase1
Builder.b_phase3 = _b_phase3


C_SCALE = (128 + 64) ** -0.5


def _c_init(self):
    dt = self.nc.dram_tensor
    if hasattr(self, "KSc"):
        return
    self.KSc = dt("KSc", [2112, LT], BF16).ap()
    self.VSc = dt("VSc", [LT, 2048], BF16).ap()


def _rope(self, x1, x1_k, x2, x2_k, cs, sn, csn_k, o1, o2, o_k, tmps):
    (ta, ta_k), (tb, tb_k) = tmps.next(), tmps.next()
    P32 = slice(0, 32)
    self.dve(lambda e: e.tensor_tensor(out=ta[P32, :], in0=x1[P32, 0:TT], in1=cs[P32, :], op=ALU.mult), [x1_k, csn_k], [ta_k])
    self.dve(lambda e: e.tensor_tensor(out=tb[P32, :], in0=x2[P32, 0:TT], in1=sn[P32, :], op=ALU.mult), [x2_k, csn_k], [tb_k])
    self.dve(lambda e: e.tensor_tensor(out=o1[P32, :], in0=ta[P32, :], in1=tb[P32, :], op=ALU.subtract), [ta_k, tb_k], [o_k])
    (tc, tc_k), (td, td_k) = tmps.next(), tmps.next()
    self.dve(lambda e: e.tensor_tensor(out=tc[P32, :], in0=x2[P32, 0:TT], in1=cs[P32, :], op=ALU.mult), [x2_k, csn_k], [tc_k])
    self.dve(lambda e: e.tensor_tensor(out=td[P32, :], in0=x1[P32, 0:TT], in1=sn[P32, :], op=ALU.mult), [x1_k, csn_k], [td_k])
    self.dve(lambda e: e.tensor_tensor(out=o2[P32, :], in0=tc[P32, :], in1=td[P32, :], op=ALU.add), [tc_k, td_k], [o_k])


def _c_phase1(self, li):
    _c_init(self)
    wd, wuq, wukv = "l%d_c_w_down" % li, "l%d_c_w_uq" % li, "l%d_c_w_ukv" % li
    self.phase_begin()
    ar = self.ar
    self.alloc_common()
    xt, xt_k = ar.alloc([KC, TT], F32, "xt")
    hts = [ar.alloc([KC, TT], BF16, "ht%d" % i) for i in range(2)]
    c32, c32_k = ar.alloc([4, TT], F32, "c32")
    cqn, cqn_k = ar.alloc([4, TT], BF16, "cqn")
    ckvn, ckvn_k = ar.alloc([4, TT], BF16, "ckvn")
    cs, csn_k = ar.alloc([TT], F32, "cos")
    sn, _ = ar.alloc([TT], F32, "sin")
    rtmps = Rot([ar.alloc([TT], F32, "rt%d" % i) for i in range(4)])
    ropo = Rot([(ar.alloc([TT], BF16, "ro1_%d" % i), ar.alloc([TT], BF16, "ro2_%d" % i)) for i in range(2)])
    jobs = []
    for t in range(NT):
        c0 = t * TT
        ht, ht_k = hts[t % 2]

        def j_down(wv, wkey, which, t=t, c0=c0, ht=ht, ht_k=ht_k):
            if which == 0:
                self.load_x_tile(self.XS, "XS", t, xt, xt_k)
                self.dma(cs[0:32, :], self.rope_d[0, :, c0:c0 + TT], [], [csn_k])
                self.dma(sn[0:32, :], self.rope_d[1, :, c0:c0 + TT], [], [csn_k])
                self.rmsnorm(xt, xt_k, KC, TT, "gvec", (2 * li) * 16, ht, ht_k, self.ntmp)
            rhs = lambda kc: ht[:, kc, :]
            if which < 2:
                def cons(m, psum, psk):
                    self.evac(c32[:, m, :], psum[:, 0:TT], [psk], [c32_k])
                self.lin_fm(wv, wkey, KC, 4, rhs, [ht_k], TT, cons)
                dst, dst_k = (cqn, cqn_k) if which == 0 else (ckvn, ckvn_k)
                self.rmsnorm(c32, c32_k, 4, TT, "cnorm", which * 4, dst, dst_k, self.ntmp)
            else:
                got = {}

                def cons(m, psum, psk):
                    got[m] = (psum, psk)
                self.lin_fm(wv, wkey, KC, 2, rhs, [ht_k], TT, cons, mw=32)
                (o1, o1_k), (o2, o2_k) = ropo.next()
                _rope(self, got[0][0], got[0][1], got[1][0], got[1][1], cs, sn, csn_k, o1, o2, o1_k, rtmps)
                self.dma(self.KSc[2048:2080, c0:c0 + TT], o1[0:32, :], [o1_k], [("KSr", 0, t)])
                self.dma(self.KSc[2080:2112, c0:c0 + TT], o2[0:32, :], [o1_k], [("KSr", 1, t)])
        jobs.append((wd, KC, 0, 512, lambda wv, wkey, f=j_down: f(wv, wkey, 0)))
        jobs.append((wd, KC, 512, 512, lambda wv, wkey, f=j_down: f(wv, wkey, 1)))
        jobs.append((wd, KC, 1024, 64, lambda wv, wkey, f=j_down: f(wv, wkey, 2)))
        for hf in range(2):
            def j_uq(wv, wkey, hf=hf, t=t, c0=c0):
                for hl in range(8):
                    h = hf * 8 + hl
                    base = hl * 192
                    psum, psk = self.psrot.next()
                    p1, p1k = self.psrot.next()
                    p2, p2k = self.psrot.next()
                    for (pp, ppk, off, mw) in ((psum, psk, base, 128), (p1, p1k, base + 128, 32), (p2, p2k, base + 160, 32)):
                        for kc in range(4):
                            self.pe(lambda e, pp=pp, off=off, mw=mw, kc=kc: e.matmul(pp[0:mw, 0:TT], wv[:, kc, off:off + mw], cqn[:, kc, :],
                                                                                 start=(kc == 0), stop=(kc == 3)),
                                    [wkey, cqn_k], [ppk])
                    self.store_stage(psum, psk, self.QS[h, :, c0:c0 + TT], ("QS", h, t))
                    (o1, o1_k), (o2, o2_k) = ropo.next()
                    _rope(self, p1, p1k, p2, p2k, cs, sn, csn_k, o1, o2, o1_k, rtmps)
                    self.dma(self.QR[h, 0:32, c0:c0 + TT], o1[0:32, :], [o1_k], [("QR", h, 0, t)])
                    self.dma(self.QR[h, 32:64, c0:c0 + TT], o2[0:32, :], [o1_k], [("QR", h, 1, t)])
            jobs.append((wuq, 4, hf * 1536, 1536, j_uq))
        for hf in range(2):
            def j_ukv(wv, wkey, hf=hf, t=t, c0=c0):
                for hl in range(8):
                    h = hf * 8 + hl
                    psum, psk = self.psrot.next()
                    for kc in range(4):
                        self.pe(lambda e, psum=psum, hl=hl, kc=kc: e.matmul(psum[:, 0:TT], wv[:, kc, hl * 256:hl * 256 + 128], ckvn[:, kc, :],
                                                                          start=(kc == 0), stop=(kc == 3)),
                                [wkey, ckvn_k], [psk])
                    self.store_stage(psum, psk, self.KSc[h * 128:(h + 1) * 128, c0:c0 + TT], ("KS", h, t))
                wvv = wv.rearrange("p k (h x) -> p k h x", x=256)
                for q4 in range(2):
                    h0 = hf * 8 + q4 * 4
                    for s in range(4):
                        psum, psk = self.psrot.next()
                        for kc in range(4):
                            self.pe(lambda e, psum=psum, q4=q4, s=s, kc=kc: e.matmul(psum[:, 0:512], ckvn[:, kc, s * 128:(s + 1) * 128],
                                                                                  wvv[:, kc, q4 * 4:q4 * 4 + 4, 128:256],
                                                                                  start=(kc == 0), stop=(kc == 3)),
                                    [wkey, ckvn_k], [psk])
                        self.store_stage(psum, psk, self.VSc[c0 + s * 128:c0 + (s + 1) * 128, h0 * 128:(h0 + 4) * 128], ("VS", h0, s, t))
            jobs.append((wukv, 4, hf * 2048, 2048, j_ukv))
    self.run_jobs(jobs)
    ksend = [("KS", h, t) for h in range(NH) for t in range(NT)] + [("KSr", i, t) for i in range(2) for t in range(NT)]
    vsend = [("VS", h0, s, t) for h0 in range(0, 16, 4) for s in range(4) for t in range(NT)]
    self.cK8v, self.cKk, (kn, kr) = self.allgather(self.KSc, 2112, LT, "Kc", ksend, "K8c", rpc_force=64)
    self.cV8v, self.cVk, (vn, vr) = self.allgather(self.VSc, LT, 2048, "Vc", vsend, "V8c")
    assert kr == 64 and vr == 128


def _c_phase3(self, li):
    self.phase_begin()
    if li + 1 < self.n_layers:
        self.convert_layer(li + 1)
    ar = self.ar
    K8v, V8v = self.cK8v, self.cV8v
    KTs = Rot([ar.alloc([8192], BF16, "cKT%d" % i) for i in range(2)])
    Vps = Rot([ar.alloc([64, 128], BF16, "cVp%d" % i) for i in range(2)])
    KRs = Rot([ar.alloc([8192], BF16, "cKR%d" % i) for i in range(2)])
    QNs = Rot([ar.alloc([512], BF16, "cQN%d" % i) for i in range(3)])
    QRs = Rot([ar.alloc([512], BF16, "cQR%d" % i) for i in range(3)])
    PTs = Rot([ar.alloc([512], BF16, "cPT%d" % i) for i in range(4)])
    recs = Rot([ar.alloc([512], F32, "crec%d" % i) for i in range(2)])
    oats = Rot([ar.alloc([512], BF16, "coat%d" % i) for i in range(3)])
    seqs = [(PSEG, 0, [0, 1])] + [(SSEG, PSEG + SSEG * b, [2 + b]) for b in range(4)]
    cnt = 0
    sidx = 0
    groups = []
    for (seg, lc0, tiles) in seqs:
        L = seg * 8
        nkb = L // 128
        KR, KR_k = KRs.next()
        for h in range(NH):
            KT, KT_k = KTs.next()
            Vp, Vp_k = Vps.next()

            def pre(seg=seg, lc0=lc0, h=h, KR=KR, KR_k=KR_k, KT=KT, KT_k=KT_k, Vp=Vp, Vp_k=Vp_k):
                if h == 0:
                    for r in range(8):
                        self.dma(KR[0:64, r * seg:(r + 1) * seg], K8v[r, 32, :, lc0:lc0 + seg], [], [KR_k])
                nb = seg // 128
                for r in range(8):
                    for n2 in range(2):
                        self.dma(KT[64 * n2:64 * n2 + 64, r * seg:(r + 1) * seg], K8v[r, 2 * h + n2, :, lc0:lc0 + seg], [], [KT_k])
                    self.dma(Vp[:, r * nb:(r + 1) * nb, :],
                             V8v[r, lc0 // 128:lc0 // 128 + nb, :, h * 128:(h + 1) * 128].rearrange("n p d -> p n d"), [], [Vp_k])
            blocks = []
            for t in tiles:
                c0 = t * TT
                QN, QN_k = QNs.next()
                QRt, QR_k = QRs.next()
                num, num_k = self.ps[4 + cnt % 2]
                den, den_k = self.ps[6 + cnt % 2]
                cnt += 1
                for kb in range(nkb):
                    S, S_k = self.ps[sidx % 4]
                    sidx += 1
                    PT, PT_k = PTs.next()

                    def fS(S=S, S_k=S_k, KT=KT, KT_k=KT_k, KR=KR, KR_k=KR_k, QN=QN, QN_k=QN_k, QRt=QRt, QR_k=QR_k, kb=kb, h=h, c0=c0):
                        if kb == 0:
                            self.dma(QN, self.QS[h, :, c0:c0 + TT], [], [QN_k])
                            self.dma(QRt[0:64, :], self.QR[h, :, c0:c0 + TT], [], [QR_k])
                        self.pe(lambda e: e.matmul(S[:, 0:TT], KT[:, kb * 128:(kb + 1) * 128], QN, start=True, stop=False),
                                [KT_k, QN_k], [S_k])
                        self.pe(lambda e: e.matmul(S[:, 0:TT], KR[0:64, kb * 128:(kb + 1) * 128], QRt[0:64, :], start=False, stop=True),
                                [KR_k, QR_k], [S_k])

                    def fsoft(S=S, S_k=S_k, PT=PT, PT_k=PT_k):
                        self.act(lambda e: e.activation(out=PT, in_=S[:, 0:TT], func=AF.Exp, scale=C_SCALE), [S_k], [PT_k])

                    def fPV(num=num, num_k=num_k, den=den, den_k=den_k, Vp=Vp, Vp_k=Vp_k, PT=PT, PT_k=PT_k, kb=kb, nkb=nkb):
                        self.pe(lambda e: e.matmul(num[:, 0:TT], Vp[:, kb, :], PT, start=(kb == 0), stop=(kb == nkb - 1)),
                                [Vp_k, PT_k], [num_k])
                        self.pe(lambda e: e.matmul(den[:, 0:TT], self.ones[:, :], PT, start=(kb == 0), stop=(kb == nkb - 1)),
                                [self.ones_k, PT_k], [den_k])
                    post = None
                    if kb == nkb - 1:
                        def post(num=num, num_k=num_k, den=den, den_k=den_k, h=h, t=t, c0=c0):
                            rec, rec_k = recs.next()
                            oat, oat_k = oats.next()
                            self.dve(lambda e: e.reciprocal(rec, den[:, 0:TT]), [den_k], [rec_k])
                            self.dve(lambda e: e.tensor_tensor(out=oat, in0=num[:, 0:TT], in1=rec, op=ALU.mult), [num_k, rec_k], [oat_k])
                            self.dma(self.ATT[h, :, c0:c0 + TT], oat, [oat_k], [("ATT", h, t)])
                    blocks.append((fS, fsoft, fPV, post))
            groups.append((pre, blocks))
    self.run_attn(groups)


Builder.c_phase1 = _c_phase1
Builder.c_phase3 = _c_phase3
```

```python
import contextlib
import numpy as np
import ml_dtypes
import concourse.bass as bass
import concourse.mybir as mybir
from concourse.bass_utils import run_bass_kernel_spmd

F32 = mybir.dt.float32
BF16 = mybir.dt.bfloat16
AF = mybir.ActivationFunctionType
ALU = mybir.AluOpType

NCORES = 8


class Op:
    __slots__ = ("eng", "fn", "deps", "is_dma", "signal", "idx", "sem", "val", "prewait", "inc", "force", "raw", "bg")

    def __init__(self, eng, fn, is_dma, inc):
        self.eng = eng
        self.fn = fn
        self.deps = set()
        self.is_dma = is_dma
        self.signal = False
        self.sem = None
        self.val = None
        self.prewait = None
        self.inc = inc
        self.force = False
        self.raw = set()
        self.bg = False


ENGS = ("pe", "act", "dve", "pool", "sp")
SEM_ROT = 12000
N_DMA_SEMS = 12


class Prog:
    def __init__(self, nc, stack):
        self.nc = nc
        self.stack = stack
        self.ops = []
        self.eng_ops = {e: [] for e in ENGS}
        self.last_w = {}
        self.readers = {}
        self.dma_rr = {e: 0 for e in ENGS}
        self.dma_sems = {}
        self.dma_sem_last = {}
        self.eng_sems = {e: [] for e in ENGS}
        self.bar_from = 0
        self.prev_bar = []
        self.bg_last_w = {}

    def op(self, eng, fn, reads=(), writes=(), dma=False, inc=16, force=False, bg=False):
        o = Op(eng, fn, dma, inc)
        o.force = force
        o.bg = bg
        o.idx = len(self.ops)
        for b in reads:
            w = self.last_w.get(b)
            if w is not None:
                o.deps.add(w)
                o.raw.add(w)
        for b in writes:
            w = self.last_w.get(b)
            if w is not None:
                o.deps.add(w)
            for r in self.readers.get(b, ()):
                o.deps.add(r)
        for b in reads:
            self.readers.setdefault(b, []).append(o.idx)
        for b in writes:
            self.last_w[b] = o.idx
            self.readers[b] = []
            if bg:
                self.bg_last_w[b] = o.idx
        o.deps.discard(o.idx)
        self.ops.append(o)
        self.eng_ops[eng].append(o)
        return o

    def barrier(self):
        lasts = []
        for e in ("pe", "act", "dve"):
            for o in reversed(self.eng_ops[e]):
                if o.fn is not None and not o.is_dma:
                    lasts.append(o.idx)
                    break
        dmas = [o.idx for o in self.ops[self.bar_from:] if o.is_dma and not o.bg]
        self.bar_from = len(self.ops)
        prev = list(self.prev_bar)
        self.prev_bar = []
        for e in ENGS:
            o = Op(e, None, False, 0)
            o.idx = len(self.ops)
            o.deps = set(lasts) | set(dmas) | set(prev)
            self.ops.append(o)
            self.eng_ops[e].append(o)
            self.prev_bar.append(o.idx)
        self.last_w = dict(self.bg_last_w)
        self.readers = {}

    def finalize(self):
        nc = self.nc
        ops = self.ops
        for o in ops:
            for d in list(o.deps):
                do = ops[d]
                if (not do.is_dma) and do.eng == o.eng and not o.is_dma and not o.force and not (d in o.raw and o.eng != "pe"):
                    o.deps.discard(d)
                    continue
                do.signal = True
        cnt = {e: 0 for e in ENGS}
        for e in ENGS:
            for o in self.eng_ops[e]:
                if o.is_dma:
                    cc = "cc" if o.inc == 1 else "d"
                    rrk = (e, cc)
                    k = self.dma_rr.get(rrk, 0) % (N_DMA_SEMS if cc == "d" else 8)
                    self.dma_rr[rrk] = self.dma_rr.get(rrk, 0) + 1
                    key = (e, cc, k)
                    if key not in self.dma_sems:
                        self.dma_sems[key] = [self.stack.enter_context(nc.semaphore("%s_%s_%d" % (cc, e, k))), 0]
                    ent = self.dma_sems[key]
                    o.prewait = (ent[0], ent[1]) if ent[1] > 0 else None
                    ent[1] += o.inc
                    o.sem, o.val = ent[0], ent[1]
                elif o.signal and o.fn is not None:
                    ph = cnt[e] // SEM_ROT
                    while len(self.eng_sems[e]) <= ph:
                        self.eng_sems[e].append(
                            self.stack.enter_context(nc.semaphore("c_%s_%d" % (e, len(self.eng_sems[e])))))
                    cnt[e] += 1
                    o.sem = self.eng_sems[e][ph]
                    o.val = cnt[e] - ph * SEM_ROT
                elif o.signal and o.fn is None:
                    pass

        def resolve(d, acc, seen):
            do = ops[d]
            if do.fn is None:
                if d in seen:
                    return
                seen.add(d)
                for dd in do.deps:
                    resolve(dd, acc, seen)
                return
            key = id(do.sem)
            if key not in acc or acc[key][1] < do.val:
                acc[key] = (do.sem, do.val)

        self._resolve = resolve

        with nc.Block() as block:
            def run(e, handle_name):
                deco = getattr(block, handle_name)

                @deco
                def _(h):
                    known = {}
                    for o in self.eng_ops[e]:
                        acc = {}
                        seen = set()
                        for d in o.deps:
                            resolve(d, acc, seen)
                        if o.prewait is not None:
                            s, v = o.prewait
                            if id(s) not in acc or acc[id(s)][1] < v:
                                acc[id(s)] = (s, v)
                        for key, (s, v) in acc.items():
                            if known.get(key, 0) >= v:
                                continue
                            known[key] = v
                            h.wait_ge(s, v)
                        if o.fn is None:
                            continue
                        ins = o.fn(h)
                        if o.sem is not None:
                            if o.is_dma:
                                ins.then_inc(o.sem, o.inc)
                            else:
                                ins.then_inc(o.sem, 1)

            run("sp", "sync")
            run("pool", "gpsimd")
            run("act", "scalar")
            run("dve", "vector")
            run("pe", "tensor")


D = 2048
KC = 16
DFF = 5632
FC = 44
TT = 512
NT = 6
LT = 3072
HD = 128
NH = 16
EPS = 1e-6
PSEG = 1024
SSEG = 512
NEG = -30000.0
BIGR = 1.0e6
B_GROUPS = ((128, 1), (512, 4), (2048, 16))
I32 = mybir.dt.int32


def slopes16():
    return [2.0 ** (-8.0 * (h + 1) / 16.0) for h in range(16)]


def tile_cols(t):
    return t * TT


def tile_type(t):
    return 0 if t == 0 else (1 if t == 1 else 2)


def window_pieces(t, halo):
    base = 0 if t < 2 else PSEG + SSEG * (t - 2)
    seg = PSEG if t < 2 else SSEG
    off = TT * t if t < 2 else 0
    lo = off - halo
    hi = off + TT + halo
    pieces = []
    rel_lo = lo // seg
    rel_hi = (hi - 1) // seg
    for rel in range(rel_lo, rel_hi + 1):
        a = max(lo, rel * seg)
        b = min(hi, (rel + 1) * seg)
        pieces.append((rel, base + a - rel * seg, b - a))
    return pieces


class Arena:
    def __init__(self, ap, nwords):
        self.ap = ap
        self.n = nwords
        self.off = 0
        self.cnt = 0

    def alloc(self, shape, dtype, key=None):
        n = int(np.prod(shape))
        sz = 4 if dtype in (F32, I32) else 2
        words = (n * sz + 3) // 4
        words = (words + 15) // 16 * 16
        assert self.off + words <= self.n, ("arena overflow", self.off, words, self.n)
        a = self.ap[:, self.off:self.off + words]
        if dtype != F32:
            a = a.bitcast(dtype)
        a = a[:, 0:n]
        if len(shape) == 2:
            a = a.rearrange("p (a b) -> p a b", b=shape[1])
        elif len(shape) == 3:
            a = a.rearrange("p (a b c) -> p a b c", b=shape[1], c=shape[2])
        self.off += words
        self.cnt += 1
        return a, (key or ("ar%d" % self.cnt)) + "@%d" % self.off


class Rot:
    def __init__(self, items):
        self.items = items
        self.i = 0

    def next(self):
        it = self.items[self.i % len(self.items)]
        self.i += 1
        return it


def weight_specs():
    specs = []
    for i in range(4):
        pre = "l%d_" % i
        kind = i % 3
        if kind == 0:
            specs += [(pre + "a_w_qkv", 2048, 3072), (pre + "a_w_o", 2048, 2048)]
        elif kind == 1:
            specs += [(pre + "b_w_qkv", 2048, 18432), (pre + "b_w_o", 2048, 2048)]
        else:
            specs += [(pre + "c_w_down", 2048, 1088), (pre + "c_w_uq", 512, 3072),
                      (pre + "c_w_ukv", 512, 4096), (pre + "c_w_o", 2048, 2048)]
        specs += [(pre + "ffn_w_in", 2048, 11264), (pre + "ffn_w_out", 5632, 2048)]
    return specs


CF = {}
_o = 0
for _n, _w in (("gvec", 144), ("convw", 528), ("convb", 176), ("sink", 32), ("cnorm", 8), ("RA", 384),
               ("RB", 256), ("EA", 18), ("EB", 45), ("fl", 2)):
    CF[_n] = _o
    _o += _w
NCF = _o


class Builder:
    def __init__(self, n_layers=4, stop_mid=False, arena_words=46000):
        self.n_layers = n_layers
        self.stop_mid = stop_mid
        self.nc = bass.Bass("TRN2", target_bir_lowering=False)
        nc = self.nc
        self.st = contextlib.ExitStack()
        self.P = Prog(nc, self.st)
        self.x0 = nc.dram_tensor("x0T", [D, LT], F32, kind="ExternalInput").ap()
        self.cf_d = nc.dram_tensor("cf32", [128, NCF], F32, kind="ExternalInput").ap()
        self.rope_d = nc.dram_tensor("rope", [2, 32, LT], F32, kind="ExternalInput").ap()
        self.nb_d = nc.dram_tensor("nb", [1, 8], I32, kind="ExternalInput").ap()
        self.yT = nc.dram_tensor("yT", [D, LT], F32, kind="ExternalOutput").ap()
        self.w32 = {}
        self.wb = {}
        self.wkeys = {}
        for name, k, n in weight_specs():
            if int(name[1]) >= n_layers:
                continue
            self.w32[name] = nc.dram_tensor(name, [k, n], F32, kind="ExternalInput").ap()
            self.wb[name] = nc.dram_tensor("wb_" + name, [k, n], BF16).ap()
        dt = nc.dram_tensor
        self.XS = dt("XS", [KC, 128, LT], F32).ap()
        self.XM = dt("XM", [KC, 128, LT], F32).ap()
        self.H2 = dt("H2", [KC, 128, LT], BF16).ap()
        self.ATT = dt("ATT", [NH, 128, LT], BF16).ap()
        self.QS = dt("QS", [48, 128, LT], BF16).ap()
        self.QR = dt("QR", [NH, 64, LT], BF16).ap()
        self.KSa = dt("KSa", [512, LT], BF16).ap()
        self.NKa = {rel: dt("NKa%d" % (rel + 2), [512, LT], BF16).ap() for rel in (-1, 1)}
        self.NVa = {rel: dt("NVa%d" % (rel + 2), [LT, 512], BF16).ap() for rel in (-1, 1)}
        self.NHB = {rel: dt("NHB%d" % (rel + 2), [128, 160], BF16).ap() for rel in (-1, 1)}
        self.VSa = dt("VSa", [LT, 512], BF16).ap()
        self.HBs = dt("HBs", [128, 160], BF16).ap()
        self.arena_t = self.st.enter_context(nc.sbuf_tensor("arena", [128, arena_words], F32))
        self.ar = Arena(self.arena_t[:, :], arena_words)
        self.ps = []
        for i in range(8):
            t = self.st.enter_context(nc.psum_tensor("ps%d" % i, [128, 512], F32))
            self.ps.append((t[:, :], "ps%d" % i))
        self.psrot = Rot(self.ps)
        self.evac_i = 0
        self.regv = {}
        self.ag_bufs = {}
        self.attn_lookahead = 2

    def dma(self, out, in_, r, w, eng="sp"):
        return self.P.op(eng, lambda e: e.dma_start(out=out, in_=in_), reads=r, writes=w, dma=True)

    def localize(self, dst, g8view, rel, r, w, eng="sp"):
        self.dyn_cnt[eng] += 1
        assert self.dyn_cnt[eng] <= 21, "dynamic DMA register budget exceeded"

        def fn(e):
            v = self.regv[(eng, rel)]
            return e.dma_start(out=dst, in_=g8view[bass.ds(v, 1)])
        return self.P.op(eng, fn, reads=r, writes=w, dma=True)

    def pe(self, fn, r, w):
        return self.P.op("pe", fn, reads=r, writes=w)

    def act(self, fn, r, w):
        return self.P.op("act", fn, reads=r, writes=w)

    def dve(self, fn, r, w, force=False):
        return self.P.op("dve", fn, reads=r, writes=w, force=force)

    def evac(self, out, in_, r, w):
        self.evac_i += 1
        if self.evac_i % 2 == 0:
            return self.act(lambda e: e.activation(out=out, in_=in_, func=AF.Copy), r, w)
        return self.dve(lambda e: e.tensor_copy(out, in_), r, w)

    def allgather(self, send, R, C, name, key_send, key_out, rpc_force=None):
        nc = self.nc
        rpc = 1
        for cand in range(1, R + 1):
            if R % cand == 0 and cand * C * 2 <= 512 * 1024:
                rpc = cand
        if rpc_force:
            rpc = rpc_force
        nch = R // rpc
        if name not in self.ag_bufs:
            self.ag_bufs[name] = (nc.dram_tensor("g4_" + name, [nch * 4 * rpc, C], BF16).ap(),
                                  nc.dram_tensor("g8_" + name, [nch * 8 * rpc, C], BF16).ap())
        g4, g8 = self.ag_bufs[name]
        for stage in (1, 2):
            for c in range(nch):
                s_ap = send[c * rpc:(c + 1) * rpc, :]
                g4c = g4[c * 4 * rpc:(c + 1) * 4 * rpc, :]
                g8c = g8[c * 8 * rpc:(c + 1) * 8 * rpc, :]
                k4 = (key_out, "g4", c)
                if stage == 1:
                    def c1(e, s_ap=s_ap, g4c=g4c):
                        return e.collective_compute("AllGather", ALU.bypass, replica_groups=[[0, 1, 2, 3], [4, 5, 6, 7]],
                                                    ins=[s_ap.opt()], outs=[g4c.opt()])
                    self.P.op("pool", c1, reads=key_send, writes=[k4], dma=True, inc=1)
                else:
                    def c2(e, g4c=g4c, g8c=g8c):
                        return e.collective_compute("AllGather", ALU.bypass, replica_groups=[[0, 4], [1, 5], [2, 6], [3, 7]],
                                                    ins=[g4c.opt()], outs=[g8c.opt()])
                    self.P.op("pool", c2, reads=[k4], writes=[(key_out, c)], dma=True, inc=1)
        keys = [(key_out, c) for c in range(nch)]
        return g8.rearrange("(n r i) c -> r n i c", r=8, i=rpc), keys, (nch, rpc)

    def convert_layer(self, li):
        for name, k, n in weight_specs():
            if name not in self.w32 or int(name[1]) != li:
                continue
            rows = 64 if n > 4096 else 256
            keys = []
            for r0 in range(0, k, rows):
                r1 = min(k, r0 + rows)
                key = ("wb", name, r0)
                keys.append(key)
                self.P.op("pool", lambda e, r0=r0, r1=r1, name=name: e.dma_start(out=self.wb[name][r0:r1, :], in_=self.w32[name][r0:r1, :]),
                          reads=[], writes=[key], dma=True, bg=True)
            self.wkeys[name] = keys

    def setup(self):
        ar = self.ar
        P = self.P
        self.cf, self.cf_k = ar.alloc([NCF], F32, "cf")
        self.cf = self.cf
        self.dma(self.cf, self.cf_d, [], [self.cf_k])
        self.nbs, self.nbs_k = ar.alloc([8], I32, "nbs")
        self.dma(self.nbs[0:1, :], self.nb_d, [], [self.nbs_k])

        self.dyn_cnt = {"sp": 0, "pool": 0}
        for eng in ("sp", "pool"):
            def ldregs(e, eng=eng):
                for rel in (-2, -1, 1, 2):
                    reg = e.alloc_register("nbr%d" % (rel + 2))
                    e.reg_load(reg, self.nbs[0:1, rel + 2:rel + 3])
                    self.regv[(eng, rel)] = e.snap(reg)
                return None
            P.op(eng, ldregs, reads=[self.nbs_k], writes=[])
        self.ones, self.ones_k = ar.alloc([128], BF16, "ones")
        self.dve(lambda e: e.memset(self.ones, 1.0), [], [self.ones_k])
        self.esink, self.esink_k = ar.alloc([32], F32, "esink")
        o = CF["sink"]
        self.act(lambda e: e.activation(out=self.esink, in_=self.cf[:, o:o + 32], func=AF.Exp),
                 [self.cf_k], [self.esink_k])
        self.haloL, self.haloL_k = ar.alloc([16, 5], BF16, "haloL")
        self.haloR, self.haloR_k = ar.alloc([16, 5], BF16, "haloR")
        self.hb, self.hb_k = ar.alloc([2, 16, 5], BF16, "hb")
        self.mark = ar.off
        self.x0v = self.x0.rearrange("(k p) t -> k p t", p=128)

    def phase_begin(self):
        self.P.barrier()
        self.ar.off = self.mark

    def cfcol(self, name, idx):
        o = CF[name] + idx
        return self.cf[:, o:o + 1]

    def rmsnorm(self, xt, xt_k, nchunks, width, gname, gidx0, out, out_k, tmp, out_fn=None, post=None):
        psum, psk = self.psrot.next()
        n_feat = nchunks * 128
        for c in range(nchunks):
            sq, sqk = tmp["sq"].next()
            self.act(lambda e, c=c, sq=sq: e.activation(out=sq[:, 0:width], in_=xt[:, c, 0:width], func=AF.Square),
                     [xt_k], [sqk])
            self.pe(lambda e, c=c, sq=sq: e.matmul(psum[:, 0:width], self.ones[:, :], sq[:, 0:width],
                                                  start=(c == 0), stop=(c == nchunks - 1)),
                    [sqk, self.ones_k], [psk])
        rs, rsk = tmp["rstd"]
        self.act(lambda e: e.activation(out=rs[:, 0:width], in_=psum[:, 0:width], func=AF.Sqrt,
                                        scale=1.0 / n_feat, bias=EPS), [psk], [rsk])
        self.dve(lambda e: e.reciprocal(rs[:, 0:width], rs[:, 0:width]), [rsk], [rsk])
        for c in range(nchunks):
            g = self.cfcol(gname, gidx0 + c)
            if out_fn is not None:
                o_ap, o_k = out_fn(c)
            else:
                o_ap, o_k = out[:, c, 0:width], out_k
            self.dve(lambda e, c=c, g=g, o_ap=o_ap: e.scalar_tensor_tensor(out=o_ap, in0=xt[:, c, 0:width],
                                                                          scalar=g, in1=rs[:, 0:width],
                                                                          op0=ALU.mult, op1=ALU.mult),
                     [xt_k, rsk, self.cf_k], [o_k])
            if post is not None:
                post(c, o_ap, o_k)

    def lin_fm(self, wv, wkey, kc_n, nchunk, rhs_fn, rhs_keys, width, consumer, m0=0, mw=128):
        for m in range(nchunk):
            psum, psk = self.psrot.next()
            for kc in range(kc_n):
                self.pe(lambda e, m=m, kc=kc, psum=psum: e.matmul(psum[0:mw, 0:width], wv[:, kc, m * mw:(m + 1) * mw],
                                                                rhs_fn(kc), start=(kc == 0), stop=(kc == kc_n - 1)),
                        [wkey] + rhs_keys, [psk])
            consumer(m0 + m, psum, psk)

    def lin_tm(self, wv_cols_fn, wkey, kc_n, ncols, lhs_fn, lhs_keys, nsub, consumer):
        for s in range(nsub):
            psum, psk = self.psrot.next()
            for kc in range(kc_n):
                self.pe(lambda e, s=s, kc=kc, psum=psum: e.matmul(psum[:, 0:ncols], lhs_fn(kc, s), wv_cols_fn(kc),
                                                                start=(kc == 0), stop=(kc == kc_n - 1)),
                        [wkey] + lhs_keys, [psk])
            consumer(s, psum, psk)

    def alloc_common(self, wsize=8192, nw=3):
        ar = self.ar
        self.wbufs = Rot([ar.alloc([wsize], BF16, "wbuf%d" % i) for i in range(nw)])
        self.stg16 = Rot([ar.alloc([512], BF16, "stg16_%d" % i) for i in range(4)])
        self.sqr = Rot([ar.alloc([512], BF16, "sq%d" % i) for i in range(2)])
        self.rstd = ar.alloc([512], F32, "rstd")
        self.ntmp = dict(sq=self.sqr, rstd=self.rstd)

    def wload(self, name, kc_n, c0, ncols):
        ap, key = self.wbufs.next()
        dst = ap[:, 0:kc_n * ncols].rearrange("p (k n) -> p k n", n=ncols)
        src = self.wb[name][:, c0:c0 + ncols].rearrange("(k p) n -> p k n", p=128)
        self.dma(dst, src, self.wkeys[name], [key])
        return dst, key

    def run_jobs(self, jobs, depth=2):
        loaded = {}
        for i in range(min(depth, len(jobs))):
            loaded[i] = self.wload(*jobs[i][0:4])
        for i, job in enumerate(jobs):
            if i + depth < len(jobs):
                loaded[i + depth] = self.wload(*jobs[i + depth][0:4])
            wv, wkey = loaded.pop(i)
            job[4](wv, wkey)

    def load_x_tile(self, src, srcname, t, xt, xt_k):
        c0 = t * TT
        self.dma(xt[:, :, 0:TT], src[:, :, c0:c0 + TT].rearrange("k p t -> p k t"), [(srcname, t)], [xt_k])

    def store_stage(self, psum, psk, dst, dst_key, width=TT, npart=128):
        st, stk = self.stg16.next()
        self.evac(st[0:npart, 0:width], psum[0:npart, 0:width], [psk], [stk])
        self.dma(dst, st[0:npart, 0:width], [stk], [dst_key])

    def a_phase1(self, li):
        wn = "l%d_a_w_qkv" % li
        xsrc, xname = (self.x0v, "x0") if li == 0 else (self.XS, "XS")
        self.phase_begin()
        ar = self.ar
        self.alloc_common()
        xts = [ar.alloc([KC, TT], F32, "xt%d" % i) for i in range(2)]
        hts = [ar.alloc([KC, TT], BF16, "ht%d" % i) for i in range(2)]
        jobs = []
        for t in range(NT):
            c0 = t * TT
            xt, xt_k = xts[t % 2]
            ht, ht_k = hts[t % 2]
            for blk in range(6):
                def fn(wv, wkey, t=t, c0=c0, blk=blk, xt=xt, xt_k=xt_k, ht=ht, ht_k=ht_k):
                    if blk == 0:
                        if t == 0:
                            self.load_x_tile(xsrc, xname, 0, xt, xt_k)
                        if t + 1 < NT:
                            self.load_x_tile(xsrc, xname, t + 1, *xts[(t + 1) % 2])
                        self.rmsnorm(xt, xt_k, KC, TT, "gvec", (2 * li) * 16, ht, ht_k, self.ntmp)
                    rhs = lambda kc: ht[:, kc, :]
                    if blk < 4:
                        def cons(m, psum, psk):
                            self.store_stage(psum, psk, self.QS[m, :, c0:c0 + TT], ("QS", m, t))
                        self.lin_fm(wv, wkey, KC, 4, rhs, [ht_k], TT, cons, m0=blk * 4)
                    elif blk == 4:
                        def cons(m, psum, psk):
                            self.store_stage(psum, psk, self.KSa[m * 128:(m + 1) * 128, c0:c0 + TT], ("KS", m, t))
                        self.lin_fm(wv, wkey, KC, 4, rhs, [ht_k], TT, cons)
                    else:
                        def cons(s, psum, psk):
                            self.store_stage(psum, psk, self.VSa[c0 + s * 128:c0 + (s + 1) * 128, :], ("VS", s, t))
                        self.lin_tm(lambda kc: wv[:, kc, 0:512], wkey, KC, 512,
                                    lambda kc, s: ht[:, kc, s * 128:(s + 1) * 128], [ht_k], 4, cons)
                jobs.append((wn, KC, blk * 512, 512, fn))
        self.run_jobs(jobs)
        ksend = [("KS", m, t) for m in range(4) for t in range(NT)]
        vsend = [("VS", s, t) for s in range(4) for t in range(NT)]
        K8v, kk, (kn, kr) = self.allgather(self.KSa, 512, LT, "Ka", ksend, "K8")
        V8v, vk, (vn, vr) = self.allgather(self.VSa, LT, 512, "Va", vsend, "V8")
        for rel in (-1, 1):
            self.localize(self.NKa[rel].rearrange("(n i) c -> n i c", i=kr), K8v, rel, kk, [("NKa", rel)])
            self.localize(self.NVa[rel].rearrange("(n i) c -> n i c", i=vr), V8v, rel, vk, [("NVa", rel)])


    def run_attn(self, groups):
        flat = [(gi, bi, b) for gi, (pre, blocks) in enumerate(groups) for bi, b in enumerate(blocks)]
        if not flat:
            return
        groups[0][0]()
        called = {0}
        LA = self.attn_lookahead
        for k in range(min(LA, len(flat))):
            gk = flat[k][0]
            if gk not in called:
                groups[gk][0]()
                called.add(gk)
            flat[k][2][0]()
        for i, (gi, bi, b) in enumerate(flat):
            if bi == 0 and gi + 1 < len(groups) and (gi + 1) not in called:
                groups[gi + 1][0]()
                called.add(gi + 1)
            b[1]()
            if i + LA < len(flat):
                gk = flat[i + LA][0]
                if gk not in called:
                    groups[gk][0]()
                    called.add(gk)
                flat[i + LA][2][0]()
            b[2]()
            if b[3] is not None:
                b[3]()

    def a_phase3(self, li):
        self.phase_begin()
        if li + 1 < self.n_layers:
            self.convert_layer(li + 1)
        ar = self.ar
        sl = slopes16()
        scale = HD ** -0.5
        sink_base = (0 if li == 0 else 1) * 16
        KTs = Rot([ar.alloc([768], BF16, "KTw%d" % i) for i in range(2)])
        Vws = Rot([ar.alloc([6, 128], BF16, "Vw%d" % i) for i in range(2)])
        QTs = Rot([ar.alloc([512], BF16, "QT%d" % i) for i in range(8)])
        tmps = Rot([ar.alloc([384], F32, "tmp%d" % i) for i in range(4)])
        PTs = Rot([ar.alloc([384], BF16, "PT%d" % i) for i in range(4)])
        recs = Rot([ar.alloc([512], F32, "rec%d" % i) for i in range(2)])
        oats = Rot([ar.alloc([512], BF16, "oat%d" % i) for i in range(3)])
        RA0 = CF["RA"]
        hh = 0
        sidx = 0
        groups = []
        for t in range(NT):
            c0 = t * TT
            tt_ = tile_type(t)
            pieces = window_pieces(t, 128)
            for kvh in range(4):
                KT, KT_k = KTs.next()
                Vw, Vw_k = Vws.next()
                qts = [QTs.next() for _ in range(4)]

                def pre(t=t, c0=c0, kvh=kvh, KT=KT, KT_k=KT_k, Vw=Vw, Vw_k=Vw_k, qts=qts, pieces=pieces):
                    w0 = 0
                    for (rel, lc, ln) in pieces:
                        ksrc = self.KSa if rel == 0 else self.NKa[rel]
                        vsrc = self.VSa if rel == 0 else self.NVa[rel]
                        self.dma(KT[:, w0:w0 + ln], ksrc[kvh * 128:(kvh + 1) * 128, lc:lc + ln], [], [KT_k])
                        b0 = w0 // 128
                        nb = ln // 128
                        self.dma(Vw[:, b0:b0 + nb, :],
                                 vsrc[lc:lc + ln, kvh * 128:(kvh + 1) * 128].rearrange("(b p) d -> p b d", p=128), [], [Vw_k])
                        w0 += ln
                    assert w0 == 768
                    for g4 in range(4):
                        h = kvh * 4 + g4
                        self.dma(qts[g4][0], self.QS[h, :, c0:c0 + TT], [], [qts[g4][1]])
                blocks = []
                for g4 in range(4):
                    h = kvh * 4 + g4
                    QT, QT_k = qts[g4]
                    num, num_k = self.ps[3 + hh % 2]
                    den, den_k = self.ps[5 + hh % 2]
                    hh += 1
                    for j in range(6):
                        q_lo = max(0, 128 * j - 256)
                        q_hi = min(512, 128 * j + 128)
                        n = q_hi - q_lo
                        cb = q_lo - (128 * j - 256)
                        S, S_k = self.ps[(0, 1, 2, 7)[sidx % 4]]
                        sidx += 1
                        tmp, tmp_k = tmps.next()
                        PT, PT_k = PTs.next()
                        coef = -sl[h] / scale
                        ecol = self.cfcol("EA", tt_ * 6 + j)

                        def fS(S=S, S_k=S_k, KT=KT, KT_k=KT_k, QT=QT, QT_k=QT_k, j=j, q_lo=q_lo, q_hi=q_hi, n=n):
                            self.pe(lambda e: e.matmul(S[:, 0:n], KT[:, 128 * j:128 * j + 128], QT[:, q_lo:q_hi], start=True, stop=True),
                                    [KT_k, QT_k], [S_k])

                        def fsoft(S=S, S_k=S_k, tmp=tmp, tmp_k=tmp_k, PT=PT, PT_k=PT_k, cb=cb, n=n, coef=coef, ecol=ecol):
                            self.dve(lambda e: e.scalar_tensor_tensor(out=tmp[:, 0:n], in0=self.cf[:, RA0 + cb:RA0 + cb + n], scalar=coef,
                                                                      in1=S[:, 0:n], op0=ALU.mult, op1=ALU.add),
                                     [S_k, self.cf_k], [tmp_k])
                            self.act(lambda e: e.activation(out=PT[:, 0:n], in_=tmp[:, 0:n], func=AF.Exp, bias=ecol, scale=scale),
                                     [tmp_k, self.cf_k], [PT_k])

                        def fPV(num=num, num_k=num_k, den=den, den_k=den_k, Vw=Vw, Vw_k=Vw_k, PT=PT, PT_k=PT_k, j=j, q_lo=q_lo, q_hi=q_hi, n=n):
                            self.pe(lambda e: e.matmul(num[:, q_lo:q_hi], Vw[:, j, :], PT[:, 0:n], start=(j == 0), stop=(j == 5),
                                                       skip_group_check=True), [Vw_k, PT_k], [num_k])
                            self.pe(lambda e: e.matmul(den[:, q_lo:q_hi], self.ones[:, :], PT[:, 0:n], start=(j == 0), stop=(j == 5),
                                                       skip_group_check=True), [self.ones_k, PT_k], [den_k])
                        post = None
                        if j == 5:
                            def post(num=num, num_k=num_k, den=den, den_k=den_k, h=h, t=t, c0=c0):
                                rec, rec_k = recs.next()
                                oat, oat_k = oats.next()
                                sk = sink_base + h
                                self.dve(lambda e: e.tensor_scalar(out=rec, in0=den, scalar1=self.esink[:, sk:sk + 1], scalar2=None, op0=ALU.add),
                                         [den_k, self.esink_k], [rec_k])
                                self.dve(lambda e: e.reciprocal(rec, rec), [rec_k], [rec_k])
                                self.dve(lambda e: e.tensor_tensor(out=oat, in0=num, in1=rec, op=ALU.mult), [num_k, rec_k], [oat_k])
                                self.dma(self.ATT[h, :, c0:c0 + TT], oat, [oat_k], [("ATT", h, t)])
                        blocks.append((fS, fsoft, fPV, post))
                groups.append((pre, blocks))
        self.run_attn(groups)

    def oproj_phase(self, li, wn):
        xsrc, xname = (self.x0v, "x0") if li == 0 else (self.XS, "XS")
        self.phase_begin()
        ar = self.ar
        self.alloc_common()
        xts = [ar.alloc([KC, TT], F32, "xt%d" % i) for i in range(2)]
        ats = [ar.alloc([KC, TT], BF16, "at%d" % i) for i in range(2)]
        h2s = [ar.alloc([KC, TT], BF16, "h2_0")] * 2
        jobs = []

        def load_tile(t):
            c0 = t * TT
            self.load_x_tile(xsrc, xname, t, *xts[t % 2])
            at, at_k = ats[t % 2]
            self.dma(at, self.ATT[:, :, c0:c0 + TT].rearrange("h p t -> p h t"), [("ATT", h, t) for h in range(NH)], [at_k])

        for t in range(NT):
            c0 = t * TT
            xt, xt_k = xts[t % 2]
            at, at_k = ats[t % 2]
            h2, h2_k = h2s[t % 2]
            for blk in range(4):
                def fn(wv, wkey, t=t, c0=c0, blk=blk, xt=xt, xt_k=xt_k, at=at, at_k=at_k, h2=h2, h2_k=h2_k):
                    if blk == 0:
                        if t == 0:
                            load_tile(0)
                        if t + 1 < NT:
                            load_tile(t + 1)

                    def cons(m, psum, psk):
                        self.dve(lambda e: e.tensor_tensor(out=xt[:, m, :], in0=xt[:, m, :], in1=psum[:, 0:TT], op=ALU.add),
                                 [psk, xt_k], [xt_k])
                    self.lin_fm(wv, wkey, KC, 4, lambda kc: at[:, kc, :], [at_k], TT, cons, m0=blk * 4)
                    if blk == 3:
                        self.dma(self.XM[:, :, c0:c0 + TT].rearrange("k p t -> p k t"), xt, [xt_k], [("XM", t)])
                        self.rmsnorm(xt, xt_k, KC, TT, "gvec", (2 * li + 1) * 16, h2, h2_k, self.ntmp)
                        self.dma(self.H2[:, :, c0:c0 + TT].rearrange("k p t -> p k t"), h2, [h2_k], [("H2", t)])
                        bl = []
                        if t == 0:
                            bl = [(0, 0, 0)]
                        elif t == 1:
                            bl = [(1, 0, TT - 1)]
                        else:
                            bl = [(0, t - 1, 0), (1, t - 1, TT - 1)]
                        for (side, seg, col) in bl:
                            self.dve(lambda e, side=side, seg=seg, col=col: e.tensor_copy(self.hb[:, side, :, seg], h2[:, :, col]),
                                     [h2_k], [self.hb_k], force=True)
                jobs.append((wn, KC, blk * 512, 512, fn))
        self.run_jobs(jobs)
        self.dma(self.HBs, self.hb.rearrange("p a b c -> p (a b c)"), [self.hb_k], ["HBs"])
        HBv, hk, (hn, hr) = self.allgather(self.HBs, 128, 160, "HB", ["HBs"], "HB8")
        tl, tl_k = ar.alloc([80], BF16, "tl")
        tr, tr_k = ar.alloc([80], BF16, "tr")
        self.localize(self.NHB[-1].rearrange("(n i) c -> n i c", i=hr), HBv, -1, hk, [("NHB", -1)])
        self.localize(self.NHB[1].rearrange("(n i) c -> n i c", i=hr), HBv, 1, hk, [("NHB", 1)])
        self.dma(tl, self.NHB[-1][:, 80:160], [("NHB", -1)], [tl_k])
        self.dma(tr, self.NHB[1][:, 0:80], [("NHB", 1)], [tr_k])
        fl = CF["fl"]
        self.dve(lambda e: e.tensor_scalar(out=self.haloL.rearrange("p a b -> p (a b)"), in0=tl,
                                           scalar1=self.cf[:, fl:fl + 1], scalar2=None, op0=ALU.mult),
                 [tl_k, self.cf_k], [self.haloL_k])
        self.dve(lambda e: e.tensor_scalar(out=self.haloR.rearrange("p a b -> p (a b)"), in0=tr,
                                           scalar1=self.cf[:, fl + 1:fl + 2], scalar2=None, op0=ALU.mult),
                 [tr_k, self.cf_k], [self.haloR_k])

    def ffn_phase(self, li, last):
        self.phase_begin()
        ar = self.ar
        self.alloc_common(wsize=5632, nw=3)
        win = "l%d_ffn_w_in" % li
        wout = "l%d_ffn_w_out" % li
        g, _ = ar.alloc([FC, TT], BF16, "g")
        h2es = [ar.alloc([KC, TT + 2], BF16, "h2e%d" % i) for i in range(2)]
        xt, xt_k = ar.alloc([KC, TT], F32, "xt")
        xmcs = Rot([ar.alloc([512], F32, "xmc%d" % i) for i in range(2)])
        aexts = Rot([ar.alloc([TT + 2], F32, "aext%d" % i) for i in range(2)])
        cbs = Rot([ar.alloc([512], F32, "cb%d" % i) for i in range(2)])
        gls = Rot([ar.alloc([512], F32, "gl%d" % i) for i in range(4)])
        cw0 = CF["convw"] + li * FC * 3
        cb0 = CF["convb"] + li * FC

        def load_h2e(t):
            c0 = t * TT
            h2e, k = h2es[t % 2]
            if t == 0:
                self.dma(h2e[:, :, 1:TT + 2], self.H2[:, :, c0:c0 + TT + 1].rearrange("k p t -> p k t"),
                         [("H2", 0), ("H2", 1)], [k])
                self.dve(lambda e: e.tensor_copy(h2e[:, :, 0], self.haloL[:, :, 0]), [self.haloL_k], [k])
            elif t == 1:
                self.dma(h2e[:, :, 0:TT + 1], self.H2[:, :, c0 - 1:c0 + TT].rearrange("k p t -> p k t"),
                         [("H2", 0), ("H2", 1)], [k])
                self.dve(lambda e: e.tensor_copy(h2e[:, :, TT + 1], self.haloR[:, :, 0]), [self.haloR_k], [k])
            else:
                self.dma(h2e[:, :, 1:TT + 1], self.H2[:, :, c0:c0 + TT].rearrange("k p t -> p k t"), [("H2", t)], [k])
                self.dve(lambda e: e.tensor_copy(h2e[:, :, 0], self.haloL[:, :, t - 1]), [self.haloL_k], [k])
                self.dve(lambda e: e.tensor_copy(h2e[:, :, TT + 1], self.haloR[:, :, t - 1]), [self.haloR_k], [k])

        jobs = []
        for t in range(NT):
            c0 = t * TT
            h2e, h2e_k = h2es[t % 2]
            glbuf = {}
            for jb in range(FC // 2):
                def gate(wv, wkey, t=t, jb=jb, h2e=h2e, h2e_k=h2e_k, glbuf=glbuf):
                    if jb == 0:
                        if t == 0:
                            load_h2e(0)
                        if t + 1 < NT:
                            load_h2e(t + 1)
                    for jj in range(2):
                        j = jb * 2 + jj
                        a_ps, a_k = self.psrot.next()
                        ah_ps, ah_k = self.psrot.next()
                        for kc in range(KC):
                            self.pe(lambda e, kc=kc, jj=jj, a_ps=a_ps: e.matmul(a_ps[:, 0:TT], wv[:, kc, jj * 128:(jj + 1) * 128],
                                                                              h2e[:, kc, 1:TT + 1], start=(kc == 0), stop=(kc == KC - 1)),
                                    [wkey, h2e_k], [a_k])
                        for kc in range(KC):
                            self.pe(lambda e, kc=kc, jj=jj, ah_ps=ah_ps: e.matmul(ah_ps[:, 0:2], wv[:, kc, jj * 128:(jj + 1) * 128],
                                                                                h2e[:, kc, 0:TT + 2:TT + 1], start=(kc == 0), stop=(kc == KC - 1)),
                                    [wkey, h2e_k], [ah_k])
                        aext, ax_k = aexts.next()
                        cb, cb_k = cbs.next()
                        gl, gl_k = gls.next()
                        glbuf[j] = (gl, gl_k)
                        self.act(lambda e, aext=aext, a_ps=a_ps: e.activation(out=aext[:, 1:TT + 1], in_=a_ps[:, 0:TT], func=AF.Copy),
                                 [a_k], [ax_k])
                        self.act(lambda e, aext=aext, ah_ps=ah_ps: e.activation(out=aext[:, 0:TT + 2:TT + 1], in_=ah_ps[:, 0:2], func=AF.Copy),
                                 [ah_k], [ax_k])
                        w0c = self.cf[:, cw0 + j * 3 + 0:cw0 + j * 3 + 1]
                        w1c = self.cf[:, cw0 + j * 3 + 1:cw0 + j * 3 + 2]
                        w2c = self.cf[:, cw0 + j * 3 + 2:cw0 + j * 3 + 3]
                        bc = self.cf[:, cb0 + j:cb0 + j + 1]
                        self.act(lambda e, cb=cb, a_ps=a_ps, w1c=w1c, bc=bc: e.activation(out=cb, in_=a_ps[:, 0:TT], func=AF.Identity,
                                                                                         bias=bc, scale=w1c),
                                 [a_k, self.cf_k], [cb_k])
                        self.dve(lambda e, cb=cb, aext=aext, w0c=w0c: e.scalar_tensor_tensor(out=cb, in0=aext[:, 0:TT], scalar=w0c, in1=cb,
                                                                                            op0=ALU.mult, op1=ALU.add),
                                 [ax_k, cb_k, self.cf_k], [cb_k])
                        self.dve(lambda e, cb=cb, aext=aext, w2c=w2c: e.scalar_tensor_tensor(out=cb, in0=aext[:, 2:TT + 2], scalar=w2c, in1=cb,
                                                                                            op0=ALU.mult, op1=ALU.add),
                                 [ax_k, cb_k, self.cf_k], [cb_k])
                        self.act(lambda e, gl=gl, cb=cb: e.activation(out=gl, in_=cb, func=AF.Gelu), [cb_k], [gl_k])

                def val(wv, wkey, t=t, jb=jb, h2e=h2e, h2e_k=h2e_k, glbuf=glbuf):
                    for jj in range(2):
                        j = jb * 2 + jj
                        gl, gl_k = glbuf[j]

                        def cons(m, psum, psk, j=j, gl=gl, gl_k=gl_k):
                            self.dve(lambda e: e.tensor_tensor(out=g[:, j, :], in0=gl, in1=psum[:, 0:TT], op=ALU.mult),
                                     [gl_k, psk], [("g", j)])
                        self.lin_fm(wv[:, :, jj * 128:(jj + 1) * 128], wkey, KC, 1, lambda kc: h2e[:, kc, 1:TT + 1], [h2e_k], TT, cons)
                jobs.append((win, KC, jb * 256, 256, gate))
                jobs.append((win, KC, DFF + jb * 256, 256, val))
            for m in range(KC):
                def outp(wv, wkey, t=t, c0=c0, m=m):
                    xmc, xmc_k = xmcs.next()
                    self.dma(xmc, self.XM[m, :, c0:c0 + TT], [("XM", t)], [xmc_k])

                    def cons(mm, psum, psk):
                        self.dve(lambda e: e.tensor_tensor(out=xt[:, m, :], in0=xmc, in1=psum[:, 0:TT], op=ALU.add),
                                 [psk, xmc_k], [xt_k])
                    self.lin_fm(wv, wkey, FC, 1, lambda kc: g[:, kc, :], [("g", j) for j in range(FC)], TT, cons)
                    if m == KC - 1:
                        if not last:
                            self.dma(self.XS[:, :, c0:c0 + TT].rearrange("k p t -> p k t"), xt, [xt_k], [("XS", t)])
                        else:
                            def out_fn(c):
                                return xmcs.next()

                            def post(c, o_ap, o_k):
                                self.dma(self.yT[c * 128:(c + 1) * 128, c0:c0 + TT], o_ap, [o_k], [("yT", c, t)])
                            self.rmsnorm(xt, xt_k, KC, TT, "gvec", 8 * 16, None, None, self.ntmp, out_fn=out_fn, post=post)
                jobs.append((wout, FC, m * 128, 128, outp))
        self.run_jobs(jobs)


def build_program(n_layers=4, stop_mid=False):
    B = Builder(n_layers, stop_mid)
    import os
    stage = int(os.environ.get("DBG_STAGE", "99"))
    B.convert_layer(0)
    B.setup()
    done = False
    for li in range(n_layers):
        kind = li % 3
        if kind == 0:
            if stage >= 1:
                B.a_phase1(li)
            if stage >= 2:
                B.a_phase3(li)
            if stage >= 3:
                B.oproj_phase(li, "l%d_a_w_o" % li)
        elif kind == 1:
            B.b_phase1(li)
            B.b_phase3(li)
            B.oproj_phase(li, "l%d_b_w_o" % li)
        else:
            B.c_phase1(li)
            B.c_phase3(li)
            B.oproj_phase(li, "l%d_c_w_o" % li)
        if stop_mid and li == n_layers - 1:
            B.phase_begin()
            for t in range(NT):
                c0 = t * TT
                B.dma(B.yT[:, c0:c0 + TT].rearrange("(k p) t -> k p t", p=128), B.XM[:, :, c0:c0 + TT], [("XM", t)], [("yT", t)])
            done = True
            break
        B.ffn_phase(li, last=(li == 3))
    if not done and n_layers < 4:
        B.phase_begin()
        for t in range(NT):
            c0 = t * TT
            B.dma(B.yT[:, c0:c0 + TT].rearrange("(k p) t -> k p t", p=128), B.XS[:, :, c0:c0 + TT], [("XS", t)], [("yT", t)])
    B.P.barrier()
    B.P.finalize()
    B.st.close()
    return B.nc


def _vec_cols(v, nch):
    return np.ascontiguousarray(np.asarray(v, np.float32).reshape(nch, 128).T)


def host_inputs(inputs, n_layers=4):
    f32 = np.float32
    xp = np.asarray(inputs["x_prompt"], f32)
    xs = np.asarray(inputs["x_sample"], f32)
    cf = np.zeros((128, NCF), f32)
    for i in range(4):
        cf[:, CF["gvec"] + (2 * i) * 16:CF["gvec"] + (2 * i + 1) * 16] = _vec_cols(inputs["l%d_mix_norm" % i], 16)
        cf[:, CF["gvec"] + (2 * i + 1) * 16:CF["gvec"] + (2 * i + 2) * 16] = _vec_cols(inputs["l%d_ffn_norm" % i], 16)
        cw = np.asarray(inputs["l%d_ffn_conv_w" % i], f32)
        cwl = cw.T.reshape(FC, 128, 3).transpose(1, 0, 2).reshape(128, FC * 3)
        cf[:, CF["convw"] + i * FC * 3:CF["convw"] + (i + 1) * FC * 3] = cwl
        cf[:, CF["convb"] + i * FC:CF["convb"] + (i + 1) * FC] = _vec_cols(inputs["l%d_ffn_conv_b" % i], FC)
    cf[:, CF["gvec"] + 128:CF["gvec"] + 144] = _vec_cols(inputs["final_norm"], 16)
    cf[:, CF["sink"]:CF["sink"] + 16] = np.asarray(inputs["l0_a_sink"], f32)[None, :]
    cf[:, CF["sink"] + 16:CF["sink"] + 32] = np.asarray(inputs["l3_a_sink"], f32)[None, :]
    cf[:, CF["cnorm"]:CF["cnorm"] + 4] = _vec_cols(inputs["l2_c_q_norm"], 4)
    cf[:, CF["cnorm"] + 4:CF["cnorm"] + 8] = _vec_cols(inputs["l2_c_kv_norm"], 4)
    p = np.arange(128)[:, None]
    c = np.arange(384)[None, :]
    ra = np.abs(c - 128 - p).astype(f32)
    ra[ra > 128] = BIGR
    cf[:, CF["RA"]:CF["RA"] + 384] = ra
    c = np.arange(256)[None, :]
    rb = np.abs(c - 64 - p).astype(f32)
    rb[rb > 64] = BIGR
    cf[:, CF["RB"]:CF["RB"] + 256] = rb
    inv = ROPE_THETA_ ** (-np.arange(0, 64, 2, dtype=np.float32) / 64.0)
    maps = []
    wfull = {}
    for core in range(NCORES):
        cfc = cf.copy()
        for tt_, t in ((0, 0), (1, 1), (2, 2)):
            pcs = window_pieces(t, 128)
            w0 = 0
            for (rel, lc, ln) in pcs:
                valid = 0 <= core + rel <= 7
                for b in range(w0 // 128, (w0 + ln) // 128):
                    cfc[:, CF["EA"] + tt_ * 6 + b] = 0.0 if valid else NEG
                w0 += ln
            for gi, (window, d) in enumerate(B_GROUPS):
                pcs = window_pieces(t, 64 * d)
                nj = (TT + 128 * d) // d
                colv = np.zeros(nj, f32)
                w0 = 0
                for (rel, lc, ln) in pcs:
                    valid = 0 <= core + rel <= 7
                    colv[w0 // d:(w0 + ln) // d] = 0.0 if valid else NEG
                    w0 += ln
                for b in range(5):
                    seg = colv[128 * b:128 * (b + 1)]
                    col = np.zeros(128, f32)
                    col[:len(seg)] = seg
                    cfc[:, CF["EB"] + (tt_ * 3 + gi) * 5 + b] = col
        cfc[:, CF["fl"]] = 1.0 if core > 0 else 0.0
        cfc[:, CF["fl"] + 1] = 1.0 if core < 7 else 0.0
        xl = np.concatenate([xp[0, PSEG * core:PSEG * (core + 1)]] + [xs[b, SSEG * core:SSEG * (core + 1)] for b in range(4)], axis=0)
        x0T = np.ascontiguousarray(xl.T)
        pos = np.concatenate([np.arange(PSEG * core, PSEG * (core + 1))] + [np.arange(SSEG * core, SSEG * (core + 1))] * 4).astype(np.float32)
        ang = pos[None, :] * inv[:, None]
        rope = np.stack([np.cos(ang), np.sin(ang)]).astype(f32)
        nb = np.zeros((1, 8), np.int32)
        for rel in range(-2, 3):
            nb[0, rel + 2] = min(7, max(0, core + rel))
        m = {"x0T": x0T, "cf32": cfc, "rope": rope, "nb": nb}
        for name, k, n in weight_specs():
            if int(name[1]) >= n_layers:
                continue
            w = inputs[name]
            m[name] = wfull.setdefault(name, np.ascontiguousarray(np.asarray(w, f32)))
        maps.append(m)
    return maps


ROPE_THETA_ = 10000.0
_NC_CACHE = {}


def run(inputs, n_layers=4, stop_mid=False):
    key = (n_layers, stop_mid)
    if key not in _NC_CACHE:
        _NC_CACHE[key] = build_program(n_layers, stop_mid)
    nc = _NC_CACHE[key]
    maps = host_inputs(inputs, n_layers)
    res = run_bass_kernel_spmd(nc, maps, core_ids=list(range(NCORES)))
    yp = np.zeros((1, 8192, D), np.float32)
    ys = np.zeros((4, 4096, D), np.float32)
    for core in range(NCORES):
        yT = np.asarray(res.results[core]["yT"])
        yp[0, PSEG * core:PSEG * (core + 1), :] = yT[:, 0:PSEG].T
        for b in range(4):
            ys[b, SSEG * core:SSEG * (core + 1), :] = yT[:, PSEG + SSEG * b:PSEG + SSEG * (b + 1)].T
    return yp, ys


def kernel(**inputs):
    return run(inputs, 4, False)


def _b_init(self):
    dt = self.nc.dram_tensor
    if hasattr(self, "KSb"):
        return
    self.KSb = [dt("KSb%d" % g, [2048, LT], BF16).ap() for g in range(3)]
    self.VSb = [dt("VSb%d" % g, [LT, 2048], BF16).ap() for g in range(3)]
    self.NKb = [{rel: dt("NKb%d_%d" % (g, rel + 2), [2048, LT], BF16).ap() for rel in ((-1, 1) if g < 2 else (-2, -1, 1, 2))}
                for g in range(3)]
    self.NVb = [{rel: dt("NVb%d_%d" % (g, rel + 2), [LT, 2048], BF16).ap() for rel in ((-1, 1) if g < 2 else (-2, -1, 1, 2))}
                for g in range(3)]


def _b_phase1(self, li):
    _b_init(self)
    wn = "l%d_b_w_qkv" % li
    xsrc, xname = (self.XS, "XS")
    self.phase_begin()
    ar = self.ar
    self.alloc_common()
    xts = [ar.alloc([KC, TT], F32, "xt%d" % i) for i in range(2)]
    hts = [ar.alloc([KC, TT], BF16, "ht%d" % i) for i in range(2)]
    ti = 0
    for g in (2, 1, 0):
        jobs = []
        for t in range(NT):
            c0 = t * TT
            for b12 in range(12):
                blk = g * 12 + b12
                kind = b12 // 4
                hb4 = b12 % 4
                slot = ti % 2
                xt, xt_k = xts[slot]
                ht, ht_k = hts[slot]
                nslot = (ti + 1) % 2

                def fn(wv, wkey, t=t, c0=c0, b12=b12, g=g, kind=kind, hb4=hb4, xt=xt, xt_k=xt_k, ht=ht, ht_k=ht_k, nslot=nslot, ti=ti):
                    if b12 == 0:
                        if ti == 0:
                            self.load_x_tile(xsrc, xname, t, xt, xt_k)
                        if ti + 1 < 3 * NT:
                            self.load_x_tile(xsrc, xname, (t + 1) % NT, *xts[nslot])
                        self.rmsnorm(xt, xt_k, KC, TT, "gvec", (2 * li) * 16, ht, ht_k, self.ntmp)
                    rhs = lambda kc: ht[:, kc, :]
                    if kind == 0:
                        def cons(m, psum, psk):
                            self.store_stage(psum, psk, self.QS[g * 16 + m, :, c0:c0 + TT], ("QS", g * 16 + m, t))
                        self.lin_fm(wv, wkey, KC, 4, rhs, [ht_k], TT, cons, m0=hb4 * 4)
                    elif kind == 1:
                        def cons(m, psum, psk):
                            self.store_stage(psum, psk, self.KSb[g][m * 128:(m + 1) * 128, c0:c0 + TT], ("KS", g, m, t))
                        self.lin_fm(wv, wkey, KC, 4, rhs, [ht_k], TT, cons, m0=hb4 * 4)
                    else:
                        def cons(s_, psum, psk):
                            self.store_stage(psum, psk, self.VSb[g][c0 + s_ * 128:c0 + (s_ + 1) * 128, hb4 * 512:(hb4 + 1) * 512],
                                             ("VS", g, hb4, s_, t))
                        self.lin_tm(lambda kc: wv[:, kc, 0:512], wkey, KC, 512,
                                    lambda kc, s_: ht[:, kc, s_ * 128:(s_ + 1) * 128], [ht_k], 4, cons)
                jobs.append((wn, KC, blk * 512, 512, fn))
            ti += 1
        self.run_jobs(jobs)
        ksend = [("KS", g, m, t) for m in range(16) for t in range(NT)]
        vsend = [("VS", g, hb4, s_, t) for hb4 in range(4) for s_ in range(4) for t in range(NT)]
        K8v, kk, (kn, kr) = self.allgather(self.KSb[g], 2048, LT, "Kb%d" % g, ksend, "K8b%d" % g)
        V8v, vk, (vn, vr) = self.allgather(self.VSb[g], LT, 2048, "Vb%d" % g, vsend, "V8b%d" % g)
        for rel in self.NKb[g].keys():
            self.localize(self.NKb[g][rel].rearrange("(n i) c -> n i c", i=kr), K8v, rel, kk, [("NKb", g, rel)], eng="pool")
            self.localize(self.NVb[g][rel].rearrange("(n i) c -> n i c", i=vr), V8v, rel, vk, [("NVb", g, rel)], eng="pool")


def _b_phase3(self, li):
    self.phase_begin()
    if li + 1 < self.n_layers:
        self.convert_layer(li + 1)
    ar = self.ar
    sl = slopes16()
    scale = HD ** -0.5
    KTs = Rot([ar.alloc([2560], BF16, "KTw%d" % i) for i in range(3)])
    Vws = Rot([ar.alloc([4096], BF16, "Vw%d" % i) for i in range(3)])
    QTs = Rot([ar.alloc([512], BF16, "QT%d" % i) for i in range(3)])
    tmps = Rot([ar.alloc([256], F32, "tmp%d" % i) for i in range(4)])
    PTs = Rot([ar.alloc([256], BF16, "PT%d" % i) for i in range(4)])
    NUMs = Rot([ar.alloc([512], F32, "NUM%d" % i) for i in range(2)])
    DENs = Rot([ar.alloc([512], F32, "DEN%d" % i) for i in range(2)])
    oats = Rot([ar.alloc([512], BF16, "oat%d" % i) for i in range(3)])
    RB0 = CF["RB"]
    cnt = 0
    sidx = 0
    groups = []
    for t in range(NT):
        c0 = t * TT
        tt_ = tile_type(t)
        for h in range(NH):
            NUM, NUM_k = NUMs.next()
            DEN, DEN_k = DENs.next()
            for g, (window, d) in enumerate(B_GROUPS):
                nq = TT // d
                nj = nq + 128
                nblk = (nj + 127) // 128
                W = TT + 128 * d
                KTf, KT_k = KTs.next()
                Vwf, Vw_k = Vws.next()
                KT = KTf[:, 0:W]
                Vw = Vwf[:, 0:d * nblk * 128].rearrange("p (r b x) -> p r b x", r=d, b=nblk)
                QT, QT_k = QTs.next()

                def pre(t=t, c0=c0, h=h, g=g, d=d, W=W, KT=KT, KT_k=KT_k, Vw=Vw, Vw_k=Vw_k, QT=QT, QT_k=QT_k):
                    self.dma(QT, self.QS[g * 16 + h, :, c0:c0 + TT], [], [QT_k])
                    w0 = 0
                    for (rel, lc, ln) in window_pieces(t, 64 * d):
                        ksrc = self.KSb[g] if rel == 0 else self.NKb[g][rel]
                        vsrc = self.VSb[g] if rel == 0 else self.NVb[g][rel]
                        self.dma(KT[:, w0:w0 + ln], ksrc[h * 128:(h + 1) * 128, lc:lc + ln], [], [KT_k])
                        ja, je = w0 // d, (w0 + ln) // d
                        j = ja
                        while j < je:
                            b = j // 128
                            jn = min(je, (b + 1) * 128)
                            p0 = j % 128
                            n = jn - j
                            r0 = lc + (j - ja) * d
                            self.dma(Vw[p0:p0 + n, :, b, :],
                                     vsrc[r0:r0 + n * d, h * 128:(h + 1) * 128].rearrange("(jj r) x -> jj r x", r=d), [], [Vw_k])
                            j = jn
                        w0 += ln
                    assert w0 == W
                num, num_k = self.ps[3 + cnt % 2]
                den, den_k = self.ps[5 + cnt % 2]
                cnt += 1
                coef = -sl[h] * d / scale
                blocks = []
                for r in range(d):
                    for b in range(nblk):
                        nk = min(128, nj - 128 * b)
                        q_lo = max(0, 128 * b - 128)
                        q_hi = min(nq, 128 * b + nk)
                        n = q_hi - q_lo
                        cb = q_lo - (128 * b - 128)
                        S, S_k = self.ps[(0, 1, 2, 7)[sidx % 4]]
                        sidx += 1
                        tmp, tmp_k = tmps.next()
                        PT, PT_k = PTs.next()
                        k0 = 128 * b * d + r
                        q0 = q_lo * d + r
                        eo = CF["EB"] + (tt_ * 3 + g) * 5 + b
                        first = (r == 0 and b == 0)
                        last = (r == d - 1 and b == nblk - 1)
                        o0 = r * nq + q_lo

                        def fS(S=S, S_k=S_k, KT=KT, KT_k=KT_k, QT=QT, QT_k=QT_k, k0=k0, q0=q0, nk=nk, n=n, d=d):
                            self.pe(lambda e: e.matmul(S[0:nk, 0:n], KT[:, k0:k0 + (nk - 1) * d + 1:d], QT[:, q0:q0 + (n - 1) * d + 1:d],
                                                       start=True, stop=True), [KT_k, QT_k], [S_k])

                        def fsoft(S=S, S_k=S_k, tmp=tmp, tmp_k=tmp_k, PT=PT, PT_k=PT_k, cb=cb, n=n, nk=nk, coef=coef, eo=eo):
                            self.dve(lambda e: e.scalar_tensor_tensor(out=tmp[0:nk, 0:n], in0=self.cf[0:nk, RB0 + cb:RB0 + cb + n], scalar=coef,
                                                                      in1=S[0:nk, 0:n], op0=ALU.mult, op1=ALU.add),
                                     [S_k, self.cf_k], [tmp_k])
                            self.act(lambda e: e.activation(out=PT[0:nk, 0:n], in_=tmp[0:nk, 0:n], func=AF.Exp,
                                                            bias=self.cf[0:nk, eo:eo + 1], scale=scale),
                                     [tmp_k, self.cf_k], [PT_k])

                        def fPV(num=num, num_k=num_k, den=den, den_k=den_k, Vw=Vw, Vw_k=Vw_k, PT=PT, PT_k=PT_k, r=r, b=b, nk=nk, n=n,
                                o0=o0, first=first, last=last):
                            self.pe(lambda e: e.matmul(num[:, o0:o0 + n], Vw[0:nk, r, b, :], PT[0:nk, 0:n], start=first, stop=last,
                                                       skip_group_check=True), [Vw_k, PT_k], [num_k])
                            self.pe(lambda e: e.matmul(den[:, o0:o0 + n], self.ones[0:nk, :], PT[0:nk, 0:n], start=first, stop=last,
                                                       skip_group_check=True), [self.ones_k, PT_k], [den_k])
                        post = None
                        if last:
                            def post(g=g, d=d, h=h, t=t, c0=c0, num=num, num_k=num_k, den=den, den_k=den_k,
                                     NUM=NUM, NUM_k=NUM_k, DEN=DEN, DEN_k=DEN_k):
                                for (ACC, ACC_k, src, src_k) in ((NUM, NUM_k, num, num_k), (DEN, DEN_k, den, den_k)):
                                    if g == 0:
                                        self.dve(lambda e, ACC=ACC, src=src: e.tensor_copy(ACC, src[:, 0:TT]), [src_k], [ACC_k])
                                    else:
                                        accv = ACC.rearrange("p (q r) -> p r q", r=d)
                                        srcv = src[:, 0:TT].rearrange("p (r q) -> p r q", r=d)
                                        self.dve(lambda e, accv=accv, srcv=srcv: e.tensor_tensor(out=accv, in0=accv, in1=srcv, op=ALU.add),
                                                 [src_k, ACC_k], [ACC_k])
                                if g == 2:
                                    oat, oat_k = oats.next()
                                    self.dve(lambda e: e.reciprocal(DEN, DEN), [DEN_k], [DEN_k])
                                    self.dve(lambda e: e.tensor_tensor(out=oat, in0=NUM, in1=DEN, op=ALU.mult), [NUM_k, DEN_k], [oat_k])
                                    self.dma(self.ATT[h, :, c0:c0 + TT], oat, [oat_k], [("ATT", h, t)])
                        blocks.append((fS, fsoft, fPV, post))
                groups.append((pre, blocks))
    self.run_attn(groups)


Builder.b_phase1 = _b_phase1
Builder.b_phase3 = _b_phase3


C_SCALE = (128 + 64) ** -0.5


def _c_init(self):
    dt = self.nc.dram_tensor
    if hasattr(self, "KSc"):
        return
    self.KSc = dt("KSc", [2112, LT], BF16).ap()
    self.VSc = dt("VSc", [LT, 2048], BF16).ap()


def _rope(self, x1, x1_k, x2, x2_k, cs, sn, csn_k, o1, o2, o_k, tmps):
    (ta, ta_k), (tb, tb_k) = tmps.next(), tmps.next()
    P32 = slice(0, 32)
    self.dve(lambda e: e.tensor_tensor(out=ta[P32, :], in0=x1[P32, 0:TT], in1=cs[P32, :], op=ALU.mult), [x1_k, csn_k], [ta_k])
    self.dve(lambda e: e.tensor_tensor(out=tb[P32, :], in0=x2[P32, 0:TT], in1=sn[P32, :], op=ALU.mult), [x2_k, csn_k], [tb_k])
    self.dve(lambda e: e.tensor_tensor(out=o1[P32, :], in0=ta[P32, :], in1=tb[P32, :], op=ALU.subtract), [ta_k, tb_k], [o_k])
    (tc, tc_k), (td, td_k) = tmps.next(), tmps.next()
    self.dve(lambda e: e.tensor_tensor(out=tc[P32, :], in0=x2[P32, 0:TT], in1=cs[P32, :], op=ALU.mult), [x2_k, csn_k], [tc_k])
    self.dve(lambda e: e.tensor_tensor(out=td[P32, :], in0=x1[P32, 0:TT], in1=sn[P32, :], op=ALU.mult), [x1_k, csn_k], [td_k])
    self.dve(lambda e: e.tensor_tensor(out=o2[P32, :], in0=tc[P32, :], in1=td[P32, :], op=ALU.add), [tc_k, td_k], [o_k])


def _c_phase1(self, li):
    _c_init(self)
    wd, wuq, wukv = "l%d_c_w_down" % li, "l%d_c_w_uq" % li, "l%d_c_w_ukv" % li
    self.phase_begin()
    ar = self.ar
    self.alloc_common()
    xt, xt_k = ar.alloc([KC, TT], F32, "xt")
    hts = [ar.alloc([KC, TT], BF16, "ht%d" % i) for i in range(2)]
    c32, c32_k = ar.alloc([4, TT], F32, "c32")
    cqn, cqn_k = ar.alloc([4, TT], BF16, "cqn")
    ckvn, ckvn_k = ar.alloc([4, TT], BF16, "ckvn")
    cs, csn_k = ar.alloc([TT], F32, "cos")
    sn, _ = ar.alloc([TT], F32, "sin")
    rtmps = Rot([ar.alloc([TT], F32, "rt%d" % i) for i in range(4)])
    ropo = Rot([(ar.alloc([TT], BF16, "ro1_%d" % i), ar.alloc([TT], BF16, "ro2_%d" % i)) for i in range(2)])
    jobs = []
    for t in range(NT):
        c0 = t * TT
        ht, ht_k = hts[t % 2]

        def j_down(wv, wkey, which, t=t, c0=c0, ht=ht, ht_k=ht_k):
            if which == 0:
                self.load_x_tile(self.XS, "XS", t, xt, xt_k)
                self.dma(cs[0:32, :], self.rope_d[0, :, c0:c0 + TT], [], [csn_k])
                self.dma(sn[0:32, :], self.rope_d[1, :, c0:c0 + TT], [], [csn_k])
                self.rmsnorm(xt, xt_k, KC, TT, "gvec", (2 * li) * 16, ht, ht_k, self.ntmp)
            rhs = lambda kc: ht[:, kc, :]
            if which < 2:
                def cons(m, psum, psk):
                    self.evac(c32[:, m, :], psum[:, 0:TT], [psk], [c32_k])
                self.lin_fm(wv, wkey, KC, 4, rhs, [ht_k], TT, cons)
                dst, dst_k = (cqn, cqn_k) if which == 0 else (ckvn, ckvn_k)
                self.rmsnorm(c32, c32_k, 4, TT, "cnorm", which * 4, dst, dst_k, self.ntmp)
            else:
                got = {}

                def cons(m, psum, psk):
                    got[m] = (psum, psk)
                self.lin_fm(wv, wkey, KC, 2, rhs, [ht_k], TT, cons, mw=32)
                (o1, o1_k), (o2, o2_k) = ropo.next()
                _rope(self, got[0][0], got[0][1], got[1][0], got[1][1], cs, sn, csn_k, o1, o2, o1_k, rtmps)
                self.dma(self.KSc[2048:2080, c0:c0 + TT], o1[0:32, :], [o1_k], [("KSr", 0, t)])
                self.dma(self.KSc[2080:2112, c0:c0 + TT], o2[0:32, :], [o1_k], [("KSr", 1, t)])
        jobs.append((wd, KC, 0, 512, lambda wv, wkey, f=j_down: f(wv, wkey, 0)))
        jobs.append((wd, KC, 512, 512, lambda wv, wkey, f=j_down: f(wv, wkey, 1)))
        jobs.append((wd, KC, 1024, 64, lambda wv, wkey, f=j_down: f(wv, wkey, 2)))
        for hf in range(2):
            def j_uq(wv, wkey, hf=hf, t=t, c0=c0):
                for hl in range(8):
                    h = hf * 8 + hl
                    base = hl * 192
                    psum, psk = self.psrot.next()
                    p1, p1k = self.psrot.next()
                    p2, p2k = self.psrot.next()
                    for (pp, ppk, off, mw) in ((psum, psk, base, 128), (p1, p1k, base + 128, 32), (p2, p2k, base + 160, 32)):
                        for kc in range(4):
                            self.pe(lambda e, pp=pp, off=off, mw=mw, kc=kc: e.matmul(pp[0:mw, 0:TT], wv[:, kc, off:off + mw], cqn[:, kc, :],
                                                                                 start=(kc == 0), stop=(kc == 3)),
                                    [wkey, cqn_k], [ppk])
                    self.store_stage(psum, psk, self.QS[h, :, c0:c0 + TT], ("QS", h, t))
                    (o1, o1_k), (o2, o2_k) = ropo.next()
                    _rope(self, p1, p1k, p2, p2k, cs, sn, csn_k, o1, o2, o1_k, rtmps)
                    self.dma(self.QR[h, 0:32, c0:c0 + TT], o1[0:32, :], [o1_k], [("QR", h, 0, t)])
                    self.dma(self.QR[h, 32:64, c0:c0 + TT], o2[0:32, :], [o1_k], [("QR", h, 1, t)])
            jobs.append((wuq, 4, hf * 1536, 1536, j_uq))
        for hf in range(2):
            def j_ukv(wv, wkey, hf=hf, t=t, c0=c0):
                for hl in range(8):
                    h = hf * 8 + hl
                    psum, psk = self.psrot.next()
                    for kc in range(4):
                        self.pe(lambda e, psum=psum, hl=hl, kc=kc: e.matmul(psum[:, 0:TT], wv[:, kc, hl * 256:hl * 256 + 128], ckvn[:, kc, :],
                                                                          start=(kc == 0), stop=(kc == 3)),
                                [wkey, ckvn_k], [psk])
                    self.store_stage(psum, psk, self.KSc[h * 128:(h + 1) * 128, c0:c0 + TT], ("KS", h, t))
                wvv = wv.rearrange("p k (h x) -> p k h x", x=256)
                for q4 in range(2):
                    h0 = hf * 8 + q4 * 4
                    for s in range(4):
                        psum, psk = self.psrot.next()
                        for kc in range(4):
                            self.pe(lambda e, psum=psum, q4=q4, s=s, kc=kc: e.matmul(psum[:, 0:512], ckvn[:, kc, s * 128:(s + 1) * 128],
                                                                                  wvv[:, kc, q4 * 4:q4 * 4 + 4, 128:256],
                                                                                  start=(kc == 0), stop=(kc == 3)),
                                    [wkey, ckvn_k], [psk])
                        self.store_stage(psum, psk, self.VSc[c0 + s * 128:c0 + (s + 1) * 128, h0 * 128:(h0 + 4) * 128], ("VS", h0, s, t))
            jobs.append((wukv, 4, hf * 2048, 2048, j_ukv))
    self.run_jobs(jobs)
    ksend = [("KS", h, t) for h in range(NH) for t in range(NT)] + [("KSr", i, t) for i in range(2) for t in range(NT)]
    vsend = [("VS", h0, s, t) for h0 in range(0, 16, 4) for s in range(4) for t in range(NT)]
    self.cK8v, self.cKk, (kn, kr) = self.allgather(self.KSc, 2112, LT, "Kc", ksend, "K8c", rpc_force=64)
    self.cV8v, self.cVk, (vn, vr) = self.allgather(self.VSc, LT, 2048, "Vc", vsend, "V8c")
    assert kr == 64 and vr == 128


def _c_phase3(self, li):
    self.phase_begin()
    if li + 1 < self.n_layers:
        self.convert_layer(li + 1)
    ar = self.ar
    K8v, V8v = self.cK8v, self.cV8v
    KTs = Rot([ar.alloc([8192], BF16, "cKT%d" % i) for i in range(2)])
    Vps = Rot([ar.alloc([64, 128], BF16, "cVp%d" % i) for i in range(2)])
    KRs = Rot([ar.alloc([8192], BF16, "cKR%d" % i) for i in range(2)])
    QNs = Rot([ar.alloc([512], BF16, "cQN%d" % i) for i in range(3)])
    QRs = Rot([ar.alloc([512], BF16, "cQR%d" % i) for i in range(3)])
    PTs = Rot([ar.alloc([512], BF16, "cPT%d" % i) for i in range(4)])
    recs = Rot([ar.alloc([512], F32, "crec%d" % i) for i in range(2)])
    oats = Rot([ar.alloc([512], BF16, "coat%d" % i) for i in range(3)])
    seqs = [(PSEG, 0, [0, 1])] + [(SSEG, PSEG + SSEG * b, [2 + b]) for b in range(4)]
    cnt = 0
    sidx = 0
    groups = []
    for (seg, lc0, tiles) in seqs:
        L = seg * 8
        nkb = L // 128
        KR, KR_k = KRs.next()
        for h in range(NH):
            KT, KT_k = KTs.next()
            Vp, Vp_k = Vps.next()

            def pre(seg=seg, lc0=lc0, h=h, KR=KR, KR_k=KR_k, KT=KT, KT_k=KT_k, Vp=Vp, Vp_k=Vp_k):
                if h == 0:
                    for r in range(8):
                        self.dma(KR[0:64, r * seg:(r + 1) * seg], K8v[r, 32, :, lc0:lc0 + seg], [], [KR_k])
                nb = seg // 128
                for r in range(8):
                    for n2 in range(2):
                        self.dma(KT[64 * n2:64 * n2 + 64, r * seg:(r + 1) * seg], K8v[r, 2 * h + n2, :, lc0:lc0 + seg], [], [KT_k])
                    self.dma(Vp[:, r * nb:(r + 1) * nb, :],
                             V8v[r, lc0 // 128:lc0 // 128 + nb, :, h * 128:(h + 1) * 128].rearrange("n p d -> p n d"), [], [Vp_k])
            blocks = []
            for t in tiles:
                c0 = t * TT
                QN, QN_k = QNs.next()
                QRt, QR_k = QRs.next()
                num, num_k = self.ps[4 + cnt % 2]
                den, den_k = self.ps[6 + cnt % 2]
                cnt += 1
                for kb in range(nkb):
                    S, S_k = self.ps[sidx % 4]
                    sidx += 1
                    PT, PT_k = PTs.next()

                    def fS(S=S, S_k=S_k, KT=KT, KT_k=KT_k, KR=KR, KR_k=KR_k, QN=QN, QN_k=QN_k, QRt=QRt, QR_k=QR_k, kb=kb, h=h, c0=c0):
                        if kb == 0:
                            self.dma(QN, self.QS[h, :, c0:c0 + TT], [], [QN_k])
                            self.dma(QRt[0:64, :], self.QR[h, :, c0:c0 + TT], [], [QR_k])
                        self.pe(lambda e: e.matmul(S[:, 0:TT], KT[:, kb * 128:(kb + 1) * 128], QN, start=True, stop=False),
                                [KT_k, QN_k], [S_k])
                        self.pe(lambda e: e.matmul(S[:, 0:TT], KR[0:64, kb * 128:(kb + 1) * 128], QRt[0:64, :], start=False, stop=True),
                                [KR_k, QR_k], [S_k])

                    def fsoft(S=S, S_k=S_k, PT=PT, PT_k=PT_k):
                        self.act(lambda e: e.activation(out=PT, in_=S[:, 0:TT], func=AF.Exp, scale=C_SCALE), [S_k], [PT_k])

                    def fPV(num=num, num_k=num_k, den=den, den_k=den_k, Vp=Vp, Vp_k=Vp_k, PT=PT, PT_k=PT_k, kb=kb, nkb=nkb):
                        self.pe(lambda e: e.matmul(num[:, 0:TT], Vp[:, kb, :], PT, start=(kb == 0), stop=(kb == nkb - 1)),
                                [Vp_k, PT_k], [num_k])
                        self.pe(lambda e: e.matmul(den[:, 0:TT], self.ones[:, :], PT, start=(kb == 0), stop=(kb == nkb - 1)),
                                [self.ones_k, PT_k], [den_k])
                    post = None
                    if kb == nkb - 1:
                        def post(num=num, num_k=num_k, den=den, den_k=den_k, h=h, t=t, c0=c0):
                            rec, rec_k = recs.next()
                            oat, oat_k = oats.next()
                            self.dve(lambda e: e.reciprocal(rec, den[:, 0:TT]), [den_k], [rec_k])
                            self.dve(lambda e: e.tensor_tensor(out=oat, in0=num[:, 0:TT], in1=rec, op=ALU.mult), [num_k, rec_k], [oat_k])
                            self.dma(self.ATT[h, :, c0:c0 + TT], oat, [oat_k], [("ATT", h, t)])
                    blocks.append((fS, fsoft, fPV, post))
            groups.append((pre, blocks))
    self.run_attn(groups)


Builder.c_phase1 = _c_phase1
Builder.c_phase3 = _c_phase3
```

```python
import contextlib
import numpy as np
import ml_dtypes
import concourse.bass as bass
import concourse.mybir as mybir
from concourse.bass_utils import run_bass_kernel_spmd

F32 = mybir.dt.float32
BF16 = mybir.dt.bfloat16
AF = mybir.ActivationFunctionType
ALU = mybir.AluOpType

NCORES = 8


class Op:
    __slots__ = ("eng", "fn", "deps", "is_dma", "signal", "idx", "sem", "val", "prewait", "inc", "force", "raw", "bg")

    def __init__(self, eng, fn, is_dma, inc):
        self.eng = eng
        self.fn = fn
        self.deps = set()
        self.is_dma = is_dma
        self.signal = False
        self.sem = None
        self.val = None
        self.prewait = None
        self.inc = inc
        self.force = False
        self.raw = set()
        self.bg = False


ENGS = ("pe", "act", "dve", "pool", "sp")
SEM_ROT = 12000
N_DMA_SEMS = 12


class Prog:
    def __init__(self, nc, stack):
        self.nc = nc
        self.stack = stack
        self.ops = []
        self.eng_ops = {e: [] for e in ENGS}
        self.last_w = {}
        self.readers = {}
        self.dma_rr = {e: 0 for e in ENGS}
        self.dma_sems = {}
        self.dma_sem_last = {}
        self.eng_sems = {e: [] for e in ENGS}
        self.bar_from = 0
        self.prev_bar = []
        self.bg_last_w = {}

    def op(self, eng, fn, reads=(), writes=(), dma=False, inc=16, force=False, bg=False):
        o = Op(eng, fn, dma, inc)
        o.force = force
        o.bg = bg
        o.idx = len(self.ops)
        for b in reads:
            w = self.last_w.get(b)
            if w is not None:
                o.deps.add(w)
                o.raw.add(w)
        for b in writes:
            w = self.last_w.get(b)
            if w is not None:
                o.deps.add(w)
            for r in self.readers.get(b, ()):
                o.deps.add(r)
        for b in reads:
            self.readers.setdefault(b, []).append(o.idx)
        for b in writes:
            self.last_w[b] = o.idx
            self.readers[b] = []
            if bg:
                self.bg_last_w[b] = o.idx
        o.deps.discard(o.idx)
        self.ops.append(o)
        self.eng_ops[eng].append(o)
        return o

    def barrier(self):
        lasts = []
        for e in ("pe", "act", "dve"):
            for o in reversed(self.eng_ops[e]):
                if o.fn is not None and not o.is_dma:
                    lasts.append(o.idx)
                    break
        dmas = [o.idx for o in self.ops[self.bar_from:] if o.is_dma and not o.bg]
        self.bar_from = len(self.ops)
        prev = list(self.prev_bar)
        self.prev_bar = []
        for e in ENGS:
            o = Op(e, None, False, 0)
            o.idx = len(self.ops)
            o.deps = set(lasts) | set(dmas) | set(prev)
            self.ops.append(o)
            self.eng_ops[e].append(o)
            self.prev_bar.append(o.idx)
        self.last_w = dict(self.bg_last_w)
        self.readers = {}

    def finalize(self):
        nc = self.nc
        ops = self.ops
        for o in ops:
            for d in list(o.deps):
                do = ops[d]
                if (not do.is_dma) and do.eng == o.eng and not o.is_dma and not o.force and not (d in o.raw and o.eng != "pe"):
                    o.deps.discard(d)
                    continue
                do.signal = True
        cnt = {e: 0 for e in ENGS}
        for e in ENGS:
            for o in self.eng_ops[e]:
                if o.is_dma:
                    cc = "cc" if o.inc == 1 else "d"
                    rrk = (e, cc)
                    k = self.dma_rr.get(rrk, 0) % (N_DMA_SEMS if cc == "d" else 8)
                    self.dma_rr[rrk] = self.dma_rr.get(rrk, 0) + 1
                    key = (e, cc, k)
                    if key not in self.dma_sems:
                        self.dma_sems[key] = [self.stack.enter_context(nc.semaphore("%s_%s_%d" % (cc, e, k))), 0]
                    ent = self.dma_sems[key]
                    o.prewait = (ent[0], ent[1]) if ent[1] > 0 else None
                    ent[1] += o.inc
                    o.sem, o.val = ent[0], ent[1]
                elif o.signal and o.fn is not None:
                    ph = cnt[e] // SEM_ROT
                    while len(self.eng_sems[e]) <= ph:
                        self.eng_sems[e].append(
                            self.stack.enter_context(nc.semaphore("c_%s_%d" % (e, len(self.eng_sems[e])))))
                    cnt[e] += 1
                    o.sem = self.eng_sems[e][ph]
                    o.val = cnt[e] - ph * SEM_ROT
                elif o.signal and o.fn is None:
                    pass

        def resolve(d, acc, seen):
            do = ops[d]
            if do.fn is None:
                if d in seen:
                    return
                seen.add(d)
                for dd in do.deps:
                    resolve(dd, acc, seen)
                return
            key = id(do.sem)
            if key not in acc or acc[key][1] < do.val:
                acc[key] = (do.sem, do.val)

        self._resolve = resolve

        with nc.Block() as block:
            def run(e, handle_name):
                deco = getattr(block, handle_name)

                @deco
                def _(h):
                    known = {}
                    for o in self.eng_ops[e]:
                        acc = {}
                        seen = set()
                        for d in o.deps:
                            resolve(d, acc, seen)
                        if o.prewait is not None:
                            s, v = o.prewait
                            if id(s) not in acc or acc[id(s)][1] < v:
                                acc[id(s)] = (s, v)
                        for key, (s, v) in acc.items():
                            if known.get(key, 0) >= v:
                                continue
                            known[key] = v
                            h.wait_ge(s, v)
                        if o.fn is None:
                            continue
                        ins = o.fn(h)
                        if o.sem is not None:
                            if o.is_dma:
                                ins.then_inc(o.sem, o.inc)
                            else:
                                ins.then_inc(o.sem, 1)

            run("sp", "sync")
            run("pool", "gpsimd")
            run("act", "scalar")
            run("dve", "vector")
            run("pe", "tensor")


D = 2048
KC = 16
DFF = 5632
FC = 44
TT = 512
NT = 6
LT = 3072
HD = 128
NH = 16
EPS = 1e-6
PSEG = 1024
SSEG = 512
NEG = -30000.0
BIGR = 1.0e6
B_GROUPS = ((128, 1), (512, 4), (2048, 16))
I32 = mybir.dt.int32


def slopes16():
    return [2.0 ** (-8.0 * (h + 1) / 16.0) for h in range(16)]


def tile_cols(t):
    return t * TT


def tile_type(t):
    return 0 if t == 0 else (1 if t == 1 else 2)


def window_pieces(t, halo):
    base = 0 if t < 2 else PSEG + SSEG * (t - 2)
    seg = PSEG if t < 2 else SSEG
    off = TT * t if t < 2 else 0
    lo = off - halo
    hi = off + TT + halo
    pieces = []
    rel_lo = lo // seg
    rel_hi = (hi - 1) // seg
    for rel in range(rel_lo, rel_hi + 1):
        a = max(lo, rel * seg)
        b = min(hi, (rel + 1) * seg)
        pieces.append((rel, base + a - rel * seg, b - a))
    return pieces


class Arena:
    def __init__(self, ap, nwords):
        self.ap = ap
        self.n = nwords
        self.off = 0
        self.cnt = 0

    def alloc(self, shape, dtype, key=None):
        n = int(np.prod(shape))
        sz = 4 if dtype in (F32, I32) else 2
        words = (n * sz + 3) // 4
        words = (words + 15) // 16 * 16
        assert self.off + words <= self.n, ("arena overflow", self.off, words, self.n)
        a = self.ap[:, self.off:self.off + words]
        if dtype != F32:
            a = a.bitcast(dtype)
        a = a[:, 0:n]
        if len(shape) == 2:
            a = a.rearrange("p (a b) -> p a b", b=shape[1])
        elif len(shape) == 3:
            a = a.rearrange("p (a b c) -> p a b c", b=shape[1], c=shape[2])
        self.off += words
        self.cnt += 1
        return a, (key or ("ar%d" % self.cnt)) + "@%d" % self.off


class Rot:
    def __init__(self, items):
        self.items = items
        self.i = 0

    def next(self):
        it = self.items[self.i % len(self.items)]
        self.i += 1
        return it


def weight_specs():
    specs = []
    for i in range(4):
        pre = "l%d_" % i
        kind = i % 3
        if kind == 0:
            specs += [(pre + "a_w_qkv", 2048, 3072), (pre + "a_w_o", 2048, 2048)]
        elif kind == 1:
            specs += [(pre + "b_w_qkv", 2048, 18432), (pre + "b_w_o", 2048, 2048)]
        else:
            specs += [(pre + "c_w_down", 2048, 1088), (pre + "c_w_uq", 512, 3072),
                      (pre + "c_w_ukv", 512, 4096), (pre + "c_w_o", 2048, 2048)]
        specs += [(pre + "ffn_w_in", 2048, 11264), (pre + "ffn_w_out", 5632, 2048)]
    return specs


CF = {}
_o = 0
for _n, _w in (("gvec", 144), ("convw", 528), ("convb", 176), ("sink", 32), ("cnorm", 8), ("RA", 384),
               ("RB", 256), ("EA", 18), ("EB", 45), ("fl", 2)):
    CF[_n] = _o
    _o += _w
NCF = _o


class Builder:
    def __init__(self, n_layers=4, stop_mid=False, arena_words=46000):
        self.n_layers = n_layers
        self.stop_mid = stop_mid
        self.nc = bass.Bass("TRN2", target_bir_lowering=False)
        nc = self.nc
        self.st = contextlib.ExitStack()
        self.P = Prog(nc, self.st)
        self.x0 = nc.dram_tensor("x0T", [D, LT], F32, kind="ExternalInput").ap()
        self.cf_d = nc.dram_tensor("cf32", [128, NCF], F32, kind="ExternalInput").ap()
        self.rope_d = nc.dram_tensor("rope", [2, 32, LT], F32, kind="ExternalInput").ap()
        self.nb_d = nc.dram_tensor("nb", [1, 8], I32, kind="ExternalInput").ap()
        self.yT = nc.dram_tensor("yT", [D, LT], F32, kind="ExternalOutput").ap()
        self.w32 = {}
        self.wb = {}
        self.wkeys = {}
        for name, k, n in weight_specs():
            if int(name[1]) >= n_layers:
                continue
            self.w32[name] = nc.dram_tensor(name, [k, n], F32, kind="ExternalInput").ap()
            self.wb[name] = nc.dram_tensor("wb_" + name, [k, n], BF16).ap()
        dt = nc.dram_tensor
        self.XS = dt("XS", [KC, 128, LT], F32).ap()
        self.XM = dt("XM", [KC, 128, LT], F32).ap()
        self.H2 = dt("H2", [KC, 128, LT], BF16).ap()
        self.ATT = dt("ATT", [NH, 128, LT], BF16).ap()
        self.QS = dt("QS", [48, 128, LT], BF16).ap()
        self.QR = dt("QR", [NH, 64, LT], BF16).ap()
        self.KSa = dt("KSa", [512, LT], BF16).ap()
        self.NKa = {rel: dt("NKa%d" % (rel + 2), [512, LT], BF16).ap() for rel in (-1, 1)}
        self.NVa = {rel: dt("NVa%d" % (rel + 2), [LT, 512], BF16).ap() for rel in (-1, 1)}
        self.NHB = {rel: dt("NHB%d" % (rel + 2), [128, 160], BF16).ap() for rel in (-1, 1)}
        self.VSa = dt("VSa", [LT, 512], BF16).ap()
        self.HBs = dt("HBs", [128, 160], BF16).ap()
        self.arena_t = self.st.enter_context(nc.sbuf_tensor("arena", [128, arena_words], F32))
        self.ar = Arena(self.arena_t[:, :], arena_words)
        self.ps = []
        for i in range(8):
            t = self.st.enter_context(nc.psum_tensor("ps%d" % i, [128, 512], F32))
            self.ps.append((t[:, :], "ps%d" % i))
        self.psrot = Rot(self.ps)
        self.evac_i = 0
        self.regv = {}
        self.ag_bufs = {}
        self.attn_lookahead = 2

    def dma(self, out, in_, r, w, eng="sp"):
        return self.P.op(eng, lambda e: e.dma_start(out=out, in_=in_), reads=r, writes=w, dma=True)

    def localize(self, dst, g8view, rel, r, w, eng="sp", bg=False):
        self.dyn_cnt[eng] += 1
        assert self.dyn_cnt[eng] <= 21, "dynamic DMA register budget exceeded"

        def fn(e):
            v = self.regv[(eng, rel)]
            return e.dma_start(out=dst, in_=g8view[bass.ds(v, 1)])
        return self.P.op(eng, fn, reads=r, writes=w, dma=True, bg=bg)

    def pe(self, fn, r, w):
        return self.P.op("pe", fn, reads=r, writes=w)

    def act(self, fn, r, w):
        return self.P.op("act", fn, reads=r, writes=w)

    def dve(self, fn, r, w, force=False):
        return self.P.op("dve", fn, reads=r, writes=w, force=force)

    def evac(self, out, in_, r, w):
        self.evac_i += 1
        if self.evac_i % 2 == 0:
            return self.act(lambda e: e.activation(out=out, in_=in_, func=AF.Copy), r, w)
        return self.dve(lambda e: e.tensor_copy(out, in_), r, w)

    def allgather(self, send, R, C, name, key_send, key_out, rpc_force=None, bg=False):
        nc = self.nc
        rpc = 1
        for cand in range(1, R + 1):
            if R % cand == 0 and cand * C * 2 <= 512 * 1024:
                rpc = cand
        if rpc_force:
            rpc = rpc_force
        nch = R // rpc
        if name not in self.ag_bufs:
            self.ag_bufs[name] = (nc.dram_tensor("g4_" + name, [nch * 4 * rpc, C], BF16).ap(),
                                  nc.dram_tensor("g8_" + name, [nch * 8 * rpc, C], BF16).ap())
        g4, g8 = self.ag_bufs[name]
        for stage in (1, 2):
            for c in range(nch):
                s_ap = send[c * rpc:(c + 1) * rpc, :]
                g4c = g4[c * 4 * rpc:(c + 1) * 4 * rpc, :]
                g8c = g8[c * 8 * rpc:(c + 1) * 8 * rpc, :]
                k4 = (key_out, "g4", c)
                if stage == 1:
                    def c1(e, s_ap=s_ap, g4c=g4c):
                        return e.collective_compute("AllGather", ALU.bypass, replica_groups=[[0, 1, 2, 3], [4, 5, 6, 7]],
                                                    ins=[s_ap.opt()], outs=[g4c.opt()])
                    self.P.op("pool", c1, reads=key_send, writes=[k4], dma=True, inc=1, bg=bg)
                else:
                    def c2(e, g4c=g4c, g8c=g8c):
                        return e.collective_compute("AllGather", ALU.bypass, replica_groups=[[0, 4], [1, 5], [2, 6], [3, 7]],
                                                    ins=[g4c.opt()], outs=[g8c.opt()])
                    self.P.op("pool", c2, reads=[k4], writes=[(key_out, c)], dma=True, inc=1, bg=bg)
        keys = [(key_out, c) for c in range(nch)]
        return g8.rearrange("(n r i) c -> r n i c", r=8, i=rpc), keys, (nch, rpc)

    def convert_layer(self, li):
        for name, k, n in weight_specs():
            if name not in self.w32 or int(name[1]) != li:
                continue
            rows = 64 if n > 4096 else 256
            keys = []
            for r0 in range(0, k, rows):
                r1 = min(k, r0 + rows)
                key = ("wb", name, r0)
                keys.append(key)
                self.P.op("pool", lambda e, r0=r0, r1=r1, name=name: e.dma_start(out=self.wb[name][r0:r1, :], in_=self.w32[name][r0:r1, :]),
                          reads=[], writes=[key], dma=True, bg=True)
            self.wkeys[name] = keys

    def setup(self):
        ar = self.ar
        P = self.P
        self.cf, self.cf_k = ar.alloc([NCF], F32, "cf")
        self.cf = self.cf
        self.dma(self.cf, self.cf_d, [], [self.cf_k])
        self.nbs, self.nbs_k = ar.alloc([8], I32, "nbs")
        self.dma(self.nbs[0:1, :], self.nb_d, [], [self.nbs_k])

        self.dyn_cnt = {"sp": 0, "pool": 0}
        for eng in ("sp", "pool"):
            def ldregs(e, eng=eng):
                for rel in (-2, -1, 1, 2):
                    reg = e.alloc_register("nbr%d" % (rel + 2))
                    e.reg_load(reg, self.nbs[0:1, rel + 2:rel + 3])
                    self.regv[(eng, rel)] = e.snap(reg)
                return None
            P.op(eng, ldregs, reads=[self.nbs_k], writes=[])
        self.ones, self.ones_k = ar.alloc([128], BF16, "ones")
        self.dve(lambda e: e.memset(self.ones, 1.0), [], [self.ones_k])
        self.esink, self.esink_k = ar.alloc([32], F32, "esink")
        o = CF["sink"]
        self.act(lambda e: e.activation(out=self.esink, in_=self.cf[:, o:o + 32], func=AF.Exp),
                 [self.cf_k], [self.esink_k])
        self.haloL, self.haloL_k = ar.alloc([16, 5], BF16, "haloL")
        self.haloR, self.haloR_k = ar.alloc([16, 5], BF16, "haloR")
        self.hb, self.hb_k = ar.alloc([2, 16, 5], BF16, "hb")
        self.mark = ar.off
        self.x0v = self.x0.rearrange("(k p) t -> k p t", p=128)

    def phase_begin(self):
        self.P.barrier()
        self.ar.off = self.mark

    def cfcol(self, name, idx):
        o = CF[name] + idx
        return self.cf[:, o:o + 1]

    def rmsnorm(self, xt, xt_k, nchunks, width, gname, gidx0, out, out_k, tmp, out_fn=None, post=None):
        psum, psk = self.psrot.next()
        n_feat = nchunks * 128
        for c in range(nchunks):
            sq, sqk = tmp["sq"].next()
            self.act(lambda e, c=c, sq=sq: e.activation(out=sq[:, 0:width], in_=xt[:, c, 0:width], func=AF.Square),
                     [xt_k], [sqk])
            self.pe(lambda e, c=c, sq=sq: e.matmul(psum[:, 0:width], self.ones[:, :], sq[:, 0:width],
                                                  start=(c == 0), stop=(c == nchunks - 1)),
                    [sqk, self.ones_k], [psk])
        rs, rsk = tmp["rstd"]
        self.act(lambda e: e.activation(out=rs[:, 0:width], in_=psum[:, 0:width], func=AF.Sqrt,
                                        scale=1.0 / n_feat, bias=EPS), [psk], [rsk])
        self.dve(lambda e: e.reciprocal(rs[:, 0:width], rs[:, 0:width]), [rsk], [rsk])
        for c in range(nchunks):
            g = self.cfcol(gname, gidx0 + c)
            if out_fn is not None:
                o_ap, o_k = out_fn(c)
            else:
                o_ap, o_k = out[:, c, 0:width], out_k
            self.dve(lambda e, c=c, g=g, o_ap=o_ap: e.scalar_tensor_tensor(out=o_ap, in0=xt[:, c, 0:width],
                                                                          scalar=g, in1=rs[:, 0:width],
                                                                          op0=ALU.mult, op1=ALU.mult),
                     [xt_k, rsk, self.cf_k], [o_k])
            if post is not None:
                post(c, o_ap, o_k)

    def lin_fm(self, wv, wkey, kc_n, nchunk, rhs_fn, rhs_keys, width, consumer, m0=0, mw=128):
        for m in range(nchunk):
            psum, psk = self.psrot.next()
            for kc in range(kc_n):
                self.pe(lambda e, m=m, kc=kc, psum=psum: e.matmul(psum[0:mw, 0:width], wv[:, kc, m * mw:(m + 1) * mw],
                                                                rhs_fn(kc), start=(kc == 0), stop=(kc == kc_n - 1)),
                        [wkey] + rhs_keys, [psk])
            consumer(m0 + m, psum, psk)

    def lin_tm(self, wv_cols_fn, wkey, kc_n, ncols, lhs_fn, lhs_keys, nsub, consumer):
        for s in range(nsub):
            psum, psk = self.psrot.next()
            for kc in range(kc_n):
                self.pe(lambda e, s=s, kc=kc, psum=psum: e.matmul(psum[:, 0:ncols], lhs_fn(kc, s), wv_cols_fn(kc),
                                                                start=(kc == 0), stop=(kc == kc_n - 1)),
                        [wkey] + lhs_keys, [psk])
            consumer(s, psum, psk)

    def alloc_common(self, wsize=8192, nw=3):
        ar = self.ar
        self.wbufs = Rot([ar.alloc([wsize], BF16, "wbuf%d" % i) for i in range(nw)])
        self.stg16 = Rot([ar.alloc([512], BF16, "stg16_%d" % i) for i in range(4)])
        self.sqr = Rot([ar.alloc([512], BF16, "sq%d" % i) for i in range(2)])
        self.rstd = ar.alloc([512], F32, "rstd")
        self.ntmp = dict(sq=self.sqr, rstd=self.rstd)

    def wload(self, name, kc_n, c0, ncols):
        ap, key = self.wbufs.next()
        dst = ap[:, 0:kc_n * ncols].rearrange("p (k n) -> p k n", n=ncols)
        src = self.wb[name][:, c0:c0 + ncols].rearrange("(k p) n -> p k n", p=128)
        self.dma(dst, src, self.wkeys[name], [key])
        return dst, key

    def run_jobs(self, jobs, depth=2):
        loaded = {}
        for i in range(min(depth, len(jobs))):
            loaded[i] = self.wload(*jobs[i][0:4])
        for i, job in enumerate(jobs):
            if i + depth < len(jobs):
                loaded[i + depth] = self.wload(*jobs[i + depth][0:4])
            wv, wkey = loaded.pop(i)
            job[4](wv, wkey)

    def load_x_tile(self, src, srcname, t, xt, xt_k):
        c0 = t * TT
        self.dma(xt[:, :, 0:TT], src[:, :, c0:c0 + TT].rearrange("k p t -> p k t"), [(srcname, t)], [xt_k])

    def store_stage(self, psum, psk, dst, dst_key, width=TT, npart=128):
        st, stk = self.stg16.next()
        self.evac(st[0:npart, 0:width], psum[0:npart, 0:width], [psk], [stk])
        self.dma(dst, st[0:npart, 0:width], [stk], [dst_key])

    def a_phase1(self, li):
        wn = "l%d_a_w_qkv" % li
        xsrc, xname = (self.x0v, "x0") if li == 0 else (self.XS, "XS")
        self.phase_begin()
        ar = self.ar
        self.alloc_common()
        xts = [ar.alloc([KC, TT], F32, "xt%d" % i) for i in range(2)]
        hts = [ar.alloc([KC, TT], BF16, "ht%d" % i) for i in range(2)]
        jobs = []
        for t in range(NT):
            c0 = t * TT
            xt, xt_k = xts[t % 2]
            ht, ht_k = hts[t % 2]
            for blk in range(6):
                def fn(wv, wkey, t=t, c0=c0, blk=blk, xt=xt, xt_k=xt_k, ht=ht, ht_k=ht_k):
                    if blk == 0:
                        if t == 0:
                            self.load_x_tile(xsrc, xname, 0, xt, xt_k)
                        if t + 1 < NT:
                            self.load_x_tile(xsrc, xname, t + 1, *xts[(t + 1) % 2])
                        self.rmsnorm(xt, xt_k, KC, TT, "gvec", (2 * li) * 16, ht, ht_k, self.ntmp)
                    rhs = lambda kc: ht[:, kc, :]
                    if blk < 4:
                        def cons(m, psum, psk):
                            self.store_stage(psum, psk, self.QS[m, :, c0:c0 + TT], ("QS", m, t))
                        self.lin_fm(wv, wkey, KC, 4, rhs, [ht_k], TT, cons, m0=blk * 4)
                    elif blk == 4:
                        def cons(m, psum, psk):
                            self.store_stage(psum, psk, self.KSa[m * 128:(m + 1) * 128, c0:c0 + TT], ("KS", m, t))
                        self.lin_fm(wv, wkey, KC, 4, rhs, [ht_k], TT, cons)
                    else:
                        def cons(s, psum, psk):
                            self.store_stage(psum, psk, self.VSa[c0 + s * 128:c0 + (s + 1) * 128, :], ("VS", s, t))
                        self.lin_tm(lambda kc: wv[:, kc, 0:512], wkey, KC, 512,
                                    lambda kc, s: ht[:, kc, s * 128:(s + 1) * 128], [ht_k], 4, cons)
                jobs.append((wn, KC, blk * 512, 512, fn))
        self.run_jobs(jobs)
        ksend = [("KS", m, t) for m in range(4) for t in range(NT)]
        vsend = [("VS", s, t) for s in range(4) for t in range(NT)]
        K8v, kk, (kn, kr) = self.allgather(self.KSa, 512, LT, "Ka", ksend, "K8")
        V8v, vk, (vn, vr) = self.allgather(self.VSa, LT, 512, "Va", vsend, "V8")
        for rel in (-1, 1):
            self.localize(self.NKa[rel].rearrange("(n i) c -> n i c", i=kr), K8v, rel, kk, [("NKa", rel)])
            self.localize(self.NVa[rel].rearrange("(n i) c -> n i c", i=vr), V8v, rel, vk, [("NVa", rel)])


    def run_attn(self, groups):
        flat = [(gi, bi, b) for gi, (pre, blocks) in enumerate(groups) for bi, b in enumerate(blocks)]
        if not flat:
            return
        groups[0][0]()
        called = {0}
        LA = self.attn_lookahead
        for k in range(min(LA, len(flat))):
            gk = flat[k][0]
            if gk not in called:
                groups[gk][0]()
                called.add(gk)
            flat[k][2][0]()
        for i, (gi, bi, b) in enumerate(flat):
            if bi == 0 and gi + 1 < len(groups) and (gi + 1) not in called:
                groups[gi + 1][0]()
                called.add(gi + 1)
            b[1]()
            if i + LA < len(flat):
                gk = flat[i + LA][0]
                if gk not in called:
                    groups[gk][0]()
                    called.add(gk)
                flat[i + LA][2][0]()
            b[2]()
            if b[3] is not None:
                b[3]()

    def a_phase3(self, li):
        self.phase_begin()
        if li + 1 < self.n_layers:
            self.convert_layer(li + 1)
        ar = self.ar
        sl = slopes16()
        scale = HD ** -0.5
        sink_base = (0 if li == 0 else 1) * 16
        KTs = Rot([ar.alloc([768], BF16, "KTw%d" % i) for i in range(2)])
        Vws = Rot([ar.alloc([6, 128], BF16, "Vw%d" % i) for i in range(2)])
        QTs = Rot([ar.alloc([512], BF16, "QT%d" % i) for i in range(8)])
        tmps = Rot([ar.alloc([384], F32, "tmp%d" % i) for i in range(4)])
        PTs = Rot([ar.alloc([384], BF16, "PT%d" % i) for i in range(4)])
        recs = Rot([ar.alloc([512], F32, "rec%d" % i) for i in range(2)])
        oats = Rot([ar.alloc([512], BF16, "oat%d" % i) for i in range(3)])
        RA0 = CF["RA"]
        hh = 0
        sidx = 0
        groups = []
        for t in range(NT):
            c0 = t * TT
            tt_ = tile_type(t)
            pieces = window_pieces(t, 128)
            for kvh in range(4):
                KT, KT_k = KTs.next()
                Vw, Vw_k = Vws.next()
                qts = [QTs.next() for _ in range(4)]

                def pre(t=t, c0=c0, kvh=kvh, KT=KT, KT_k=KT_k, Vw=Vw, Vw_k=Vw_k, qts=qts, pieces=pieces):
                    w0 = 0
                    for (rel, lc, ln) in pieces:
                        ksrc = self.KSa if rel == 0 else self.NKa[rel]
                        vsrc = self.VSa if rel == 0 else self.NVa[rel]
                        self.dma(KT[:, w0:w0 + ln], ksrc[kvh * 128:(kvh + 1) * 128, lc:lc + ln], [], [KT_k])
                        b0 = w0 // 128
                        nb = ln // 128
                        self.dma(Vw[:, b0:b0 + nb, :],
                                 vsrc[lc:lc + ln, kvh * 128:(kvh + 1) * 128].rearrange("(b p) d -> p b d", p=128), [], [Vw_k])
                        w0 += ln
                    assert w0 == 768
                    for g4 in range(4):
                        h = kvh * 4 + g4
                        self.dma(qts[g4][0], self.QS[h, :, c0:c0 + TT], [], [qts[g4][1]])
                blocks = []
                for g4 in range(4):
                    h = kvh * 4 + g4
                    QT, QT_k = qts[g4]
                    num, num_k = self.ps[3 + hh % 2]
                    den, den_k = self.ps[5 + hh % 2]
                    hh += 1
                    for j in range(6):
                        q_lo = max(0, 128 * j - 256)
                        q_hi = min(512, 128 * j + 128)
                        n = q_hi - q_lo
                        cb = q_lo - (128 * j - 256)
                        S, S_k = self.ps[(0, 1, 2, 7)[sidx % 4]]
                        sidx += 1
                        tmp, tmp_k = tmps.next()
                        PT, PT_k = PTs.next()
                        coef = -sl[h] / scale
                        ecol = self.cfcol("EA", tt_ * 6 + j)

                        def fS(S=S, S_k=S_k, KT=KT, KT_k=KT_k, QT=QT, QT_k=QT_k, j=j, q_lo=q_lo, q_hi=q_hi, n=n):
                            self.pe(lambda e: e.matmul(S[:, 0:n], KT[:, 128 * j:128 * j + 128], QT[:, q_lo:q_hi], start=True, stop=True),
                                    [KT_k, QT_k], [S_k])

                        def fsoft(S=S, S_k=S_k, tmp=tmp, tmp_k=tmp_k, PT=PT, PT_k=PT_k, cb=cb, n=n, coef=coef, ecol=ecol):
                            self.dve(lambda e: e.scalar_tensor_tensor(out=tmp[:, 0:n], in0=self.cf[:, RA0 + cb:RA0 + cb + n], scalar=coef,
                                                                      in1=S[:, 0:n], op0=ALU.mult, op1=ALU.add),
                                     [S_k, self.cf_k], [tmp_k])
                            self.act(lambda e: e.activation(out=PT[:, 0:n], in_=tmp[:, 0:n], func=AF.Exp, bias=ecol, scale=scale),
                                     [tmp_k, self.cf_k], [PT_k])

                        def fPV(num=num, num_k=num_k, den=den, den_k=den_k, Vw=Vw, Vw_k=Vw_k, PT=PT, PT_k=PT_k, j=j, q_lo=q_lo, q_hi=q_hi, n=n):
                            self.pe(lambda e: e.matmul(num[:, q_lo:q_hi], Vw[:, j, :], PT[:, 0:n], start=(j == 0), stop=(j == 5),
                                                       skip_group_check=True), [Vw_k, PT_k], [num_k])
                            self.pe(lambda e: e.matmul(den[:, q_lo:q_hi], self.ones[:, :], PT[:, 0:n], start=(j == 0), stop=(j == 5),
                                                       skip_group_check=True), [self.ones_k, PT_k], [den_k])
                        post = None
                        if j == 5:
                            def post(num=num, num_k=num_k, den=den, den_k=den_k, h=h, t=t, c0=c0):
                                rec, rec_k = recs.next()
                                oat, oat_k = oats.next()
                                sk = sink_base + h
                                self.dve(lambda e: e.tensor_scalar(out=rec, in0=den, scalar1=self.esink[:, sk:sk + 1], scalar2=None, op0=ALU.add),
                                         [den_k, self.esink_k], [rec_k])
                                self.dve(lambda e: e.reciprocal(rec, rec), [rec_k], [rec_k])
                                self.dve(lambda e: e.tensor_tensor(out=oat, in0=num, in1=rec, op=ALU.mult), [num_k, rec_k], [oat_k])
                                self.dma(self.ATT[h, :, c0:c0 + TT], oat, [oat_k], [("ATT", h, t)])
                        blocks.append((fS, fsoft, fPV, post))
                groups.append((pre, blocks))
        self.run_attn(groups)

    def oproj_phase(self, li, wn):
        xsrc, xname = (self.x0v, "x0") if li == 0 else (self.XS, "XS")
        self.phase_begin()
        ar = self.ar
        self.alloc_common()
        xts = [ar.alloc([KC, TT], F32, "xt%d" % i) for i in range(2)]
        ats = [ar.alloc([KC, TT], BF16, "at%d" % i) for i in range(2)]
        h2s = [ar.alloc([KC, TT], BF16, "h2_0")] * 2
        jobs = []

        def load_tile(t):
            c0 = t * TT
            self.load_x_tile(xsrc, xname, t, *xts[t % 2])
            at, at_k = ats[t % 2]
            self.dma(at, self.ATT[:, :, c0:c0 + TT].rearrange("h p t -> p h t"), [("ATT", h, t) for h in range(NH)], [at_k])

        for t in range(NT):
            c0 = t * TT
            xt, xt_k = xts[t % 2]
            at, at_k = ats[t % 2]
            h2, h2_k = h2s[t % 2]
            for blk in range(4):
                def fn(wv, wkey, t=t, c0=c0, blk=blk, xt=xt, xt_k=xt_k, at=at, at_k=at_k, h2=h2, h2_k=h2_k):
                    if blk == 0:
                        if t == 0:
                            load_tile(0)
                        if t + 1 < NT:
                            load_tile(t + 1)

                    def cons(m, psum, psk):
                        self.dve(lambda e: e.tensor_tensor(out=xt[:, m, :], in0=xt[:, m, :], in1=psum[:, 0:TT], op=ALU.add),
                                 [psk, xt_k], [xt_k])
                    self.lin_fm(wv, wkey, KC, 4, lambda kc: at[:, kc, :], [at_k], TT, cons, m0=blk * 4)
                    if blk == 3:
                        self.dma(self.XM[:, :, c0:c0 + TT].rearrange("k p t -> p k t"), xt, [xt_k], [("XM", t)])
                        self.rmsnorm(xt, xt_k, KC, TT, "gvec", (2 * li + 1) * 16, h2, h2_k, self.ntmp)
                        self.dma(self.H2[:, :, c0:c0 + TT].rearrange("k p t -> p k t"), h2, [h2_k], [("H2", t)])
                        bl = []
                        if t == 0:
                            bl = [(0, 0, 0)]
                        elif t == 1:
                            bl = [(1, 0, TT - 1)]
                        else:
                            bl = [(0, t - 1, 0), (1, t - 1, TT - 1)]
                        for (side, seg, col) in bl:
                            self.dve(lambda e, side=side, seg=seg, col=col: e.tensor_copy(self.hb[:, side, :, seg], h2[:, :, col]),
                                     [h2_k], [self.hb_k], force=True)
                jobs.append((wn, KC, blk * 512, 512, fn))
        self.run_jobs(jobs)
        self.dma(self.HBs, self.hb.rearrange("p a b c -> p (a b c)"), [self.hb_k], ["HBs"])
        HBv, hk, (hn, hr) = self.allgather(self.HBs, 128, 160, "HB", ["HBs"], "HB8")
        tl, tl_k = ar.alloc([80], BF16, "tl")
        tr, tr_k = ar.alloc([80], BF16, "tr")
        self.localize(self.NHB[-1].rearrange("(n i) c -> n i c", i=hr), HBv, -1, hk, [("NHB", -1)])
        self.localize(self.NHB[1].rearrange("(n i) c -> n i c", i=hr), HBv, 1, hk, [("NHB", 1)])
        self.dma(tl, self.NHB[-1][:, 80:160], [("NHB", -1)], [tl_k])
        self.dma(tr, self.NHB[1][:, 0:80], [("NHB", 1)], [tr_k])
        fl = CF["fl"]
        self.dve(lambda e: e.tensor_scalar(out=self.haloL.rearrange("p a b -> p (a b)"), in0=tl,
                                           scalar1=self.cf[:, fl:fl + 1], scalar2=None, op0=ALU.mult),
                 [tl_k, self.cf_k], [self.haloL_k])
        self.dve(lambda e: e.tensor_scalar(out=self.haloR.rearrange("p a b -> p (a b)"), in0=tr,
                                           scalar1=self.cf[:, fl + 1:fl + 2], scalar2=None, op0=ALU.mult),
                 [tr_k, self.cf_k], [self.haloR_k])

    def ffn_phase(self, li, last):
        self.phase_begin()
        ar = self.ar
        self.alloc_common(wsize=5632, nw=3)
        win = "l%d_ffn_w_in" % li
        wout = "l%d_ffn_w_out" % li
        g, _ = ar.alloc([FC, TT], BF16, "g")
        h2es = [ar.alloc([KC, TT + 2], BF16, "h2e%d" % i) for i in range(2)]
        xt, xt_k = ar.alloc([KC, TT], F32, "xt")
        xmcs = Rot([ar.alloc([512], F32, "xmc%d" % i) for i in range(2)])
        aexts = Rot([ar.alloc([TT + 2], F32, "aext%d" % i) for i in range(2)])
        cbs = Rot([ar.alloc([512], F32, "cb%d" % i) for i in range(2)])
        gls = Rot([ar.alloc([512], F32, "gl%d" % i) for i in range(4)])
        cw0 = CF["convw"] + li * FC * 3
        cb0 = CF["convb"] + li * FC

        def load_h2e(t):
            c0 = t * TT
            h2e, k = h2es[t % 2]
            if t == 0:
                self.dma(h2e[:, :, 1:TT + 2], self.H2[:, :, c0:c0 + TT + 1].rearrange("k p t -> p k t"),
                         [("H2", 0), ("H2", 1)], [k])
                self.dve(lambda e: e.tensor_copy(h2e[:, :, 0], self.haloL[:, :, 0]), [self.haloL_k], [k])
            elif t == 1:
                self.dma(h2e[:, :, 0:TT + 1], self.H2[:, :, c0 - 1:c0 + TT].rearrange("k p t -> p k t"),
                         [("H2", 0), ("H2", 1)], [k])
                self.dve(lambda e: e.tensor_copy(h2e[:, :, TT + 1], self.haloR[:, :, 0]), [self.haloR_k], [k])
            else:
                self.dma(h2e[:, :, 1:TT + 1], self.H2[:, :, c0:c0 + TT].rearrange("k p t -> p k t"), [("H2", t)], [k])
                self.dve(lambda e: e.tensor_copy(h2e[:, :, 0], self.haloL[:, :, t - 1]), [self.haloL_k], [k])
                self.dve(lambda e: e.tensor_copy(h2e[:, :, TT + 1], self.haloR[:, :, t - 1]), [self.haloR_k], [k])

        jobs = []
        for t in range(NT):
            c0 = t * TT
            h2e, h2e_k = h2es[t % 2]
            glbuf = {}
            for jb in range(FC // 2):
                def gate(wv, wkey, t=t, jb=jb, h2e=h2e, h2e_k=h2e_k, glbuf=glbuf):
                    if jb == 0:
                        if t == 0:
                            load_h2e(0)
                        if t + 1 < NT:
                            load_h2e(t + 1)
                    for jj in range(2):
                        j = jb * 2 + jj
                        a_ps, a_k = self.psrot.next()
                        ah_ps, ah_k = self.psrot.next()
                        for kc in range(KC):
                            self.pe(lambda e, kc=kc, jj=jj, a_ps=a_ps: e.matmul(a_ps[:, 0:TT], wv[:, kc, jj * 128:(jj + 1) * 128],
                                                                              h2e[:, kc, 1:TT + 1], start=(kc == 0), stop=(kc == KC - 1)),
                                    [wkey, h2e_k], [a_k])
                        for kc in range(KC):
                            self.pe(lambda e, kc=kc, jj=jj, ah_ps=ah_ps: e.matmul(ah_ps[:, 0:2], wv[:, kc, jj * 128:(jj + 1) * 128],
                                                                                h2e[:, kc, 0:TT + 2:TT + 1], start=(kc == 0), stop=(kc == KC - 1)),
                                    [wkey, h2e_k], [ah_k])
                        aext, ax_k = aexts.next()
                        cb, cb_k = cbs.next()
                        gl, gl_k = gls.next()
                        glbuf[j] = (gl, gl_k)
                        self.act(lambda e, aext=aext, a_ps=a_ps: e.activation(out=aext[:, 1:TT + 1], in_=a_ps[:, 0:TT], func=AF.Copy),
                                 [a_k], [ax_k])
                        self.act(lambda e, aext=aext, ah_ps=ah_ps: e.activation(out=aext[:, 0:TT + 2:TT + 1], in_=ah_ps[:, 0:2], func=AF.Copy),
                                 [ah_k], [ax_k])
                        w0c = self.cf[:, cw0 + j * 3 + 0:cw0 + j * 3 + 1]
                        w1c = self.cf[:, cw0 + j * 3 + 1:cw0 + j * 3 + 2]
                        w2c = self.cf[:, cw0 + j * 3 + 2:cw0 + j * 3 + 3]
                        bc = self.cf[:, cb0 + j:cb0 + j + 1]
                        self.act(lambda e, cb=cb, a_ps=a_ps, w1c=w1c, bc=bc: e.activation(out=cb, in_=a_ps[:, 0:TT], func=AF.Identity,
                                                                                         bias=bc, scale=w1c),
                                 [a_k, self.cf_k], [cb_k])
                        self.dve(lambda e, cb=cb, aext=aext, w0c=w0c: e.scalar_tensor_tensor(out=cb, in0=aext[:, 0:TT], scalar=w0c, in1=cb,
                                                                                            op0=ALU.mult, op1=ALU.add),
                                 [ax_k, cb_k, self.cf_k], [cb_k])
                        self.dve(lambda e, cb=cb, aext=aext, w2c=w2c: e.scalar_tensor_tensor(out=cb, in0=aext[:, 2:TT + 2], scalar=w2c, in1=cb,
                                                                                            op0=ALU.mult, op1=ALU.add),
                                 [ax_k, cb_k, self.cf_k], [cb_k])
                        self.act(lambda e, gl=gl, cb=cb: e.activation(out=gl, in_=cb, func=AF.Gelu), [cb_k], [gl_k])

                def val(wv, wkey, t=t, jb=jb, h2e=h2e, h2e_k=h2e_k, glbuf=glbuf):
                    for jj in range(2):
                        j = jb * 2 + jj
                        gl, gl_k = glbuf[j]

                        def cons(m, psum, psk, j=j, gl=gl, gl_k=gl_k):
                            self.dve(lambda e: e.tensor_tensor(out=g[:, j, :], in0=gl, in1=psum[:, 0:TT], op=ALU.mult),
                                     [gl_k, psk], [("g", j)])
                        self.lin_fm(wv[:, :, jj * 128:(jj + 1) * 128], wkey, KC, 1, lambda kc: h2e[:, kc, 1:TT + 1], [h2e_k], TT, cons)
                jobs.append((win, KC, jb * 256, 256, gate))
                jobs.append((win, KC, DFF + jb * 256, 256, val))
            for m in range(KC):
                def outp(wv, wkey, t=t, c0=c0, m=m):
                    xmc, xmc_k = xmcs.next()
                    self.dma(xmc, self.XM[m, :, c0:c0 + TT], [("XM", t)], [xmc_k])

                    def cons(mm, psum, psk):
                        self.dve(lambda e: e.tensor_tensor(out=xt[:, m, :], in0=xmc, in1=psum[:, 0:TT], op=ALU.add),
                                 [psk, xmc_k], [xt_k])
                    self.lin_fm(wv, wkey, FC, 1, lambda kc: g[:, kc, :], [("g", j) for j in range(FC)], TT, cons)
                    if m == KC - 1:
                        if not last:
                            self.dma(self.XS[:, :, c0:c0 + TT].rearrange("k p t -> p k t"), xt, [xt_k], [("XS", t)])
                        else:
                            def out_fn(c):
                                return xmcs.next()

                            def post(c, o_ap, o_k):
                                self.dma(self.yT[c * 128:(c + 1) * 128, c0:c0 + TT], o_ap, [o_k], [("yT", c, t)])
                            self.rmsnorm(xt, xt_k, KC, TT, "gvec", 8 * 16, None, None, self.ntmp, out_fn=out_fn, post=post)
                jobs.append((wout, FC, m * 128, 128, outp))
        self.run_jobs(jobs)


def build_program(n_layers=4, stop_mid=False):
    B = Builder(n_layers, stop_mid)
    import os
    stage = int(os.environ.get("DBG_STAGE", "99"))
    B.convert_layer(0)
    B.setup()
    done = False
    for li in range(n_layers):
        kind = li % 3
        if kind == 0:
            if stage >= 1:
                B.a_phase1(li)
            if stage >= 2:
                B.a_phase3(li)
            if stage >= 3:
                B.oproj_phase(li, "l%d_a_w_o" % li)
        elif kind == 1:
            B.b_phase1(li)
            B.b_phase3(li)
            B.oproj_phase(li, "l%d_b_w_o" % li)
        else:
            B.c_phase1(li)
            B.c_phase3(li)
            B.oproj_phase(li, "l%d_c_w_o" % li)
        if stop_mid and li == n_layers - 1:
            B.phase_begin()
            for t in range(NT):
                c0 = t * TT
                B.dma(B.yT[:, c0:c0 + TT].rearrange("(k p) t -> k p t", p=128), B.XM[:, :, c0:c0 + TT], [("XM", t)], [("yT", t)])
            done = True
            break
        B.ffn_phase(li, last=(li == 3))
    if not done and n_layers < 4:
        B.phase_begin()
        for t in range(NT):
            c0 = t * TT
            B.dma(B.yT[:, c0:c0 + TT].rearrange("(k p) t -> k p t", p=128), B.XS[:, :, c0:c0 + TT], [("XS", t)], [("yT", t)])
    B.P.barrier()
    B.P.finalize()
    B.st.close()
    return B.nc


def _vec_cols(v, nch):
    return np.ascontiguousarray(np.asarray(v, np.float32).reshape(nch, 128).T)


def host_inputs(inputs, n_layers=4):
    f32 = np.float32
    xp = np.asarray(inputs["x_prompt"], f32)
    xs = np.asarray(inputs["x_sample"], f32)
    cf = np.zeros((128, NCF), f32)
    for i in range(4):
        cf[:, CF["gvec"] + (2 * i) * 16:CF["gvec"] + (2 * i + 1) * 16] = _vec_cols(inputs["l%d_mix_norm" % i], 16)
        cf[:, CF["gvec"] + (2 * i + 1) * 16:CF["gvec"] + (2 * i + 2) * 16] = _vec_cols(inputs["l%d_ffn_norm" % i], 16)
        cw = np.asarray(inputs["l%d_ffn_conv_w" % i], f32)
        cwl = cw.T.reshape(FC, 128, 3).transpose(1, 0, 2).reshape(128, FC * 3)
        cf[:, CF["convw"] + i * FC * 3:CF["convw"] + (i + 1) * FC * 3] = cwl
        cf[:, CF["convb"] + i * FC:CF["convb"] + (i + 1) * FC] = _vec_cols(inputs["l%d_ffn_conv_b" % i], FC)
    cf[:, CF["gvec"] + 128:CF["gvec"] + 144] = _vec_cols(inputs["final_norm"], 16)
    cf[:, CF["sink"]:CF["sink"] + 16] = np.asarray(inputs["l0_a_sink"], f32)[None, :]
    cf[:, CF["sink"] + 16:CF["sink"] + 32] = np.asarray(inputs["l3_a_sink"], f32)[None, :]
    cf[:, CF["cnorm"]:CF["cnorm"] + 4] = _vec_cols(inputs["l2_c_q_norm"], 4)
    cf[:, CF["cnorm"] + 4:CF["cnorm"] + 8] = _vec_cols(inputs["l2_c_kv_norm"], 4)
    p = np.arange(128)[:, None]
    c = np.arange(384)[None, :]
    ra = np.abs(c - 128 - p).astype(f32)
    ra[ra > 128] = BIGR
    cf[:, CF["RA"]:CF["RA"] + 384] = ra
    c = np.arange(256)[None, :]
    rb = np.abs(c - 64 - p).astype(f32)
    rb[rb > 64] = BIGR
    cf[:, CF["RB"]:CF["RB"] + 256] = rb
    inv = ROPE_THETA_ ** (-np.arange(0, 64, 2, dtype=np.float32) / 64.0)
    maps = []
    wfull = {}
    for core in range(NCORES):
        cfc = cf.copy()
        for tt_, t in ((0, 0), (1, 1), (2, 2)):
            pcs = window_pieces(t, 128)
            w0 = 0
            for (rel, lc, ln) in pcs:
                valid = 0 <= core + rel <= 7
                for b in range(w0 // 128, (w0 + ln) // 128):
                    cfc[:, CF["EA"] + tt_ * 6 + b] = 0.0 if valid else NEG
                w0 += ln
            for gi, (window, d) in enumerate(B_GROUPS):
                pcs = window_pieces(t, 64 * d)
                nj = (TT + 128 * d) // d
                colv = np.zeros(nj, f32)
                w0 = 0
                for (rel, lc, ln) in pcs:
                    valid = 0 <= core + rel <= 7
                    colv[w0 // d:(w0 + ln) // d] = 0.0 if valid else NEG
                    w0 += ln
                for b in range(5):
                    seg = colv[128 * b:128 * (b + 1)]
                    col = np.zeros(128, f32)
                    col[:len(seg)] = seg
                    cfc[:, CF["EB"] + (tt_ * 3 + gi) * 5 + b] = col
        cfc[:, CF["fl"]] = 1.0 if core > 0 else 0.0
        cfc[:, CF["fl"] + 1] = 1.0 if core < 7 else 0.0
        xl = np.concatenate([xp[0, PSEG * core:PSEG * (core + 1)]] + [xs[b, SSEG * core:SSEG * (core + 1)] for b in range(4)], axis=0)
        x0T = np.ascontiguousarray(xl.T)
        pos = np.concatenate([np.arange(PSEG * core, PSEG * (core + 1))] + [np.arange(SSEG * core, SSEG * (core + 1))] * 4).astype(np.float32)
        ang = pos[None, :] * inv[:, None]
        rope = np.stack([np.cos(ang), np.sin(ang)]).astype(f32)
        nb = np.zeros((1, 8), np.int32)
        for rel in range(-2, 3):
            nb[0, rel + 2] = min(7, max(0, core + rel))
        m = {"x0T": x0T, "cf32": cfc, "rope": rope, "nb": nb}
        for name, k, n in weight_specs():
            if int(name[1]) >= n_layers:
                continue
            w = inputs[name]
            m[name] = wfull.setdefault(name, np.ascontiguousarray(np.asarray(w, f32)))
        maps.append(m)
    return maps


ROPE_THETA_ = 10000.0
_NC_CACHE = {}


def run(inputs, n_layers=4, stop_mid=False):
    key = (n_layers, stop_mid)
    if key not in _NC_CACHE:
        _NC_CACHE[key] = build_program(n_layers, stop_mid)
    nc = _NC_CACHE[key]
    maps = host_inputs(inputs, n_layers)
    res = run_bass_kernel_spmd(nc, maps, core_ids=list(range(NCORES)))
    yp = np.zeros((1, 8192, D), np.float32)
    ys = np.zeros((4, 4096, D), np.float32)
    for core in range(NCORES):
        yT = np.asarray(res.results[core]["yT"])
        yp[0, PSEG * core:PSEG * (core + 1), :] = yT[:, 0:PSEG].T
        for b in range(4):
            ys[b, SSEG * core:SSEG * (core + 1), :] = yT[:, PSEG + SSEG * b:PSEG + SSEG * (b + 1)].T
    return yp, ys


def kernel(**inputs):
    return run(inputs, 4, False)


def _b_init(self):
    dt = self.nc.dram_tensor
    if hasattr(self, "KSb"):
        return
    self.KSb = [dt("KSb%d" % g, [2048, LT], BF16).ap() for g in range(3)]
    self.VSb = [dt("VSb%d" % g, [LT, 2048], BF16).ap() for g in range(3)]
    self.NKb = [{rel: dt("NKb%d_%d" % (g, rel + 2), [2048, LT], BF16).ap() for rel in ((-1, 1) if g < 2 else (-2, -1, 1, 2))}
                for g in range(3)]
    self.NVb = [{rel: dt("NVb%d_%d" % (g, rel + 2), [LT, 2048], BF16).ap() for rel in ((-1, 1) if g < 2 else (-2, -1, 1, 2))}
                for g in range(3)]


def _b_phase1(self, li):
    _b_init(self)
    wn = "l%d_b_w_qkv" % li
    xsrc, xname = (self.XS, "XS")
    self.phase_begin()
    ar = self.ar
    self.alloc_common()
    xts = [ar.alloc([KC, TT], F32, "xt%d" % i) for i in range(2)]
    hts = [ar.alloc([KC, TT], BF16, "ht%d" % i) for i in range(2)]
    ti = 0
    for g in (2, 1, 0):
        jobs = []
        for t in range(NT):
            c0 = t * TT
            for b12 in range(12):
                blk = g * 12 + b12
                kind = b12 // 4
                hb4 = b12 % 4
                slot = ti % 2
                xt, xt_k = xts[slot]
                ht, ht_k = hts[slot]
                nslot = (ti + 1) % 2

                def fn(wv, wkey, t=t, c0=c0, b12=b12, g=g, kind=kind, hb4=hb4, xt=xt, xt_k=xt_k, ht=ht, ht_k=ht_k, nslot=nslot, ti=ti):
                    if b12 == 0:
                        if ti == 0:
                            self.load_x_tile(xsrc, xname, t, xt, xt_k)
                        if ti + 1 < 3 * NT:
                            self.load_x_tile(xsrc, xname, (t + 1) % NT, *xts[nslot])
                        self.rmsnorm(xt, xt_k, KC, TT, "gvec", (2 * li) * 16, ht, ht_k, self.ntmp)
                    rhs = lambda kc: ht[:, kc, :]
                    if kind == 0:
                        def cons(m, psum, psk):
                            self.store_stage(psum, psk, self.QS[g * 16 + m, :, c0:c0 + TT], ("QS", g * 16 + m, t))
                        self.lin_fm(wv, wkey, KC, 4, rhs, [ht_k], TT, cons, m0=hb4 * 4)
                    elif kind == 1:
                        def cons(m, psum, psk):
                            self.store_stage(psum, psk, self.KSb[g][m * 128:(m + 1) * 128, c0:c0 + TT], ("KS", g, m, t))
                        self.lin_fm(wv, wkey, KC, 4, rhs, [ht_k], TT, cons, m0=hb4 * 4)
                    else:
                        def cons(s_, psum, psk):
                            self.store_stage(psum, psk, self.VSb[g][c0 + s_ * 128:c0 + (s_ + 1) * 128, hb4 * 512:(hb4 + 1) * 512],
                                             ("VS", g, hb4, s_, t))
                        self.lin_tm(lambda kc: wv[:, kc, 0:512], wkey, KC, 512,
                                    lambda kc, s_: ht[:, kc, s_ * 128:(s_ + 1) * 128], [ht_k], 4, cons)
                jobs.append((wn, KC, blk * 512, 512, fn))
            ti += 1
        self.run_jobs(jobs)
        ksend = [("KS", g, m, t) for m in range(16) for t in range(NT)]
        vsend = [("VS", g, hb4, s_, t) for hb4 in range(4) for s_ in range(4) for t in range(NT)]
        K8v, kk, (kn, kr) = self.allgather(self.KSb[g], 2048, LT, "Kb%d" % g, ksend, "K8b%d" % g, bg=True)
        V8v, vk, (vn, vr) = self.allgather(self.VSb[g], LT, 2048, "Vb%d" % g, vsend, "V8b%d" % g, bg=True)
        for rel in self.NKb[g].keys():
            self.localize(self.NKb[g][rel].rearrange("(n i) c -> n i c", i=kr), K8v, rel, kk, [("NKb", g, rel)], eng="pool", bg=True)
            self.localize(self.NVb[g][rel].rearrange("(n i) c -> n i c", i=vr), V8v, rel, vk, [("NVb", g, rel)], eng="pool", bg=True)


def _b_phase3(self, li):
    self.phase_begin()
    if li + 1 < self.n_layers:
        self.convert_layer(li + 1)
    ar = self.ar
    sl = slopes16()
    scale = HD ** -0.5
    KTs = Rot([ar.alloc([2560], BF16, "KTw%d" % i) for i in range(3)])
    Vws = Rot([ar.alloc([4096], BF16, "Vw%d" % i) for i in range(3)])
    QTs = Rot([ar.alloc([512], BF16, "QT%d" % i) for i in range(3)])
    tmps = Rot([ar.alloc([256], F32, "tmp%d" % i) for i in range(4)])
    PTs = Rot([ar.alloc([256], BF16, "PT%d" % i) for i in range(4)])
    NUMs = Rot([ar.alloc([512], F32, "NUM%d" % i) for i in range(3)])
    DENs = Rot([ar.alloc([512], F32, "DEN%d" % i) for i in range(3)])
    if not hasattr(self, "NUMD"):
        self.NUMD = self.nc.dram_tensor("NUMD", [NH, 128, LT], F32).ap()
        self.DEND = self.nc.dram_tensor("DEND", [NH, 128, LT], F32).ap()
    GORDER = (2, 1, 0)
    oats = Rot([ar.alloc([512], BF16, "oat%d" % i) for i in range(3)])
    RB0 = CF["RB"]
    cnt = 0
    sidx = 0
    groups = []
    for g in GORDER:
        (window, d) = B_GROUPS[g]
        for t in range(NT):
            c0 = t * TT
            tt_ = tile_type(t)
            for h in range(NH):
                NUM, NUM_k = NUMs.next()
                DEN, DEN_k = DENs.next()
                nq = TT // d
                nj = nq + 128
                nblk = (nj + 127) // 128
                W = TT + 128 * d
                KTf, KT_k = KTs.next()
                Vwf, Vw_k = Vws.next()
                KT = KTf[:, 0:W]
                Vw = Vwf[:, 0:d * nblk * 128].rearrange("p (r b x) -> p r b x", r=d, b=nblk)
                QT, QT_k = QTs.next()

                def pre(t=t, c0=c0, h=h, g=g, d=d, W=W, KT=KT, KT_k=KT_k, Vw=Vw, Vw_k=Vw_k, QT=QT, QT_k=QT_k,
                        NUM=NUM, NUM_k=NUM_k, DEN=DEN, DEN_k=DEN_k):
                    self.dma(QT, self.QS[g * 16 + h, :, c0:c0 + TT], [], [QT_k])
                    if g != GORDER[0]:
                        self.dma(NUM, self.NUMD[h, :, c0:c0 + TT], [("NUMD", h, t)], [NUM_k])
                        self.dma(DEN, self.DEND[h, :, c0:c0 + TT], [("DEND", h, t)], [DEN_k])
                    w0 = 0
                    for (rel, lc, ln) in window_pieces(t, 64 * d):
                        ksrc = self.KSb[g] if rel == 0 else self.NKb[g][rel]
                        vsrc = self.VSb[g] if rel == 0 else self.NVb[g][rel]
                        kdep = [] if rel == 0 else [("NKb", g, rel)]
                        vdep = [] if rel == 0 else [("NVb", g, rel)]
                        self.dma(KT[:, w0:w0 + ln], ksrc[h * 128:(h + 1) * 128, lc:lc + ln], kdep, [KT_k])
                        ja, je = w0 // d, (w0 + ln) // d
                        j = ja
                        while j < je:
                            b = j // 128
                            jn = min(je, (b + 1) * 128)
                            p0 = j % 128
                            n = jn - j
                            r0 = lc + (j - ja) * d
                            self.dma(Vw[p0:p0 + n, :, b, :],
                                     vsrc[r0:r0 + n * d, h * 128:(h + 1) * 128].rearrange("(jj r) x -> jj r x", r=d), vdep, [Vw_k])
                            j = jn
                        w0 += ln
                    assert w0 == W
                num, num_k = self.ps[3 + cnt % 2]
                den, den_k = self.ps[5 + cnt % 2]
                cnt += 1
                coef = -sl[h] * d / scale
                blocks = []
                for r in range(d):
                    for b in range(nblk):
                        nk = min(128, nj - 128 * b)
                        q_lo = max(0, 128 * b - 128)
                        q_hi = min(nq, 128 * b + nk)
                        n = q_hi - q_lo
                        cb = q_lo - (128 * b - 128)
                        S, S_k = self.ps[(0, 1, 2, 7)[sidx % 4]]
                        sidx += 1
                        tmp, tmp_k = tmps.next()
                        PT, PT_k = PTs.next()
                        k0 = 128 * b * d + r
                        q0 = q_lo * d + r
                        eo = CF["EB"] + (tt_ * 3 + g) * 5 + b
                        first = (r == 0 and b == 0)
                        last = (r == d - 1 and b == nblk - 1)
                        o0 = r * nq + q_lo

                        def fS(S=S, S_k=S_k, KT=KT, KT_k=KT_k, QT=QT, QT_k=QT_k, k0=k0, q0=q0, nk=nk, n=n, d=d):
                            self.pe(lambda e: e.matmul(S[0:nk, 0:n], KT[:, k0:k0 + (nk - 1) * d + 1:d], QT[:, q0:q0 + (n - 1) * d + 1:d],
                                                       start=True, stop=True), [KT_k, QT_k], [S_k])

                        def fsoft(S=S, S_k=S_k, tmp=tmp, tmp_k=tmp_k, PT=PT, PT_k=PT_k, cb=cb, n=n, nk=nk, coef=coef, eo=eo):
                            self.dve(lambda e: e.scalar_tensor_tensor(out=tmp[0:nk, 0:n], in0=self.cf[0:nk, RB0 + cb:RB0 + cb + n], scalar=coef,
                                                                      in1=S[0:nk, 0:n], op0=ALU.mult, op1=ALU.add),
                                     [S_k, self.cf_k], [tmp_k])
                            self.act(lambda e: e.activation(out=PT[0:nk, 0:n], in_=tmp[0:nk, 0:n], func=AF.Exp,
                                                            bias=self.cf[0:nk, eo:eo + 1], scale=scale),
                                     [tmp_k, self.cf_k], [PT_k])

                        def fPV(num=num, num_k=num_k, den=den, den_k=den_k, Vw=Vw, Vw_k=Vw_k, PT=PT, PT_k=PT_k, r=r, b=b, nk=nk, n=n,
                                o0=o0, first=first, last=last):
                            self.pe(lambda e: e.matmul(num[:, o0:o0 + n], Vw[0:nk, r, b, :], PT[0:nk, 0:n], start=first, stop=last,
                                                       skip_group_check=True), [Vw_k, PT_k], [num_k])
                            self.pe(lambda e: e.matmul(den[:, o0:o0 + n], self.ones[0:nk, :], PT[0:nk, 0:n], start=first, stop=last,
                                                       skip_group_check=True), [self.ones_k, PT_k], [den_k])
                        post = None
                        if last:
                            def post(g=g, d=d, h=h, t=t, c0=c0, num=num, num_k=num_k, den=den, den_k=den_k,
                                     NUM=NUM, NUM_k=NUM_k, DEN=DEN, DEN_k=DEN_k):
                                for (ACC, ACC_k, src, src_k) in ((NUM, NUM_k, num, num_k), (DEN, DEN_k, den, den_k)):
                                    if d == 1:
                                        accv, srcv = ACC, src[:, 0:TT]
                                    else:
                                        accv = ACC.rearrange("p (q r) -> p r q", r=d)
                                        srcv = src[:, 0:TT].rearrange("p (r q) -> p r q", r=d)
                                    if g == GORDER[0]:
                                        self.dve(lambda e, accv=accv, srcv=srcv: e.tensor_copy(accv, srcv), [src_k], [ACC_k])
                                    else:
                                        self.dve(lambda e, accv=accv, srcv=srcv: e.tensor_tensor(out=accv, in0=accv, in1=srcv, op=ALU.add),
                                                 [src_k, ACC_k], [ACC_k])
                                if g == GORDER[-1]:
                                    oat, oat_k = oats.next()
                                    self.dve(lambda e: e.reciprocal(DEN, DEN), [DEN_k], [DEN_k])
                                    self.dve(lambda e: e.tensor_tensor(out=oat, in0=NUM, in1=DEN, op=ALU.mult), [NUM_k, DEN_k], [oat_k])
                                    self.dma(self.ATT[h, :, c0:c0 + TT], oat, [oat_k], [("ATT", h, t)])
                                else:
                                    self.dma(self.NUMD[h, :, c0:c0 + TT], NUM, [NUM_k], [("NUMD", h, t)])
                                    self.dma(self.DEND[h, :, c0:c0 + TT], DEN, [DEN_k], [("DEND", h, t)])
                        blocks.append((fS, fsoft, fPV, post))
                groups.append((pre, blocks))
    self.run_attn(groups)


Builder.b_phase1 = _b_phase1
Builder.b_phase3 = _b_phase3


C_SCALE = (128 + 64) ** -0.5


def _c_init(self):
    dt = self.nc.dram_tensor
    if hasattr(self, "KSc"):
        return
    self.KSc = dt("KSc", [2112, LT], BF16).ap()
    self.VSc = dt("VSc", [LT, 2048], BF16).ap()


def _rope(self, x1, x1_k, x2, x2_k, cs, sn, csn_k, o1, o2, o_k, tmps):
    (ta, ta_k), (tb, tb_k) = tmps.next(), tmps.next()
    P32 = slice(0, 32)
    self.dve(lambda e: e.tensor_tensor(out=ta[P32, :], in0=x1[P32, 0:TT], in1=cs[P32, :], op=ALU.mult), [x1_k, csn_k], [ta_k])
    self.dve(lambda e: e.tensor_tensor(out=tb[P32, :], in0=x2[P32, 0:TT], in1=sn[P32, :], op=ALU.mult), [x2_k, csn_k], [tb_k])
    self.dve(lambda e: e.tensor_tensor(out=o1[P32, :], in0=ta[P32, :], in1=tb[P32, :], op=ALU.subtract), [ta_k, tb_k], [o_k])
    (tc, tc_k), (td, td_k) = tmps.next(), tmps.next()
    self.dve(lambda e: e.tensor_tensor(out=tc[P32, :], in0=x2[P32, 0:TT], in1=cs[P32, :], op=ALU.mult), [x2_k, csn_k], [tc_k])
    self.dve(lambda e: e.tensor_tensor(out=td[P32, :], in0=x1[P32, 0:TT], in1=sn[P32, :], op=ALU.mult), [x1_k, csn_k], [td_k])
    self.dve(lambda e: e.tensor_tensor(out=o2[P32, :], in0=tc[P32, :], in1=td[P32, :], op=ALU.add), [tc_k, td_k], [o_k])


def _c_phase1(self, li):
    _c_init(self)
    wd, wuq, wukv = "l%d_c_w_down" % li, "l%d_c_w_uq" % li, "l%d_c_w_ukv" % li
    self.phase_begin()
    ar = self.ar
    self.alloc_common()
    xt, xt_k = ar.alloc([KC, TT], F32, "xt")
    hts = [ar.alloc([KC, TT], BF16, "ht%d" % i) for i in range(2)]
    c32, c32_k = ar.alloc([4, TT], F32, "c32")
    cqn, cqn_k = ar.alloc([4, TT], BF16, "cqn")
    ckvn, ckvn_k = ar.alloc([4, TT], BF16, "ckvn")
    cs, csn_k = ar.alloc([TT], F32, "cos")
    sn, _ = ar.alloc([TT], F32, "sin")
    rtmps = Rot([ar.alloc([TT], F32, "rt%d" % i) for i in range(4)])
    ropo = Rot([(ar.alloc([TT], BF16, "ro1_%d" % i), ar.alloc([TT], BF16, "ro2_%d" % i)) for i in range(2)])
    jobs = []
    for t in range(NT):
        c0 = t * TT
        ht, ht_k = hts[t % 2]

        def j_down(wv, wkey, which, t=t, c0=c0, ht=ht, ht_k=ht_k):
            if which == 0:
                self.load_x_tile(self.XS, "XS", t, xt, xt_k)
                self.dma(cs[0:32, :], self.rope_d[0, :, c0:c0 + TT], [], [csn_k])
                self.dma(sn[0:32, :], self.rope_d[1, :, c0:c0 + TT], [], [csn_k])
                self.rmsnorm(xt, xt_k, KC, TT, "gvec", (2 * li) * 16, ht, ht_k, self.ntmp)
            rhs = lambda kc: ht[:, kc, :]
            if which < 2:
                def cons(m, psum, psk):
                    self.evac(c32[:, m, :], psum[:, 0:TT], [psk], [c32_k])
                self.lin_fm(wv, wkey, KC, 4, rhs, [ht_k], TT, cons)
                dst, dst_k = (cqn, cqn_k) if which == 0 else (ckvn, ckvn_k)
                self.rmsnorm(c32, c32_k, 4, TT, "cnorm", which * 4, dst, dst_k, self.ntmp)
            else:
                got = {}

                def cons(m, psum, psk):
                    got[m] = (psum, psk)
                self.lin_fm(wv, wkey, KC, 2, rhs, [ht_k], TT, cons, mw=32)
                (o1, o1_k), (o2, o2_k) = ropo.next()
                _rope(self, got[0][0], got[0][1], got[1][0], got[1][1], cs, sn, csn_k, o1, o2, o1_k, rtmps)
                self.dma(self.KSc[2048:2080, c0:c0 + TT], o1[0:32, :], [o1_k], [("KSr", 0, t)])
                self.dma(self.KSc[2080:2112, c0:c0 + TT], o2[0:32, :], [o1_k], [("KSr", 1, t)])
        jobs.append((wd, KC, 0, 512, lambda wv, wkey, f=j_down: f(wv, wkey, 0)))
        jobs.append((wd, KC, 512, 512, lambda wv, wkey, f=j_down: f(wv, wkey, 1)))
        jobs.append((wd, KC, 1024, 64, lambda wv, wkey, f=j_down: f(wv, wkey, 2)))
        for hf in range(2):
            def j_uq(wv, wkey, hf=hf, t=t, c0=c0):
                for hl in range(8):
                    h = hf * 8 + hl
                    base = hl * 192
                    psum, psk = self.psrot.next()
                    p1, p1k = self.psrot.next()
                    p2, p2k = self.psrot.next()
                    for (pp, ppk, off, mw) in ((psum, psk, base, 128), (p1, p1k, base + 128, 32), (p2, p2k, base + 160, 32)):
                        for kc in range(4):
                            self.pe(lambda e, pp=pp, off=off, mw=mw, kc=kc: e.matmul(pp[0:mw, 0:TT], wv[:, kc, off:off + mw], cqn[:, kc, :],
                                                                                 start=(kc == 0), stop=(kc == 3)),
                                    [wkey, cqn_k], [ppk])
                    self.store_stage(psum, psk, self.QS[h, :, c0:c0 + TT], ("QS", h, t))
                    (o1, o1_k), (o2, o2_k) = ropo.next()
                    _rope(self, p1, p1k, p2, p2k, cs, sn, csn_k, o1, o2, o1_k, rtmps)
                    self.dma(self.QR[h, 0:32, c0:c0 + TT], o1[0:32, :], [o1_k], [("QR", h, 0, t)])
                    self.dma(self.QR[h, 32:64, c0:c0 + TT], o2[0:32, :], [o1_k], [("QR", h, 1, t)])
            jobs.append((wuq, 4, hf * 1536, 1536, j_uq))
        for hf in range(2):
            def j_ukv(wv, wkey, hf=hf, t=t, c0=c0):
                for hl in range(8):
                    h = hf * 8 + hl
                    psum, psk = self.psrot.next()
                    for kc in range(4):
                        self.pe(lambda e, psum=psum, hl=hl, kc=kc: e.matmul(psum[:, 0:TT], wv[:, kc, hl * 256:hl * 256 + 128], ckvn[:, kc, :],
                                                                          start=(kc == 0), stop=(kc == 3)),
                                [wkey, ckvn_k], [psk])
                    self.store_stage(psum, psk, self.KSc[h * 128:(h + 1) * 128, c0:c0 + TT], ("KS", h, t))
                wvv = wv.rearrange("p k (h x) -> p k h x", x=256)
                for q4 in range(2):
                    h0 = hf * 8 + q4 * 4
                    for s in range(4):
                        psum, psk = self.psrot.next()
                        for kc in range(4):
                            self.pe(lambda e, psum=psum, q4=q4, s=s, kc=kc: e.matmul(psum[:, 0:512], ckvn[:, kc, s * 128:(s + 1) * 128],
                                                                                  wvv[:, kc, q4 * 4:q4 * 4 + 4, 128:256],
                                                                                  start=(kc == 0), stop=(kc == 3)),
                                    [wkey, ckvn_k], [psk])
                        self.store_stage(psum, psk, self.VSc[c0 + s * 128:c0 + (s + 1) * 128, h0 * 128:(h0 + 4) * 128], ("VS", h0, s, t))
            jobs.append((wukv, 4, hf * 2048, 2048, j_ukv))
    self.run_jobs(jobs)
    ksend = [("KS", h, t) for h in range(NH) for t in range(NT)] + [("KSr", i, t) for i in range(2) for t in range(NT)]
    vsend = [("VS", h0, s, t) for h0 in range(0, 16, 4) for s in range(4) for t in range(NT)]
    self.cK8v, self.cKk, (kn, kr) = self.allgather(self.KSc, 2112, LT, "Kc", ksend, "K8c", rpc_force=64)
    self.cV8v, self.cVk, (vn, vr) = self.allgather(self.VSc, LT, 2048, "Vc", vsend, "V8c")
    assert kr == 64 and vr == 128


def _c_phase3(self, li):
    self.phase_begin()
    if li + 1 < self.n_layers:
        self.convert_layer(li + 1)
    ar = self.ar
    K8v, V8v = self.cK8v, self.cV8v
    KTs = Rot([ar.alloc([8192], BF16, "cKT%d" % i) for i in range(2)])
    Vps = Rot([ar.alloc([64, 128], BF16, "cVp%d" % i) for i in range(2)])
    KRs = Rot([ar.alloc([8192], BF16, "cKR%d" % i) for i in range(2)])
    QNs = Rot([ar.alloc([512], BF16, "cQN%d" % i) for i in range(3)])
    QRs = Rot([ar.alloc([512], BF16, "cQR%d" % i) for i in range(3)])
    PTs = Rot([ar.alloc([512], BF16, "cPT%d" % i) for i in range(4)])
    recs = Rot([ar.alloc([512], F32, "crec%d" % i) for i in range(2)])
    oats = Rot([ar.alloc([512], BF16, "coat%d" % i) for i in range(3)])
    seqs = [(PSEG, 0, [0, 1])] + [(SSEG, PSEG + SSEG * b, [2 + b]) for b in range(4)]
    cnt = 0
    sidx = 0
    groups = []
    for (seg, lc0, tiles) in seqs:
        L = seg * 8
        nkb = L // 128
        KR, KR_k = KRs.next()
        for h in range(NH):
            KT, KT_k = KTs.next()
            Vp, Vp_k = Vps.next()

            def pre(seg=seg, lc0=lc0, h=h, KR=KR, KR_k=KR_k, KT=KT, KT_k=KT_k, Vp=Vp, Vp_k=Vp_k):
                if h == 0:
                    for r in range(8):
                        self.dma(KR[0:64, r * seg:(r + 1) * seg], K8v[r, 32, :, lc0:lc0 + seg], [], [KR_k])
                nb = seg // 128
                for r in range(8):
                    for n2 in range(2):
                        self.dma(KT[64 * n2:64 * n2 + 64, r * seg:(r + 1) * seg], K8v[r, 2 * h + n2, :, lc0:lc0 + seg], [], [KT_k])
                    self.dma(Vp[:, r * nb:(r + 1) * nb, :],
                             V8v[r, lc0 // 128:lc0 // 128 + nb, :, h * 128:(h + 1) * 128].rearrange("n p d -> p n d"), [], [Vp_k])
            blocks = []
            for t in tiles:
                c0 = t * TT
                QN, QN_k = QNs.next()
                QRt, QR_k = QRs.next()
                num, num_k = self.ps[4 + cnt % 2]
                den, den_k = self.ps[6 + cnt % 2]
                cnt += 1
                for kb in range(nkb):
                    S, S_k = self.ps[sidx % 4]
                    sidx += 1
                    PT, PT_k = PTs.next()

                    def fS(S=S, S_k=S_k, KT=KT, KT_k=KT_k, KR=KR, KR_k=KR_k, QN=QN, QN_k=QN_k, QRt=QRt, QR_k=QR_k, kb=kb, h=h, c0=c0):
                        if kb == 0:
                            self.dma(QN, self.QS[h, :, c0:c0 + TT], [], [QN_k])
                            self.dma(QRt[0:64, :], self.QR[h, :, c0:c0 + TT], [], [QR_k])
                        self.pe(lambda e: e.matmul(S[:, 0:TT], KT[:, kb * 128:(kb + 1) * 128], QN, start=True, stop=False),
                                [KT_k, QN_k], [S_k])
                        self.pe(lambda e: e.matmul(S[:, 0:TT], KR[0:64, kb * 128:(kb + 1) * 128], QRt[0:64, :], start=False, stop=True),
                                [KR_k, QR_k], [S_k])

                    def fsoft(S=S, S_k=S_k, PT=PT, PT_k=PT_k):
                        self.act(lambda e: e.activation(out=PT, in_=S[:, 0:TT], func=AF.Exp, scale=C_SCALE), [S_k], [PT_k])

                    def fPV(num=num, num_k=num_k, den=den, den_k=den_k, Vp=Vp, Vp_k=Vp_k, PT=PT, PT_k=PT_k, kb=kb, nkb=nkb):
                        self.pe(lambda e: e.matmul(num[:, 0:TT], Vp[:, kb, :], PT, start=(kb == 0), stop=(kb == nkb - 1)),
                                [Vp_k, PT_k], [num_k])
                        self.pe(lambda e: e.matmul(den[:, 0:TT], self.ones[:, :], PT, start=(kb == 0), stop=(kb == nkb - 1)),
                                [self.ones_k, PT_k], [den_k])
                    post = None
                    if kb == nkb - 1:
                        def post(num=num, num_k=num_k, den=den, den_k=den_k, h=h, t=t, c0=c0):
                            rec, rec_k = recs.next()
                            oat, oat_k = oats.next()
                            self.dve(lambda e: e.reciprocal(rec, den[:, 0:TT]), [den_k], [rec_k])
                            self.dve(lambda e: e.tensor_tensor(out=oat, in0=num[:, 0:TT], in1=rec, op=ALU.mult), [num_k, rec_k], [oat_k])
                            self.dma(self.ATT[h, :, c0:c0 + TT], oat, [oat_k], [("ATT", h, t)])
                    blocks.append((fS, fsoft, fPV, post))
            groups.append((pre, blocks))
    self.run_attn(groups)


Builder.c_phase1 = _c_phase1
Builder.c_phase3 = _c_phase3
```

```python
import contextlib
import numpy as np
import ml_dtypes
import concourse.bass as bass
import concourse.mybir as mybir
from concourse.bass_utils import run_bass_kernel_spmd

F32 = mybir.dt.float32
BF16 = mybir.dt.bfloat16
AF = mybir.ActivationFunctionType
ALU = mybir.AluOpType

NCORES = 8


class Op:
    __slots__ = ("eng", "fn", "deps", "is_dma", "signal", "idx", "sem", "val", "prewait", "inc", "force", "raw", "bg")

    def __init__(self, eng, fn, is_dma, inc):
        self.eng = eng
        self.fn = fn
        self.deps = set()
        self.is_dma = is_dma
        self.signal = False
        self.sem = None
        self.val = None
        self.prewait = None
        self.inc = inc
        self.force = False
        self.raw = set()
        self.bg = False


ENGS = ("pe", "act", "dve", "pool", "sp")
SEM_ROT = 12000
N_DMA_SEMS = 12


class Prog:
    def __init__(self, nc, stack):
        self.nc = nc
        self.stack = stack
        self.ops = []
        self.eng_ops = {e: [] for e in ENGS}
        self.last_w = {}
        self.readers = {}
        self.dma_rr = {e: 0 for e in ENGS}
        self.dma_sems = {}
        self.dma_sem_last = {}
        self.eng_sems = {e: [] for e in ENGS}
        self.bar_from = 0
        self.prev_bar = []
        self.bg_last_w = {}

    def op(self, eng, fn, reads=(), writes=(), dma=False, inc=16, force=False, bg=False):
        o = Op(eng, fn, dma, inc)
        o.force = force
        o.bg = bg
        o.idx = len(self.ops)
        for b in reads:
            w = self.last_w.get(b)
            if w is not None:
                o.deps.add(w)
                o.raw.add(w)
        for b in writes:
            w = self.last_w.get(b)
            if w is not None:
                o.deps.add(w)
            for r in self.readers.get(b, ()):
                o.deps.add(r)
        for b in reads:
            self.readers.setdefault(b, []).append(o.idx)
        for b in writes:
            self.last_w[b] = o.idx
            self.readers[b] = []
            if bg:
                self.bg_last_w[b] = o.idx
        o.deps.discard(o.idx)
        self.ops.append(o)
        self.eng_ops[eng].append(o)
        return o

    def barrier(self):
        lasts = []
        for e in ("pe", "act", "dve"):
            for o in reversed(self.eng_ops[e]):
                if o.fn is not None and not o.is_dma:
                    lasts.append(o.idx)
                    break
        dmas = [o.idx for o in self.ops[self.bar_from:] if o.is_dma and not o.bg]
        self.bar_from = len(self.ops)
        prev = list(self.prev_bar)
        self.prev_bar = []
        for e in ENGS:
            o = Op(e, None, False, 0)
            o.idx = len(self.ops)
            o.deps = set(lasts) | set(dmas) | set(prev)
            self.ops.append(o)
            self.eng_ops[e].append(o)
            self.prev_bar.append(o.idx)
        self.last_w = dict(self.bg_last_w)
        self.readers = {}

    def finalize(self):
        nc = self.nc
        ops = self.ops
        for o in ops:
            for d in list(o.deps):
                do = ops[d]
                if (not do.is_dma) and do.eng == o.eng and not o.is_dma and not o.force and not (d in o.raw and o.eng != "pe"):
                    o.deps.discard(d)
                    continue
                do.signal = True
        cnt = {e: 0 for e in ENGS}
        for e in ENGS:
            for o in self.eng_ops[e]:
                if o.is_dma:
                    cc = "cc" if o.inc == 1 else "d"
                    rrk = (e, cc)
                    k = self.dma_rr.get(rrk, 0) % (N_DMA_SEMS if cc == "d" else 8)
                    self.dma_rr[rrk] = self.dma_rr.get(rrk, 0) + 1
                    key = (e, cc, k)
                    if key not in self.dma_sems:
                        self.dma_sems[key] = [self.stack.enter_context(nc.semaphore("%s_%s_%d" % (cc, e, k))), 0]
                    ent = self.dma_sems[key]
                    o.prewait = (ent[0], ent[1]) if ent[1] > 0 else None
                    ent[1] += o.inc
                    o.sem, o.val = ent[0], ent[1]
                elif o.signal and o.fn is not None:
                    ph = cnt[e] // SEM_ROT
                    while len(self.eng_sems[e]) <= ph:
                        self.eng_sems[e].append(
                            self.stack.enter_context(nc.semaphore("c_%s_%d" % (e, len(self.eng_sems[e])))))
                    cnt[e] += 1
                    o.sem = self.eng_sems[e][ph]
                    o.val = cnt[e] - ph * SEM_ROT
                elif o.signal and o.fn is None:
                    pass

        def resolve(d, acc, seen):
            do = ops[d]
            if do.fn is None:
                if d in seen:
                    return
                seen.add(d)
                for dd in do.deps:
                    resolve(dd, acc, seen)
                return
            key = id(do.sem)
            if key not in acc or acc[key][1] < do.val:
                acc[key] = (do.sem, do.val)

        self._resolve = resolve

        with nc.Block() as block:
            def run(e, handle_name):
                deco = getattr(block, handle_name)

                @deco
                def _(h):
                    known = {}
                    for o in self.eng_ops[e]:
                        acc = {}
                        seen = set()
                        for d in o.deps:
                            resolve(d, acc, seen)
                        if o.prewait is not None:
                            s, v = o.prewait
                            if id(s) not in acc or acc[id(s)][1] < v:
                                acc[id(s)] = (s, v)
                        for key, (s, v) in acc.items():
                            if known.get(key, 0) >= v:
                                continue
                            known[key] = v
                            h.wait_ge(s, v)
                        if o.fn is None:
                            continue
                        ins = o.fn(h)
                        if o.sem is not None:
                            if o.is_dma:
                                ins.then_inc(o.sem, o.inc)
                            else:
                                ins.then_inc(o.sem, 1)

            run("sp", "sync")
            run("pool", "gpsimd")
            run("act", "scalar")
            run("dve", "vector")
            run("pe", "tensor")


D = 2048
KC = 16
DFF = 5632
FC = 44
TT = 512
NT = 6
LT = 3072
HD = 128
NH = 16
EPS = 1e-6
PSEG = 1024
SSEG = 512
NEG = -30000.0
BIGR = 1.0e6
B_GROUPS = ((128, 1), (512, 4), (2048, 16))
I32 = mybir.dt.int32


def slopes16():
    return [2.0 ** (-8.0 * (h + 1) / 16.0) for h in range(16)]


def tile_cols(t):
    return t * TT


def tile_type(t):
    return 0 if t == 0 else (1 if t == 1 else 2)


def window_pieces(t, halo):
    base = 0 if t < 2 else PSEG + SSEG * (t - 2)
    seg = PSEG if t < 2 else SSEG
    off = TT * t if t < 2 else 0
    lo = off - halo
    hi = off + TT + halo
    pieces = []
    rel_lo = lo // seg
    rel_hi = (hi - 1) // seg
    for rel in range(rel_lo, rel_hi + 1):
        a = max(lo, rel * seg)
        b = min(hi, (rel + 1) * seg)
        pieces.append((rel, base + a - rel * seg, b - a))
    return pieces


class Arena:
    def __init__(self, ap, nwords):
        self.ap = ap
        self.n = nwords
        self.off = 0
        self.cnt = 0

    def alloc(self, shape, dtype, key=None):
        n = int(np.prod(shape))
        sz = 4 if dtype in (F32, I32) else 2
        words = (n * sz + 3) // 4
        words = (words + 15) // 16 * 16
        assert self.off + words <= self.n, ("arena overflow", self.off, words, self.n)
        a = self.ap[:, self.off:self.off + words]
        if dtype != F32:
            a = a.bitcast(dtype)
        a = a[:, 0:n]
        if len(shape) == 2:
            a = a.rearrange("p (a b) -> p a b", b=shape[1])
        elif len(shape) == 3:
            a = a.rearrange("p (a b c) -> p a b c", b=shape[1], c=shape[2])
        self.off += words
        self.cnt += 1
        return a, (key or ("ar%d" % self.cnt)) + "@%d" % self.off


class Rot:
    def __init__(self, items):
        self.items = items
        self.i = 0

    def next(self):
        it = self.items[self.i % len(self.items)]
        self.i += 1
        return it


def weight_specs():
    specs = []
    for i in range(4):
        pre = "l%d_" % i
        kind = i % 3
        if kind == 0:
            specs += [(pre + "a_w_qkv", 2048, 3072), (pre + "a_w_o", 2048, 2048)]
        elif kind == 1:
            specs += [(pre + "b_w_qkv", 2048, 18432), (pre + "b_w_o", 2048, 2048)]
        else:
            specs += [(pre + "c_w_down", 2048, 1088), (pre + "c_w_uq", 512, 3072),
                      (pre + "c_w_ukv", 512, 4096), (pre + "c_w_o", 2048, 2048)]
        specs += [(pre + "ffn_w_in", 2048, 11264), (pre + "ffn_w_out", 5632, 2048)]
    return specs


CF = {}
_o = 0
for _n, _w in (("gvec", 144), ("convw", 528), ("convb", 176), ("sink", 32), ("cnorm", 8), ("RA", 384),
               ("RB", 256), ("EA", 18), ("EB", 45), ("fl", 2)):
    CF[_n] = _o
    _o += _w
NCF = _o


class Builder:
    def __init__(self, n_layers=4, stop_mid=False, arena_words=46000):
        self.n_layers = n_layers
        self.stop_mid = stop_mid
        self.nc = bass.Bass("TRN2", target_bir_lowering=False)
        nc = self.nc
        self.st = contextlib.ExitStack()
        self.P = Prog(nc, self.st)
        self.x0 = nc.dram_tensor("x0T", [D, LT], F32, kind="ExternalInput").ap()
        self.cf_d = nc.dram_tensor("cf32", [128, NCF], F32, kind="ExternalInput").ap()
        self.rope_d = nc.dram_tensor("rope", [2, 32, LT], F32, kind="ExternalInput").ap()
        self.nb_d = nc.dram_tensor("nb", [1, 8], I32, kind="ExternalInput").ap()
        self.yT = nc.dram_tensor("yT", [D, LT], F32, kind="ExternalOutput").ap()
        self.w32 = {}
        self.wb = {}
        self.wkeys = {}
        for name, k, n in weight_specs():
            if int(name[1]) >= n_layers:
                continue
            self.w32[name] = nc.dram_tensor(name, [k, n], F32, kind="ExternalInput").ap()
            self.wb[name] = nc.dram_tensor("wb_" + name, [k, n], BF16).ap()
        dt = nc.dram_tensor
        self.XS = dt("XS", [KC, 128, LT], F32).ap()
        self.XM = dt("XM", [KC, 128, LT], F32).ap()
        self.H2 = dt("H2", [KC, 128, LT], BF16).ap()
        self.ATT = dt("ATT", [NH, 128, LT], BF16).ap()
        self.QS = dt("QS", [48, 128, LT], BF16).ap()
        self.QR = dt("QR", [NH, 64, LT], BF16).ap()
        self.KSa = dt("KSa", [512, LT], BF16).ap()
        self.NKa = {rel: dt("NKa%d" % (rel + 2), [512, LT], BF16).ap() for rel in (-1, 1)}
        self.NVa = {rel: dt("NVa%d" % (rel + 2), [LT, 512], BF16).ap() for rel in (-1, 1)}
        self.NHB = {rel: dt("NHB%d" % (rel + 2), [128, 160], BF16).ap() for rel in (-1, 1)}
        self.VSa = dt("VSa", [LT, 512], BF16).ap()
        self.HBs = dt("HBs", [128, 160], BF16).ap()
        self.arena_t = self.st.enter_context(nc.sbuf_tensor("arena", [128, arena_words], F32))
        self.ar = Arena(self.arena_t[:, :], arena_words)
        self.ps = []
        for i in range(8):
            t = self.st.enter_context(nc.psum_tensor("ps%d" % i, [128, 512], F32))
            self.ps.append((t[:, :], "ps%d" % i))
        self.psrot = Rot(self.ps)
        self.evac_i = 0
        self.regv = {}
        self.ag_bufs = {}
        self.attn_lookahead = 2

    def dma(self, out, in_, r, w, eng="sp"):
        return self.P.op(eng, lambda e: e.dma_start(out=out, in_=in_), reads=r, writes=w, dma=True)

    def localize(self, dst, g8view, rel, r, w, eng="sp", bg=False):
        self.dyn_cnt[eng] += 1
        assert self.dyn_cnt[eng] <= 21, "dynamic DMA register budget exceeded"

        def fn(e):
            v = self.regv[(eng, rel)]
            return e.dma_start(out=dst, in_=g8view[bass.ds(v, 1)])
        return self.P.op(eng, fn, reads=r, writes=w, dma=True, bg=bg)

    def pe(self, fn, r, w):
        return self.P.op("pe", fn, reads=r, writes=w)

    def act(self, fn, r, w):
        return self.P.op("act", fn, reads=r, writes=w)

    def dve(self, fn, r, w, force=False):
        return self.P.op("dve", fn, reads=r, writes=w, force=force)

    def evac(self, out, in_, r, w):
        self.evac_i += 1
        if self.evac_i % 2 == 0:
            return self.act(lambda e: e.activation(out=out, in_=in_, func=AF.Copy), r, w)
        return self.dve(lambda e: e.tensor_copy(out, in_), r, w)

    def allgather(self, send, R, C, name, key_send, key_out, rpc_force=None, bg=False):
        nc = self.nc
        rpc = 1
        for cand in range(1, R + 1):
            if R % cand == 0 and cand * C * 2 <= 512 * 1024:
                rpc = cand
        if rpc_force:
            rpc = rpc_force
        nch = R // rpc
        if name not in self.ag_bufs:
            self.ag_bufs[name] = (nc.dram_tensor("g4_" + name, [nch * 4 * rpc, C], BF16).ap(),
                                  nc.dram_tensor("g8_" + name, [nch * 8 * rpc, C], BF16).ap())
        g4, g8 = self.ag_bufs[name]
        for stage in (1, 2):
            for c in range(nch):
                s_ap = send[c * rpc:(c + 1) * rpc, :]
                g4c = g4[c * 4 * rpc:(c + 1) * 4 * rpc, :]
                g8c = g8[c * 8 * rpc:(c + 1) * 8 * rpc, :]
                k4 = (key_out, "g4", c)
                if stage == 1:
                    def c1(e, s_ap=s_ap, g4c=g4c):
                        return e.collective_compute("AllGather", ALU.bypass, replica_groups=[[0, 1, 2, 3], [4, 5, 6, 7]],
                                                    ins=[s_ap.opt()], outs=[g4c.opt()])
                    self.P.op("pool", c1, reads=key_send, writes=[k4], dma=True, inc=1, bg=bg)
                else:
                    def c2(e, g4c=g4c, g8c=g8c):
                        return e.collective_compute("AllGather", ALU.bypass, replica_groups=[[0, 4], [1, 5], [2, 6], [3, 7]],
                                                    ins=[g4c.opt()], outs=[g8c.opt()])
                    self.P.op("pool", c2, reads=[k4], writes=[(key_out, c)], dma=True, inc=1, bg=bg)
        keys = [(key_out, c) for c in range(nch)]
        return g8.rearrange("(n r i) c -> r n i c", r=8, i=rpc), keys, (nch, rpc)

    def convert_layer(self, li):
        for name, k, n in weight_specs():
            if name not in self.w32 or int(name[1]) != li:
                continue
            rows = 64 if n > 4096 else 256
            keys = []
            for r0 in range(0, k, rows):
                r1 = min(k, r0 + rows)
                key = ("wb", name, r0)
                keys.append(key)
                self.P.op("pool", lambda e, r0=r0, r1=r1, name=name: e.dma_start(out=self.wb[name][r0:r1, :], in_=self.w32[name][r0:r1, :]),
                          reads=[], writes=[key], dma=True, bg=True)
            self.wkeys[name] = keys

    def setup(self):
        ar = self.ar
        P = self.P
        self.cf, self.cf_k = ar.alloc([NCF], F32, "cf")
        self.cf = self.cf
        self.dma(self.cf, self.cf_d, [], [self.cf_k])
        self.nbs, self.nbs_k = ar.alloc([8], I32, "nbs")
        self.dma(self.nbs[0:1, :], self.nb_d, [], [self.nbs_k])

        self.dyn_cnt = {"sp": 0, "pool": 0}
        for eng in ("sp", "pool"):
            def ldregs(e, eng=eng):
                for rel in (-2, -1, 1, 2):
                    reg = e.alloc_register("nbr%d" % (rel + 2))
                    e.reg_load(reg, self.nbs[0:1, rel + 2:rel + 3])
                    self.regv[(eng, rel)] = e.snap(reg)
                return None
            P.op(eng, ldregs, reads=[self.nbs_k], writes=[])
        self.ones, self.ones_k = ar.alloc([128], BF16, "ones")
        self.dve(lambda e: e.memset(self.ones, 1.0), [], [self.ones_k])
        self.esink, self.esink_k = ar.alloc([32], F32, "esink")
        o = CF["sink"]
        self.act(lambda e: e.activation(out=self.esink, in_=self.cf[:, o:o + 32], func=AF.Exp),
                 [self.cf_k], [self.esink_k])
        self.haloL, self.haloL_k = ar.alloc([16, 5], BF16, "haloL")
        self.haloR, self.haloR_k = ar.alloc([16, 5], BF16, "haloR")
        self.hb, self.hb_k = ar.alloc([2, 16, 5], BF16, "hb")
        self.mark = ar.off
        self.x0v = self.x0.rearrange("(k p) t -> k p t", p=128)

    def phase_begin(self):
        self.P.barrier()
        self.ar.off = self.mark

    def cfcol(self, name, idx):
        o = CF[name] + idx
        return self.cf[:, o:o + 1]

    def rmsnorm(self, xt, xt_k, nchunks, width, gname, gidx0, out, out_k, tmp, out_fn=None, post=None):
        psum, psk = self.psrot.next()
        n_feat = nchunks * 128
        for c in range(nchunks):
            sq, sqk = tmp["sq"].next()
            self.act(lambda e, c=c, sq=sq: e.activation(out=sq[:, 0:width], in_=xt[:, c, 0:width], func=AF.Square),
                     [xt_k], [sqk])
            self.pe(lambda e, c=c, sq=sq: e.matmul(psum[:, 0:width], self.ones[:, :], sq[:, 0:width],
                                                  start=(c == 0), stop=(c == nchunks - 1)),
                    [sqk, self.ones_k], [psk])
        rs, rsk = tmp["rstd"]
        self.act(lambda e: e.activation(out=rs[:, 0:width], in_=psum[:, 0:width], func=AF.Sqrt,
                                        scale=1.0 / n_feat, bias=EPS), [psk], [rsk])
        self.dve(lambda e: e.reciprocal(rs[:, 0:width], rs[:, 0:width]), [rsk], [rsk])
        for c in range(nchunks):
            g = self.cfcol(gname, gidx0 + c)
            if out_fn is not None:
                o_ap, o_k = out_fn(c)
            else:
                o_ap, o_k = out[:, c, 0:width], out_k
            self.dve(lambda e, c=c, g=g, o_ap=o_ap: e.scalar_tensor_tensor(out=o_ap, in0=xt[:, c, 0:width],
                                                                          scalar=g, in1=rs[:, 0:width],
                                                                          op0=ALU.mult, op1=ALU.mult),
                     [xt_k, rsk, self.cf_k], [o_k])
            if post is not None:
                post(c, o_ap, o_k)

    def lin_fm(self, wv, wkey, kc_n, nchunk, rhs_fn, rhs_keys, width, consumer, m0=0, mw=128):
        for m in range(nchunk):
            psum, psk = self.psrot.next()
            for kc in range(kc_n):
                self.pe(lambda e, m=m, kc=kc, psum=psum: e.matmul(psum[0:mw, 0:width], wv[:, kc, m * mw:(m + 1) * mw],
                                                                rhs_fn(kc), start=(kc == 0), stop=(kc == kc_n - 1)),
                        [wkey] + rhs_keys, [psk])
            consumer(m0 + m, psum, psk)

    def lin_tm(self, wv_cols_fn, wkey, kc_n, ncols, lhs_fn, lhs_keys, nsub, consumer):
        for s in range(nsub):
            psum, psk = self.psrot.next()
            for kc in range(kc_n):
                self.pe(lambda e, s=s, kc=kc, psum=psum: e.matmul(psum[:, 0:ncols], lhs_fn(kc, s), wv_cols_fn(kc),
                                                                start=(kc == 0), stop=(kc == kc_n - 1)),
                        [wkey] + lhs_keys, [psk])
            consumer(s, psum, psk)

    def alloc_common(self, wsize=8192, nw=3):
        ar = self.ar
        self.wbufs = Rot([ar.alloc([wsize], BF16, "wbuf%d" % i) for i in range(nw)])
        self.stg16 = Rot([ar.alloc([512], BF16, "stg16_%d" % i) for i in range(4)])
        self.sqr = Rot([ar.alloc([512], BF16, "sq%d" % i) for i in range(2)])
        self.rstd = ar.alloc([512], F32, "rstd")
        self.ntmp = dict(sq=self.sqr, rstd=self.rstd)

    def wload(self, name, kc_n, c0, ncols):
        ap, key = self.wbufs.next()
        dst = ap[:, 0:kc_n * ncols].rearrange("p (k n) -> p k n", n=ncols)
        src = self.wb[name][:, c0:c0 + ncols].rearrange("(k p) n -> p k n", p=128)
        self.dma(dst, src, self.wkeys[name], [key])
        return dst, key

    def run_jobs(self, jobs, depth=2):
        loaded = {}
        for i in range(min(depth, len(jobs))):
            loaded[i] = self.wload(*jobs[i][0:4])
        for i, job in enumerate(jobs):
            if i + depth < len(jobs):
                loaded[i + depth] = self.wload(*jobs[i + depth][0:4])
            wv, wkey = loaded.pop(i)
            job[4](wv, wkey)

    def load_x_tile(self, src, srcname, t, xt, xt_k):
        c0 = t * TT
        self.dma(xt[:, :, 0:TT], src[:, :, c0:c0 + TT].rearrange("k p t -> p k t"), [(srcname, t)], [xt_k])

    def store_stage(self, psum, psk, dst, dst_key, width=TT, npart=128):
        st, stk = self.stg16.next()
        self.evac(st[0:npart, 0:width], psum[0:npart, 0:width], [psk], [stk])
        self.dma(dst, st[0:npart, 0:width], [stk], [dst_key])

    def a_phase1(self, li):
        wn = "l%d_a_w_qkv" % li
        xsrc, xname = (self.x0v, "x0") if li == 0 else (self.XS, "XS")
        self.phase_begin()
        ar = self.ar
        self.alloc_common()
        xts = [ar.alloc([KC, TT], F32, "xt%d" % i) for i in range(2)]
        hts = [ar.alloc([KC, TT], BF16, "ht%d" % i) for i in range(2)]
        jobs = []
        for t in range(NT):
            c0 = t * TT
            xt, xt_k = xts[t % 2]
            ht, ht_k = hts[t % 2]
            for blk in range(6):
                def fn(wv, wkey, t=t, c0=c0, blk=blk, xt=xt, xt_k=xt_k, ht=ht, ht_k=ht_k):
                    if blk == 0:
                        if t == 0:
                            self.load_x_tile(xsrc, xname, 0, xt, xt_k)
                        if t + 1 < NT:
                            self.load_x_tile(xsrc, xname, t + 1, *xts[(t + 1) % 2])
                        self.rmsnorm(xt, xt_k, KC, TT, "gvec", (2 * li) * 16, ht, ht_k, self.ntmp)
                    rhs = lambda kc: ht[:, kc, :]
                    if blk < 4:
                        def cons(m, psum, psk):
                            self.store_stage(psum, psk, self.QS[m, :, c0:c0 + TT], ("QS", m, t))
                        self.lin_fm(wv, wkey, KC, 4, rhs, [ht_k], TT, cons, m0=blk * 4)
                    elif blk == 4:
                        def cons(m, psum, psk):
                            self.store_stage(psum, psk, self.KSa[m * 128:(m + 1) * 128, c0:c0 + TT], ("KS", m, t))
                        self.lin_fm(wv, wkey, KC, 4, rhs, [ht_k], TT, cons)
                    else:
                        def cons(s, psum, psk):
                            self.store_stage(psum, psk, self.VSa[c0 + s * 128:c0 + (s + 1) * 128, :], ("VS", s, t))
                        self.lin_tm(lambda kc: wv[:, kc, 0:512], wkey, KC, 512,
                                    lambda kc, s: ht[:, kc, s * 128:(s + 1) * 128], [ht_k], 4, cons)
                jobs.append((wn, KC, blk * 512, 512, fn))
        self.run_jobs(jobs)
        ksend = [("KS", m, t) for m in range(4) for t in range(NT)]
        vsend = [("VS", s, t) for s in range(4) for t in range(NT)]
        K8v, kk, (kn, kr) = self.allgather(self.KSa, 512, LT, "Ka", ksend, "K8")
        V8v, vk, (vn, vr) = self.allgather(self.VSa, LT, 512, "Va", vsend, "V8")
        for rel in (-1, 1):
            self.localize(self.NKa[rel].rearrange("(n i) c -> n i c", i=kr), K8v, rel, kk, [("NKa", rel)])
            self.localize(self.NVa[rel].rearrange("(n i) c -> n i c", i=vr), V8v, rel, vk, [("NVa", rel)])


    def run_attn(self, groups):
        flat = [(gi, bi, b) for gi, (pre, blocks) in enumerate(groups) for bi, b in enumerate(blocks)]
        if not flat:
            return
        groups[0][0]()
        called = {0}
        LA = self.attn_lookahead
        for k in range(min(LA, len(flat))):
            gk = flat[k][0]
            if gk not in called:
                groups[gk][0]()
                called.add(gk)
            flat[k][2][0]()
        for i, (gi, bi, b) in enumerate(flat):
            if bi == 0 and gi + 1 < len(groups) and (gi + 1) not in called:
                groups[gi + 1][0]()
                called.add(gi + 1)
            b[1]()
            if i + LA < len(flat):
                gk = flat[i + LA][0]
                if gk not in called:
                    groups[gk][0]()
                    called.add(gk)
                flat[i + LA][2][0]()
            b[2]()
            if b[3] is not None:
                b[3]()

    def a_phase3(self, li):
        self.phase_begin()
        if li + 1 < self.n_layers:
            self.convert_layer(li + 1)
        ar = self.ar
        sl = slopes16()
        scale = HD ** -0.5
        sink_base = (0 if li == 0 else 1) * 16
        KTs = Rot([ar.alloc([768], BF16, "KTw%d" % i) for i in range(2)])
        Vws = Rot([ar.alloc([6, 128], BF16, "Vw%d" % i) for i in range(2)])
        QTs = Rot([ar.alloc([512], BF16, "QT%d" % i) for i in range(8)])
        tmps = Rot([ar.alloc([384], F32, "tmp%d" % i) for i in range(4)])
        PTs = Rot([ar.alloc([384], BF16, "PT%d" % i) for i in range(4)])
        recs = Rot([ar.alloc([512], F32, "rec%d" % i) for i in range(2)])
        oats = Rot([ar.alloc([512], BF16, "oat%d" % i) for i in range(3)])
        RA0 = CF["RA"]
        hh = 0
        sidx = 0
        groups = []
        for t in range(NT):
            c0 = t * TT
            tt_ = tile_type(t)
            pieces = window_pieces(t, 128)
            for kvh in range(4):
                KT, KT_k = KTs.next()
                Vw, Vw_k = Vws.next()
                qts = [QTs.next() for _ in range(4)]

                def pre(t=t, c0=c0, kvh=kvh, KT=KT, KT_k=KT_k, Vw=Vw, Vw_k=Vw_k, qts=qts, pieces=pieces):
                    w0 = 0
                    for (rel, lc, ln) in pieces:
                        ksrc = self.KSa if rel == 0 else self.NKa[rel]
                        vsrc = self.VSa if rel == 0 else self.NVa[rel]
                        self.dma(KT[:, w0:w0 + ln], ksrc[kvh * 128:(kvh + 1) * 128, lc:lc + ln], [], [KT_k])
                        b0 = w0 // 128
                        nb = ln // 128
                        self.dma(Vw[:, b0:b0 + nb, :],
                                 vsrc[lc:lc + ln, kvh * 128:(kvh + 1) * 128].rearrange("(b p) d -> p b d", p=128), [], [Vw_k])
                        w0 += ln
                    assert w0 == 768
                    for g4 in range(4):
                        h = kvh * 4 + g4
                        self.dma(qts[g4][0], self.QS[h, :, c0:c0 + TT], [], [qts[g4][1]])
                blocks = []
                for g4 in range(4):
                    h = kvh * 4 + g4
                    QT, QT_k = qts[g4]
                    num, num_k = self.ps[3 + hh % 2]
                    den, den_k = self.ps[5 + hh % 2]
                    hh += 1
                    for j in range(6):
                        q_lo = max(0, 128 * j - 256)
                        q_hi = min(512, 128 * j + 128)
                        n = q_hi - q_lo
                        cb = q_lo - (128 * j - 256)
                        S, S_k = self.ps[(0, 1, 2, 7)[sidx % 4]]
                        sidx += 1
                        tmp, tmp_k = tmps.next()
                        PT, PT_k = PTs.next()
                        coef = -sl[h] / scale
                        ecol = self.cfcol("EA", tt_ * 6 + j)

                        def fS(S=S, S_k=S_k, KT=KT, KT_k=KT_k, QT=QT, QT_k=QT_k, j=j, q_lo=q_lo, q_hi=q_hi, n=n):
                            self.pe(lambda e: e.matmul(S[:, 0:n], KT[:, 128 * j:128 * j + 128], QT[:, q_lo:q_hi], start=True, stop=True),
                                    [KT_k, QT_k], [S_k])

                        def fsoft(S=S, S_k=S_k, tmp=tmp, tmp_k=tmp_k, PT=PT, PT_k=PT_k, cb=cb, n=n, coef=coef, ecol=ecol):
                            self.dve(lambda e: e.scalar_tensor_tensor(out=tmp[:, 0:n], in0=self.cf[:, RA0 + cb:RA0 + cb + n], scalar=coef,
                                                                      in1=S[:, 0:n], op0=ALU.mult, op1=ALU.add),
                                     [S_k, self.cf_k], [tmp_k])
                            self.act(lambda e: e.activation(out=PT[:, 0:n], in_=tmp[:, 0:n], func=AF.Exp, bias=ecol, scale=scale),
                                     [tmp_k, self.cf_k], [PT_k])

                        def fPV(num=num, num_k=num_k, den=den, den_k=den_k, Vw=Vw, Vw_k=Vw_k, PT=PT, PT_k=PT_k, j=j, q_lo=q_lo, q_hi=q_hi, n=n):
                            self.pe(lambda e: e.matmul(num[:, q_lo:q_hi], Vw[:, j, :], PT[:, 0:n], start=(j == 0), stop=(j == 5),
                                                       skip_group_check=True), [Vw_k, PT_k], [num_k])
                            self.pe(lambda e: e.matmul(den[:, q_lo:q_hi], self.ones[:, :], PT[:, 0:n], start=(j == 0), stop=(j == 5),
                                                       skip_group_check=True), [self.ones_k, PT_k], [den_k])
                        post = None
                        if j == 5:
                            def post(num=num, num_k=num_k, den=den, den_k=den_k, h=h, t=t, c0=c0):
                                rec, rec_k = recs.next()
                                oat, oat_k = oats.next()
                                sk = sink_base + h
                                self.dve(lambda e: e.tensor_scalar(out=rec, in0=den, scalar1=self.esink[:, sk:sk + 1], scalar2=None, op0=ALU.add),
                                         [den_k, self.esink_k], [rec_k])
                                self.dve(lambda e: e.reciprocal(rec, rec), [rec_k], [rec_k])
                                self.dve(lambda e: e.tensor_tensor(out=oat, in0=num, in1=rec, op=ALU.mult), [num_k, rec_k], [oat_k])
                                self.dma(self.ATT[h, :, c0:c0 + TT], oat, [oat_k], [("ATT", h, t)])
                        blocks.append((fS, fsoft, fPV, post))
                groups.append((pre, blocks))
        self.run_attn(groups)

    def oproj_phase(self, li, wn):
        xsrc, xname = (self.x0v, "x0") if li == 0 else (self.XS, "XS")
        self.phase_begin()
        ar = self.ar
        self.alloc_common()
        xts = [ar.alloc([KC, TT], F32, "xt%d" % i) for i in range(2)]
        ats = [ar.alloc([KC, TT], BF16, "at%d" % i) for i in range(2)]
        h2s = [ar.alloc([KC, TT], BF16, "h2_0")] * 2
        jobs = []

        def load_tile(t):
            c0 = t * TT
            self.load_x_tile(xsrc, xname, t, *xts[t % 2])
            at, at_k = ats[t % 2]
            self.dma(at, self.ATT[:, :, c0:c0 + TT].rearrange("h p t -> p h t"), [("ATT", h, t) for h in range(NH)], [at_k])

        for t in range(NT):
            c0 = t * TT
            xt, xt_k = xts[t % 2]
            at, at_k = ats[t % 2]
            h2, h2_k = h2s[t % 2]
            for blk in range(4):
                def fn(wv, wkey, t=t, c0=c0, blk=blk, xt=xt, xt_k=xt_k, at=at, at_k=at_k, h2=h2, h2_k=h2_k):
                    if blk == 0:
                        if t == 0:
                            load_tile(0)
                        if t + 1 < NT:
                            load_tile(t + 1)

                    def cons(m, psum, psk):
                        self.dve(lambda e: e.tensor_tensor(out=xt[:, m, :], in0=xt[:, m, :], in1=psum[:, 0:TT], op=ALU.add),
                                 [psk, xt_k], [xt_k])
                    self.lin_fm(wv, wkey, KC, 4, lambda kc: at[:, kc, :], [at_k], TT, cons, m0=blk * 4)
                    if blk == 3:
                        self.dma(self.XM[:, :, c0:c0 + TT].rearrange("k p t -> p k t"), xt, [xt_k], [("XM", t)])
                        self.rmsnorm(xt, xt_k, KC, TT, "gvec", (2 * li + 1) * 16, h2, h2_k, self.ntmp)
                        self.dma(self.H2[:, :, c0:c0 + TT].rearrange("k p t -> p k t"), h2, [h2_k], [("H2", t)])
                        bl = []
                        if t == 0:
                            bl = [(0, 0, 0)]
                        elif t == 1:
                            bl = [(1, 0, TT - 1)]
                        else:
                            bl = [(0, t - 1, 0), (1, t - 1, TT - 1)]
                        for (side, seg, col) in bl:
                            self.dve(lambda e, side=side, seg=seg, col=col: e.tensor_copy(self.hb[:, side, :, seg], h2[:, :, col]),
                                     [h2_k], [self.hb_k], force=True)
                jobs.append((wn, KC, blk * 512, 512, fn))
        self.run_jobs(jobs)
        self.dma(self.HBs, self.hb.rearrange("p a b c -> p (a b c)"), [self.hb_k], ["HBs"])
        HBv, hk, (hn, hr) = self.allgather(self.HBs, 128, 160, "HB", ["HBs"], "HB8")
        tl, tl_k = ar.alloc([80], BF16, "tl")
        tr, tr_k = ar.alloc([80], BF16, "tr")
        self.localize(self.NHB[-1].rearrange("(n i) c -> n i c", i=hr), HBv, -1, hk, [("NHB", -1)])
        self.localize(self.NHB[1].rearrange("(n i) c -> n i c", i=hr), HBv, 1, hk, [("NHB", 1)])
        self.dma(tl, self.NHB[-1][:, 80:160], [("NHB", -1)], [tl_k])
        self.dma(tr, self.NHB[1][:, 0:80], [("NHB", 1)], [tr_k])
        fl = CF["fl"]
        self.dve(lambda e: e.tensor_scalar(out=self.haloL.rearrange("p a b -> p (a b)"), in0=tl,
                                           scalar1=self.cf[:, fl:fl + 1], scalar2=None, op0=ALU.mult),
                 [tl_k, self.cf_k], [self.haloL_k])
        self.dve(lambda e: e.tensor_scalar(out=self.haloR.rearrange("p a b -> p (a b)"), in0=tr,
                                           scalar1=self.cf[:, fl + 1:fl + 2], scalar2=None, op0=ALU.mult),
                 [tr_k, self.cf_k], [self.haloR_k])

    def ffn_phase(self, li, last):
        self.phase_begin()
        ar = self.ar
        self.alloc_common(wsize=5632, nw=3)
        win = "l%d_ffn_w_in" % li
        wout = "l%d_ffn_w_out" % li
        g, _ = ar.alloc([FC, TT], BF16, "g")
        h2es = [ar.alloc([KC, TT + 2], BF16, "h2e%d" % i) for i in range(2)]
        xt, xt_k = ar.alloc([KC, TT], F32, "xt")
        xmcs = Rot([ar.alloc([512], F32, "xmc%d" % i) for i in range(2)])
        aexts = Rot([ar.alloc([TT + 2], F32, "aext%d" % i) for i in range(2)])
        cbs = Rot([ar.alloc([512], F32, "cb%d" % i) for i in range(2)])
        gls = Rot([ar.alloc([512], F32, "gl%d" % i) for i in range(4)])
        cw0 = CF["convw"] + li * FC * 3
        cb0 = CF["convb"] + li * FC

        def load_h2e(t):
            c0 = t * TT
            h2e, k = h2es[t % 2]
            if t == 0:
                self.dma(h2e[:, :, 1:TT + 2], self.H2[:, :, c0:c0 + TT + 1].rearrange("k p t -> p k t"),
                         [("H2", 0), ("H2", 1)], [k])
                self.dve(lambda e: e.tensor_copy(h2e[:, :, 0], self.haloL[:, :, 0]), [self.haloL_k], [k])
            elif t == 1:
                self.dma(h2e[:, :, 0:TT + 1], self.H2[:, :, c0 - 1:c0 + TT].rearrange("k p t -> p k t"),
                         [("H2", 0), ("H2", 1)], [k])
                self.dve(lambda e: e.tensor_copy(h2e[:, :, TT + 1], self.haloR[:, :, 0]), [self.haloR_k], [k])
            else:
                self.dma(h2e[:, :, 1:TT + 1], self.H2[:, :, c0:c0 + TT].rearrange("k p t -> p k t"), [("H2", t)], [k])
                self.dve(lambda e: e.tensor_copy(h2e[:, :, 0], self.haloL[:, :, t - 1]), [self.haloL_k], [k])
                self.dve(lambda e: e.tensor_copy(h2e[:, :, TT + 1], self.haloR[:, :, t - 1]), [self.haloR_k], [k])

        jobs = []
        for t in range(NT):
            c0 = t * TT
            h2e, h2e_k = h2es[t % 2]
            glbuf = {}
            for jb in range(FC // 2):
                def gate(wv, wkey, t=t, jb=jb, h2e=h2e, h2e_k=h2e_k, glbuf=glbuf):
                    if jb == 0:
                        if t == 0:
                            load_h2e(0)
                        if t + 1 < NT:
                            load_h2e(t + 1)
                    for jj in range(2):
                        j = jb * 2 + jj
                        a_ps, a_k = self.psrot.next()
                        ah_ps, ah_k = self.psrot.next()
                        for kc in range(KC):
                            self.pe(lambda e, kc=kc, jj=jj, a_ps=a_ps: e.matmul(a_ps[:, 0:TT], wv[:, kc, jj * 128:(jj + 1) * 128],
                                                                              h2e[:, kc, 1:TT + 1], start=(kc == 0), stop=(kc == KC - 1)),
                                    [wkey, h2e_k], [a_k])
                        for kc in range(KC):
                            self.pe(lambda e, kc=kc, jj=jj, ah_ps=ah_ps: e.matmul(ah_ps[:, 0:2], wv[:, kc, jj * 128:(jj + 1) * 128],
                                                                                h2e[:, kc, 0:TT + 2:TT + 1], start=(kc == 0), stop=(kc == KC - 1)),
                                    [wkey, h2e_k], [ah_k])
                        aext, ax_k = aexts.next()
                        cb, cb_k = cbs.next()
                        gl, gl_k = gls.next()
                        glbuf[j] = (gl, gl_k)
                        self.act(lambda e, aext=aext, a_ps=a_ps: e.activation(out=aext[:, 1:TT + 1], in_=a_ps[:, 0:TT], func=AF.Copy),
                                 [a_k], [ax_k])
                        self.act(lambda e, aext=aext, ah_ps=ah_ps: e.activation(out=aext[:, 0:TT + 2:TT + 1], in_=ah_ps[:, 0:2], func=AF.Copy),
                                 [ah_k], [ax_k])
                        w0c = self.cf[:, cw0 + j * 3 + 0:cw0 + j * 3 + 1]
                        w1c = self.cf[:, cw0 + j * 3 + 1:cw0 + j * 3 + 2]
                        w2c = self.cf[:, cw0 + j * 3 + 2:cw0 + j * 3 + 3]
                        bc = self.cf[:, cb0 + j:cb0 + j + 1]
                        self.act(lambda e, cb=cb, a_ps=a_ps, w1c=w1c, bc=bc: e.activation(out=cb, in_=a_ps[:, 0:TT], func=AF.Identity,
                                                                                         bias=bc, scale=w1c),
                                 [a_k, self.cf_k], [cb_k])
                        self.dve(lambda e, cb=cb, aext=aext, w0c=w0c: e.scalar_tensor_tensor(out=cb, in0=aext[:, 0:TT], scalar=w0c, in1=cb,
                                                                                            op0=ALU.mult, op1=ALU.add),
                                 [ax_k, cb_k, self.cf_k], [cb_k])
                        self.dve(lambda e, cb=cb, aext=aext, w2c=w2c: e.scalar_tensor_tensor(out=cb, in0=aext[:, 2:TT + 2], scalar=w2c, in1=cb,
                                                                                            op0=ALU.mult, op1=ALU.add),
                                 [ax_k, cb_k, self.cf_k], [cb_k])
                        self.act(lambda e, gl=gl, cb=cb: e.activation(out=gl, in_=cb, func=AF.Gelu), [cb_k], [gl_k])

                def val(wv, wkey, t=t, jb=jb, h2e=h2e, h2e_k=h2e_k, glbuf=glbuf):
                    for jj in range(2):
                        j = jb * 2 + jj
                        gl, gl_k = glbuf[j]

                        def cons(m, psum, psk, j=j, gl=gl, gl_k=gl_k):
                            self.dve(lambda e: e.tensor_tensor(out=g[:, j, :], in0=gl, in1=psum[:, 0:TT], op=ALU.mult),
                                     [gl_k, psk], [("g", j)])
                        self.lin_fm(wv[:, :, jj * 128:(jj + 1) * 128], wkey, KC, 1, lambda kc: h2e[:, kc, 1:TT + 1], [h2e_k], TT, cons)
                jobs.append((win, KC, jb * 256, 256, gate))
                jobs.append((win, KC, DFF + jb * 256, 256, val))
            for m in range(KC):
                def outp(wv, wkey, t=t, c0=c0, m=m):
                    xmc, xmc_k = xmcs.next()
                    self.dma(xmc, self.XM[m, :, c0:c0 + TT], [("XM", t)], [xmc_k])

                    def cons(mm, psum, psk):
                        self.dve(lambda e: e.tensor_tensor(out=xt[:, m, :], in0=xmc, in1=psum[:, 0:TT], op=ALU.add),
                                 [psk, xmc_k], [xt_k])
                    self.lin_fm(wv, wkey, FC, 1, lambda kc: g[:, kc, :], [("g", j) for j in range(FC)], TT, cons)
                    if m == KC - 1:
                        if not last:
                            self.dma(self.XS[:, :, c0:c0 + TT].rearrange("k p t -> p k t"), xt, [xt_k], [("XS", t)])
                        else:
                            def out_fn(c):
                                return xmcs.next()

                            def post(c, o_ap, o_k):
                                self.dma(self.yT[c * 128:(c + 1) * 128, c0:c0 + TT], o_ap, [o_k], [("yT", c, t)])
                            self.rmsnorm(xt, xt_k, KC, TT, "gvec", 8 * 16, None, None, self.ntmp, out_fn=out_fn, post=post)
                jobs.append((wout, FC, m * 128, 128, outp))
        self.run_jobs(jobs)


def build_program(n_layers=4, stop_mid=False):
    B = Builder(n_layers, stop_mid)
    import os
    stage = int(os.environ.get("DBG_STAGE", "99"))
    B.convert_layer(0)
    B.setup()
    done = False
    for li in range(n_layers):
        kind = li % 3
        if kind == 0:
            if stage >= 1:
                B.a_phase1(li)
            if stage >= 2:
                B.a_phase3(li)
            if stage >= 3:
                B.oproj_phase(li, "l%d_a_w_o" % li)
        elif kind == 1:
            B.b_phase1(li)
            B.b_phase3(li)
            B.oproj_phase(li, "l%d_b_w_o" % li)
        else:
            B.c_phase1(li)
            B.c_phase3(li)
            B.oproj_phase(li, "l%d_c_w_o" % li)
        if stop_mid and li == n_layers - 1:
            B.phase_begin()
            for t in range(NT):
                c0 = t * TT
                B.dma(B.yT[:, c0:c0 + TT].rearrange("(k p) t -> k p t", p=128), B.XM[:, :, c0:c0 + TT], [("XM", t)], [("yT", t)])
            done = True
            break
        B.ffn_phase(li, last=(li == 3))
    if not done and n_layers < 4:
        B.phase_begin()
        for t in range(NT):
            c0 = t * TT
            B.dma(B.yT[:, c0:c0 + TT].rearrange("(k p) t -> k p t", p=128), B.XS[:, :, c0:c0 + TT], [("XS", t)], [("yT", t)])
    B.P.barrier()
    B.P.finalize()
    B.st.close()
    return B.nc


def _vec_cols(v, nch):
    return np.ascontiguousarray(np.asarray(v, np.float32).reshape(nch, 128).T)


def host_inputs(inputs, n_layers=4):
    f32 = np.float32
    xp = np.asarray(inputs["x_prompt"], f32)
    xs = np.asarray(inputs["x_sample"], f32)
    cf = np.zeros((128, NCF), f32)
    for i in range(4):
        cf[:, CF["gvec"] + (2 * i) * 16:CF["gvec"] + (2 * i + 1) * 16] = _vec_cols(inputs["l%d_mix_norm" % i], 16)
        cf[:, CF["gvec"] + (2 * i + 1) * 16:CF["gvec"] + (2 * i + 2) * 16] = _vec_cols(inputs["l%d_ffn_norm" % i], 16)
        cw = np.asarray(inputs["l%d_ffn_conv_w" % i], f32)
        cwl = cw.T.reshape(FC, 128, 3).transpose(1, 0, 2).reshape(128, FC * 3)
        cf[:, CF["convw"] + i * FC * 3:CF["convw"] + (i + 1) * FC * 3] = cwl
        cf[:, CF["convb"] + i * FC:CF["convb"] + (i + 1) * FC] = _vec_cols(inputs["l%d_ffn_conv_b" % i], FC)
    cf[:, CF["gvec"] + 128:CF["gvec"] + 144] = _vec_cols(inputs["final_norm"], 16)
    cf[:, CF["sink"]:CF["sink"] + 16] = np.asarray(inputs["l0_a_sink"], f32)[None, :]
    cf[:, CF["sink"] + 16:CF["sink"] + 32] = np.asarray(inputs["l3_a_sink"], f32)[None, :]
    cf[:, CF["cnorm"]:CF["cnorm"] + 4] = _vec_cols(inputs["l2_c_q_norm"], 4)
    cf[:, CF["cnorm"] + 4:CF["cnorm"] + 8] = _vec_cols(inputs["l2_c_kv_norm"], 4)
    p = np.arange(128)[:, None]
    c = np.arange(384)[None, :]
    ra = np.abs(c - 128 - p).astype(f32)
    ra[ra > 128] = BIGR
    cf[:, CF["RA"]:CF["RA"] + 384] = ra
    c = np.arange(256)[None, :]
    rb = np.abs(c - 64 - p).astype(f32)
    rb[rb > 64] = BIGR
    cf[:, CF["RB"]:CF["RB"] + 256] = rb
    inv = ROPE_THETA_ ** (-np.arange(0, 64, 2, dtype=np.float32) / 64.0)
    maps = []
    wfull = {}
    for core in range(NCORES):
        cfc = cf.copy()
        for tt_, t in ((0, 0), (1, 1), (2, 2)):
            pcs = window_pieces(t, 128)
            w0 = 0
            for (rel, lc, ln) in pcs:
                valid = 0 <= core + rel <= 7
                for b in range(w0 // 128, (w0 + ln) // 128):
                    cfc[:, CF["EA"] + tt_ * 6 + b] = 0.0 if valid else NEG
                w0 += ln
            for gi, (window, d) in enumerate(B_GROUPS):
                pcs = window_pieces(t, 64 * d)
                nj = (TT + 128 * d) // d
                colv = np.zeros(nj, f32)
                w0 = 0
                for (rel, lc, ln) in pcs:
                    valid = 0 <= core + rel <= 7
                    colv[w0 // d:(w0 + ln) // d] = 0.0 if valid else NEG
                    w0 += ln
                for b in range(5):
                    seg = colv[128 * b:128 * (b + 1)]
                    col = np.zeros(128, f32)
                    col[:len(seg)] = seg
                    cfc[:, CF["EB"] + (tt_ * 3 + gi) * 5 + b] = col
        cfc[:, CF["fl"]] = 1.0 if core > 0 else 0.0
        cfc[:, CF["fl"] + 1] = 1.0 if core < 7 else 0.0
        xl = np.concatenate([xp[0, PSEG * core:PSEG * (core + 1)]] + [xs[b, SSEG * core:SSEG * (core + 1)] for b in range(4)], axis=0)
        x0T = np.ascontiguousarray(xl.T)
        pos = np.concatenate([np.arange(PSEG * core, PSEG * (core + 1))] + [np.arange(SSEG * core, SSEG * (core + 1))] * 4).astype(np.float32)
        ang = pos[None, :] * inv[:, None]
        rope = np.stack([np.cos(ang), np.sin(ang)]).astype(f32)
        nb = np.zeros((1, 8), np.int32)
        for rel in range(-2, 3):
            nb[0, rel + 2] = min(7, max(0, core + rel))
        m = {"x0T": x0T, "cf32": cfc, "rope": rope, "nb": nb}
        for name, k, n in weight_specs():
            if int(name[1]) >= n_layers:
                continue
            w = inputs[name]
            m[name] = wfull.setdefault(name, np.ascontiguousarray(np.asarray(w, f32)))
        maps.append(m)
    return maps


ROPE_THETA_ = 10000.0
_NC_CACHE = {}


def run(inputs, n_layers=4, stop_mid=False):
    key = (n_layers, stop_mid)
    if key not in _NC_CACHE:
        _NC_CACHE[key] = build_program(n_layers, stop_mid)
    nc = _NC_CACHE[key]
    maps = host_inputs(inputs, n_layers)
    res = run_bass_kernel_spmd(nc, maps, core_ids=list(range(NCORES)))
    yp = np.zeros((1, 8192, D), np.float32)
    ys = np.zeros((4, 4096, D), np.float32)
    for core in range(NCORES):
        yT = np.asarray(res.results[core]["yT"])
        yp[0, PSEG * core:PSEG * (core + 1), :] = yT[:, 0:PSEG].T
        for b in range(4):
            ys[b, SSEG * core:SSEG * (core + 1), :] = yT[:, PSEG + SSEG * b:PSEG + SSEG * (b + 1)].T
    return yp, ys


def kernel(**inputs):
    return run(inputs, 4, False)


def _b_init(self):
    dt = self.nc.dram_tensor
    if hasattr(self, "KSb"):
        return
    self.KSb = [dt("KSb%d" % g, [2048, LT], BF16).ap() for g in range(3)]
    self.VSb = [dt("VSb%d" % g, [LT, 2048], BF16).ap() for g in range(3)]
    rels = {0: (), 1: (-1, 1), 2: (-2, -1, 1, 2)}
    self.NKb = [{rel: dt("NKb%d_%d" % (g, rel + 2), [2048, LT], BF16).ap() for rel in rels[g]} for g in range(3)]
    self.NVb = [{rel: dt("NVb%d_%d" % (g, rel + 2), [LT, 2048], BF16).ap() for rel in rels[g]} for g in range(3)]
    self.KB0 = dt("KB0", [2048, 640], BF16).ap()
    self.VB0 = dt("VB0", [640, 2048], BF16).ap()
    self.NKB0 = {rel: dt("NKB0_%d" % (rel + 2), [2048, 640], BF16).ap() for rel in (-1, 1)}
    self.NVB0 = {rel: dt("NVB0_%d" % (rel + 2), [640, 2048], BF16).ap() for rel in (-1, 1)}


def _b_phase1(self, li):
    _b_init(self)
    wn = "l%d_b_w_qkv" % li
    xsrc, xname = (self.XS, "XS")
    self.phase_begin()
    ar = self.ar
    self.alloc_common()
    xts = [ar.alloc([KC, TT], F32, "xt%d" % i) for i in range(2)]
    hts = [ar.alloc([KC, TT], BF16, "ht%d" % i) for i in range(2)]
    ti = 0
    self.kb0_keys = []
    self.vb0_keys = []

    def b0_seg(t):
        return 0 if t < 2 else t - 1

    def b0_sides(t):
        return (0,) if t == 0 else ((1,) if t == 1 else (0, 1))
    for g in (2, 1, 0):
        jobs = []
        for t in range(NT):
            c0 = t * TT
            for b12 in range(12):
                blk = g * 12 + b12
                kind = b12 // 4
                hb4 = b12 % 4
                slot = ti % 2
                xt, xt_k = xts[slot]
                ht, ht_k = hts[slot]
                nslot = (ti + 1) % 2

                def fn(wv, wkey, t=t, c0=c0, b12=b12, g=g, kind=kind, hb4=hb4, xt=xt, xt_k=xt_k, ht=ht, ht_k=ht_k, nslot=nslot, ti=ti):
                    if b12 == 0:
                        if ti == 0:
                            self.load_x_tile(xsrc, xname, t, xt, xt_k)
                        if ti + 1 < 3 * NT:
                            self.load_x_tile(xsrc, xname, (t + 1) % NT, *xts[nslot])
                        self.rmsnorm(xt, xt_k, KC, TT, "gvec", (2 * li) * 16, ht, ht_k, self.ntmp)
                    rhs = lambda kc: ht[:, kc, :]
                    if kind == 0:
                        def cons(m, psum, psk):
                            self.store_stage(psum, psk, self.QS[g * 16 + m, :, c0:c0 + TT], ("QS", g * 16 + m, t))
                        self.lin_fm(wv, wkey, KC, 4, rhs, [ht_k], TT, cons, m0=hb4 * 4)
                    elif kind == 1:
                        def cons(m, psum, psk):
                            st, stk = self.stg16.next()
                            self.evac(st[:, 0:TT], psum[:, 0:TT], [psk], [stk])
                            self.dma(self.KSb[g][m * 128:(m + 1) * 128, c0:c0 + TT], st[:, 0:TT], [stk], [("KS", g, m, t)])
                            if g == 0:
                                for side in b0_sides(t):
                                    bc = (b0_seg(t) * 2 + side) * 64
                                    a0 = 0 if side == 0 else TT - 64
                                    key = ("KB0", m, t, side)
                                    self.kb0_keys.append(key)
                                    self.dma(self.KB0[m * 128:(m + 1) * 128, bc:bc + 64], st[:, a0:a0 + 64], [stk], [key])
                        self.lin_fm(wv, wkey, KC, 4, rhs, [ht_k], TT, cons, m0=hb4 * 4)
                    else:
                        def cons(s_, psum, psk):
                            st, stk = self.stg16.next()
                            self.evac(st[:, 0:TT], psum[:, 0:TT], [psk], [stk])
                            self.dma(self.VSb[g][c0 + s_ * 128:c0 + (s_ + 1) * 128, hb4 * 512:(hb4 + 1) * 512], st[:, 0:TT], [stk],
                                     [("VS", g, hb4, s_, t)])
                            if g == 0:
                                for side in b0_sides(t):
                                    if (side == 0 and s_ == 0) or (side == 1 and s_ == 3):
                                        bc = (b0_seg(t) * 2 + side) * 64
                                        p0 = 0 if side == 0 else 64
                                        key = ("VB0", hb4, t, side)
                                        self.vb0_keys.append(key)
                                        self.dma(self.VB0[bc:bc + 64, hb4 * 512:(hb4 + 1) * 512], st[p0:p0 + 64, 0:TT], [stk], [key])
                        self.lin_tm(lambda kc: wv[:, kc, 0:512], wkey, KC, 512,
                                    lambda kc, s_: ht[:, kc, s_ * 128:(s_ + 1) * 128], [ht_k], 4, cons)
                jobs.append((wn, KC, blk * 512, 512, fn))
            ti += 1
        self.run_jobs(jobs)
        if g == 0:
            K8v, kk, (kn, kr) = self.allgather(self.KB0, 2048, 640, "KB0", self.kb0_keys, "K8B0", bg=True)
            V8v, vk, (vn, vr) = self.allgather(self.VB0, 640, 2048, "VB0", self.vb0_keys, "V8B0", bg=True)
            for rel in (-1, 1):
                self.localize(self.NKB0[rel].rearrange("(n i) c -> n i c", i=kr), K8v, rel, kk, [("NKb", 0, rel)], eng="pool", bg=True)
                self.localize(self.NVB0[rel].rearrange("(n i) c -> n i c", i=vr), V8v, rel, vk, [("NVb", 0, rel)], eng="pool", bg=True)
            continue
        ksend = [("KS", g, m, t) for m in range(16) for t in range(NT)]
        vsend = [("VS", g, hb4, s_, t) for hb4 in range(4) for s_ in range(4) for t in range(NT)]
        K8v, kk, (kn, kr) = self.allgather(self.KSb[g], 2048, LT, "Kb%d" % g, ksend, "K8b%d" % g, bg=True)
        V8v, vk, (vn, vr) = self.allgather(self.VSb[g], LT, 2048, "Vb%d" % g, vsend, "V8b%d" % g, bg=True)
        for rel in self.NKb[g].keys():
            self.localize(self.NKb[g][rel].rearrange("(n i) c -> n i c", i=kr), K8v, rel, kk, [("NKb", g, rel)], eng="pool", bg=True)
            self.localize(self.NVb[g][rel].rearrange("(n i) c -> n i c", i=vr), V8v, rel, vk, [("NVb", g, rel)], eng="pool", bg=True)


def _b_phase3(self, li):
    self.phase_begin()
    if li + 1 < self.n_layers:
        self.convert_layer(li + 1)
    ar = self.ar
    sl = slopes16()
    scale = HD ** -0.5
    KTs = Rot([ar.alloc([2560], BF16, "KTw%d" % i) for i in range(3)])
    Vws = Rot([ar.alloc([4096], BF16, "Vw%d" % i) for i in range(3)])
    QTs = Rot([ar.alloc([512], BF16, "QT%d" % i) for i in range(3)])
    tmps = Rot([ar.alloc([256], F32, "tmp%d" % i) for i in range(4)])
    PTs = Rot([ar.alloc([256], BF16, "PT%d" % i) for i in range(4)])
    NUMs = Rot([ar.alloc([512], F32, "NUM%d" % i) for i in range(3)])
    DENs = Rot([ar.alloc([512], F32, "DEN%d" % i) for i in range(3)])
    if not hasattr(self, "NUMD"):
        self.NUMD = self.nc.dram_tensor("NUMD", [NH, 128, LT], F32).ap()
        self.DEND = self.nc.dram_tensor("DEND", [NH, 128, LT], F32).ap()
    GORDER = (2, 1, 0)
    oats = Rot([ar.alloc([512], BF16, "oat%d" % i) for i in range(3)])
    RB0 = CF["RB"]
    cnt = 0
    sidx = 0
    groups = []
    for g in GORDER:
        (window, d) = B_GROUPS[g]
        for t in range(NT):
            c0 = t * TT
            tt_ = tile_type(t)
            for h in range(NH):
                NUM, NUM_k = NUMs.next()
                DEN, DEN_k = DENs.next()
                nq = TT // d
                nj = nq + 128
                nblk = (nj + 127) // 128
                W = TT + 128 * d
                KTf, KT_k = KTs.next()
                Vwf, Vw_k = Vws.next()
                KT = KTf[:, 0:W]
                Vw = Vwf[:, 0:d * nblk * 128].rearrange("p (r b x) -> p r b x", r=d, b=nblk)
                QT, QT_k = QTs.next()

                def pre(t=t, c0=c0, h=h, g=g, d=d, W=W, KT=KT, KT_k=KT_k, Vw=Vw, Vw_k=Vw_k, QT=QT, QT_k=QT_k,
                        NUM=NUM, NUM_k=NUM_k, DEN=DEN, DEN_k=DEN_k):
                    self.dma(QT, self.QS[g * 16 + h, :, c0:c0 + TT], [], [QT_k])
                    if g != GORDER[0]:
                        self.dma(NUM, self.NUMD[h, :, c0:c0 + TT], [("NUMD", h, t)], [NUM_k])
                        self.dma(DEN, self.DEND[h, :, c0:c0 + TT], [("DEND", h, t)], [DEN_k])
                    w0 = 0
                    for (rel, lc, ln) in window_pieces(t, 64 * d):
                        if g == 0 and rel != 0:
                            assert ln == 64
                            seg_ = 0 if lc < PSEG else 1 + (lc - PSEG) // SSEG
                            lc = (seg_ * 2 + (1 if rel < 0 else 0)) * 64
                            ksrc, vsrc = self.NKB0[rel], self.NVB0[rel]
                        else:
                            ksrc = self.KSb[g] if rel == 0 else self.NKb[g][rel]
                            vsrc = self.VSb[g] if rel == 0 else self.NVb[g][rel]
                        kdep = [] if rel == 0 else [("NKb", g, rel)]
                        vdep = [] if rel == 0 else [("NVb", g, rel)]
                        self.dma(KT[:, w0:w0 + ln], ksrc[h * 128:(h + 1) * 128, lc:lc + ln], kdep, [KT_k])
                        ja, je = w0 // d, (w0 + ln) // d
                        j = ja
                        while j < je:
                            b = j // 128
                            jn = min(je, (b + 1) * 128)
                            p0 = j % 128
                            n = jn - j
                            r0 = lc + (j - ja) * d
                            self.dma(Vw[p0:p0 + n, :, b, :],
                                     vsrc[r0:r0 + n * d, h * 128:(h + 1) * 128].rearrange("(jj r) x -> jj r x", r=d), vdep, [Vw_k])
                            j = jn
                        w0 += ln
                    assert w0 == W
                num, num_k = self.ps[3 + cnt % 2]
                den, den_k = self.ps[5 + cnt % 2]
                cnt += 1
                coef = -sl[h] * d / scale
                blocks = []
                for r in range(d):
                    for b in range(nblk):
                        nk = min(128, nj - 128 * b)
                        q_lo = max(0, 128 * b - 128)
                        q_hi = min(nq, 128 * b + nk)
                        n = q_hi - q_lo
                        cb = q_lo - (128 * b - 128)
                        S, S_k = self.ps[(0, 1, 2, 7)[sidx % 4]]
                        sidx += 1
                        tmp, tmp_k = tmps.next()
                        PT, PT_k = PTs.next()
                        k0 = 128 * b * d + r
                        q0 = q_lo * d + r
                        eo = CF["EB"] + (tt_ * 3 + g) * 5 + b
                        first = (r == 0 and b == 0)
                        last = (r == d - 1 and b == nblk - 1)
                        o0 = r * nq + q_lo

                        def fS(S=S, S_k=S_k, KT=KT, KT_k=KT_k, QT=QT, QT_k=QT_k, k0=k0, q0=q0, nk=nk, n=n, d=d):
                            self.pe(lambda e: e.matmul(S[0:nk, 0:n], KT[:, k0:k0 + (nk - 1) * d + 1:d], QT[:, q0:q0 + (n - 1) * d + 1:d],
                                                       start=True, stop=True), [KT_k, QT_k], [S_k])

                        def fsoft(S=S, S_k=S_k, tmp=tmp, tmp_k=tmp_k, PT=PT, PT_k=PT_k, cb=cb, n=n, nk=nk, coef=coef, eo=eo):
                            self.dve(lambda e: e.scalar_tensor_tensor(out=tmp[0:nk, 0:n], in0=self.cf[0:nk, RB0 + cb:RB0 + cb + n], scalar=coef,
                                                                      in1=S[0:nk, 0:n], op0=ALU.mult, op1=ALU.add),
                                     [S_k, self.cf_k], [tmp_k])
                            self.act(lambda e: e.activation(out=PT[0:nk, 0:n], in_=tmp[0:nk, 0:n], func=AF.Exp,
                                                            bias=self.cf[0:nk, eo:eo + 1], scale=scale),
                                     [tmp_k, self.cf_k], [PT_k])

                        def fPV(num=num, num_k=num_k, den=den, den_k=den_k, Vw=Vw, Vw_k=Vw_k, PT=PT, PT_k=PT_k, r=r, b=b, nk=nk, n=n,
                                o0=o0, first=first, last=last):
                            self.pe(lambda e: e.matmul(num[:, o0:o0 + n], Vw[0:nk, r, b, :], PT[0:nk, 0:n], start=first, stop=last,
                                                       skip_group_check=True), [Vw_k, PT_k], [num_k])
                            self.pe(lambda e: e.matmul(den[:, o0:o0 + n], self.ones[0:nk, :], PT[0:nk, 0:n], start=first, stop=last,
                                                       skip_group_check=True), [self.ones_k, PT_k], [den_k])
                        post = None
                        if last:
                            def post(g=g, d=d, h=h, t=t, c0=c0, num=num, num_k=num_k, den=den, den_k=den_k,
                                     NUM=NUM, NUM_k=NUM_k, DEN=DEN, DEN_k=DEN_k):
                                for (ACC, ACC_k, src, src_k) in ((NUM, NUM_k, num, num_k), (DEN, DEN_k, den, den_k)):
                                    if d == 1:
                                        accv, srcv = ACC, src[:, 0:TT]
                                    else:
                                        accv = ACC.rearrange("p (q r) -> p r q", r=d)
                                        srcv = src[:, 0:TT].rearrange("p (r q) -> p r q", r=d)
                                    if g == GORDER[0]:
                                        self.dve(lambda e, accv=accv, srcv=srcv: e.tensor_copy(accv, srcv), [src_k], [ACC_k])
                                    else:
                                        self.dve(lambda e, accv=accv, srcv=srcv: e.tensor_tensor(out=accv, in0=accv, in1=srcv, op=ALU.add),
                                                 [src_k, ACC_k], [ACC_k])
                                if g == GORDER[-1]:
                                    oat, oat_k = oats.next()
                                    self.dve(lambda e: e.reciprocal(DEN, DEN), [DEN_k], [DEN_k])
                                    self.dve(lambda e: e.tensor_tensor(out=oat, in0=NUM, in1=DEN, op=ALU.mult), [NUM_k, DEN_k], [oat_k])
                                    self.dma(self.ATT[h, :, c0:c0 + TT], oat, [oat_k], [("ATT", h, t)])
                                else:
                                    self.dma(self.NUMD[h, :, c0:c0 + TT], NUM, [NUM_k], [("NUMD", h, t)])
                                    self.dma(self.DEND[h, :, c0:c0 + TT], DEN, [DEN_k], [("DEND", h, t)])
                        blocks.append((fS, fsoft, fPV, post))
                groups.append((pre, blocks))
    self.run_attn(groups)


Builder.b_phase1 = _b_phase1
Builder.b_phase3 = _b_phase3


C_SCALE = (128 + 64) ** -0.5


def _c_init(self):
    dt = self.nc.dram_tensor
    if hasattr(self, "KSc"):
        return
    self.KSc = dt("KSc", [2112, LT], BF16).ap()
    self.VSc = dt("VSc", [LT, 2048], BF16).ap()


def _rope(self, x1, x1_k, x2, x2_k, cs, sn, csn_k, o1, o2, o_k, tmps):
    (ta, ta_k), (tb, tb_k) = tmps.next(), tmps.next()
    P32 = slice(0, 32)
    self.dve(lambda e: e.tensor_tensor(out=ta[P32, :], in0=x1[P32, 0:TT], in1=cs[P32, :], op=ALU.mult), [x1_k, csn_k], [ta_k])
    self.dve(lambda e: e.tensor_tensor(out=tb[P32, :], in0=x2[P32, 0:TT], in1=sn[P32, :], op=ALU.mult), [x2_k, csn_k], [tb_k])
    self.dve(lambda e: e.tensor_tensor(out=o1[P32, :], in0=ta[P32, :], in1=tb[P32, :], op=ALU.subtract), [ta_k, tb_k], [o_k])
    (tc, tc_k), (td, td_k) = tmps.next(), tmps.next()
    self.dve(lambda e: e.tensor_tensor(out=tc[P32, :], in0=x2[P32, 0:TT], in1=cs[P32, :], op=ALU.mult), [x2_k, csn_k], [tc_k])
    self.dve(lambda e: e.tensor_tensor(out=td[P32, :], in0=x1[P32, 0:TT], in1=sn[P32, :], op=ALU.mult), [x1_k, csn_k], [td_k])
    self.dve(lambda e: e.tensor_tensor(out=o2[P32, :], in0=tc[P32, :], in1=td[P32, :], op=ALU.add), [tc_k, td_k], [o_k])


def _c_phase1(self, li):
    _c_init(self)
    wd, wuq, wukv = "l%d_c_w_down" % li, "l%d_c_w_uq" % li, "l%d_c_w_ukv" % li
    self.phase_begin()
    ar = self.ar
    self.alloc_common()
    xt, xt_k = ar.alloc([KC, TT], F32, "xt")
    hts = [ar.alloc([KC, TT], BF16, "ht%d" % i) for i in range(2)]
    c32, c32_k = ar.alloc([4, TT], F32, "c32")
    cqn, cqn_k = ar.alloc([4, TT], BF16, "cqn")
    ckvn, ckvn_k = ar.alloc([4, TT], BF16, "ckvn")
    cs, csn_k = ar.alloc([TT], F32, "cos")
    sn, _ = ar.alloc([TT], F32, "sin")
    rtmps = Rot([ar.alloc([TT], F32, "rt%d" % i) for i in range(4)])
    ropo = Rot([(ar.alloc([TT], BF16, "ro1_%d" % i), ar.alloc([TT], BF16, "ro2_%d" % i)) for i in range(2)])
    jobs = []
    for t in range(NT):
        c0 = t * TT
        ht, ht_k = hts[t % 2]

        def j_down(wv, wkey, which, t=t, c0=c0, ht=ht, ht_k=ht_k):
            if which == 0:
                self.load_x_tile(self.XS, "XS", t, xt, xt_k)
                self.dma(cs[0:32, :], self.rope_d[0, :, c0:c0 + TT], [], [csn_k])
                self.dma(sn[0:32, :], self.rope_d[1, :, c0:c0 + TT], [], [csn_k])
                self.rmsnorm(xt, xt_k, KC, TT, "gvec", (2 * li) * 16, ht, ht_k, self.ntmp)
            rhs = lambda kc: ht[:, kc, :]
            if which < 2:
                def cons(m, psum, psk):
                    self.evac(c32[:, m, :], psum[:, 0:TT], [psk], [c32_k])
                self.lin_fm(wv, wkey, KC, 4, rhs, [ht_k], TT, cons)
                dst, dst_k = (cqn, cqn_k) if which == 0 else (ckvn, ckvn_k)
                self.rmsnorm(c32, c32_k, 4, TT, "cnorm", which * 4, dst, dst_k, self.ntmp)
            else:
                got = {}

                def cons(m, psum, psk):
                    got[m] = (psum, psk)
                self.lin_fm(wv, wkey, KC, 2, rhs, [ht_k], TT, cons, mw=32)
                (o1, o1_k), (o2, o2_k) = ropo.next()
                _rope(self, got[0][0], got[0][1], got[1][0], got[1][1], cs, sn, csn_k, o1, o2, o1_k, rtmps)
                self.dma(self.KSc[2048:2080, c0:c0 + TT], o1[0:32, :], [o1_k], [("KSr", 0, t)])
                self.dma(self.KSc[2080:2112, c0:c0 + TT], o2[0:32, :], [o1_k], [("KSr", 1, t)])
        jobs.append((wd, KC, 0, 512, lambda wv, wkey, f=j_down: f(wv, wkey, 0)))
        jobs.append((wd, KC, 512, 512, lambda wv, wkey, f=j_down: f(wv, wkey, 1)))
        jobs.append((wd, KC, 1024, 64, lambda wv, wkey, f=j_down: f(wv, wkey, 2)))
        for hf in range(2):
            def j_uq(wv, wkey, hf=hf, t=t, c0=c0):
                for hl in range(8):
                    h = hf * 8 + hl
                    base = hl * 192
                    psum, psk = self.psrot.next()
                    p1, p1k = self.psrot.next()
                    p2, p2k = self.psrot.next()
                    for (pp, ppk, off, mw) in ((psum, psk, base, 128), (p1, p1k, base + 128, 32), (p2, p2k, base + 160, 32)):
                        for kc in range(4):
                            self.pe(lambda e, pp=pp, off=off, mw=mw, kc=kc: e.matmul(pp[0:mw, 0:TT], wv[:, kc, off:off + mw], cqn[:, kc, :],
                                                                                 start=(kc == 0), stop=(kc == 3)),
                                    [wkey, cqn_k], [ppk])
                    self.store_stage(psum, psk, self.QS[h, :, c0:c0 + TT], ("QS", h, t))
                    (o1, o1_k), (o2, o2_k) = ropo.next()
                    _rope(self, p1, p1k, p2, p2k, cs, sn, csn_k, o1, o2, o1_k, rtmps)
                    self.dma(self.QR[h, 0:32, c0:c0 + TT], o1[0:32, :], [o1_k], [("QR", h, 0, t)])
                    self.dma(self.QR[h, 32:64, c0:c0 + TT], o2[0:32, :], [o1_k], [("QR", h, 1, t)])
            jobs.append((wuq, 4, hf * 1536, 1536, j_uq))
        for hf in range(2):
            def j_ukv(wv, wkey, hf=hf, t=t, c0=c0):
                for hl in range(8):
                    h = hf * 8 + hl
                    psum, psk = self.psrot.next()
                    for kc in range(4):
                        self.pe(lambda e, psum=psum, hl=hl, kc=kc: e.matmul(psum[:, 0:TT], wv[:, kc, hl * 256:hl * 256 + 128], ckvn[:, kc, :],
                                                                          start=(kc == 0), stop=(kc == 3)),
                                [wkey, ckvn_k], [psk])
                    self.store_stage(psum, psk, self.KSc[h * 128:(h + 1) * 128, c0:c0 + TT], ("KS", h, t))
                wvv = wv.rearrange("p k (h x) -> p k h x", x=256)
                for q4 in range(2):
                    h0 = hf * 8 + q4 * 4
                    for s in range(4):
                        psum, psk = self.psrot.next()
                        for kc in range(4):
                            self.pe(lambda e, psum=psum, q4=q4, s=s, kc=kc: e.matmul(psum[:, 0:512], ckvn[:, kc, s * 128:(s + 1) * 128],
                                                                                  wvv[:, kc, q4 * 4:q4 * 4 + 4, 128:256],
                                                                                  start=(kc == 0), stop=(kc == 3)),
                                    [wkey, ckvn_k], [psk])
                        self.store_stage(psum, psk, self.VSc[c0 + s * 128:c0 + (s + 1) * 128, h0 * 128:(h0 + 4) * 128], ("VS", h0, s, t))
            jobs.append((wukv, 4, hf * 2048, 2048, j_ukv))
    self.run_jobs(jobs)
    ksend = [("KS", h, t) for h in range(NH) for t in range(NT)] + [("KSr", i, t) for i in range(2) for t in range(NT)]
    vsend = [("VS", h0, s, t) for h0 in range(0, 16, 4) for s in range(4) for t in range(NT)]
    self.cK8v, self.cKk, (kn, kr) = self.allgather(self.KSc, 2112, LT, "Kc", ksend, "K8c", rpc_force=64)
    self.cV8v, self.cVk, (vn, vr) = self.allgather(self.VSc, LT, 2048, "Vc", vsend, "V8c")
    assert kr == 64 and vr == 128


def _c_phase3(self, li):
    self.phase_begin()
    if li + 1 < self.n_layers:
        self.convert_layer(li + 1)
    ar = self.ar
    K8v, V8v = self.cK8v, self.cV8v
    KTs = Rot([ar.alloc([8192], BF16, "cKT%d" % i) for i in range(2)])
    Vps = Rot([ar.alloc([64, 128], BF16, "cVp%d" % i) for i in range(2)])
    KRs = Rot([ar.alloc([8192], BF16, "cKR%d" % i) for i in range(2)])
    QNs = Rot([ar.alloc([512], BF16, "cQN%d" % i) for i in range(3)])
    QRs = Rot([ar.alloc([512], BF16, "cQR%d" % i) for i in range(3)])
    PTs = Rot([ar.alloc([512], BF16, "cPT%d" % i) for i in range(4)])
    recs = Rot([ar.alloc([512], F32, "crec%d" % i) for i in range(2)])
    oats = Rot([ar.alloc([512], BF16, "coat%d" % i) for i in range(3)])
    seqs = [(PSEG, 0, [0, 1])] + [(SSEG, PSEG + SSEG * b, [2 + b]) for b in range(4)]
    cnt = 0
    sidx = 0
    groups = []
    for (seg, lc0, tiles) in seqs:
        L = seg * 8
        nkb = L // 128
        KR, KR_k = KRs.next()
        for h in range(NH):
            KT, KT_k = KTs.next()
            Vp, Vp_k = Vps.next()

            def pre(seg=seg, lc0=lc0, h=h, KR=KR, KR_k=KR_k, KT=KT, KT_k=KT_k, Vp=Vp, Vp_k=Vp_k):
                if h == 0:
                    for r in range(8):
                        self.dma(KR[0:64, r * seg:(r + 1) * seg], K8v[r, 32, :, lc0:lc0 + seg], [], [KR_k])
                nb = seg // 128
                for r in range(8):
                    for n2 in range(2):
                        self.dma(KT[64 * n2:64 * n2 + 64, r * seg:(r + 1) * seg], K8v[r, 2 * h + n2, :, lc0:lc0 + seg], [], [KT_k])
                    self.dma(Vp[:, r * nb:(r + 1) * nb, :],
                             V8v[r, lc0 // 128:lc0 // 128 + nb, :, h * 128:(h + 1) * 128].rearrange("n p d -> p n d"), [], [Vp_k])
            blocks = []
            for t in tiles:
                c0 = t * TT
                QN, QN_k = QNs.next()
                QRt, QR_k = QRs.next()
                num, num_k = self.ps[4 + cnt % 2]
                den, den_k = self.ps[6 + cnt % 2]
                cnt += 1
                for kb in range(nkb):
                    S, S_k = self.ps[sidx % 4]
                    sidx += 1
                    PT, PT_k = PTs.next()

                    def fS(S=S, S_k=S_k, KT=KT, KT_k=KT_k, KR=KR, KR_k=KR_k, QN=QN, QN_k=QN_k, QRt=QRt, QR_k=QR_k, kb=kb, h=h, c0=c0):
                        if kb == 0:
                            self.dma(QN, self.QS[h, :, c0:c0 + TT], [], [QN_k])
                            self.dma(QRt[0:64, :], self.QR[h, :, c0:c0 + TT], [], [QR_k])
                        self.pe(lambda e: e.matmul(S[:, 0:TT], KT[:, kb * 128:(kb + 1) * 128], QN, start=True, stop=False),
                                [KT_k, QN_k], [S_k])
                        self.pe(lambda e: e.matmul(S[:, 0:TT], KR[0:64, kb * 128:(kb + 1) * 128], QRt[0:64, :], start=False, stop=True),
                                [KR_k, QR_k], [S_k])

                    def fsoft(S=S, S_k=S_k, PT=PT, PT_k=PT_k):
                        self.act(lambda e: e.activation(out=PT, in_=S[:, 0:TT], func=AF.Exp, scale=C_SCALE), [S_k], [PT_k])

                    def fPV(num=num, num_k=num_k, den=den, den_k=den_k, Vp=Vp, Vp_k=Vp_k, PT=PT, PT_k=PT_k, kb=kb, nkb=nkb):
                        self.pe(lambda e: e.matmul(num[:, 0:TT], Vp[:, kb, :], PT, start=(kb == 0), stop=(kb == nkb - 1)),
                                [Vp_k, PT_k], [num_k])
                        self.pe(lambda e: e.matmul(den[:, 0:TT], self.ones[:, :], PT, start=(kb == 0), stop=(kb == nkb - 1)),
                                [self.ones_k, PT_k], [den_k])
                    post = None
                    if kb == nkb - 1:
                        def post(num=num, num_k=num_k, den=den, den_k=den_k, h=h, t=t, c0=c0):
                            rec, rec_k = recs.next()
                            oat, oat_k = oats.next()
                            self.dve(lambda e: e.reciprocal(rec, den[:, 0:TT]), [den_k], [rec_k])
                            self.dve(lambda e: e.tensor_tensor(out=oat, in0=num[:, 0:TT], in1=rec, op=ALU.mult), [num_k, rec_k], [oat_k])
                            self.dma(self.ATT[h, :, c0:c0 + TT], oat, [oat_k], [("ATT", h, t)])
                    blocks.append((fS, fsoft, fPV, post))
            groups.append((pre, blocks))
    self.run_attn(groups)


Builder.c_phase1 = _c_phase1
Builder.c_phase3 = _c_phase3
```

```python
import contextlib
import numpy as np
import ml_dtypes
import concourse.bass as bass
import concourse.mybir as mybir
from concourse.bass_utils import run_bass_kernel_spmd

F32 = mybir.dt.float32
BF16 = mybir.dt.bfloat16
AF = mybir.ActivationFunctionType
ALU = mybir.AluOpType

NCORES = 8


class Op:
    __slots__ = ("eng", "fn", "deps", "is_dma", "signal", "idx", "sem", "val", "prewait", "inc", "force", "raw", "bg")

    def __init__(self, eng, fn, is_dma, inc):
        self.eng = eng
        self.fn = fn
        self.deps = set()
        self.is_dma = is_dma
        self.signal = False
        self.sem = None
        self.val = None
        self.prewait = None
        self.inc = inc
        self.force = False
        self.raw = set()
        self.bg = False


ENGS = ("pe", "act", "dve", "pool", "sp")
SEM_ROT = 12000
N_DMA_SEMS = 12


class Prog:
    def __init__(self, nc, stack):
        self.nc = nc
        self.stack = stack
        self.ops = []
        self.eng_ops = {e: [] for e in ENGS}
        self.last_w = {}
        self.readers = {}
        self.dma_rr = {e: 0 for e in ENGS}
        self.dma_sems = {}
        self.dma_sem_last = {}
        self.eng_sems = {e: [] for e in ENGS}
        self.bar_from = 0
        self.prev_bar = []
        self.bg_last_w = {}

    def op(self, eng, fn, reads=(), writes=(), dma=False, inc=16, force=False, bg=False):
        o = Op(eng, fn, dma, inc)
        o.force = force
        o.bg = bg
        o.idx = len(self.ops)
        for b in reads:
            w = self.last_w.get(b)
            if w is not None:
                o.deps.add(w)
                o.raw.add(w)
        for b in writes:
            w = self.last_w.get(b)
            if w is not None:
                o.deps.add(w)
            for r in self.readers.get(b, ()):
                o.deps.add(r)
        for b in reads:
            self.readers.setdefault(b, []).append(o.idx)
        for b in writes:
            self.last_w[b] = o.idx
            self.readers[b] = []
            if bg:
                self.bg_last_w[b] = o.idx
        o.deps.discard(o.idx)
        self.ops.append(o)
        self.eng_ops[eng].append(o)
        return o

    def barrier(self):
        lasts = []
        for e in ("pe", "act", "dve"):
            for o in reversed(self.eng_ops[e]):
                if o.fn is not None and not o.is_dma:
                    lasts.append(o.idx)
                    break
        dmas = [o.idx for o in self.ops[self.bar_from:] if o.is_dma and not o.bg]
        self.bar_from = len(self.ops)
        prev = list(self.prev_bar)
        self.prev_bar = []
        for e in ENGS:
            o = Op(e, None, False, 0)
            o.idx = len(self.ops)
            o.deps = set(lasts) | set(dmas) | set(prev)
            self.ops.append(o)
            self.eng_ops[e].append(o)
            self.prev_bar.append(o.idx)
        self.last_w = dict(self.bg_last_w)
        self.readers = {}

    def finalize(self):
        nc = self.nc
        ops = self.ops
        for o in ops:
            for d in list(o.deps):
                do = ops[d]
                if (not do.is_dma) and do.eng == o.eng and not o.is_dma and not o.force and not (d in o.raw and o.eng != "pe"):
                    o.deps.discard(d)
                    continue
                do.signal = True
        cnt = {e: 0 for e in ENGS}
        for e in ENGS:
            for o in self.eng_ops[e]:
                if o.is_dma:
                    cc = "cc" if o.inc == 1 else "d"
                    rrk = (e, cc)
                    k = self.dma_rr.get(rrk, 0) % (N_DMA_SEMS if cc == "d" else 8)
                    self.dma_rr[rrk] = self.dma_rr.get(rrk, 0) + 1
                    key = (e, cc, k)
                    if key not in self.dma_sems:
                        self.dma_sems[key] = [self.stack.enter_context(nc.semaphore("%s_%s_%d" % (cc, e, k))), 0]
                    ent = self.dma_sems[key]
                    o.prewait = (ent[0], ent[1]) if ent[1] > 0 else None
                    ent[1] += o.inc
                    o.sem, o.val = ent[0], ent[1]
                elif o.signal and o.fn is not None:
                    ph = cnt[e] // SEM_ROT
                    while len(self.eng_sems[e]) <= ph:
                        self.eng_sems[e].append(
                            self.stack.enter_context(nc.semaphore("c_%s_%d" % (e, len(self.eng_sems[e])))))
                    cnt[e] += 1
                    o.sem = self.eng_sems[e][ph]
                    o.val = cnt[e] - ph * SEM_ROT
                elif o.signal and o.fn is None:
                    pass

        def resolve(d, acc, seen):
            do = ops[d]
            if do.fn is None:
                if d in seen:
                    return
                seen.add(d)
                for dd in do.deps:
                    resolve(dd, acc, seen)
                return
            key = id(do.sem)
            if key not in acc or acc[key][1] < do.val:
                acc[key] = (do.sem, do.val)

        self._resolve = resolve

        with nc.Block() as block:
            def run(e, handle_name):
                deco = getattr(block, handle_name)

                @deco
                def _(h):
                    known = {}
                    for o in self.eng_ops[e]:
                        acc = {}
                        seen = set()
                        for d in o.deps:
                            resolve(d, acc, seen)
                        if o.prewait is not None:
                            s, v = o.prewait
                            if id(s) not in acc or acc[id(s)][1] < v:
                                acc[id(s)] = (s, v)
                        for key, (s, v) in acc.items():
                            if known.get(key, 0) >= v:
                                continue
                            known[key] = v
                            h.wait_ge(s, v)
                        if o.fn is None:
                            continue
                        ins = o.fn(h)
                        if o.sem is not None:
                            if o.is_dma:
                                ins.then_inc(o.sem, o.inc)
                            else:
                                ins.then_inc(o.sem, 1)

            run("sp", "sync")
            run("pool", "gpsimd")
            run("act", "scalar")
            run("dve", "vector")
            run("pe", "tensor")


D = 2048
KC = 16
DFF = 5632
FC = 44
TT = 512
NT = 6
LT = 3072
HD = 128
NH = 16
EPS = 1e-6
PSEG = 1024
SSEG = 512
NEG = -30000.0
BIGR = 1.0e6
B_GROUPS = ((128, 1), (512, 4), (2048, 16))
I32 = mybir.dt.int32


def slopes16():
    return [2.0 ** (-8.0 * (h + 1) / 16.0) for h in range(16)]


def tile_cols(t):
    return t * TT


def tile_type(t):
    return 0 if t == 0 else (1 if t == 1 else 2)


def window_pieces(t, halo):
    base = 0 if t < 2 else PSEG + SSEG * (t - 2)
    seg = PSEG if t < 2 else SSEG
    off = TT * t if t < 2 else 0
    lo = off - halo
    hi = off + TT + halo
    pieces = []
    rel_lo = lo // seg
    rel_hi = (hi - 1) // seg
    for rel in range(rel_lo, rel_hi + 1):
        a = max(lo, rel * seg)
        b = min(hi, (rel + 1) * seg)
        pieces.append((rel, base + a - rel * seg, b - a))
    return pieces


class Arena:
    def __init__(self, ap, nwords):
        self.ap = ap
        self.n = nwords
        self.off = 0
        self.cnt = 0

    def alloc(self, shape, dtype, key=None):
        n = int(np.prod(shape))
        sz = 4 if dtype in (F32, I32) else 2
        words = (n * sz + 3) // 4
        words = (words + 15) // 16 * 16
        assert self.off + words <= self.n, ("arena overflow", self.off, words, self.n)
        a = self.ap[:, self.off:self.off + words]
        if dtype != F32:
            a = a.bitcast(dtype)
        a = a[:, 0:n]
        if len(shape) == 2:
            a = a.rearrange("p (a b) -> p a b", b=shape[1])
        elif len(shape) == 3:
            a = a.rearrange("p (a b c) -> p a b c", b=shape[1], c=shape[2])
        self.off += words
        self.cnt += 1
        return a, (key or ("ar%d" % self.cnt)) + "@%d" % self.off


class Rot:
    def __init__(self, items):
        self.items = items
        self.i = 0

    def next(self):
        it = self.items[self.i % len(self.items)]
        self.i += 1
        return it


def weight_specs():
    specs = []
    for i in range(4):
        pre = "l%d_" % i
        kind = i % 3
        if kind == 0:
            specs += [(pre + "a_w_qkv", 2048, 3072), (pre + "a_w_o", 2048, 2048)]
        elif kind == 1:
            specs += [(pre + "b_w_qkv", 2048, 18432), (pre + "b_w_o", 2048, 2048)]
        else:
            specs += [(pre + "c_w_down", 2048, 1088), (pre + "c_w_uq", 512, 3072),
                      (pre + "c_w_ukv", 512, 4096), (pre + "c_w_o", 2048, 2048)]
        specs += [(pre + "ffn_w_in", 2048, 11264), (pre + "ffn_w_out", 5632, 2048)]
    return specs


CF = {}
_o = 0
for _n, _w in (("gvec", 144), ("convw", 528), ("convb", 176), ("sink", 32), ("cnorm", 8), ("RA", 384),
               ("RB", 256), ("EA", 18), ("EB", 45), ("fl", 2)):
    CF[_n] = _o
    _o += _w
NCF = _o


class Builder:
    def __init__(self, n_layers=4, stop_mid=False, arena_words=46000):
        self.n_layers = n_layers
        self.stop_mid = stop_mid
        self.nc = bass.Bass("TRN2", target_bir_lowering=False)
        nc = self.nc
        self.st = contextlib.ExitStack()
        self.P = Prog(nc, self.st)
        self.x0 = nc.dram_tensor("x0T", [D, LT], F32, kind="ExternalInput").ap()
        self.cf_d = nc.dram_tensor("cf32", [128, NCF], F32, kind="ExternalInput").ap()
        self.rope_d = nc.dram_tensor("rope", [2, 32, LT], F32, kind="ExternalInput").ap()
        self.nb_d = nc.dram_tensor("nb", [1, 8], I32, kind="ExternalInput").ap()
        self.yT = nc.dram_tensor("yT", [D, LT], F32, kind="ExternalOutput").ap()
        self.w32 = {}
        self.wb = {}
        self.wkeys = {}
        for name, k, n in weight_specs():
            if int(name[1]) >= n_layers:
                continue
            self.w32[name] = nc.dram_tensor(name, [k, n], F32, kind="ExternalInput").ap()
            self.wb[name] = nc.dram_tensor("wb_" + name, [k, n], BF16).ap()
        dt = nc.dram_tensor
        self.XS = dt("XS", [KC, 128, LT], F32).ap()
        self.XM = dt("XM", [KC, 128, LT], F32).ap()
        self.H2 = dt("H2", [KC, 128, LT], BF16).ap()
        self.ATT = dt("ATT", [NH, 128, LT], BF16).ap()
        self.QS = dt("QS", [48, 128, LT], BF16).ap()
        self.QR = dt("QR", [NH, 64, LT], BF16).ap()
        self.KSa = dt("KSa", [512, LT], BF16).ap()
        self.NKa = {rel: dt("NKa%d" % (rel + 2), [512, 1280], BF16).ap() for rel in (-1, 1)}
        self.NVa = {rel: dt("NVa%d" % (rel + 2), [1280, 512], BF16).ap() for rel in (-1, 1)}
        self.KA0 = dt("KA0", [512, 1280], BF16).ap()
        self.VA0 = dt("VA0", [1280, 512], BF16).ap()
        self.NHB = {rel: dt("NHB%d" % (rel + 2), [128, 160], BF16).ap() for rel in (-1, 1)}
        self.VSa = dt("VSa", [LT, 512], BF16).ap()
        self.HBs = dt("HBs", [128, 160], BF16).ap()
        self.arena_t = self.st.enter_context(nc.sbuf_tensor("arena", [128, arena_words], F32))
        self.ar = Arena(self.arena_t[:, :], arena_words)
        self.ps = []
        for i in range(8):
            t = self.st.enter_context(nc.psum_tensor("ps%d" % i, [128, 512], F32))
            self.ps.append((t[:, :], "ps%d" % i))
        self.psrot = Rot(self.ps)
        self.evac_i = 0
        self.regv = {}
        self.ag_bufs = {}
        self.attn_lookahead = 2

    def dma(self, out, in_, r, w, eng="sp"):
        return self.P.op(eng, lambda e: e.dma_start(out=out, in_=in_), reads=r, writes=w, dma=True)

    def localize(self, dst, g8view, rel, r, w, eng="sp", bg=False):
        self.dyn_cnt[eng] += 1
        assert self.dyn_cnt[eng] <= 21, "dynamic DMA register budget exceeded"

        def fn(e):
            v = self.regv[(eng, rel)]
            return e.dma_start(out=dst, in_=g8view[bass.ds(v, 1)])
        return self.P.op(eng, fn, reads=r, writes=w, dma=True, bg=bg)

    def pe(self, fn, r, w):
        return self.P.op("pe", fn, reads=r, writes=w)

    def act(self, fn, r, w):
        return self.P.op("act", fn, reads=r, writes=w)

    def dve(self, fn, r, w, force=False):
        return self.P.op("dve", fn, reads=r, writes=w, force=force)

    def evac(self, out, in_, r, w):
        self.evac_i += 1
        if self.evac_i % 2 == 0:
            return self.act(lambda e: e.activation(out=out, in_=in_, func=AF.Copy), r, w)
        return self.dve(lambda e: e.tensor_copy(out, in_), r, w)

    def allgather(self, send, R, C, name, key_send, key_out, rpc_force=None, bg=False):
        nc = self.nc
        rpc = 1
        for cand in range(1, R + 1):
            if R % cand == 0 and cand * C * 2 <= 512 * 1024:
                rpc = cand
        if rpc_force:
            rpc = rpc_force
        nch = R // rpc
        if name not in self.ag_bufs:
            self.ag_bufs[name] = (nc.dram_tensor("g4_" + name, [nch * 4 * rpc, C], BF16).ap(),
                                  nc.dram_tensor("g8_" + name, [nch * 8 * rpc, C], BF16).ap())
        g4, g8 = self.ag_bufs[name]
        for stage in (1, 2):
            for c in range(nch):
                s_ap = send[c * rpc:(c + 1) * rpc, :]
                g4c = g4[c * 4 * rpc:(c + 1) * 4 * rpc, :]
                g8c = g8[c * 8 * rpc:(c + 1) * 8 * rpc, :]
                k4 = (key_out, "g4", c)
                if stage == 1:
                    def c1(e, s_ap=s_ap, g4c=g4c):
                        return e.collective_compute("AllGather", ALU.bypass, replica_groups=[[0, 1, 2, 3], [4, 5, 6, 7]],
                                                    ins=[s_ap.opt()], outs=[g4c.opt()])
                    self.P.op("pool", c1, reads=key_send, writes=[k4], dma=True, inc=1, bg=bg)
                else:
                    def c2(e, g4c=g4c, g8c=g8c):
                        return e.collective_compute("AllGather", ALU.bypass, replica_groups=[[0, 4], [1, 5], [2, 6], [3, 7]],
                                                    ins=[g4c.opt()], outs=[g8c.opt()])
                    self.P.op("pool", c2, reads=[k4], writes=[(key_out, c)], dma=True, inc=1, bg=bg)
        keys = [(key_out, c) for c in range(nch)]
        return g8.rearrange("(n r i) c -> r n i c", r=8, i=rpc), keys, (nch, rpc)

    def convert_layer(self, li):
        for name, k, n in weight_specs():
            if name not in self.w32 or int(name[1]) != li:
                continue
            rows = 64 if n > 4096 else 256
            keys = []
            for r0 in range(0, k, rows):
                r1 = min(k, r0 + rows)
                key = ("wb", name, r0)
                keys.append(key)
                self.P.op("pool", lambda e, r0=r0, r1=r1, name=name: e.dma_start(out=self.wb[name][r0:r1, :], in_=self.w32[name][r0:r1, :]),
                          reads=[], writes=[key], dma=True, bg=True)
            self.wkeys[name] = keys

    def setup(self):
        ar = self.ar
        P = self.P
        self.cf, self.cf_k = ar.alloc([NCF], F32, "cf")
        self.cf = self.cf
        self.dma(self.cf, self.cf_d, [], [self.cf_k])
        self.nbs, self.nbs_k = ar.alloc([8], I32, "nbs")
        self.dma(self.nbs[0:1, :], self.nb_d, [], [self.nbs_k])

        self.dyn_cnt = {"sp": 0, "pool": 0}
        for eng in ("sp", "pool"):
            def ldregs(e, eng=eng):
                for rel in (-2, -1, 1, 2):
                    reg = e.alloc_register("nbr%d" % (rel + 2))
                    e.reg_load(reg, self.nbs[0:1, rel + 2:rel + 3])
                    self.regv[(eng, rel)] = e.snap(reg)
                return None
            P.op(eng, ldregs, reads=[self.nbs_k], writes=[])
        self.ones, self.ones_k = ar.alloc([128], BF16, "ones")
        self.dve(lambda e: e.memset(self.ones, 1.0), [], [self.ones_k])
        self.esink, self.esink_k = ar.alloc([32], F32, "esink")
        o = CF["sink"]
        self.act(lambda e: e.activation(out=self.esink, in_=self.cf[:, o:o + 32], func=AF.Exp),
                 [self.cf_k], [self.esink_k])
        self.haloL, self.haloL_k = ar.alloc([16, 5], BF16, "haloL")
        self.haloR, self.haloR_k = ar.alloc([16, 5], BF16, "haloR")
        self.hb, self.hb_k = ar.alloc([2, 16, 5], BF16, "hb")
        self.mark = ar.off
        self.x0v = self.x0.rearrange("(k p) t -> k p t", p=128)

    def phase_begin(self):
        self.P.barrier()
        self.ar.off = self.mark

    def cfcol(self, name, idx):
        o = CF[name] + idx
        return self.cf[:, o:o + 1]

    def rmsnorm(self, xt, xt_k, nchunks, width, gname, gidx0, out, out_k, tmp, out_fn=None, post=None):
        psum, psk = self.psrot.next()
        n_feat = nchunks * 128
        for c in range(nchunks):
            sq, sqk = tmp["sq"].next()
            self.act(lambda e, c=c, sq=sq: e.activation(out=sq[:, 0:width], in_=xt[:, c, 0:width], func=AF.Square),
                     [xt_k], [sqk])
            self.pe(lambda e, c=c, sq=sq: e.matmul(psum[:, 0:width], self.ones[:, :], sq[:, 0:width],
                                                  start=(c == 0), stop=(c == nchunks - 1)),
                    [sqk, self.ones_k], [psk])
        rs, rsk = tmp["rstd"]
        self.act(lambda e: e.activation(out=rs[:, 0:width], in_=psum[:, 0:width], func=AF.Sqrt,
                                        scale=1.0 / n_feat, bias=EPS), [psk], [rsk])
        self.dve(lambda e: e.reciprocal(rs[:, 0:width], rs[:, 0:width]), [rsk], [rsk])
        for c in range(nchunks):
            g = self.cfcol(gname, gidx0 + c)
            if out_fn is not None:
                o_ap, o_k = out_fn(c)
            else:
                o_ap, o_k = out[:, c, 0:width], out_k
            self.dve(lambda e, c=c, g=g, o_ap=o_ap: e.scalar_tensor_tensor(out=o_ap, in0=xt[:, c, 0:width],
                                                                          scalar=g, in1=rs[:, 0:width],
                                                                          op0=ALU.mult, op1=ALU.mult),
                     [xt_k, rsk, self.cf_k], [o_k])
            if post is not None:
                post(c, o_ap, o_k)

    def lin_fm(self, wv, wkey, kc_n, nchunk, rhs_fn, rhs_keys, width, consumer, m0=0, mw=128):
        for m in range(nchunk):
            psum, psk = self.psrot.next()
            for kc in range(kc_n):
                self.pe(lambda e, m=m, kc=kc, psum=psum: e.matmul(psum[0:mw, 0:width], wv[:, kc, m * mw:(m + 1) * mw],
                                                                rhs_fn(kc), start=(kc == 0), stop=(kc == kc_n - 1)),
                        [wkey] + rhs_keys, [psk])
            consumer(m0 + m, psum, psk)

    def lin_tm(self, wv_cols_fn, wkey, kc_n, ncols, lhs_fn, lhs_keys, nsub, consumer):
        for s in range(nsub):
            psum, psk = self.psrot.next()
            for kc in range(kc_n):
                self.pe(lambda e, s=s, kc=kc, psum=psum: e.matmul(psum[:, 0:ncols], lhs_fn(kc, s), wv_cols_fn(kc),
                                                                start=(kc == 0), stop=(kc == kc_n - 1)),
                        [wkey] + lhs_keys, [psk])
            consumer(s, psum, psk)

    def alloc_common(self, wsize=8192, nw=3):
        ar = self.ar
        self.wbufs = Rot([ar.alloc([wsize], BF16, "wbuf%d" % i) for i in range(nw)])
        self.stg16 = Rot([ar.alloc([512], BF16, "stg16_%d" % i) for i in range(4)])
        self.sqr = Rot([ar.alloc([512], BF16, "sq%d" % i) for i in range(2)])
        self.rstd = ar.alloc([512], F32, "rstd")
        self.ntmp = dict(sq=self.sqr, rstd=self.rstd)

    def wload(self, name, kc_n, c0, ncols):
        ap, key = self.wbufs.next()
        dst = ap[:, 0:kc_n * ncols].rearrange("p (k n) -> p k n", n=ncols)
        src = self.wb[name][:, c0:c0 + ncols].rearrange("(k p) n -> p k n", p=128)
        self.dma(dst, src, self.wkeys[name], [key])
        return dst, key

    def run_jobs(self, jobs, depth=2):
        loaded = {}
        for i in range(min(depth, len(jobs))):
            loaded[i] = self.wload(*jobs[i][0:4])
        for i, job in enumerate(jobs):
            if i + depth < len(jobs):
                loaded[i + depth] = self.wload(*jobs[i + depth][0:4])
            wv, wkey = loaded.pop(i)
            job[4](wv, wkey)

    def load_x_tile(self, src, srcname, t, xt, xt_k):
        c0 = t * TT
        self.dma(xt[:, :, 0:TT], src[:, :, c0:c0 + TT].rearrange("k p t -> p k t"), [(srcname, t)], [xt_k])

    def store_stage(self, psum, psk, dst, dst_key, width=TT, npart=128):
        st, stk = self.stg16.next()
        self.evac(st[0:npart, 0:width], psum[0:npart, 0:width], [psk], [stk])
        self.dma(dst, st[0:npart, 0:width], [stk], [dst_key])

    def a_phase1(self, li):
        wn = "l%d_a_w_qkv" % li
        xsrc, xname = (self.x0v, "x0") if li == 0 else (self.XS, "XS")
        self.phase_begin()
        ar = self.ar
        self.alloc_common()
        xts = [ar.alloc([KC, TT], F32, "xt%d" % i) for i in range(2)]
        hts = [ar.alloc([KC, TT], BF16, "ht%d" % i) for i in range(2)]
        jobs = []
        ka0_keys = []
        va0_keys = []
        for t in range(NT):
            c0 = t * TT
            xt, xt_k = xts[t % 2]
            ht, ht_k = hts[t % 2]
            for blk in range(6):
                def fn(wv, wkey, t=t, c0=c0, blk=blk, xt=xt, xt_k=xt_k, ht=ht, ht_k=ht_k):
                    if blk == 0:
                        if t == 0:
                            self.load_x_tile(xsrc, xname, 0, xt, xt_k)
                        if t + 1 < NT:
                            self.load_x_tile(xsrc, xname, t + 1, *xts[(t + 1) % 2])
                        self.rmsnorm(xt, xt_k, KC, TT, "gvec", (2 * li) * 16, ht, ht_k, self.ntmp)
                    rhs = lambda kc: ht[:, kc, :]
                    if blk < 4:
                        def cons(m, psum, psk):
                            self.store_stage(psum, psk, self.QS[m, :, c0:c0 + TT], ("QS", m, t))
                        self.lin_fm(wv, wkey, KC, 4, rhs, [ht_k], TT, cons, m0=blk * 4)
                    elif blk == 4:
                        def cons(m, psum, psk):
                            st, stk = self.stg16.next()
                            self.evac(st[:, 0:TT], psum[:, 0:TT], [psk], [stk])
                            self.dma(self.KSa[m * 128:(m + 1) * 128, c0:c0 + TT], st[:, 0:TT], [stk], [("KS", m, t)])
                            for side in ((0,) if t == 0 else ((1,) if t == 1 else (0, 1))):
                                bc = ((0 if t < 2 else t - 1) * 2 + side) * 128
                                a0_ = 0 if side == 0 else TT - 128
                                key = ("KA0", m, t, side)
                                ka0_keys.append(key)
                                self.dma(self.KA0[m * 128:(m + 1) * 128, bc:bc + 128], st[:, a0_:a0_ + 128], [stk], [key])
                        self.lin_fm(wv, wkey, KC, 4, rhs, [ht_k], TT, cons)
                    else:
                        def cons(s, psum, psk):
                            st, stk = self.stg16.next()
                            self.evac(st[:, 0:TT], psum[:, 0:TT], [psk], [stk])
                            self.dma(self.VSa[c0 + s * 128:c0 + (s + 1) * 128, :], st[:, 0:TT], [stk], [("VS", s, t)])
                            for side in ((0,) if t == 0 else ((1,) if t == 1 else (0, 1))):
                                if (side == 0 and s == 0) or (side == 1 and s == 3):
                                    bc = ((0 if t < 2 else t - 1) * 2 + side) * 128
                                    key = ("VA0", t, side)
                                    va0_keys.append(key)
                                    self.dma(self.VA0[bc:bc + 128, :], st[:, 0:TT], [stk], [key])
                        self.lin_tm(lambda kc: wv[:, kc, 0:512], wkey, KC, 512,
                                    lambda kc, s: ht[:, kc, s * 128:(s + 1) * 128], [ht_k], 4, cons)
                jobs.append((wn, KC, blk * 512, 512, fn))
        self.run_jobs(jobs)
        ksend = [("KS", m, t) for m in range(4) for t in range(NT)]
        vsend = [("VS", s, t) for s in range(4) for t in range(NT)]
        K8v, kk, (kn, kr) = self.allgather(self.KA0, 512, 1280, "Ka", ka0_keys, "K8")
        V8v, vk, (vn, vr) = self.allgather(self.VA0, 1280, 512, "Va", va0_keys, "V8")
        for rel in (-1, 1):
            self.localize(self.NKa[rel].rearrange("(n i) c -> n i c", i=kr), K8v, rel, kk, [("NKa", rel)])
            self.localize(self.NVa[rel].rearrange("(n i) c -> n i c", i=vr), V8v, rel, vk, [("NVa", rel)])


    def run_attn(self, groups):
        flat = [(gi, bi, b) for gi, (pre, blocks) in enumerate(groups) for bi, b in enumerate(blocks)]
        if not flat:
            return
        groups[0][0]()
        called = {0}
        LA = self.attn_lookahead
        for k in range(min(LA, len(flat))):
            gk = flat[k][0]
            if gk not in called:
                groups[gk][0]()
                called.add(gk)
            flat[k][2][0]()
        for i, (gi, bi, b) in enumerate(flat):
            if bi == 0 and gi + 1 < len(groups) and (gi + 1) not in called:
                groups[gi + 1][0]()
                called.add(gi + 1)
            b[1]()
            if i + LA < len(flat):
                gk = flat[i + LA][0]
                if gk not in called:
                    groups[gk][0]()
                    called.add(gk)
                flat[i + LA][2][0]()
            b[2]()
            if b[3] is not None:
                b[3]()

    def a_phase3(self, li):
        self.phase_begin()
        ar = self.ar
        sl = slopes16()
        scale = HD ** -0.5
        sink_base = (0 if li == 0 else 1) * 16
        KTs = Rot([ar.alloc([768], BF16, "KTw%d" % i) for i in range(2)])
        Vws = Rot([ar.alloc([6, 128], BF16, "Vw%d" % i) for i in range(2)])
        QTs = Rot([ar.alloc([512], BF16, "QT%d" % i) for i in range(8)])
        tmps = Rot([ar.alloc([384], F32, "tmp%d" % i) for i in range(4)])
        PTs = Rot([ar.alloc([384], BF16, "PT%d" % i) for i in range(4)])
        recs = Rot([ar.alloc([512], F32, "rec%d" % i) for i in range(2)])
        oats = Rot([ar.alloc([512], BF16, "oat%d" % i) for i in range(3)])
        RA0 = CF["RA"]
        hh = 0
        sidx = 0
        groups = []
        for t in range(NT):
            c0 = t * TT
            tt_ = tile_type(t)
            pieces = window_pieces(t, 128)
            for kvh in range(4):
                KT, KT_k = KTs.next()
                Vw, Vw_k = Vws.next()
                qts = [QTs.next() for _ in range(4)]

                def pre(t=t, c0=c0, kvh=kvh, KT=KT, KT_k=KT_k, Vw=Vw, Vw_k=Vw_k, qts=qts, pieces=pieces):
                    w0 = 0
                    for (rel, lc, ln) in pieces:
                        if rel != 0:
                            assert ln == 128
                            seg_ = 0 if lc < PSEG else 1 + (lc - PSEG) // SSEG
                            lc = (seg_ * 2 + (1 if rel < 0 else 0)) * 128
                        ksrc = self.KSa if rel == 0 else self.NKa[rel]
                        vsrc = self.VSa if rel == 0 else self.NVa[rel]
                        self.dma(KT[:, w0:w0 + ln], ksrc[kvh * 128:(kvh + 1) * 128, lc:lc + ln], [], [KT_k])
                        b0 = w0 // 128
                        nb = ln // 128
                        self.dma(Vw[:, b0:b0 + nb, :],
                                 vsrc[lc:lc + ln, kvh * 128:(kvh + 1) * 128].rearrange("(b p) d -> p b d", p=128), [], [Vw_k])
                        w0 += ln
                    assert w0 == 768
                    for g4 in range(4):
                        h = kvh * 4 + g4
                        self.dma(qts[g4][0], self.QS[h, :, c0:c0 + TT], [], [qts[g4][1]])
                blocks = []
                for g4 in range(4):
                    h = kvh * 4 + g4
                    QT, QT_k = qts[g4]
                    num, num_k = self.ps[3 + hh % 2]
                    den, den_k = self.ps[5 + hh % 2]
                    hh += 1
                    for j in range(6):
                        q_lo = max(0, 128 * j - 256)
                        q_hi = min(512, 128 * j + 128)
                        n = q_hi - q_lo
                        cb = q_lo - (128 * j - 256)
                        S, S_k = self.ps[(0, 1, 2, 7)[sidx % 4]]
                        sidx += 1
                        tmp, tmp_k = tmps.next()
                        PT, PT_k = PTs.next()
                        coef = -sl[h] / scale
                        ecol = self.cfcol("EA", tt_ * 6 + j)

                        def fS(S=S, S_k=S_k, KT=KT, KT_k=KT_k, QT=QT, QT_k=QT_k, j=j, q_lo=q_lo, q_hi=q_hi, n=n):
                            self.pe(lambda e: e.matmul(S[:, 0:n], KT[:, 128 * j:128 * j + 128], QT[:, q_lo:q_hi], start=True, stop=True),
                                    [KT_k, QT_k], [S_k])

                        def fsoft(S=S, S_k=S_k, tmp=tmp, tmp_k=tmp_k, PT=PT, PT_k=PT_k, cb=cb, n=n, coef=coef, ecol=ecol):
                            self.dve(lambda e: e.scalar_tensor_tensor(out=tmp[:, 0:n], in0=self.cf[:, RA0 + cb:RA0 + cb + n], scalar=coef,
                                                                      in1=S[:, 0:n], op0=ALU.mult, op1=ALU.add),
                                     [S_k, self.cf_k], [tmp_k])
                            self.act(lambda e: e.activation(out=PT[:, 0:n], in_=tmp[:, 0:n], func=AF.Exp, bias=ecol, scale=scale),
                                     [tmp_k, self.cf_k], [PT_k])

                        def fPV(num=num, num_k=num_k, den=den, den_k=den_k, Vw=Vw, Vw_k=Vw_k, PT=PT, PT_k=PT_k, j=j, q_lo=q_lo, q_hi=q_hi, n=n):
                            self.pe(lambda e: e.matmul(num[:, q_lo:q_hi], Vw[:, j, :], PT[:, 0:n], start=(j == 0), stop=(j == 5),
                                                       skip_group_check=True), [Vw_k, PT_k], [num_k])
                            self.pe(lambda e: e.matmul(den[:, q_lo:q_hi], self.ones[:, :], PT[:, 0:n], start=(j == 0), stop=(j == 5),
                                                       skip_group_check=True), [self.ones_k, PT_k], [den_k])
                        post = None
                        if j == 5:
                            def post(num=num, num_k=num_k, den=den, den_k=den_k, h=h, t=t, c0=c0):
                                rec, rec_k = recs.next()
                                oat, oat_k = oats.next()
                                sk = sink_base + h
                                self.dve(lambda e: e.tensor_scalar(out=rec, in0=den, scalar1=self.esink[:, sk:sk + 1], scalar2=None, op0=ALU.add),
                                         [den_k, self.esink_k], [rec_k])
                                self.dve(lambda e: e.reciprocal(rec, rec), [rec_k], [rec_k])
                                self.dve(lambda e: e.tensor_tensor(out=oat, in0=num, in1=rec, op=ALU.mult), [num_k, rec_k], [oat_k])
                                self.dma(self.ATT[h, :, c0:c0 + TT], oat, [oat_k], [("ATT", h, t)])
                        blocks.append((fS, fsoft, fPV, post))
                groups.append((pre, blocks))
        self.run_attn(groups)

    def oproj_phase(self, li, wn):
        xsrc, xname = (self.x0v, "x0") if li == 0 else (self.XS, "XS")
        self.phase_begin()
        ar = self.ar
        self.alloc_common()
        xts = [ar.alloc([KC, TT], F32, "xt%d" % i) for i in range(2)]
        ats = [ar.alloc([KC, TT], BF16, "at%d" % i) for i in range(2)]
        h2s = [ar.alloc([KC, TT], BF16, "h2_0")] * 2
        jobs = []

        def load_tile(t):
            c0 = t * TT
            self.load_x_tile(xsrc, xname, t, *xts[t % 2])
            at, at_k = ats[t % 2]
            self.dma(at, self.ATT[:, :, c0:c0 + TT].rearrange("h p t -> p h t"), [("ATT", h, t) for h in range(NH)], [at_k])

        for t in range(NT):
            c0 = t * TT
            xt, xt_k = xts[t % 2]
            at, at_k = ats[t % 2]
            h2, h2_k = h2s[t % 2]
            for blk in range(4):
                def fn(wv, wkey, t=t, c0=c0, blk=blk, xt=xt, xt_k=xt_k, at=at, at_k=at_k, h2=h2, h2_k=h2_k):
                    if blk == 0:
                        if t == 0:
                            load_tile(0)
                        if t + 1 < NT:
                            load_tile(t + 1)

                    def cons(m, psum, psk):
                        self.dve(lambda e: e.tensor_tensor(out=xt[:, m, :], in0=xt[:, m, :], in1=psum[:, 0:TT], op=ALU.add),
                                 [psk, xt_k], [xt_k])
                    self.lin_fm(wv, wkey, KC, 4, lambda kc: at[:, kc, :], [at_k], TT, cons, m0=blk * 4)
                    if blk == 3:
                        self.dma(self.XM[:, :, c0:c0 + TT].rearrange("k p t -> p k t"), xt, [xt_k], [("XM", t)])
                        self.rmsnorm(xt, xt_k, KC, TT, "gvec", (2 * li + 1) * 16, h2, h2_k, self.ntmp)
                        self.dma(self.H2[:, :, c0:c0 + TT].rearrange("k p t -> p k t"), h2, [h2_k], [("H2", t)])
                        bl = []
                        if t == 0:
                            bl = [(0, 0, 0)]
                        elif t == 1:
                            bl = [(1, 0, TT - 1)]
                        else:
                            bl = [(0, t - 1, 0), (1, t - 1, TT - 1)]
                        for (side, seg, col) in bl:
                            self.dve(lambda e, side=side, seg=seg, col=col: e.tensor_copy(self.hb[:, side, :, seg], h2[:, :, col]),
                                     [h2_k], [self.hb_k], force=True)
                jobs.append((wn, KC, blk * 512, 512, fn))
        self.run_jobs(jobs)
        self.dma(self.HBs, self.hb.rearrange("p a b c -> p (a b c)"), [self.hb_k], ["HBs"])
        HBv, hk, (hn, hr) = self.allgather(self.HBs, 128, 160, "HB", ["HBs"], "HB8")
        tl, tl_k = ar.alloc([80], BF16, "tl")
        tr, tr_k = ar.alloc([80], BF16, "tr")
        self.localize(self.NHB[-1].rearrange("(n i) c -> n i c", i=hr), HBv, -1, hk, [("NHB", -1)])
        self.localize(self.NHB[1].rearrange("(n i) c -> n i c", i=hr), HBv, 1, hk, [("NHB", 1)])
        self.dma(tl, self.NHB[-1][:, 80:160], [("NHB", -1)], [tl_k])
        self.dma(tr, self.NHB[1][:, 0:80], [("NHB", 1)], [tr_k])
        fl = CF["fl"]
        self.dve(lambda e: e.tensor_scalar(out=self.haloL.rearrange("p a b -> p (a b)"), in0=tl,
                                           scalar1=self.cf[:, fl:fl + 1], scalar2=None, op0=ALU.mult),
                 [tl_k, self.cf_k], [self.haloL_k])
        self.dve(lambda e: e.tensor_scalar(out=self.haloR.rearrange("p a b -> p (a b)"), in0=tr,
                                           scalar1=self.cf[:, fl + 1:fl + 2], scalar2=None, op0=ALU.mult),
                 [tr_k, self.cf_k], [self.haloR_k])

    def ffn_phase(self, li, last):
        self.phase_begin()
        if li % 3 == 0 and li + 1 < self.n_layers:
            self.convert_layer(li + 1)
        ar = self.ar
        self.alloc_common(wsize=5632, nw=3)
        win = "l%d_ffn_w_in" % li
        wout = "l%d_ffn_w_out" % li
        g, _ = ar.alloc([FC, TT], BF16, "g")
        h2es = [ar.alloc([KC, TT + 2], BF16, "h2e%d" % i) for i in range(2)]
        xt, xt_k = ar.alloc([KC, TT], F32, "xt")
        xmcs = Rot([ar.alloc([512], F32, "xmc%d" % i) for i in range(2)])
        aexts = Rot([ar.alloc([TT + 2], F32, "aext%d" % i) for i in range(2)])
        cbs = Rot([ar.alloc([512], F32, "cb%d" % i) for i in range(2)])
        gls = Rot([ar.alloc([512], F32, "gl%d" % i) for i in range(4)])
        cw0 = CF["convw"] + li * FC * 3
        cb0 = CF["convb"] + li * FC

        def load_h2e(t):
            c0 = t * TT
            h2e, k = h2es[t % 2]
            if t == 0:
                self.dma(h2e[:, :, 1:TT + 2], self.H2[:, :, c0:c0 + TT + 1].rearrange("k p t -> p k t"),
                         [("H2", 0), ("H2", 1)], [k])
                self.dve(lambda e: e.tensor_copy(h2e[:, :, 0], self.haloL[:, :, 0]), [self.haloL_k], [k])
            elif t == 1:
                self.dma(h2e[:, :, 0:TT + 1], self.H2[:, :, c0 - 1:c0 + TT].rearrange("k p t -> p k t"),
                         [("H2", 0), ("H2", 1)], [k])
                self.dve(lambda e: e.tensor_copy(h2e[:, :, TT + 1], self.haloR[:, :, 0]), [self.haloR_k], [k])
            else:
                self.dma(h2e[:, :, 1:TT + 1], self.H2[:, :, c0:c0 + TT].rearrange("k p t -> p k t"), [("H2", t)], [k])
                self.dve(lambda e: e.tensor_copy(h2e[:, :, 0], self.haloL[:, :, t - 1]), [self.haloL_k], [k])
                self.dve(lambda e: e.tensor_copy(h2e[:, :, TT + 1], self.haloR[:, :, t - 1]), [self.haloR_k], [k])

        jobs = []
        for t in range(NT):
            c0 = t * TT
            h2e, h2e_k = h2es[t % 2]
            glbuf = {}
            for jb in range(FC // 2):
                def gate(wv, wkey, t=t, jb=jb, h2e=h2e, h2e_k=h2e_k, glbuf=glbuf):
                    if jb == 0:
                        if t == 0:
                            load_h2e(0)
                        if t + 1 < NT:
                            load_h2e(t + 1)
                    for jj in range(2):
                        j = jb * 2 + jj
                        a_ps, a_k = self.psrot.next()
                        ah_ps, ah_k = self.psrot.next()
                        for kc in range(KC):
                            self.pe(lambda e, kc=kc, jj=jj, a_ps=a_ps: e.matmul(a_ps[:, 0:TT], wv[:, kc, jj * 128:(jj + 1) * 128],
                                                                              h2e[:, kc, 1:TT + 1], start=(kc == 0), stop=(kc == KC - 1)),
                                    [wkey, h2e_k], [a_k])
                        for kc in range(KC):
                            self.pe(lambda e, kc=kc, jj=jj, ah_ps=ah_ps: e.matmul(ah_ps[:, 0:2], wv[:, kc, jj * 128:(jj + 1) * 128],
                                                                                h2e[:, kc, 0:TT + 2:TT + 1], start=(kc == 0), stop=(kc == KC - 1)),
                                    [wkey, h2e_k], [ah_k])
                        aext, ax_k = aexts.next()
                        cb, cb_k = cbs.next()
                        gl, gl_k = gls.next()
                        glbuf[j] = (gl, gl_k)
                        self.act(lambda e, aext=aext, a_ps=a_ps: e.activation(out=aext[:, 1:TT + 1], in_=a_ps[:, 0:TT], func=AF.Copy),
                                 [a_k], [ax_k])
                        self.act(lambda e, aext=aext, ah_ps=ah_ps: e.activation(out=aext[:, 0:TT + 2:TT + 1], in_=ah_ps[:, 0:2], func=AF.Copy),
                                 [ah_k], [ax_k])
                        w0c = self.cf[:, cw0 + j * 3 + 0:cw0 + j * 3 + 1]
                        w1c = self.cf[:, cw0 + j * 3 + 1:cw0 + j * 3 + 2]
                        w2c = self.cf[:, cw0 + j * 3 + 2:cw0 + j * 3 + 3]
                        bc = self.cf[:, cb0 + j:cb0 + j + 1]
                        self.act(lambda e, cb=cb, a_ps=a_ps, w1c=w1c, bc=bc: e.activation(out=cb, in_=a_ps[:, 0:TT], func=AF.Identity,
                                                                                         bias=bc, scale=w1c),
                                 [a_k, self.cf_k], [cb_k])
                        self.dve(lambda e, cb=cb, aext=aext, w0c=w0c: e.scalar_tensor_tensor(out=cb, in0=aext[:, 0:TT], scalar=w0c, in1=cb,
                                                                                            op0=ALU.mult, op1=ALU.add),
                                 [ax_k, cb_k, self.cf_k], [cb_k])
                        self.dve(lambda e, cb=cb, aext=aext, w2c=w2c: e.scalar_tensor_tensor(out=cb, in0=aext[:, 2:TT + 2], scalar=w2c, in1=cb,
                                                                                            op0=ALU.mult, op1=ALU.add),
                                 [ax_k, cb_k, self.cf_k], [cb_k])
                        self.act(lambda e, gl=gl, cb=cb: e.activation(out=gl, in_=cb, func=AF.Gelu), [cb_k], [gl_k])

                def val(wv, wkey, t=t, jb=jb, h2e=h2e, h2e_k=h2e_k, glbuf=glbuf):
                    for jj in range(2):
                        j = jb * 2 + jj
                        gl, gl_k = glbuf[j]

                        def cons(m, psum, psk, j=j, gl=gl, gl_k=gl_k):
                            self.dve(lambda e: e.tensor_tensor(out=g[:, j, :], in0=gl, in1=psum[:, 0:TT], op=ALU.mult),
                                     [gl_k, psk], [("g", j)])
                        self.lin_fm(wv[:, :, jj * 128:(jj + 1) * 128], wkey, KC, 1, lambda kc: h2e[:, kc, 1:TT + 1], [h2e_k], TT, cons)
                jobs.append((win, KC, jb * 256, 256, gate))
                jobs.append((win, KC, DFF + jb * 256, 256, val))
            for m in range(KC):
                def outp(wv, wkey, t=t, c0=c0, m=m):
                    xmc, xmc_k = xmcs.next()
                    self.dma(xmc, self.XM[m, :, c0:c0 + TT], [("XM", t)], [xmc_k])

                    def cons(mm, psum, psk):
                        self.dve(lambda e: e.tensor_tensor(out=xt[:, m, :], in0=xmc, in1=psum[:, 0:TT], op=ALU.add),
                                 [psk, xmc_k], [xt_k])
                    self.lin_fm(wv, wkey, FC, 1, lambda kc: g[:, kc, :], [("g", j) for j in range(FC)], TT, cons)
                    if m == KC - 1:
                        if not last:
                            self.dma(self.XS[:, :, c0:c0 + TT].rearrange("k p t -> p k t"), xt, [xt_k], [("XS", t)])
                        else:
                            def out_fn(c):
                                return xmcs.next()

                            def post(c, o_ap, o_k):
                                self.dma(self.yT[c * 128:(c + 1) * 128, c0:c0 + TT], o_ap, [o_k], [("yT", c, t)])
                            self.rmsnorm(xt, xt_k, KC, TT, "gvec", 8 * 16, None, None, self.ntmp, out_fn=out_fn, post=post)
                jobs.append((wout, FC, m * 128, 128, outp))
        self.run_jobs(jobs)


def build_program(n_layers=4, stop_mid=False):
    B = Builder(n_layers, stop_mid)
    import os
    stage = int(os.environ.get("DBG_STAGE", "99"))
    B.convert_layer(0)
    B.setup()
    done = False
    for li in range(n_layers):
        kind = li % 3
        if kind == 0:
            if stage >= 1:
                B.a_phase1(li)
            if stage >= 2:
                B.a_phase3(li)
            if stage >= 3:
                B.oproj_phase(li, "l%d_a_w_o" % li)
        elif kind == 1:
            B.b_phase1(li)
            B.b_phase3(li)
            B.oproj_phase(li, "l%d_b_w_o" % li)
        else:
            B.c_phase1(li)
            B.c_phase3(li)
            B.oproj_phase(li, "l%d_c_w_o" % li)
        if stop_mid and li == n_layers - 1:
            B.phase_begin()
            for t in range(NT):
                c0 = t * TT
                B.dma(B.yT[:, c0:c0 + TT].rearrange("(k p) t -> k p t", p=128), B.XM[:, :, c0:c0 + TT], [("XM", t)], [("yT", t)])
            done = True
            break
        B.ffn_phase(li, last=(li == 3))
    if not done and n_layers < 4:
        B.phase_begin()
        for t in range(NT):
            c0 = t * TT
            B.dma(B.yT[:, c0:c0 + TT].rearrange("(k p) t -> k p t", p=128), B.XS[:, :, c0:c0 + TT], [("XS", t)], [("yT", t)])
    B.P.barrier()
    B.P.finalize()
    B.st.close()
    return B.nc


def _vec_cols(v, nch):
    return np.ascontiguousarray(np.asarray(v, np.float32).reshape(nch, 128).T)


def host_inputs(inputs, n_layers=4):
    f32 = np.float32
    xp = np.asarray(inputs["x_prompt"], f32)
    xs = np.asarray(inputs["x_sample"], f32)
    cf = np.zeros((128, NCF), f32)
    for i in range(4):
        cf[:, CF["gvec"] + (2 * i) * 16:CF["gvec"] + (2 * i + 1) * 16] = _vec_cols(inputs["l%d_mix_norm" % i], 16)
        cf[:, CF["gvec"] + (2 * i + 1) * 16:CF["gvec"] + (2 * i + 2) * 16] = _vec_cols(inputs["l%d_ffn_norm" % i], 16)
        cw = np.asarray(inputs["l%d_ffn_conv_w" % i], f32)
        cwl = cw.T.reshape(FC, 128, 3).transpose(1, 0, 2).reshape(128, FC * 3)
        cf[:, CF["convw"] + i * FC * 3:CF["convw"] + (i + 1) * FC * 3] = cwl
        cf[:, CF["convb"] + i * FC:CF["convb"] + (i + 1) * FC] = _vec_cols(inputs["l%d_ffn_conv_b" % i], FC)
    cf[:, CF["gvec"] + 128:CF["gvec"] + 144] = _vec_cols(inputs["final_norm"], 16)
    cf[:, CF["sink"]:CF["sink"] + 16] = np.asarray(inputs["l0_a_sink"], f32)[None, :]
    cf[:, CF["sink"] + 16:CF["sink"] + 32] = np.asarray(inputs["l3_a_sink"], f32)[None, :]
    cf[:, CF["cnorm"]:CF["cnorm"] + 4] = _vec_cols(inputs["l2_c_q_norm"], 4)
    cf[:, CF["cnorm"] + 4:CF["cnorm"] + 8] = _vec_cols(inputs["l2_c_kv_norm"], 4)
    p = np.arange(128)[:, None]
    c = np.arange(384)[None, :]
    ra = np.abs(c - 128 - p).astype(f32)
    ra[ra > 128] = BIGR
    cf[:, CF["RA"]:CF["RA"] + 384] = ra
    c = np.arange(256)[None, :]
    rb = np.abs(c - 64 - p).astype(f32)
    rb[rb > 64] = BIGR
    cf[:, CF["RB"]:CF["RB"] + 256] = rb
    inv = ROPE_THETA_ ** (-np.arange(0, 64, 2, dtype=np.float32) / 64.0)
    maps = []
    wfull = {}
    for core in range(NCORES):
        cfc = cf.copy()
        for tt_, t in ((0, 0), (1, 1), (2, 2)):
            pcs = window_pieces(t, 128)
            w0 = 0
            for (rel, lc, ln) in pcs:
                valid = 0 <= core + rel <= 7
                for b in range(w0 // 128, (w0 + ln) // 128):
                    cfc[:, CF["EA"] + tt_ * 6 + b] = 0.0 if valid else NEG
                w0 += ln
            for gi, (window, d) in enumerate(B_GROUPS):
                pcs = window_pieces(t, 64 * d)
                nj = (TT + 128 * d) // d
                colv = np.zeros(nj, f32)
                w0 = 0
                for (rel, lc, ln) in pcs:
                    valid = 0 <= core + rel <= 7
                    colv[w0 // d:(w0 + ln) // d] = 0.0 if valid else NEG
                    w0 += ln
                for b in range(5):
                    seg = colv[128 * b:128 * (b + 1)]
                    col = np.zeros(128, f32)
                    col[:len(seg)] = seg
                    cfc[:, CF["EB"] + (tt_ * 3 + gi) * 5 + b] = col
        cfc[:, CF["fl"]] = 1.0 if core > 0 else 0.0
        cfc[:, CF["fl"] + 1] = 1.0 if core < 7 else 0.0
        xl = np.concatenate([xp[0, PSEG * core:PSEG * (core + 1)]] + [xs[b, SSEG * core:SSEG * (core + 1)] for b in range(4)], axis=0)
        x0T = np.ascontiguousarray(xl.T)
        pos = np.concatenate([np.arange(PSEG * core, PSEG * (core + 1))] + [np.arange(SSEG * core, SSEG * (core + 1))] * 4).astype(np.float32)
        ang = pos[None, :] * inv[:, None]
        rope = np.stack([np.cos(ang), np.sin(ang)]).astype(f32)
        nb = np.zeros((1, 8), np.int32)
        for rel in range(-2, 3):
            nb[0, rel + 2] = min(7, max(0, core + rel))
        m = {"x0T": x0T, "cf32": cfc, "rope": rope, "nb": nb}
        for name, k, n in weight_specs():
            if int(name[1]) >= n_layers:
                continue
            w = inputs[name]
            m[name] = wfull.setdefault(name, np.ascontiguousarray(np.asarray(w, f32)))
        maps.append(m)
    return maps


ROPE_THETA_ = 10000.0
_NC_CACHE = {}


def run(inputs, n_layers=4, stop_mid=False):
    key = (n_layers, stop_mid)
    if key not in _NC_CACHE:
        _NC_CACHE[key] = build_program(n_layers, stop_mid)
    nc = _NC_CACHE[key]
    maps = host_inputs(inputs, n_layers)
    res = run_bass_kernel_spmd(nc, maps, core_ids=list(range(NCORES)))
    yp = np.zeros((1, 8192, D), np.float32)
    ys = np.zeros((4, 4096, D), np.float32)
    for core in range(NCORES):
        yT = np.asarray(res.results[core]["yT"])
        yp[0, PSEG * core:PSEG * (core + 1), :] = yT[:, 0:PSEG].T
        for b in range(4):
            ys[b, SSEG * core:SSEG * (core + 1), :] = yT[:, PSEG + SSEG * b:PSEG + SSEG * (b + 1)].T
    return yp, ys


def kernel(**inputs):
    return run(inputs, 4, False)


def _b_init(self):
    dt = self.nc.dram_tensor
    if hasattr(self, "KSb"):
        return
    self.KSb = [dt("KSb%d" % g, [2048, LT], BF16).ap() for g in range(3)]
    self.VSb = [dt("VSb%d" % g, [LT, 2048], BF16).ap() for g in range(3)]
    rels = {0: (), 1: (-1, 1), 2: (-2, -1, 1, 2)}
    self.NKb = [{rel: dt("NKb%d_%d" % (g, rel + 2), [2048, LT], BF16).ap() for rel in rels[g]} for g in range(3)]
    self.NVb = [{rel: dt("NVb%d_%d" % (g, rel + 2), [LT, 2048], BF16).ap() for rel in rels[g]} for g in range(3)]
    self.KB0 = dt("KB0", [2048, 640], BF16).ap()
    self.VB0 = dt("VB0", [640, 2048], BF16).ap()
    self.NKB0 = {rel: dt("NKB0_%d" % (rel + 2), [2048, 640], BF16).ap() for rel in (-1, 1)}
    self.NVB0 = {rel: dt("NVB0_%d" % (rel + 2), [640, 2048], BF16).ap() for rel in (-1, 1)}


def _b_phase1(self, li):
    _b_init(self)
    wn = "l%d_b_w_qkv" % li
    xsrc, xname = (self.XS, "XS")
    self.phase_begin()
    ar = self.ar
    self.alloc_common()
    xts = [ar.alloc([KC, TT], F32, "xt%d" % i) for i in range(2)]
    hts = [ar.alloc([KC, TT], BF16, "ht%d" % i) for i in range(2)]
    ti = 0
    self.kb0_keys = []
    self.vb0_keys = []

    def b0_seg(t):
        return 0 if t < 2 else t - 1

    def b0_sides(t):
        return (0,) if t == 0 else ((1,) if t == 1 else (0, 1))
    for g in (2, 1, 0):
        jobs = []
        for t in range(NT):
            c0 = t * TT
            for b12 in range(12):
                blk = g * 12 + b12
                kind = b12 // 4
                hb4 = b12 % 4
                slot = ti % 2
                xt, xt_k = xts[slot]
                ht, ht_k = hts[slot]
                nslot = (ti + 1) % 2

                def fn(wv, wkey, t=t, c0=c0, b12=b12, g=g, kind=kind, hb4=hb4, xt=xt, xt_k=xt_k, ht=ht, ht_k=ht_k, nslot=nslot, ti=ti):
                    if b12 == 0:
                        if ti == 0:
                            self.load_x_tile(xsrc, xname, t, xt, xt_k)
                        if ti + 1 < 3 * NT:
                            self.load_x_tile(xsrc, xname, (t + 1) % NT, *xts[nslot])
                        self.rmsnorm(xt, xt_k, KC, TT, "gvec", (2 * li) * 16, ht, ht_k, self.ntmp)
                    rhs = lambda kc: ht[:, kc, :]
                    if kind == 0:
                        def cons(m, psum, psk):
                            self.store_stage(psum, psk, self.QS[g * 16 + m, :, c0:c0 + TT], ("QS", g * 16 + m, t))
                        self.lin_fm(wv, wkey, KC, 4, rhs, [ht_k], TT, cons, m0=hb4 * 4)
                    elif kind == 1:
                        def cons(m, psum, psk):
                            st, stk = self.stg16.next()
                            self.evac(st[:, 0:TT], psum[:, 0:TT], [psk], [stk])
                            self.dma(self.KSb[g][m * 128:(m + 1) * 128, c0:c0 + TT], st[:, 0:TT], [stk], [("KS", g, m, t)])
                            if g == 0:
                                for side in b0_sides(t):
                                    bc = (b0_seg(t) * 2 + side) * 64
                                    a0 = 0 if side == 0 else TT - 64
                                    key = ("KB0", m, t, side)
                                    self.kb0_keys.append(key)
                                    self.dma(self.KB0[m * 128:(m + 1) * 128, bc:bc + 64], st[:, a0:a0 + 64], [stk], [key])
                        self.lin_fm(wv, wkey, KC, 4, rhs, [ht_k], TT, cons, m0=hb4 * 4)
                    else:
                        def cons(s_, psum, psk):
                            st, stk = self.stg16.next()
                            self.evac(st[:, 0:TT], psum[:, 0:TT], [psk], [stk])
                            self.dma(self.VSb[g][c0 + s_ * 128:c0 + (s_ + 1) * 128, hb4 * 512:(hb4 + 1) * 512], st[:, 0:TT], [stk],
                                     [("VS", g, hb4, s_, t)])
                            if g == 0:
                                for side in b0_sides(t):
                                    if (side == 0 and s_ == 0) or (side == 1 and s_ == 3):
                                        bc = (b0_seg(t) * 2 + side) * 64
                                        p0 = 0 if side == 0 else 64
                                        key = ("VB0", hb4, t, side)
                                        self.vb0_keys.append(key)
                                        self.dma(self.VB0[bc:bc + 64, hb4 * 512:(hb4 + 1) * 512], st[p0:p0 + 64, 0:TT], [stk], [key])
                        self.lin_tm(lambda kc: wv[:, kc, 0:512], wkey, KC, 512,
                                    lambda kc, s_: ht[:, kc, s_ * 128:(s_ + 1) * 128], [ht_k], 4, cons)
                jobs.append((wn, KC, blk * 512, 512, fn))
            ti += 1
        self.run_jobs(jobs)
        if g == 0:
            K8v, kk, (kn, kr) = self.allgather(self.KB0, 2048, 640, "KB0", self.kb0_keys, "K8B0", bg=True)
            V8v, vk, (vn, vr) = self.allgather(self.VB0, 640, 2048, "VB0", self.vb0_keys, "V8B0", bg=True)
            for rel in (-1, 1):
                self.localize(self.NKB0[rel].rearrange("(n i) c -> n i c", i=kr), K8v, rel, kk, [("NKb", 0, rel)], eng="pool", bg=True)
                self.localize(self.NVB0[rel].rearrange("(n i) c -> n i c", i=vr), V8v, rel, vk, [("NVb", 0, rel)], eng="pool", bg=True)
            continue
        ksend = [("KS", g, m, t) for m in range(16) for t in range(NT)]
        vsend = [("VS", g, hb4, s_, t) for hb4 in range(4) for s_ in range(4) for t in range(NT)]
        K8v, kk, (kn, kr) = self.allgather(self.KSb[g], 2048, LT, "Kb%d" % g, ksend, "K8b%d" % g, bg=True)
        V8v, vk, (vn, vr) = self.allgather(self.VSb[g], LT, 2048, "Vb%d" % g, vsend, "V8b%d" % g, bg=True)
        for rel in self.NKb[g].keys():
            self.localize(self.NKb[g][rel].rearrange("(n i) c -> n i c", i=kr), K8v, rel, kk, [("NKb", g, rel)], eng="pool", bg=True)
            self.localize(self.NVb[g][rel].rearrange("(n i) c -> n i c", i=vr), V8v, rel, vk, [("NVb", g, rel)], eng="pool", bg=True)


def _b_phase3(self, li):
    self.phase_begin()
    if li + 1 < self.n_layers:
        self.convert_layer(li + 1)
    ar = self.ar
    sl = slopes16()
    scale = HD ** -0.5
    KTs = Rot([ar.alloc([2560], BF16, "KTw%d" % i) for i in range(3)])
    Vws = Rot([ar.alloc([4096], BF16, "Vw%d" % i) for i in range(3)])
    QTs = Rot([ar.alloc([512], BF16, "QT%d" % i) for i in range(3)])
    tmps = Rot([ar.alloc([256], F32, "tmp%d" % i) for i in range(4)])
    PTs = Rot([ar.alloc([256], BF16, "PT%d" % i) for i in range(4)])
    NUMs = Rot([ar.alloc([512], F32, "NUM%d" % i) for i in range(3)])
    DENs = Rot([ar.alloc([512], F32, "DEN%d" % i) for i in range(3)])
    if not hasattr(self, "NUMD"):
        self.NUMD = self.nc.dram_tensor("NUMD", [NH, 128, LT], F32).ap()
        self.DEND = self.nc.dram_tensor("DEND", [NH, 128, LT], F32).ap()
    GORDER = (2, 1, 0)
    oats = Rot([ar.alloc([512], BF16, "oat%d" % i) for i in range(3)])
    RB0 = CF["RB"]
    cnt = 0
    sidx = 0
    groups = []
    for g in GORDER:
        (window, d) = B_GROUPS[g]
        for t in range(NT):
            c0 = t * TT
            tt_ = tile_type(t)
            for h in range(NH):
                NUM, NUM_k = NUMs.next()
                DEN, DEN_k = DENs.next()
                nq = TT // d
                nj = nq + 128
                nblk = (nj + 127) // 128
                W = TT + 128 * d
                KTf, KT_k = KTs.next()
                Vwf, Vw_k = Vws.next()
                KT = KTf[:, 0:W]
                Vw = Vwf[:, 0:d * nblk * 128].rearrange("p (r b x) -> p r b x", r=d, b=nblk)
                QT, QT_k = QTs.next()

                def pre(t=t, c0=c0, h=h, g=g, d=d, W=W, KT=KT, KT_k=KT_k, Vw=Vw, Vw_k=Vw_k, QT=QT, QT_k=QT_k,
                        NUM=NUM, NUM_k=NUM_k, DEN=DEN, DEN_k=DEN_k):
                    self.dma(QT, self.QS[g * 16 + h, :, c0:c0 + TT], [], [QT_k])
                    if g != GORDER[0]:
                        self.dma(NUM, self.NUMD[h, :, c0:c0 + TT], [("NUMD", h, t)], [NUM_k])
                        self.dma(DEN, self.DEND[h, :, c0:c0 + TT], [("DEND", h, t)], [DEN_k])
                    w0 = 0
                    for (rel, lc, ln) in window_pieces(t, 64 * d):
                        if g == 0 and rel != 0:
                            assert ln == 64
                            seg_ = 0 if lc < PSEG else 1 + (lc - PSEG) // SSEG
                            lc = (seg_ * 2 + (1 if rel < 0 else 0)) * 64
                            ksrc, vsrc = self.NKB0[rel], self.NVB0[rel]
                        else:
                            ksrc = self.KSb[g] if rel == 0 else self.NKb[g][rel]
                            vsrc = self.VSb[g] if rel == 0 else self.NVb[g][rel]
                        kdep = [] if rel == 0 else [("NKb", g, rel)]
                        vdep = [] if rel == 0 else [("NVb", g, rel)]
                        self.dma(KT[:, w0:w0 + ln], ksrc[h * 128:(h + 1) * 128, lc:lc + ln], kdep, [KT_k])
                        ja, je = w0 // d, (w0 + ln) // d
                        j = ja
                        while j < je:
                            b = j // 128
                            jn = min(je, (b + 1) * 128)
                            p0 = j % 128
                            n = jn - j
                            r0 = lc + (j - ja) * d
                            self.dma(Vw[p0:p0 + n, :, b, :],
                                     vsrc[r0:r0 + n * d, h * 128:(h + 1) * 128].rearrange("(jj r) x -> jj r x", r=d), vdep, [Vw_k])
                            j = jn
                        w0 += ln
                    assert w0 == W
                num, num_k = self.ps[3 + cnt % 2]
                den, den_k = self.ps[5 + cnt % 2]
                cnt += 1
                coef = -sl[h] * d / scale
                blocks = []
                for r in range(d):
                    for b in range(nblk):
                        nk = min(128, nj - 128 * b)
                        q_lo = max(0, 128 * b - 128)
                        q_hi = min(nq, 128 * b + nk)
                        n = q_hi - q_lo
                        cb = q_lo - (128 * b - 128)
                        S, S_k = self.ps[(0, 1, 2, 7)[sidx % 4]]
                        sidx += 1
                        tmp, tmp_k = tmps.next()
                        PT, PT_k = PTs.next()
                        k0 = 128 * b * d + r
                        q0 = q_lo * d + r
                        eo = CF["EB"] + (tt_ * 3 + g) * 5 + b
                        first = (r == 0 and b == 0)
                        last = (r == d - 1 and b == nblk - 1)
                        o0 = r * nq + q_lo

                        def fS(S=S, S_k=S_k, KT=KT, KT_k=KT_k, QT=QT, QT_k=QT_k, k0=k0, q0=q0, nk=nk, n=n, d=d):
                            self.pe(lambda e: e.matmul(S[0:nk, 0:n], KT[:, k0:k0 + (nk - 1) * d + 1:d], QT[:, q0:q0 + (n - 1) * d + 1:d],
                                                       start=True, stop=True), [KT_k, QT_k], [S_k])

                        def fsoft(S=S, S_k=S_k, tmp=tmp, tmp_k=tmp_k, PT=PT, PT_k=PT_k, cb=cb, n=n, nk=nk, coef=coef, eo=eo):
                            self.dve(lambda e: e.scalar_tensor_tensor(out=tmp[0:nk, 0:n], in0=self.cf[0:nk, RB0 + cb:RB0 + cb + n], scalar=coef,
                                                                      in1=S[0:nk, 0:n], op0=ALU.mult, op1=ALU.add),
                                     [S_k, self.cf_k], [tmp_k])
                            self.act(lambda e: e.activation(out=PT[0:nk, 0:n], in_=tmp[0:nk, 0:n], func=AF.Exp,
                                                            bias=self.cf[0:nk, eo:eo + 1], scale=scale),
                                     [tmp_k, self.cf_k], [PT_k])

                        def fPV(num=num, num_k=num_k, den=den, den_k=den_k, Vw=Vw, Vw_k=Vw_k, PT=PT, PT_k=PT_k, r=r, b=b, nk=nk, n=n,
                                o0=o0, first=first, last=last):
                            self.pe(lambda e: e.matmul(num[:, o0:o0 + n], Vw[0:nk, r, b, :], PT[0:nk, 0:n], start=first, stop=last,
                                                       skip_group_check=True), [Vw_k, PT_k], [num_k])
                            self.pe(lambda e: e.matmul(den[:, o0:o0 + n], self.ones[0:nk, :], PT[0:nk, 0:n], start=first, stop=last,
                                                       skip_group_check=True), [self.ones_k, PT_k], [den_k])
                        post = None
                        if last:
                            def post(g=g, d=d, h=h, t=t, c0=c0, num=num, num_k=num_k, den=den, den_k=den_k,
                                     NUM=NUM, NUM_k=NUM_k, DEN=DEN, DEN_k=DEN_k):
                                for (ACC, ACC_k, src, src_k) in ((NUM, NUM_k, num, num_k), (DEN, DEN_k, den, den_k)):
                                    if d == 1:
                                        accv, srcv = ACC, src[:, 0:TT]
                                    else:
                                        accv = ACC.rearrange("p (q r) -> p r q", r=d)
                                        srcv = src[:, 0:TT].rearrange("p (r q) -> p r q", r=d)
                                    if g == GORDER[0]:
                                        self.dve(lambda e, accv=accv, srcv=srcv: e.tensor_copy(accv, srcv), [src_k], [ACC_k])
                                    else:
                                        self.dve(lambda e, accv=accv, srcv=srcv: e.tensor_tensor(out=accv, in0=accv, in1=srcv, op=ALU.add),
                                                 [src_k, ACC_k], [ACC_k])
                                if g == GORDER[-1]:
                                    oat, oat_k = oats.next()
                                    self.dve(lambda e: e.reciprocal(DEN, DEN), [DEN_k], [DEN_k])
                                    self.dve(lambda e: e.tensor_tensor(out=oat, in0=NUM, in1=DEN, op=ALU.mult), [NUM_k, DEN_k], [oat_k])
                                    self.dma(self.ATT[h, :, c0:c0 + TT], oat, [oat_k], [("ATT", h, t)])
                                else:
                                    self.dma(self.NUMD[h, :, c0:c0 + TT], NUM, [NUM_k], [("NUMD", h, t)])
                                    self.dma(self.DEND[h, :, c0:c0 + TT], DEN, [DEN_k], [("DEND", h, t)])
                        blocks.append((fS, fsoft, fPV, post))
                groups.append((pre, blocks))
    self.run_attn(groups)


Builder.b_phase1 = _b_phase1
Builder.b_phase3 = _b_phase3


C_SCALE = (128 + 64) ** -0.5


def _c_init(self):
    dt = self.nc.dram_tensor
    if hasattr(self, "KSc"):
        return
    self.KSc = dt("KSc", [2112, LT], BF16).ap()
    self.VSc = dt("VSc", [LT, 2048], BF16).ap()


def _rope(self, x1, x1_k, x2, x2_k, cs, sn, csn_k, o1, o2, o_k, tmps):
    (ta, ta_k), (tb, tb_k) = tmps.next(), tmps.next()
    P32 = slice(0, 32)
    self.dve(lambda e: e.tensor_tensor(out=ta[P32, :], in0=x1[P32, 0:TT], in1=cs[P32, :], op=ALU.mult), [x1_k, csn_k], [ta_k])
    self.dve(lambda e: e.tensor_tensor(out=tb[P32, :], in0=x2[P32, 0:TT], in1=sn[P32, :], op=ALU.mult), [x2_k, csn_k], [tb_k])
    self.dve(lambda e: e.tensor_tensor(out=o1[P32, :], in0=ta[P32, :], in1=tb[P32, :], op=ALU.subtract), [ta_k, tb_k], [o_k])
    (tc, tc_k), (td, td_k) = tmps.next(), tmps.next()
    self.dve(lambda e: e.tensor_tensor(out=tc[P32, :], in0=x2[P32, 0:TT], in1=cs[P32, :], op=ALU.mult), [x2_k, csn_k], [tc_k])
    self.dve(lambda e: e.tensor_tensor(out=td[P32, :], in0=x1[P32, 0:TT], in1=sn[P32, :], op=ALU.mult), [x1_k, csn_k], [td_k])
    self.dve(lambda e: e.tensor_tensor(out=o2[P32, :], in0=tc[P32, :], in1=td[P32, :], op=ALU.add), [tc_k, td_k], [o_k])


def _c_phase1(self, li):
    _c_init(self)
    wd, wuq, wukv = "l%d_c_w_down" % li, "l%d_c_w_uq" % li, "l%d_c_w_ukv" % li
    self.phase_begin()
    ar = self.ar
    self.alloc_common()
    xt, xt_k = ar.alloc([KC, TT], F32, "xt")
    hts = [ar.alloc([KC, TT], BF16, "ht%d" % i) for i in range(2)]
    c32, c32_k = ar.alloc([4, TT], F32, "c32")
    cqn, cqn_k = ar.alloc([4, TT], BF16, "cqn")
    ckvn, ckvn_k = ar.alloc([4, TT], BF16, "ckvn")
    cs, csn_k = ar.alloc([TT], F32, "cos")
    sn, _ = ar.alloc([TT], F32, "sin")
    rtmps = Rot([ar.alloc([TT], F32, "rt%d" % i) for i in range(4)])
    ropo = Rot([(ar.alloc([TT], BF16, "ro1_%d" % i), ar.alloc([TT], BF16, "ro2_%d" % i)) for i in range(2)])
    jobs = []
    for t in range(NT):
        c0 = t * TT
        ht, ht_k = hts[t % 2]

        def j_down(wv, wkey, which, t=t, c0=c0, ht=ht, ht_k=ht_k):
            if which == 0:
                self.load_x_tile(self.XS, "XS", t, xt, xt_k)
                self.dma(cs[0:32, :], self.rope_d[0, :, c0:c0 + TT], [], [csn_k])
                self.dma(sn[0:32, :], self.rope_d[1, :, c0:c0 + TT], [], [csn_k])
                self.rmsnorm(xt, xt_k, KC, TT, "gvec", (2 * li) * 16, ht, ht_k, self.ntmp)
            rhs = lambda kc: ht[:, kc, :]
            if which < 2:
                def cons(m, psum, psk):
                    self.evac(c32[:, m, :], psum[:, 0:TT], [psk], [c32_k])
                self.lin_fm(wv, wkey, KC, 4, rhs, [ht_k], TT, cons)
                dst, dst_k = (cqn, cqn_k) if which == 0 else (ckvn, ckvn_k)
                self.rmsnorm(c32, c32_k, 4, TT, "cnorm", which * 4, dst, dst_k, self.ntmp)
            else:
                got = {}

                def cons(m, psum, psk):
                    got[m] = (psum, psk)
                self.lin_fm(wv, wkey, KC, 2, rhs, [ht_k], TT, cons, mw=32)
                (o1, o1_k), (o2, o2_k) = ropo.next()
                _rope(self, got[0][0], got[0][1], got[1][0], got[1][1], cs, sn, csn_k, o1, o2, o1_k, rtmps)
                self.dma(self.KSc[2048:2080, c0:c0 + TT], o1[0:32, :], [o1_k], [("KSr", 0, t)])
                self.dma(self.KSc[2080:2112, c0:c0 + TT], o2[0:32, :], [o1_k], [("KSr", 1, t)])
        jobs.append((wd, KC, 0, 512, lambda wv, wkey, f=j_down: f(wv, wkey, 0)))
        jobs.append((wd, KC, 512, 512, lambda wv, wkey, f=j_down: f(wv, wkey, 1)))
        jobs.append((wd, KC, 1024, 64, lambda wv, wkey, f=j_down: f(wv, wkey, 2)))
        for hf in range(2):
            def j_uq(wv, wkey, hf=hf, t=t, c0=c0):
                for hl in range(8):
                    h = hf * 8 + hl
                    base = hl * 192
                    psum, psk = self.psrot.next()
                    p1, p1k = self.psrot.next()
                    p2, p2k = self.psrot.next()
                    for (pp, ppk, off, mw) in ((psum, psk, base, 128), (p1, p1k, base + 128, 32), (p2, p2k, base + 160, 32)):
                        for kc in range(4):
                            self.pe(lambda e, pp=pp, off=off, mw=mw, kc=kc: e.matmul(pp[0:mw, 0:TT], wv[:, kc, off:off + mw], cqn[:, kc, :],
                                                                                 start=(kc == 0), stop=(kc == 3)),
                                    [wkey, cqn_k], [ppk])
                    self.store_stage(psum, psk, self.QS[h, :, c0:c0 + TT], ("QS", h, t))
                    (o1, o1_k), (o2, o2_k) = ropo.next()
                    _rope(self, p1, p1k, p2, p2k, cs, sn, csn_k, o1, o2, o1_k, rtmps)
                    self.dma(self.QR[h, 0:32, c0:c0 + TT], o1[0:32, :], [o1_k], [("QR", h, 0, t)])
                    self.dma(self.QR[h, 32:64, c0:c0 + TT], o2[0:32, :], [o1_k], [("QR", h, 1, t)])
            jobs.append((wuq, 4, hf * 1536, 1536, j_uq))
        for hf in range(2):
            def j_ukv(wv, wkey, hf=hf, t=t, c0=c0):
                for hl in range(8):
                    h = hf * 8 + hl
                    psum, psk = self.psrot.next()
                    for kc in range(4):
                        self.pe(lambda e, psum=psum, hl=hl, kc=kc: e.matmul(psum[:, 0:TT], wv[:, kc, hl * 256:hl * 256 + 128], ckvn[:, kc, :],
                                                                          start=(kc == 0), stop=(kc == 3)),
                                [wkey, ckvn_k], [psk])
                    self.store_stage(psum, psk, self.KSc[h * 128:(h + 1) * 128, c0:c0 + TT], ("KS", h, t))
                wvv = wv.rearrange("p k (h x) -> p k h x", x=256)
                for q4 in range(2):
                    h0 = hf * 8 + q4 * 4
                    for s in range(4):
                        psum, psk = self.psrot.next()
                        for kc in range(4):
                            self.pe(lambda e, psum=psum, q4=q4, s=s, kc=kc: e.matmul(psum[:, 0:512], ckvn[:, kc, s * 128:(s + 1) * 128],
                                                                                  wvv[:, kc, q4 * 4:q4 * 4 + 4, 128:256],
                                                                                  start=(kc == 0), stop=(kc == 3)),
                                    [wkey, ckvn_k], [psk])
                        self.store_stage(psum, psk, self.VSc[c0 + s * 128:c0 + (s + 1) * 128, h0 * 128:(h0 + 4) * 128], ("VS", h0, s, t))
            jobs.append((wukv, 4, hf * 2048, 2048, j_ukv))
    self.run_jobs(jobs)
    ksend = [("KS", h, t) for h in range(NH) for t in range(NT)] + [("KSr", i, t) for i in range(2) for t in range(NT)]
    vsend = [("VS", h0, s, t) for h0 in range(0, 16, 4) for s in range(4) for t in range(NT)]
    self.cK8v, self.cKk, (kn, kr) = self.allgather(self.KSc, 2112, LT, "Kc", ksend, "K8c", rpc_force=64)
    self.cV8v, self.cVk, (vn, vr) = self.allgather(self.VSc, LT, 2048, "Vc", vsend, "V8c")
    assert kr == 64 and vr == 128


def _c_phase3(self, li):
    self.phase_begin()
    if li + 1 < self.n_layers:
        self.convert_layer(li + 1)
    ar = self.ar
    K8v, V8v = self.cK8v, self.cV8v
    KTs = Rot([ar.alloc([8192], BF16, "cKT%d" % i) for i in range(2)])
    Vps = Rot([ar.alloc([64, 128], BF16, "cVp%d" % i) for i in range(2)])
    KRs = Rot([ar.alloc([8192], BF16, "cKR%d" % i) for i in range(2)])
    QNs = Rot([ar.alloc([512], BF16, "cQN%d" % i) for i in range(3)])
    QRs = Rot([ar.alloc([512], BF16, "cQR%d" % i) for i in range(3)])
    PTs = Rot([ar.alloc([512], BF16, "cPT%d" % i) for i in range(4)])
    recs = Rot([ar.alloc([512], F32, "crec%d" % i) for i in range(2)])
    oats = Rot([ar.alloc([512], BF16, "coat%d" % i) for i in range(3)])
    seqs = [(PSEG, 0, [0, 1])] + [(SSEG, PSEG + SSEG * b, [2 + b]) for b in range(4)]
    cnt = 0
    sidx = 0
    groups = []
    for (seg, lc0, tiles) in seqs:
        L = seg * 8
        nkb = L // 128
        KR, KR_k = KRs.next()
        for h in range(NH):
            KT, KT_k = KTs.next()
            Vp, Vp_k = Vps.next()

            def pre(seg=seg, lc0=lc0, h=h, KR=KR, KR_k=KR_k, KT=KT, KT_k=KT_k, Vp=Vp, Vp_k=Vp_k):
                if h == 0:
                    for r in range(8):
                        self.dma(KR[0:64, r * seg:(r + 1) * seg], K8v[r, 32, :, lc0:lc0 + seg], [], [KR_k])
                nb = seg // 128
                for r in range(8):
                    for n2 in range(2):
                        self.dma(KT[64 * n2:64 * n2 + 64, r * seg:(r + 1) * seg], K8v[r, 2 * h + n2, :, lc0:lc0 + seg], [], [KT_k])
                    self.dma(Vp[:, r * nb:(r + 1) * nb, :],
                             V8v[r, lc0 // 128:lc0 // 128 + nb, :, h * 128:(h + 1) * 128].rearrange("n p d -> p n d"), [], [Vp_k])
            blocks = []
            for t in tiles:
                c0 = t * TT
                QN, QN_k = QNs.next()
                QRt, QR_k = QRs.next()
                num, num_k = self.ps[4 + cnt % 2]
                den, den_k = self.ps[6 + cnt % 2]
                cnt += 1
                for kb in range(nkb):
                    S, S_k = self.ps[sidx % 4]
                    sidx += 1
                    PT, PT_k = PTs.next()

                    def fS(S=S, S_k=S_k, KT=KT, KT_k=KT_k, KR=KR, KR_k=KR_k, QN=QN, QN_k=QN_k, QRt=QRt, QR_k=QR_k, kb=kb, h=h, c0=c0):
                        if kb == 0:
                            self.dma(QN, self.QS[h, :, c0:c0 + TT], [], [QN_k])
                            self.dma(QRt[0:64, :], self.QR[h, :, c0:c0 + TT], [], [QR_k])
                        self.pe(lambda e: e.matmul(S[:, 0:TT], KT[:, kb * 128:(kb + 1) * 128], QN, start=True, stop=False),
                                [KT_k, QN_k], [S_k])
                        self.pe(lambda e: e.matmul(S[:, 0:TT], KR[0:64, kb * 128:(kb + 1) * 128], QRt[0:64, :], start=False, stop=True),
                                [KR_k, QR_k], [S_k])

                    def fsoft(S=S, S_k=S_k, PT=PT, PT_k=PT_k):
                        self.act(lambda e: e.activation(out=PT, in_=S[:, 0:TT], func=AF.Exp, scale=C_SCALE), [S_k], [PT_k])

                    def fPV(num=num, num_k=num_k, den=den, den_k=den_k, Vp=Vp, Vp_k=Vp_k, PT=PT, PT_k=PT_k, kb=kb, nkb=nkb):
                        self.pe(lambda e: e.matmul(num[:, 0:TT], Vp[:, kb, :], PT, start=(kb == 0), stop=(kb == nkb - 1)),
                                [Vp_k, PT_k], [num_k])
                        self.pe(lambda e: e.matmul(den[:, 0:TT], self.ones[:, :], PT, start=(kb == 0), stop=(kb == nkb - 1)),
                                [self.ones_k, PT_k], [den_k])
                    post = None
                    if kb == nkb - 1:
                        def post(num=num, num_k=num_k, den=den, den_k=den_k, h=h, t=t, c0=c0):
                            rec, rec_k = recs.next()
                            oat, oat_k = oats.next()
                            self.dve(lambda e: e.reciprocal(rec, den[:, 0:TT]), [den_k], [rec_k])
                            self.dve(lambda e: e.tensor_tensor(out=oat, in0=num[:, 0:TT], in1=rec, op=ALU.mult), [num_k, rec_k], [oat_k])
                            self.dma(self.ATT[h, :, c0:c0 + TT], oat, [oat_k], [("ATT", h, t)])
                    blocks.append((fS, fsoft, fPV, post))
            groups.append((pre, blocks))
    self.run_attn(groups)


Builder.c_phase1 = _c_phase1
Builder.c_phase3 = _c_phase3
```
